# Optimizing a Trainium2 kernel written in Bass

```python
import math
import jax
import jax.numpy as jnp
from jax import lax
import numpy as np

D_MODEL = 2048
BATCH = 4
SEQ = 2048
DEPTH = 4
DEC_BATCH = 32
DEC_SEQ = 4
PAST_LEN = 16384
PAGE_SIZE = 128

N_MIXERS = 4
LPM = DEPTH // N_MIXERS
PLE_DIM = 256
ALPHA = (2 * DEPTH) ** 0.25
BETA = (8 * DEPTH) ** -0.25
LN_EPS = 1e-5
N_BUCKETS = 32
REL_MAX_DIST = 2048
N_HEADS = 32
HEAD_DIM = 64
ATT_WIDTH = N_HEADS * HEAD_DIM
WINDOW = 128
KV_A = 4
A_KV = KV_A * HEAD_DIM
A_IN = 2 * ATT_WIDTH + 2 * A_KV
KV_D = 8
D_KV = KV_D * HEAD_DIM
IDX_HEADS = 16
IDX_DIM = 128
TOPK_MAX = 256
Q_BLOCK = 128
D_IN = 2 * ATT_WIDTH + 2 * D_KV + IDX_HEADS * IDX_DIM + IDX_DIM + IDX_HEADS
S5_WIDTH = D_MODEL
S5_GROUP = 16
S5_GROUPS = S5_WIDTH // S5_GROUP
S5_STATE = 64
GDN_QK_HEADS = 16
GDN_V_HEADS = 32
GDN_DK = 128
GDN_DV = 128
GDN_CONV = 4
GDN_CHUNK = 64
GDN_QK_WIDTH = GDN_QK_HEADS * GDN_DK
GDN_V_WIDTH = GDN_V_HEADS * GDN_DV
GDN_CONV_CH = 2 * GDN_QK_WIDTH + GDN_V_WIDTH
GDN_IN = GDN_CONV_CH + GDN_V_WIDTH + 2 * GDN_V_HEADS

kernel_name = 'hybrid_swa_s5_gdn_dsa_step'

F32 = jnp.float32


def layer_norm(x, g, b):
    xf = x.astype(F32)
    mu = jnp.mean(xf, axis=-1, keepdims=True)
    var = jnp.mean(jnp.square(xf - mu), axis=-1, keepdims=True)
    return ((xf - mu) * lax.rsqrt(var + LN_EPS) * g.astype(F32) + b.astype(F32)).astype(x.dtype)


def post_norm_ple(x, h, g, b, p, w_gate, w_ple):
    x = layer_norm(ALPHA * x + h, g, b)
    return x + jax.nn.sigmoid(x @ w_gate) * (p @ w_ple)


def rel_bucket(dist):
    n = jnp.maximum(dist, 0)
    exact = N_BUCKETS // 2
    logb = exact + (jnp.log(jnp.maximum(n, exact).astype(F32) / exact)
                    / math.log(REL_MAX_DIST / exact) * (N_BUCKETS - exact)).astype(jnp.int32)
    return jnp.where(n < exact, n, jnp.minimum(logb, N_BUCKETS - 1))


def head_bias(rel_bias, dist, n_kv, g):
    b = jnp.moveaxis(rel_bias[rel_bucket(dist)].astype(F32), -1, -3)
    return b.reshape(b.shape[:-3] + (n_kv, g) + b.shape[-2:])


def masked_softmax(logits, mask, sink=None):
    logits = jnp.where(mask, logits, -jnp.inf)
    m = jnp.max(logits, axis=-1, keepdims=True)
    if sink is not None:
        m = jnp.maximum(m, sink)
    e = jnp.exp(logits - m)
    den = jnp.sum(e, axis=-1, keepdims=True)
    if sink is not None:
        den = den + jnp.exp(sink - m)
    return e / den


def take_rows(rows, idx):
    return jax.vmap(lambda r, i: r[i])(rows, idx)


def window_attend(q, k, v, qpos, kpos, sinks, rel_bias):
    n_kv, g = q.shape[-3], q.shape[-2]
    dist = qpos[..., :, None] - kpos[..., None, :]
    mask = (dist >= 0) & (dist < WINDOW) & (kpos[..., None, :] >= 0)
    logits = jnp.einsum('...qhgd,...khd->...hgqk', q, k).astype(F32) * (HEAD_DIM ** -0.5)
    logits = logits + head_bias(rel_bias, dist, n_kv, g)
    sink = sinks.astype(F32).reshape(n_kv, g, 1, 1)
    p = masked_softmax(logits, mask[..., None, None, :, :], sink)
    return jnp.einsum('...hgqk,...khd->...qhgd', p.astype(v.dtype), v)


def swa_mixer(x, kv_cache, start, w_in, sinks, w_out, rel_bias):
    Bn, L, _ = x.shape
    G = N_HEADS // KV_A
    q, k, v, z = jnp.split(x @ w_in, [ATT_WIDTH, ATT_WIDTH + A_KV, ATT_WIDTH + 2 * A_KV], axis=-1)
    q = q.reshape(Bn, L, KV_A, G, HEAD_DIM)
    k = k.reshape(Bn, L, KV_A, HEAD_DIM)
    v = v.reshape(Bn, L, KV_A, HEAD_DIM)
    if kv_cache is None:
        nb = L // WINDOW
        qb = q.reshape(Bn, nb, WINDOW, KV_A, G, HEAD_DIM)
        kb = k.reshape(Bn, nb, WINDOW, KV_A, HEAD_DIM)
        vb = v.reshape(Bn, nb, WINDOW, KV_A, HEAD_DIM)
        prev = lambda t: jnp.concatenate([jnp.zeros_like(t[:, :1]), t[:, :-1]], axis=1)
        kk = jnp.concatenate([prev(kb), kb], axis=2)
        vv = jnp.concatenate([prev(vb), vb], axis=2)
        qpos = jnp.arange(L).reshape(nb, WINDOW)
        kpos = jnp.concatenate([qpos - WINDOW, qpos], axis=1)
        o = window_attend(qb, kk, vv, qpos, kpos, sinks, rel_bias)
        new_kv = jnp.stack([k[:, L - WINDOW:], v[:, L - WINDOW:]], axis=2)
    else:
        kk = jnp.concatenate([kv_cache[:, :, 0].astype(k.dtype), k], axis=1)
        vv = jnp.concatenate([kv_cache[:, :, 1].astype(v.dtype), v], axis=1)
        qpos = start + jnp.arange(L)
        kpos = start - WINDOW + jnp.arange(WINDOW + L)
        o = window_attend(q, kk, vv, qpos, kpos, sinks, rel_bias)
        new_kv = jnp.stack([kk[:, -WINDOW:], vv[:, -WINDOW:]], axis=2)
    o = o.reshape(Bn, L, ATT_WIDTH)
    return (o * jax.nn.silu(z)) @ w_out, new_kv


def _linear_combine(l, r):
    return (l[0] * r[0], r[0] * l[1] + r[1])


def s5_mixer(x, h0, w_in, a_re, a_im, b_re, b_im, c_re, c_im, d_skip, log_dt, w_glu, w_out):
    Bn, L, _ = x.shape
    u, z = jnp.split(x @ w_in, 2, axis=-1)
    uf = u.astype(F32).reshape(Bn, L, S5_GROUPS, S5_GROUP)
    a = lax.complex(a_re.astype(F32), a_im.astype(F32))
    dt = jnp.exp(log_dt.astype(F32))[:, None]
    a_bar = jnp.exp(a * dt)
    b_bar = ((a_bar - 1.0) / a)[..., None] * lax.complex(b_re.astype(F32), b_im.astype(F32))
    c = lax.complex(c_re.astype(F32), c_im.astype(F32))
    bu = jnp.einsum('gpc,blgc->blgp', b_bar, uf.astype(jnp.complex64))
    if h0 is not None:
        h0c = lax.complex(h0[..., 0].astype(F32), h0[..., 1].astype(F32))
        bu = bu.at[:, 0].add(a_bar * h0c)
    a_seq = jnp.broadcast_to(a_bar, bu.shape)
    _, h = lax.associative_scan(_linear_combine, (a_seq, bu), axis=1)
    y = jnp.einsum('gcp,blgp->blgc', c, h).real + d_skip.astype(F32).reshape(S5_GROUPS, S5_GROUP) * uf
    y = jax.nn.gelu(y.reshape(Bn, L, S5_WIDTH))
    y = y * jax.nn.sigmoid(y @ w_glu.astype(F32))
    out = (y.astype(x.dtype) * jax.nn.silu(z)) @ w_out
    h_last = h[:, -1]
    return out, jnp.stack([h_last.real, h_last.imag], axis=-1)


def l2_normalize(t, eps=1e-6):
    tf = t.astype(F32)
    return tf * lax.rsqrt(jnp.sum(tf * tf, axis=-1, keepdims=True) + eps)


def chunk_gated_delta(q, k, v, g, beta, S0):
    Bn, L, H, dk = k.shape
    dv = v.shape[-1]
    C = min(GDN_CHUNK, L)
    n = -(-L // C)
    pad = n * C - L

    def chunks(t):
        t = jnp.pad(t, [(0, 0), (0, pad)] + [(0, 0)] * (t.ndim - 2))
        t = t.reshape((Bn, n, C) + t.shape[2:])
        return jnp.moveaxis(t, 3, 2)

    qc, kc, vc, gc, bc = [chunks(t) for t in (q, k, v, g, beta)]
    gam = jnp.cumsum(gc, axis=-1)
    pos = jnp.arange(C)
    causal = pos[:, None] >= pos[None, :]
    strict = pos[:, None] > pos[None, :]
    decay = jnp.exp(jnp.where(causal, gam[..., :, None] - gam[..., None, :], -jnp.inf))
    kk = jnp.einsum('bnhid,bnhjd->bnhij', kc, kc)
    tri = jnp.eye(C, dtype=F32) + jnp.where(strict, bc[..., :, None] * kk * decay, 0.0)
    rhs = jnp.concatenate([bc[..., None] * vc, (bc * jnp.exp(gam))[..., None] * kc], axis=-1)
    sol = lax.linalg.triangular_solve(tri, rhs, left_side=True, lower=True, unit_diagonal=True)
    u, w = sol[..., :dv], sol[..., dv:]
    qk = jnp.einsum('bnhid,bnhjd->bnhij', qc, kc) * decay
    q_dec = qc * jnp.exp(gam)[..., None]
    k_dec = kc * jnp.exp(gam[..., -1:] - gam)[..., None]
    g_tot = jnp.exp(gam[..., -1])

    def step(S, xs):
        u_c, w_c, qk_c, qd_c, kd_c, gt_c = xs
        v_new = u_c - jnp.einsum('bhcd,bhde->bhce', w_c, S)
        o = jnp.einsum('bhcd,bhde->bhce', qd_c, S) + jnp.einsum('bhij,bhje->bhie', qk_c, v_new)
        S = S * gt_c[..., None, None] + jnp.einsum('bhcd,bhce->bhde', kd_c, v_new)
        return S, o

    xs = tuple(jnp.moveaxis(t, 1, 0) for t in (u, w, qk, q_dec, k_dec, g_tot))
    S, o = lax.scan(step, S0, xs)
    o = o.transpose(1, 0, 3, 2, 4).reshape(Bn, n * C, H, dv)[:, :L]
    return o, S


def gdn_mixer(x, S0, conv_buf, w_in, conv_w, a_log, dt_bias, norm_w, w_out):
    Bn, L, _ = x.shape
    qkv, z, a, b = jnp.split(x @ w_in, [GDN_CONV_CH, GDN_CONV_CH + GDN_V_WIDTH,
                                        GDN_CONV_CH + GDN_V_WIDTH + GDN_V_HEADS], axis=-1)
    if conv_buf is None:
        conv_buf = jnp.zeros((Bn, GDN_CONV - 1, GDN_CONV_CH), qkv.dtype)
    xx = jnp.concatenate([conv_buf.astype(qkv.dtype), qkv], axis=1)
    conv = jax.nn.silu(sum(xx[:, j:j + L] * conv_w[j] for j in range(GDN_CONV)))
    new_buf = xx[:, L:]
    q, k, v = jnp.split(conv, [GDN_QK_WIDTH, 2 * GDN_QK_WIDTH], axis=-1)
    rep = GDN_V_HEADS // GDN_QK_HEADS
    q = jnp.repeat(l2_normalize(q.reshape(Bn, L, GDN_QK_HEADS, GDN_DK)), rep, axis=2) * (GDN_DK ** -0.5)
    k = jnp.repeat(l2_normalize(k.reshape(Bn, L, GDN_QK_HEADS, GDN_DK)), rep, axis=2)
    v = v.reshape(Bn, L, GDN_V_HEADS, GDN_DV).astype(F32)
    beta = jax.nn.sigmoid(b.astype(F32))
    g = -jnp.exp(a_log.astype(F32)) * jax.nn.softplus(a.astype(F32) + dt_bias.astype(F32))
    if S0 is None:
        S0 = jnp.zeros((Bn, GDN_V_HEADS, GDN_DK, GDN_DV), F32)
    o, S = chunk_gated_delta(q, k, v, g, beta, S0.astype(F32))
    of = o * lax.rsqrt(jnp.mean(o * o, axis=-1, keepdims=True) + 1e-6) * norm_w.astype(F32)
    of = of * jax.nn.silu(z.astype(F32).reshape(Bn, L, GDN_V_HEADS, GDN_DV))
    return of.reshape(Bn, L, GDN_V_WIDTH).astype(x.dtype) @ w_out, S, new_buf


def dsa_project(x, w_in):
    Bn, L, _ = x.shape
    c0 = ATT_WIDTH
    c1 = c0 + D_KV
    c2 = c1 + D_KV
    c3 = c2 + ATT_WIDTH
    c4 = c3 + IDX_HEADS * IDX_DIM
    c5 = c4 + IDX_DIM
    q, k, v, z, qi, ki, wi = jnp.split(x @ w_in, [c0, c1, c2, c3, c4, c5], axis=-1)
    q = q.reshape(Bn, L, KV_D, N_HEADS // KV_D, HEAD_DIM)
    kv = jnp.stack([k.reshape(Bn, L, KV_D, HEAD_DIM), v.reshape(Bn, L, KV_D, HEAD_DIM)], axis=2)
    qi = qi.reshape(Bn, L, IDX_HEADS, IDX_DIM)
    return q, kv, z, qi, ki, wi


def index_scores(qi, ki, wi):
    s = jnp.einsum('bthd,bsd->bths', qi, ki).astype(F32) * (IDX_DIM ** -0.5)
    return jnp.einsum('bths,bth->bts', jax.nn.relu(s), wi.astype(F32) * (IDX_HEADS ** -0.5))


def gathered_attend(q, kvs, qpos, kpos, valid, rel_bias):
    n_kv, g = q.shape[-3], q.shape[-2]
    ks, vs = kvs[..., 0, :, :], kvs[..., 1, :, :]
    logits = jnp.einsum('bthgd,btkhd->bhgtk', q, ks).astype(F32) * (HEAD_DIM ** -0.5)
    dist = qpos[None, :, None] - kpos
    logits = logits + head_bias(rel_bias, dist, n_kv, g)
    p = masked_softmax(logits, (valid & (dist >= 0))[:, None, None])
    return jnp.einsum('bhgtk,btkhd->bthgd', p.astype(vs.dtype), vs)


def dsa_prompt(x, w_in, w_out, rel_bias):
    Bn, L, _ = x.shape
    q, kv, z, qi, ki, wi = dsa_project(x, w_in)
    topk = min(TOPK_MAX, L // 4)
    key_pos = jnp.arange(L)

    def block(t0):
        qb = lax.dynamic_slice_in_dim(q, t0, Q_BLOCK, axis=1)
        qib = lax.dynamic_slice_in_dim(qi, t0, Q_BLOCK, axis=1)
        wib = lax.dynamic_slice_in_dim(wi, t0, Q_BLOCK, axis=1)
        qpos = t0 + jnp.arange(Q_BLOCK)
        sc = index_scores(qib, ki, wib)
        sc = jnp.where(key_pos[None, None, :] <= qpos[None, :, None], sc, -jnp.inf)
        vals, idx = lax.top_k(sc, topk)
        return gathered_attend(qb, take_rows(kv, idx), qpos, idx, vals > -jnp.inf, rel_bias)

    o = lax.map(block, jnp.arange(0, L, Q_BLOCK))
    o = jnp.moveaxis(o, 0, 1).reshape(Bn, L, ATT_WIDTH)
    return (o * jax.nn.silu(z)) @ w_out, kv, ki


def dsa_sample(x, kv_pool, kidx_pool, layer, page_table, w_in, w_out, rel_bias):
    Bd, L, _ = x.shape
    n_pages = page_table.shape[1]
    past = n_pages * PAGE_SIZE
    q, kv, z, qi, ki, wi = dsa_project(x, w_in)
    ki_past = kidx_pool[layer, page_table].reshape(Bd, past, IDX_DIM).astype(ki.dtype)
    ki_all = jnp.concatenate([ki_past, ki], axis=1)
    total = past + L
    topk = min(TOPK_MAX, total // 4)
    qpos = past + jnp.arange(L)
    sc = index_scores(qi, ki_all, wi)
    sc = jnp.where(jnp.arange(total)[None, None, :] <= qpos[None, :, None], sc, -jnp.inf)
    vals, idx = lax.top_k(sc, topk)
    pidx = jnp.minimum(idx, past - 1)
    phys = jnp.take_along_axis(page_table, (pidx // PAGE_SIZE).reshape(Bd, -1), axis=1).reshape(idx.shape)
    kv_past = kv_pool[layer, phys, pidx % PAGE_SIZE].astype(kv.dtype)
    kv_new = take_rows(kv, jnp.clip(idx - past, 0, L - 1))
    kv_sel = jnp.where((idx >= past)[..., None, None, None], kv_new, kv_past)
    o = gathered_attend(q, kv_sel, qpos, idx, vals > -jnp.inf, rel_bias).reshape(Bd, L, ATT_WIDTH)
    return (o * jax.nn.silu(z)) @ w_out, kv, ki


def setup_inputs(seed: int = 0) -> dict:
    key = jax.random.key(seed)
    ks = iter(jax.random.split(key, 64))
    nrm = lambda shape, scale=1.0: jax.random.normal(next(ks), shape, F32) * scale
    n_pages = PAST_LEN // PAGE_SIZE
    n_pool = (5 * DEC_BATCH * n_pages) // 4
    page_table = jax.random.permutation(next(ks), n_pool)[:DEC_BATCH * n_pages].reshape(DEC_BATCH, n_pages).astype(jnp.int32)
    s5_log_dt = jax.random.uniform(next(ks), (LPM, S5_GROUPS), F32, math.log(1e-3), math.log(1e-1))
    gdn_a_log = jnp.log(jax.random.uniform(next(ks), (LPM, GDN_V_HEADS), F32, 1.0, 16.0))
    dt = jnp.exp(jax.random.uniform(next(ks), (LPM, GDN_V_HEADS), F32, math.log(1e-3), math.log(1e-1)))
    gdn_dt_bias = dt + jnp.log(-jnp.expm1(-dt))
    a_im = jnp.pi * jnp.arange(S5_STATE, dtype=F32)[None, None, :] + nrm((LPM, S5_GROUPS, S5_STATE), 0.01)
    return {
        'x_prompt': nrm((BATCH, SEQ, D_MODEL)),
        'x_sample': nrm((DEC_BATCH, DEC_SEQ, D_MODEL)),
        'cache_a_kv': nrm((LPM, DEC_BATCH, WINDOW, 2, KV_A, HEAD_DIM)),
        'state_s5': nrm((LPM, DEC_BATCH, S5_GROUPS, S5_STATE, 2), 0.5),
        'state_gdn': nrm((LPM, DEC_BATCH, GDN_V_HEADS, GDN_DK, GDN_DV), 0.05),
        'state_gdn_conv': nrm((LPM, DEC_BATCH, GDN_CONV - 1, GDN_CONV_CH)),
        'cache_d_kv': nrm((LPM, n_pool, PAGE_SIZE, 2, KV_D, HEAD_DIM)),
        'cache_d_kidx': nrm((LPM, n_pool, PAGE_SIZE, IDX_DIM)),
        'page_table': page_table,
        'p_prompt': nrm((DEPTH, BATCH, SEQ, PLE_DIM)),
        'p_sample': nrm((DEPTH, DEC_BATCH, DEC_SEQ, PLE_DIM)),
        'rel_bias': nrm((N_BUCKETS, N_HEADS), 0.2),
        'ln_g': 1.0 + nrm((DEPTH, D_MODEL), 0.02),
        'ln_b': nrm((DEPTH, D_MODEL), 0.02),
        'ple_gate_w': nrm((DEPTH, D_MODEL, D_MODEL), D_MODEL ** -0.5),
        'ple_w': nrm((DEPTH, PLE_DIM, D_MODEL), 0.5 * PLE_DIM ** -0.5),
        'a_w_in': nrm((LPM, D_MODEL, A_IN), D_MODEL ** -0.5),
        'a_sinks': nrm((LPM, N_HEADS), 0.5),
        'a_w_out': nrm((LPM, ATT_WIDTH, D_MODEL), BETA * ATT_WIDTH ** -0.5),
        's5_w_in': nrm((LPM, D_MODEL, 2 * S5_WIDTH), D_MODEL ** -0.5),
        's5_a_re': -0.5 + nrm((LPM, S5_GROUPS, S5_STATE), 0.01),
        's5_a_im': a_im,
        's5_b_re': nrm((LPM, S5_GROUPS, S5_STATE, S5_GROUP), (2 * S5_GROUP) ** -0.5),
        's5_b_im': nrm((LPM, S5_GROUPS, S5_STATE, S5_GROUP), (2 * S5_GROUP) ** -0.5),
        's5_c_re': nrm((LPM, S5_GROUPS, S5_GROUP, S5_STATE), (2 * S5_STATE) ** -0.5),
        's5_c_im': nrm((LPM, S5_GROUPS, S5_GROUP, S5_STATE), (2 * S5_STATE) ** -0.5),
        's5_d': nrm((LPM, S5_WIDTH)),
        's5_log_dt': s5_log_dt,
        's5_w_glu': nrm((LPM, S5_WIDTH, S5_WIDTH), S5_WIDTH ** -0.5),
        's5_w_out': nrm((LPM, S5_WIDTH, D_MODEL), BETA * S5_WIDTH ** -0.5),
        'gdn_w_in': nrm((LPM, D_MODEL, GDN_IN), D_MODEL ** -0.5),
        'gdn_conv_w': nrm((LPM, GDN_CONV, GDN_CONV_CH), GDN_CONV ** -0.5),
        'gdn_a_log': gdn_a_log,
        'gdn_dt_bias': gdn_dt_bias,
        'gdn_norm_w': 1.0 + nrm((LPM, GDN_DV), 0.02),
        'gdn_w_out': nrm((LPM, GDN_V_WIDTH, D_MODEL), BETA * GDN_V_WIDTH ** -0.5),
        'dsa_w_in': nrm((LPM, D_MODEL, D_IN), D_MODEL ** -0.5),
        'dsa_w_out': nrm((LPM, ATT_WIDTH, D_MODEL), BETA * ATT_WIDTH ** -0.5),
    }


def reference(x_prompt, x_sample, cache_a_kv, state_s5, state_gdn, state_gdn_conv, cache_d_kv, cache_d_kidx,
              page_table, p_prompt, p_sample, rel_bias, ln_g, ln_b, ple_gate_w, ple_w,
              a_w_in, a_sinks, a_w_out,
              s5_w_in, s5_a_re, s5_a_im, s5_b_re, s5_b_im, s5_c_re, s5_c_im, s5_d, s5_log_dt, s5_w_glu, s5_w_out,
              gdn_w_in, gdn_conv_w, gdn_a_log, gdn_dt_bias, gdn_norm_w, gdn_w_out,
              dsa_w_in, dsa_w_out):
    past_len = page_table.shape[1] * PAGE_SIZE
    xp, xs = x_prompt, x_sample
    a_p, a_s, s5_p, s5_s, gd_p, gd_s, gc_p, gc_s, dkv_p, dkv_s, dki_p, dki_s = [[] for _ in range(12)]
    for i in range(DEPTH):
        kind, r = i % N_MIXERS, i // N_MIXERS
        if kind == 0:
            hp, st_p = swa_mixer(xp, None, 0, a_w_in[r], a_sinks[r], a_w_out[r], rel_bias)
            hs, st_s = swa_mixer(xs, cache_a_kv[r], past_len, a_w_in[r], a_sinks[r], a_w_out[r], rel_bias)
            a_p.append(st_p)
            a_s.append(st_s)
        elif kind == 1:
            s5w = (s5_w_in[r], s5_a_re[r], s5_a_im[r], s5_b_re[r], s5_b_im[r], s5_c_re[r], s5_c_im[r],
                   s5_d[r], s5_log_dt[r], s5_w_glu[r], s5_w_out[r])
            hp, st_p = s5_mixer(xp, None, *s5w)
            hs, st_s = s5_mixer(xs, state_s5[r], *s5w)
            s5_p.append(st_p)
            s5_s.append(st_s)
        elif kind == 2:
            gw = (gdn_w_in[r], gdn_conv_w[r], gdn_a_log[r], gdn_dt_bias[r], gdn_norm_w[r], gdn_w_out[r])
            hp, sp, cp = gdn_mixer(xp, None, None, *gw)
            hs, ss, cs = gdn_mixer(xs, state_gdn[r], state_gdn_conv[r], *gw)
            gd_p.append(sp)
            gd_s.append(ss)
            gc_p.append(cp)
            gc_s.append(cs)
        else:
            hp, kvp, kip = dsa_prompt(xp, dsa_w_in[r], dsa_w_out[r], rel_bias)
            hs, kvs, kis = dsa_sample(xs, cache_d_kv, cache_d_kidx, r, page_table, dsa_w_in[r], dsa_w_out[r], rel_bias)
            dkv_p.append(kvp)
            dkv_s.append(kvs)
            dki_p.append(kip)
            dki_s.append(kis)
        xp = post_norm_ple(xp, hp, ln_g[i], ln_b[i], p_prompt[i], ple_gate_w[i], ple_w[i])
        xs = post_norm_ple(xs, hs, ln_g[i], ln_b[i], p_sample[i], ple_gate_w[i], ple_w[i])
    return (xp, xs, jnp.stack(a_p), jnp.stack(a_s), jnp.stack(s5_p), jnp.stack(s5_s),
            jnp.stack(gd_p), jnp.stack(gd_s), jnp.stack(gc_p), jnp.stack(gc_s),
            jnp.stack(dkv_p), jnp.stack(dkv_s), jnp.stack(dki_p), jnp.stack(dki_s))
```

```python
import os
import numpy as np
import concourse.bass as bass
import concourse.mybir as mybir
from concourse.bass_utils import run_bass_kernel_spmd
from contextlib import ExitStack

F32, BF16, I32 = mybir.dt.float32, mybir.dt.bfloat16, mybir.dt.int32
AF = mybir.ActivationFunctionType
ALU = mybir.AluOpType
AX = mybir.AxisListType

D = 2048
SEQ = 2048
NS = 16
TT = SEQ + NS
NTILE = 17
ALPHA = 8 ** 0.25
LN_EPS = 1e-5
NEG = -30000.0
NPOOL = int(os.environ.get('DS_NPOOL', '5120'))
TBS = [(0, 512), (512, 512), (1024, 512), (1536, 512), (2048, NS)]
TILES = [(i * 128, 128) for i in range(16)] + [(2048, NS)]


class Res:
    __slots__ = ("w", "r")

    def __init__(self):
        self.w = None
        self.r = {}


class T:
    def __init__(self, t, n=1):
        self.t = t
        self.rs = [Res() for _ in range(n)]

    @property
    def r(self):
        return self.rs[0]


class KB:
    LIM = 30000
    DLIM = 1800
    NSLOT = 8

    def __init__(self):
        self.nc = bass.Bass("TRN2", target_bir_lowering=False)
        self.es = ExitStack()
        nc = self.nc
        self.eng = {"pe": nc.tensor, "act": nc.scalar, "dve": nc.vector, "pool": nc.gpsimd, "sp": nc.sync}
        self.nsem = 0
        self.csem = {}
        self.ccnt = {}
        self.seen = {e: {} for e in self.eng}
        for e in ("pe", "act", "dve", "pool"):
            self.csem[e] = self._newsem()
            self.ccnt[e] = 0
        self.dq = {}
        for q in ("sp", "pool", "act"):
            self.dq[q] = dict(j=0, sems=[self._newsem() for _ in range(self.NSLOT)], cnt=[0] * self.NSLOT,
                              last=[None] * self.NSLOT)
        self.pend_r = []
        self.pend_w = []
        self.ntens = 0
        self.psb = []
        self.psi = 0
        self.scopes = []

    def _newsem(self):
        h = self.es.enter_context(self.nc.semaphore("sem%d" % self.nsem))
        self.nsem += 1
        return (self.nsem - 1, h)

    def _wait(self, e, ev):
        if ev is None:
            return
        sid, h, v, _ = ev
        if self.seen[e].get(sid, 0) >= v:
            return
        self.eng[e].wait_ge(h, v)
        self.seen[e][sid] = v

    @staticmethod
    def _deps(reads, writes):
        evs = []
        for r in reads:
            if r.w is not None:
                evs.append(r.w)
        for w in writes:
            if w.w is not None:
                evs.append(w.w)
            evs.extend(w.r.values())
        return evs

    @staticmethod
    def _commit(ev, reads, writes):
        for r in reads:
            r.r[ev[0]] = ev
        for w in writes:
            w.w = ev
            w.r = {}

    def op(self, e, fn, reads=(), writes=(), inc=True):
        reads = [x for x in reads]
        writes = [x for x in writes]
        for ev in self._deps(reads, writes):
            if e == "pe" and ev[3] == "pe":
                continue
            self._wait(e, ev)
        ins = fn(self.eng[e])
        if e == "pe" and not inc:
            self.pend_r += reads
            self.pend_w += writes
            return ins
        if self.ccnt[e] >= self.LIM:
            self.csem[e] = self._newsem()
            self.ccnt[e] = 0
        self.ccnt[e] += 1
        ins.then_inc(self.csem[e][1], 1)
        ev = (self.csem[e][0], self.csem[e][1], self.ccnt[e], e)
        if e == "pe":
            reads = reads + self.pend_r
            writes = writes + self.pend_w
            self.pend_r = []
            self.pend_w = []
        self._commit(ev, reads, writes)
        return ins

    def dma(self, q, out, in_, reads=(), writes=(), **kw):
        d = self.dq[q]
        slot = d["j"] % self.NSLOT
        d["j"] += 1
        self._wait(q, d["last"][slot])
        for ev in self._deps(reads, writes):
            self._wait(q, ev)
        if d["cnt"][slot] >= self.DLIM:
            d["sems"][slot] = self._newsem()
            d["cnt"][slot] = 0
        ins = self.eng[q].dma_start(out=out, in_=in_, **kw)
        d["cnt"][slot] += 1
        sid, h = d["sems"][slot]
        ins.then_inc(h, 16)
        ev = (sid, h, 16 * d["cnt"][slot], "dma_" + q)
        d["last"][slot] = ev
        self._commit(ev, list(reads), list(writes))
        return ins

    def dma_gather(self, out, in_, idx_ap, reads=(), writes=()):
        q = "pool"
        d = self.dq[q]
        slot = d["j"] % self.NSLOT
        d["j"] += 1
        self._wait(q, d["last"][slot])
        for ev in self._deps(reads, writes):
            self._wait(q, ev)
        if d["cnt"][slot] >= self.DLIM:
            d["sems"][slot] = self._newsem()
            d["cnt"][slot] = 0
        ins = self.nc.gpsimd.indirect_dma_start(out=out, out_offset=None, in_=in_,
                                                in_offset=bass.IndirectOffsetOnAxis(ap=idx_ap, axis=0))
        d["cnt"][slot] += 1
        sid, h = d["sems"][slot]
        ins.then_inc(h, 16)
        ev = (sid, h, 16 * d["cnt"][slot], "dma_" + q)
        d["last"][slot] = ev
        self._commit(ev, list(reads), list(writes))
        return ins

    def finish(self):
        for q in self.dq:
            for ev in self.dq[q]["last"]:
                self._wait("sp", ev)
        for e in ("pe", "act", "dve", "pool"):
            if self.ccnt[e] > 0:
                self._wait("sp", (self.csem[e][0], self.csem[e][1], self.ccnt[e], e))
        self.es.close()

    def barrier(self):
        evs = []
        for q in self.dq:
            evs += [ev for ev in self.dq[q]["last"] if ev is not None]
        for e in ("pe", "act", "dve", "pool"):
            if self.ccnt[e] > 0:
                evs.append((self.csem[e][0], self.csem[e][1], self.ccnt[e], e))
        for e in self.eng:
            for ev in evs:
                self._wait(e, ev)

    class _Scope:
        def __init__(self, kb):
            self.kb = kb

        def __enter__(self):
            self.kb.scopes.append(ExitStack())
            return self

        def __exit__(self, *a):
            self.kb.barrier()
            self.kb.scopes.pop().close()
            return False

    def scope(self):
        return KB._Scope(self)

    def sb(self, shape, dt, n=1, name=None):
        self.ntens += 1
        st = self.scopes[-1] if self.scopes else self.es
        t = st.enter_context(self.nc.sbuf_tensor("%s_%d" % (name or "sb", self.ntens), list(shape), dt))
        return T(t, n)

    def pool(self, shape, dt, bufs):
        return [self.sb(shape, dt) for _ in range(bufs)]

    def dram(self, name, shape, dt, kind="Internal", n=1):
        t = self.nc.dram_tensor(name, list(shape), dt, kind=kind)
        return T(t, n)

    def init_psum(self):
        for i in range(8):
            t = self.es.enter_context(self.nc.psum_tensor("psb%d" % i, [128, 512], F32))
            self.psb.append(T(t))

    def ps(self):
        p = self.psb[self.psi % 8]
        self.psi += 1
        return p

    def mm(self, out, lhsT, rhs, start, stop, reads, writes, inc=None):
        if inc is None:
            inc = stop
        return self.op("pe", lambda e: e.matmul(out, lhsT, rhs, start=start, stop=stop), reads, writes, inc=inc)

    def tp(self, out, in_, ident, reads, writes, inc=True):
        return self.op("pe", lambda e: e.transpose(out, in_, ident), reads, writes, inc=inc)

    def act(self, out, in_, func, reads, writes, **kw):
        return self.op("act", lambda e: e.activation(out=out, in_=in_, func=func, **kw), reads, writes)

    def tt(self, e, out, in0, in1, op, reads, writes):
        return self.op(e, lambda g: g.tensor_tensor(out=out, in0=in0, in1=in1, op=op), reads, writes)

    def ts(self, e, out, in0, s1, s2, op0, op1, reads, writes, **kw):
        if op1 is None:
            return self.op(e, lambda g: g.tensor_scalar(out=out, in0=in0, scalar1=s1, scalar2=None, op0=op0, **kw),
                           reads, writes)
        return self.op(e, lambda g: g.tensor_scalar(out=out, in0=in0, scalar1=s1, scalar2=s2, op0=op0, op1=op1, **kw),
                       reads, writes)

    def stt(self, e, out, in0, scalar, in1, op0, op1, reads, writes):
        return self.op(e, lambda g: g.scalar_tensor_tensor(out=out, in0=in0, scalar=scalar, in1=in1, op0=op0, op1=op1),
                       reads, writes)

    def red(self, e, out, in_, op, reads, writes):
        return self.op(e, lambda g: g.tensor_reduce(out=out, in_=in_, axis=AX.X, op=op), reads, writes)

    def cp(self, e, out, in_, reads, writes):
        if e == "act":
            return self.act(out, in_, AF.Copy, reads, writes)
        return self.op(e, lambda g: g.tensor_copy(out, in_), reads, writes)

    def memset(self, e, ap, val, writes):
        return self.op(e, lambda g: g.memset(ap, val), [], writes)


class Prog(KB):
    def __init__(self, n_layers=4, debug=False, stop=None):
        super().__init__()
        self.stop = stop
        self.n_layers = n_layers
        self.debug = debug
        self.inp = {}
        self.out = {}
        self.evi = 0

    def din(self, name, shape, dt=F32):
        t = self.nc.dram_tensor(name, list(shape), dt, kind="ExternalInput")
        self.inp[name] = T(t)
        return self.inp[name]

    def dout(self, name, shape, dt=F32):
        t = self.nc.dram_tensor(name, list(shape), dt, kind="ExternalOutput")
        self.out[name] = T(t)
        return self.out[name]

    def ev_eng(self):
        self.evi += 1
        return "act" if self.evi % 2 else "dve"

    def nxt(self, pool, attr):
        i = getattr(self, attr, 0)
        setattr(self, attr, i + 1)
        return pool[i % len(pool)]

    def load_w(self, wt, wap, K, n0, nw, c0=0):
        kc = K // 128
        for k0 in range(0, kc, 4):
            k1 = min(kc, k0 + 4)
            src = wap[k0 * 128:k1 * 128, n0:n0 + nw].rearrange("(kc p) n -> p kc n", p=128)
            self.dma("pool", wt.t[:, k0:k1, c0:c0 + nw], src, [], [wt.r])

    def ws_mm(self, ps_ap, psr, wt, cols, xT, kcs, t0, tn, xr):
        n = len(kcs)
        for i, kc in enumerate(kcs):
            self.mm(ps_ap, wt.t[:, kc, cols[0]:cols[1]], xT.t[:, kc, t0:t0 + tn], i == 0, i == n - 1,
                    [wt.r] + xr, [psr])

    def as_mm(self, ps_ap, psr, wt, cols, xT, kcs, t0, tn, xr):
        n = len(kcs)
        for i, kc in enumerate(kcs):
            self.mm(ps_ap, xT.t[:, kc, t0:t0 + tn], wt.t[:, kc, cols[0]:cols[1]], i == 0, i == n - 1,
                    [wt.r] + xr, [psr])

    def ld_grp(self, xs, dap, dres, n0, grp):
        full = [i for i in grp if i < 16]
        a, b = full[0], full[-1] + 1
        self.dma("sp", xs.t[:, 0:b - a, :], dap[a * 128:b * 128, n0:n0 + 256].rearrange("(i p) n -> p i n", p=128),
                 [dres], [xs.r])
        if 16 in grp:
            self.dma("sp", xs.t[0:NS, grp.index(16), :], dap[2048:TT, n0:n0 + 256], [dres], [xs.r])

    def st_grp(self, xs, dap, dres, n0, grp):
        full = [i for i in grp if i < 16]
        a, b = full[0], full[-1] + 1
        self.dma("sp", dap[a * 128:b * 128, n0:n0 + 256].rearrange("(i p) n -> p i n", p=128), xs.t[:, 0:b - a, :],
                 [xs.r], [dres])
        if 16 in grp:
            self.dma("sp", dap[2048:TT, n0:n0 + 256], xs.t[0:NS, grp.index(16), :], [xs.r], [dres])

    def setup(self):
        s = self
        s.init_psum()
        s.din("xin", [TT, D])
        s.din("p_all", [4, TT, 256])
        s.din("ident", [128, 128])
        s.din("ln_g", [4, D])
        s.din("ln_b", [4, D])
        s.din("ple_gate_w", [4, D, D])
        s.din("ple_w", [4, 256, D])
        s.dout("y", [TT, D])
        s.X = [s.dram("Xtok0", [TT, D], F32), s.dram("Xtok1", [TT, D], F32)]
        s.Rt = s.dram("Rt", [TT, D], F32)
        s.XN = s.dram("XN", [TT, D], F32)
        s.GTd = s.dram("GTd", [128, 32, TT], BF16)
        s.XTd = s.dram("XTd", [128, 16, TT], BF16)
        s.wp = s.pool([128, 16, 256], BF16, 2)
        s.idf = s.sb([128, 128], F32, name="idf")
        s.idb = s.sb([128, 128], BF16, name="idb")
        s.dma("sp", s.idf.t[:], s.inp["ident"].t.ap()[:, :], [], [s.idf.r])
        s.cp("dve", s.idb.t[:], s.idf.t[:], [s.idf.r], [s.idb.r])
        s.tokp = s.pool([128, 2048], F32, 2)
        s.stg = s.pool([128, 9, 256], F32, 2)
        s.sm = [s.sb([128, 16], F32, name="sm") for i in range(4)]
        s.ev4 = s.pool([128, 512], F32, 2)

    GRPS = [list(range(0, 9)), list(range(9, 17))]

    def load_input(self, xT):
        s = self
        xin = s.inp["xin"].t.ap()
        for i, (t0, tn) in enumerate(TILES):
            xt = s.nxt(s.tokp, "toki")
            s.dma("sp", xt.t[0:tn, :], xin[t0:t0 + tn, :], [], [xt.r])
            s.transpose_tile(xt, tn, xT, t0, 16)

    def transpose_tile(self, xt, tn, xT, t0, nchunk, c0=0, src_c0=0):
        s = self
        for g in range(0, nchunk, 4):
            ps = s.ps()
            ng = min(4, nchunk - g)
            for j in range(ng):
                c = src_c0 + (g + j) * 128
                s.tp(ps.t[:, j * 128:j * 128 + tn], xt.t[0:tn, c:c + 128], s.idf.t[0:tn, 0:tn],
                     [xt.r, s.idf.r], [ps.r], inc=(j == ng - 1))
            s.cp(s.ev_eng(), xT.t[:, c0 + g:c0 + g + ng, t0:t0 + tn],
                 ps.t[:, 0:ng * 128].rearrange("p (a b) -> p a b", a=ng)[:, :, 0:tn], [ps.r], [xT.r])

    def out_proj(self, li, nkc, Wo, Xcur):
        s = self
        Xap = Xcur.t.ap()
        Rap = s.Rt.t.ap()
        Gap = s.GTd.t.ap()
        gtp = s.pool([128, nkc, 128], BF16, 2)
        for nb in range(8):
            n0 = nb * 256
            wts = []
            for k0 in range(0, nkc, 16):
                wt = s.nxt(s.wp, "wi")
                s.load_w(wt, Wo[k0 * 128:(k0 + 16) * 128, :], 2048, n0, 256)
                wts.append(wt)
            for grp in s.GRPS:
                xs = s.nxt(s.stg, "stgi")
                s.ld_grp(xs, Xap, Xcur.r, n0, grp)
                for li_, i in enumerate(grp):
                    t0, tn = TILES[i]
                    gt = s.nxt(gtp, "gti")
                    s.dma("sp", gt.t[:, :, 0:tn], Gap[:, 0:nkc, t0:t0 + tn], [s.GTd.r], [gt.r])
                    ps = s.ps()
                    for kc in range(nkc):
                        wt = wts[kc // 16]
                        s.mm(ps.t[0:tn, 0:256], gt.t[:, kc, 0:tn], wt.t[:, kc % 16, :], kc == 0, kc == nkc - 1,
                             [gt.r, wt.r], [ps.r])
                    s.stt("dve", xs.t[0:tn, li_, :], xs.t[0:tn, li_, :], ALPHA, ps.t[0:tn, 0:256], ALU.mult, ALU.add,
                          [xs.r, ps.r], [xs.r])
                s.st_grp(xs, Rap, s.Rt.r, n0, grp)

    def ln_pass(self, li, xnT):
        s = self
        gb = s.sb([128, 2, D], F32, name="gb")
        gap = s.inp["ln_g"].t.ap()
        bap = s.inp["ln_b"].t.ap()
        s.dma("sp", gb.t[:, 0, :], gap[li:li + 1, :].to_broadcast([128, D]), [], [gb.r])
        s.dma("sp", gb.t[:, 1, :], bap[li:li + 1, :].to_broadcast([128, D]), [], [gb.r])
        Rap = s.Rt.t.ap()
        XNap = s.XN.t.ap()
        for i, (t0, tn) in enumerate(TILES):
            rt = s.nxt(s.tokp, "toki")
            s.dma("sp", rt.t[0:tn, :], Rap[t0:t0 + tn, :], [s.Rt.r], [rt.r])
            sm = s.nxt(s.sm, "smi")
            jk = s.nxt(s.tokp, "toki")
            s.red("dve", sm.t[0:tn, 0:1], rt.t[0:tn, :], ALU.add, [rt.r], [sm.r])
            s.ts("dve", sm.t[0:tn, 1:2], sm.t[0:tn, 0:1], -1.0 / D, None, ALU.mult, None, [sm.r], [sm.r])
            s.act(jk.t[0:tn, :], rt.t[0:tn, :], AF.Square, [rt.r, sm.r], [jk.r, sm.r], bias=sm.t[0:tn, 1:2], scale=1.0,
                  accum_out=sm.t[0:tn, 2:3])
            s.ts("dve", sm.t[0:tn, 3:4], sm.t[0:tn, 2:3], 1.0 / D, LN_EPS, ALU.mult, ALU.add, [sm.r], [sm.r])
            s.act(sm.t[0:tn, 5:6], sm.t[0:tn, 3:4], AF.Sqrt, [sm.r], [sm.r])
            s.op("dve", lambda g: g.reciprocal(sm.t[0:tn, 4:5], sm.t[0:tn, 5:6]), [sm.r], [sm.r])
            s.ts("dve", jk.t[0:tn, :], rt.t[0:tn, :], sm.t[0:tn, 1:2], sm.t[0:tn, 4:5], ALU.add, ALU.mult,
                 [rt.r, sm.r], [jk.r])
            s.tt("dve", jk.t[0:tn, :], jk.t[0:tn, :], gb.t[0:tn, 0, :], ALU.mult, [jk.r, gb.r], [jk.r])
            s.tt("pool", jk.t[0:tn, :], jk.t[0:tn, :], gb.t[0:tn, 1, :], ALU.add, [jk.r, gb.r], [jk.r])
            s.dma("sp", XNap[t0:t0 + tn, :], jk.t[0:tn, :], [jk.r], [s.XN.r])
            s.transpose_tile(jk, tn, xnT, t0, 16)

    def ple_stage(self, li, xnT, Xnext, last):
        s = self
        pT = s.sb([128, 2, TT], BF16, name="pT")
        xst_p = s.pool([128, 2, TT], BF16, 2)
        wpl_p = s.pool([128, 512], BF16, 2)
        pap = s.inp["p_all"].t.ap()
        for grp in s.GRPS:
            pst = s.nxt(s.stg, "stgi")
            s.ld_grp(pst, pap[li], Res(), 0, grp)
            for li_, i in enumerate(grp):
                t0, tn = TILES[i]
                ps = s.ps()
                for j in range(2):
                    s.tp(ps.t[:, j * 128:j * 128 + tn], pst.t[0:tn, li_, j * 128:(j + 1) * 128], s.idf.t[0:tn, 0:tn],
                         [pst.r, s.idf.r], [ps.r], inc=(j == 1))
                s.cp(s.ev_eng(), pT.t[:, :, t0:t0 + tn], ps.t[:, 0:256].rearrange("p (a b) -> p a b", a=2)[:, :, 0:tn],
                     [ps.r], [pT.r])
        Wg = s.inp["ple_gate_w"].t.ap()
        Wp = s.inp["ple_w"].t.ap()
        XNap = s.XN.t.ap()
        Xap = Xnext.t.ap()
        XTap = s.XTd.t.ap()
        for nb in range(8):
            n0 = nb * 256
            wg = s.nxt(s.wp, "wi")
            s.load_w(wg, Wg[li], D, n0, 256)
            wpl = s.nxt(wpl_p, "wpli")
            wplb = wpl.t[:, :]
            s.dma("pool", wplb[:, 0:512].rearrange("p (a b) -> p a b", a=2),
                  Wp[li][:, n0:n0 + 256].rearrange("(kc p) n -> p kc n", p=128), [], [wpl.r])
            xst = s.nxt(xst_p, "xsti")
            for grp in s.GRPS:
                xs = s.nxt(s.stg, "stgi")
                s.ld_grp(xs, XNap, s.XN.r, n0, grp)
                for li_, i in enumerate(grp):
                    t0, tn = TILES[i]
                    psg = s.ps()
                    s.as_mm(psg.t[0:tn, 0:256], psg.r, wg, (0, 256), xnT, list(range(16)), t0, tn, [xnT.r])
                    for kc in range(2):
                        s.mm(psg.t[0:tn, 256:512], pT.t[:, kc, t0:t0 + tn], wplb[:, kc * 256:(kc + 1) * 256], kc == 0,
                             kc == 1, [pT.r, wpl.r], [psg.r])
                    gt = s.nxt(s.ev4, "ev4i")
                    s.act(gt.t[0:tn, 0:256], psg.t[0:tn, 0:256], AF.Sigmoid, [psg.r], [gt.r])
                    s.tt("dve", gt.t[0:tn, 0:256], gt.t[0:tn, 0:256], psg.t[0:tn, 256:512], ALU.mult, [gt.r, psg.r], [gt.r])
                    s.tt("pool", xs.t[0:tn, li_, :], xs.t[0:tn, li_, :], gt.t[0:tn, 0:256], ALU.add, [xs.r, gt.r], [xs.r])
                    if not last:
                        ps = s.ps()
                        for j in range(2):
                            s.tp(ps.t[:, j * 128:j * 128 + tn], xs.t[0:tn, li_, j * 128:(j + 1) * 128],
                                 s.idf.t[0:tn, 0:tn], [xs.r, s.idf.r], [ps.r], inc=(j == 1))
                        s.cp(s.ev_eng(), xst.t[:, :, t0:t0 + tn],
                             ps.t[:, 0:256].rearrange("p (a b) -> p a b", a=2)[:, :, 0:tn], [ps.r], [xst.r])
                s.st_grp(xs, Xap, Xnext.r, n0, grp)
            if not last:
                s.dma("sp", XTap[:, nb * 2:nb * 2 + 2, :], xst.t[:, :, :], [xst.r], [s.XTd.r])

    def setup_swa(self):
        s = self
        s.din("a_w_in", [D, 4608])
        s.din("a_w_out", [D, D])
        s.din("a_bias_p", [128, 32, 256])
        s.din("a_mask_p", [128, 256])
        s.din("a_bias_s", [64, 1152])
        s.din("a_mask_s", [64, 1152])
        s.din("a_sink_p", [128, 32])
        s.din("a_sink_s", [64, 2])
        s.din("a_cache", [4, 128, 512])
        s.dout("a_kv_p", [128, 512])
        s.dout("a_kv_s", [4, 128, 512])
        s.QT = s.dram("QT", [64, 32, TT], BF16)
        s.ZT = s.dram("ZT", [128, 32, TT], BF16)

    def swa_proj(self, xT, kT, vtok, kv32):
        s = self
        W = s.inp["a_w_in"].t.ap()
        qst = s.pool([64, 4, 512], BF16, 2)
        zst = s.pool([128, 2, 512], BF16, 2)
        qi = 0
        QTap = s.QT.t.ap()
        ZTap = s.ZT.t.ap()
        KC = list(range(16))
        for blk in range(int(os.environ.get("NBLK", "18"))):
            wt = s.nxt(s.wp, "wi")
            s.load_w(wt, W, D, blk * 256, 256)
            if blk < 8:
                for (t0, tn) in TBS:
                    st = qst[qi % 2]
                    qi += 1
                    for j in range(4):
                        ps = s.ps()
                        s.ws_mm(ps.t[0:64, 0:tn], ps.r, wt, (j * 64, j * 64 + 64), xT, KC, t0, tn, [xT.r])
                        if j % 2:
                            s.act(st.t[:, j, 0:tn], ps.t[0:64, 0:tn], AF.Copy, [ps.r], [st.r], scale=0.125)
                        else:
                            s.ts("dve", st.t[:, j, 0:tn], ps.t[0:64, 0:tn], 0.125, None, ALU.mult, None, [ps.r], [st.r])
                    s.dma("sp", QTap[:, blk * 4:blk * 4 + 4, t0:t0 + tn], st.t[:, :, 0:tn], [st.r], [s.QT.r])
            elif blk == 8:
                for (t0, tn) in TBS:
                    for j in range(4):
                        ps = s.ps()
                        s.ws_mm(ps.t[0:64, 0:tn], ps.r, wt, (j * 64, j * 64 + 64), xT, KC, t0, tn, [xT.r])
                        s.cp(s.ev_eng(), kT.t[:, j, t0:t0 + tn], ps.t[0:64, 0:tn], [ps.r], [kT.r])
                for ii, i in enumerate((15, 16)):
                    t0, tn = TILES[i]
                    ps = s.ps()
                    s.as_mm(ps.t[0:tn, 0:256], ps.r, wt, (0, 256), xT, KC, t0, tn, [xT.r])
                    s.cp("dve", kv32.t[0:tn, ii, 0:256], ps.t[0:tn, 0:256], [ps.r], [kv32.r])
            elif blk == 9:
                for i, (t0, tn) in enumerate(TILES):
                    ps = s.ps()
                    s.as_mm(ps.t[0:tn, 0:256], ps.r, wt, (0, 256), xT, KC, t0, tn, [xT.r])
                    s.cp("act", vtok.t[0:tn, i, :], ps.t[0:tn, 0:256], [ps.r], [vtok.r])
                    if i >= 15:
                        s.cp("dve", kv32.t[0:tn, i - 15, 256:512], ps.t[0:tn, 0:256], [ps.r], [kv32.r, ps.r])
            else:
                zb = blk - 10
                for (t0, tn) in TBS:
                    st = zst[qi % 2]
                    qi += 1
                    for j in range(2):
                        ps = s.ps()
                        s.ws_mm(ps.t[:, 0:tn], ps.r, wt, (j * 128, j * 128 + 128), xT, KC, t0, tn, [xT.r])
                        s.act(st.t[:, j, 0:tn], ps.t[:, 0:tn], AF.Silu, [ps.r], [st.r])
                    s.dma("sp", ZTap[:, zb * 2:zb * 2 + 2, t0:t0 + tn], st.t[:, :, 0:tn], [st.r], [s.ZT.r])
        if os.environ.get("SKIPOUT"):
            return
        s.dma("sp", s.out["a_kv_p"].t.ap()[:, :], kv32.t[:, 0, :], [kv32.r], [s.out["a_kv_p"].r])
        cache = s.inp["a_cache"].t.ap()
        oks = s.out["a_kv_s"].t.ap()
        for bi in range(4):
            s.dma("sp", oks[bi, 0:124, :], cache[bi, 4:128, :], [], [s.out["a_kv_s"].r])
            s.dma("sp", oks[bi, 124:128, :], kv32.t[bi * 4:bi * 4 + 4, 1, :], [kv32.r], [s.out["a_kv_s"].r])

    def swa_attn(self, kT, vtok):
        s = self
        QTap = s.QT.t.ap()
        ZTap = s.ZT.t.ap()
        Gap = s.GTd.t.ap()
        cache = s.inp["a_cache"].t.ap()
        bm = s.sb([128, 32, 256], BF16, name="a_bm")
        bs = s.sb([64, 2, 4, 144], BF16, name="a_bs")
        sink = s.sb([128, 34], F32, name="a_sink")
        tmpb = s.nxt(s.tokp, "toki")
        s.dma("sp", sink.t[:, 0:32], s.inp["a_sink_p"].t.ap()[:, :], [], [sink.r])
        s.dma("sp", sink.t[0:64, 32:34], s.inp["a_sink_s"].t.ap()[:, :], [], [sink.r])
        mk = s.nxt(s.ev4, "ev4i")
        s.dma("sp", mk.t[:, 0:256], s.inp["a_mask_p"].t.ap()[:, :], [], [mk.r])
        for hq in range(4):
            tmpb = s.nxt(s.tokp, "toki")
            s.dma("sp", tmpb.t[:, :].rearrange("p (a b) -> p a b", a=8), s.inp["a_bias_p"].t.ap()[:, hq * 8:hq * 8 + 8, :],
                  [], [tmpb.r])
            s.tt("dve", bm.t[:, hq * 8:hq * 8 + 8, :], tmpb.t[:, :].rearrange("p (a b) -> p a b", a=8),
                 mk.t[:, 0:256].unsqueeze(1).to_broadcast([128, 8, 256]), ALU.add, [tmpb.r, mk.r], [bm.r])
        tmps = s.nxt(s.tokp, "toki")
        s.dma("sp", tmps.t[0:64, 0:1152], s.inp["a_bias_s"].t.ap()[:, :], [], [tmps.r])
        s.dma("sp", tmps.t[0:64, 1152:2304 - 256], s.inp["a_mask_s"].t.ap()[:, 0:896], [], [tmps.r])
        tmps2 = s.nxt(s.ev4, "ev4i")
        s.dma("sp", tmps2.t[0:64, 0:256], s.inp["a_mask_s"].t.ap()[:, 896:1152], [], [tmps2.r])
        bsf = bs.t[:, :, :, :].rearrange("p a b c -> p (a b c)")
        s.tt("dve", bsf[:, 0:896], tmps.t[0:64, 0:896], tmps.t[0:64, 1152:2048], ALU.add, [tmps.r], [bs.r])
        s.tt("dve", bsf[:, 896:1152], tmps.t[0:64, 896:1152], tmps2.t[0:64, 0:256], ALU.add, [tmps.r, tmps2.r], [bs.r])
        qb_p = s.pool([64, 32, 128], BF16, 2)
        zb_p = s.pool([128, 16, 128], BF16, 2)
        gs_p = s.pool([128, 16, 128], BF16, 2)
        Ep = s.pool([128, 2, 256], BF16, 2)
        Pp = s.pool([128, 2, 256], BF16, 2)
        PTp = s.pool([128, 4, 128], BF16, 2)
        smp = s.pool([128, 16], F32, 4)
        it = 0
        for n in range(16):
            t0 = n * 128
            qb = qb_p[n % 2]
            zb = zb_p[n % 2]
            gs = gs_p[n % 2]
            s.dma("sp", qb.t[:, :, :], QTap[:, :, t0:t0 + 128], [s.QT.r], [qb.r])
            s.dma("sp", zb.t[:, :, :], ZTap[:, 0:16, t0:t0 + 128], [s.ZT.r], [zb.r])
            nk = 128 if n == 0 else 256
            k0 = 0 if n == 0 else t0 - 128
            bo = 128 if n == 0 else 0
            nkt = nk // 128
            for pr in range(16):
                kvh = pr // 4
                E = Ep[it % 2]
                P = Pp[it % 2]
                PT = PTp[it % 2]
                sm = smp[it % 4]
                it += 1
                ps = s.ps()
                for j in range(2):
                    h = pr * 2 + j
                    s.mm(ps.t[:, j * 256:j * 256 + nk], s.idb.t[:, :], bm.t[:, h, bo:bo + nk], True, False,
                         [s.idb.r, bm.r], [ps.r], inc=False)
                    s.mm(ps.t[:, j * 256:j * 256 + nk], qb.t[:, h, :], kT.t[:, kvh, k0:k0 + nk], False, True,
                         [qb.r, kT.r], [ps.r], inc=(j == 1))
                pv = ps.t[:, :].rearrange("p (a b) -> p a b", a=2)[:, :, 0:nk]
                s.red("dve", sm.t[:, 0:2], pv, ALU.max, [ps.r], [sm.r])
                s.tt("dve", sm.t[:, 0:2], sm.t[:, 0:2], sink.t[:, pr * 2:pr * 2 + 2], ALU.max, [sm.r, sink.r], [sm.r])
                s.ts("dve", sm.t[:, 2:4], sm.t[:, 0:2], -1.0, None, ALU.mult, None, [sm.r], [sm.r])
                for j in range(2):
                    s.act(E.t[:, j, 0:nk], ps.t[:, j * 256:j * 256 + nk], AF.Exp, [ps.r, sm.r], [E.r, sm.r],
                          bias=sm.t[:, 2 + j:3 + j], scale=1.0, accum_out=sm.t[:, 4 + j:5 + j])
                s.tt("dve", sm.t[:, 6:8], sink.t[:, pr * 2:pr * 2 + 2], sm.t[:, 0:2], ALU.subtract, [sm.r, sink.r], [sm.r])
                s.act(sm.t[:, 8:10], sm.t[:, 6:8], AF.Exp, [sm.r], [sm.r])
                s.tt("dve", sm.t[:, 10:12], sm.t[:, 8:10], sm.t[:, 4:6], ALU.add, [sm.r], [sm.r])
                s.op("dve", lambda g: g.reciprocal(sm.t[:, 12:14], sm.t[:, 10:12]), [sm.r], [sm.r])
                s.tt("dve", P.t[:, :, 0:nk], E.t[:, :, 0:nk], sm.t[:, 12:14].unsqueeze(2).to_broadcast([128, 2, nk]),
                     ALU.mult, [E.r, sm.r], [P.r])
                pst = s.ps()
                pstb = pst.t[:, :].bitcast(BF16)
                for j in range(2):
                    for kt in range(nkt):
                        ix = j * nkt + kt
                        s.tp(pstb[:, ix * 128:(ix + 1) * 128], P.t[:, j, kt * 128:(kt + 1) * 128], s.idb.t[:, :],
                             [P.r, s.idb.r], [pst.r], inc=(ix == 2 * nkt - 1))
                s.cp(s.ev_eng(), PT.t[:, 0:2 * nkt, :], pstb[:, 0:2 * nkt * 128].rearrange("p (a b) -> p a b", a=2 * nkt),
                     [pst.r], [PT.r])
                po = s.ps()
                for j in range(2):
                    for kt in range(nkt):
                        ktile = n if n == 0 else n - 1 + kt
                        s.mm(po.t[j * 64:(j + 1) * 64, 0:128], vtok.t[:, ktile, kvh * 64:(kvh + 1) * 64],
                             PT.t[:, j * nkt + kt, :], kt == 0, kt == nkt - 1, [vtok.r, PT.r], [po.r],
                             inc=(j == 1 and kt == nkt - 1))
                s.tt("dve", gs.t[:, pr, :], po.t[:, 0:128], zb.t[:, pr, :], ALU.mult, [po.r, zb.r], [gs.r])
            s.dma("sp", Gap[:, 0:16, t0:t0 + 128], gs.t[:, :, :], [gs.r], [s.GTd.r])
        qs = s.sb([64, 32, NS], BF16, name="a_qs")
        zs = s.sb([128, 16, NS], BF16, name="a_zs")
        gss = s.sb([128, 16, NS], BF16, name="a_gss")
        s.dma("sp", qs.t[:, :, :], QTap[:, :, 2048:TT], [s.QT.r], [qs.r])
        qs2 = s.sb([64, 4, 128], BF16, name="a_qs2")
        for kvh in range(4):
            for par in range(2):
                o_ = qs2.t[:, :, kvh * 32 + par * 16:kvh * 32 + par * 16 + 16].rearrange("p b (g t) -> p g b t", g=4)
                i_ = qs.t[:, kvh * 8 + par:kvh * 8 + 8:2, :].rearrange("p g (b t) -> p g b t", b=4)
                s.cp("dve", o_, i_, [qs.r], [qs2.r])
        s.dma("sp", zs.t[:, :, :], ZTap[:, 0:16, 2048:TT], [s.ZT.r], [zs.r])
        cst = s.pool([128, 512], F32, 2)
        cbf = s.pool([128, 512], BF16, 2)
        kTs = s.pool([64, 4, 144], BF16, 2)
        Es = s.pool([64, 2, 144], BF16, 2)
        PTs = s.pool([128, 2, 128], BF16, 2)
        for bi in range(4):
            c32 = cst[bi % 2]
            cb = cbf[bi % 2]
            kts = kTs[bi % 2]
            E = Es[bi % 2]
            PT = PTs[bi % 2]
            sm = smp[bi % 4]
            s.dma("sp", c32.t[:, :], cache[bi, :, :], [], [c32.r])
            s.cp("dve", cb.t[:, :], c32.t[:, :], [c32.r], [cb.r])
            pst = s.ps()
            pstb = pst.t[:, :].bitcast(BF16)
            for kvh in range(4):
                s.tp(pstb[0:64, kvh * 128:(kvh + 1) * 128], cb.t[:, kvh * 64:(kvh + 1) * 64], s.idb.t[:, :],
                     [cb.r, s.idb.r], [pst.r], inc=(kvh == 3))
            s.cp("act", kts.t[:, :, 0:128], pstb[0:64, 0:512].rearrange("p (a b) -> p a b", a=4), [pst.r], [kts.r])
            s.cp("dve", kts.t[:, :, 128:144], kT.t[:, :, 2048:TT], [kT.r], [kts.r])
            pst2 = s.ps()
            pst2b = pst2.t[:, :].bitcast(BF16)
            for half in range(2):
                sm = smp[(bi * 2 + half) % 4]
                ps = s.ps()
                s.mm(ps.t[0:64, 0:144], s.idb.t[0:64, 0:64], bs.t[:, half, bi, :], True, False, [s.idb.r, bs.r], [ps.r],
                     inc=False)
                for k2 in range(2):
                    kvh = half * 2 + k2
                    s.mm(ps.t[k2 * 32:(k2 + 1) * 32, 0:144], qs2.t[:, bi, kvh * 32:(kvh + 1) * 32],
                         kts.t[:, kvh, :], False, k2 == 1, [qs2.r, kts.r], [ps.r], inc=(k2 == 1))
                sk = sink.t[0:64, 32 + half:33 + half]
                s.red("dve", sm.t[0:64, 0:1], ps.t[0:64, 0:144], ALU.max, [ps.r], [sm.r])
                s.tt("dve", sm.t[0:64, 0:1], sm.t[0:64, 0:1], sk, ALU.max, [sm.r, sink.r], [sm.r])
                s.ts("dve", sm.t[0:64, 2:3], sm.t[0:64, 0:1], -1.0, None, ALU.mult, None, [sm.r], [sm.r])
                s.act(E.t[0:64, half, :], ps.t[0:64, 0:144], AF.Exp, [ps.r, sm.r], [E.r, sm.r], bias=sm.t[0:64, 2:3],
                      scale=1.0, accum_out=sm.t[0:64, 4:5])
                s.tt("dve", sm.t[0:64, 6:7], sk, sm.t[0:64, 0:1], ALU.subtract, [sm.r, sink.r], [sm.r])
                s.act(sm.t[0:64, 8:9], sm.t[0:64, 6:7], AF.Exp, [sm.r], [sm.r])
                s.tt("dve", sm.t[0:64, 10:11], sm.t[0:64, 8:9], sm.t[0:64, 4:5], ALU.add, [sm.r], [sm.r])
                s.op("dve", lambda g: g.reciprocal(sm.t[0:64, 12:13], sm.t[0:64, 10:11]), [sm.r], [sm.r])
                s.ts("dve", E.t[0:64, half, :], E.t[0:64, half, :], sm.t[0:64, 12:13], None, ALU.mult, None,
                     [E.r, sm.r], [E.r])
                s.tp(pst2b[:, half * 64:(half + 1) * 64], E.t[0:64, half, 0:128], s.idb.t[0:64, 0:64],
                     [E.r, s.idb.r], [pst2.r], inc=False)
                s.tp(pst2b[0:16, 128 + half * 64:128 + (half + 1) * 64], E.t[0:64, half, 128:144], s.idb.t[0:64, 0:64],
                     [E.r, s.idb.r], [pst2.r], inc=True)
            s.cp("act", PT.t[:, 0, :], pst2b[:, 0:128], [pst2.r], [PT.r])
            s.cp("dve", PT.t[0:16, 1, :], pst2b[0:16, 128:256], [pst2.r], [PT.r, pst2.r])
            po = s.ps()
            for kvh in range(4):
                for par in range(2):
                    r0 = kvh * 32 + par * 16
                    rows = PT.t[:, 0, r0:r0 + 16]
                    rows2 = PT.t[0:16, 1, r0:r0 + 16]
                    o_ap = po.t[par * 64:(par + 1) * 64, kvh * 16:(kvh + 1) * 16]
                    s.mm(o_ap, cb.t[:, 256 + kvh * 64:256 + (kvh + 1) * 64], rows, True, False, [cb.r, PT.r], [po.r],
                         inc=False)
                    s.mm(o_ap, vtok.t[0:16, 16, kvh * 64:(kvh + 1) * 64], rows2, False, True, [vtok.r, PT.r], [po.r],
                         inc=(kvh == 3 and par == 1))
            s.tt("dve", gss.t[:, :, bi * 4:bi * 4 + 4],
                 po.t[:, 0:64].rearrange("p (a b) -> p a b", a=16), zs.t[:, :, bi * 4:bi * 4 + 4], ALU.mult,
                 [po.r, zs.r], [gss.r])
        s.dma("sp", Gap[:, 0:16, 2048:TT], gss.t[:, :, :], [gss.r], [s.GTd.r])

    def setup_s5(self):
        s = self
        s.din("s5_w_in", [D, 4096])
        s.din("s5_w_glu", [D, D])
        s.din("s5_w_out", [D, D])
        s.din("s5_a_re", [128, 64])
        s.din("s5_a_im", [128, 64])
        s.din("s5_log_dt", [128, 1])
        s.din("s5_b_re", [128, 64, 16])
        s.din("s5_b_im", [128, 64, 16])
        s.din("s5_c_re", [128, 16, 64])
        s.din("s5_c_im", [128, 16, 64])
        s.din("s5_d_fm", [128, 16])
        s.din("s5_state", [4, 128, 128])
        s.din("s5_const", [128, 256])
        s.dout("s5_p", [128, 128])
        s.dout("s5_s", [4, 128, 128])
        s.UT = s.dram("UT", [128, 16, TT], BF16)
        s.Y1T = s.dram("Y1T", [128, 16, TT], BF16)

    def s5_proj(self, xT):
        s = self
        W = s.inp["s5_w_in"].t.ap()
        ust = s.pool([128, 2, 512], BF16, 2)
        KC = list(range(16))
        UTap = s.UT.t.ap()
        ZTap = s.ZT.t.ap()
        qi = 0
        for blk in range(16):
            wt = s.nxt(s.wp, "wi")
            s.load_w(wt, W, D, blk * 256, 256)
            for (t0, tn) in TBS:
                st = ust[qi % 2]
                qi += 1
                for j in range(2):
                    ps = s.ps()
                    s.ws_mm(ps.t[:, 0:tn], ps.r, wt, (j * 128, j * 128 + 128), xT, KC, t0, tn, [xT.r])
                    if blk < 8:
                        s.cp(s.ev_eng(), st.t[:, j, 0:tn], ps.t[:, 0:tn], [ps.r], [st.r])
                    else:
                        s.act(st.t[:, j, 0:tn], ps.t[:, 0:tn], AF.Silu, [ps.r], [st.r])
                if blk < 8:
                    s.dma("sp", UTap[:, blk * 2:blk * 2 + 2, t0:t0 + tn], st.t[:, :, 0:tn], [st.r], [s.UT.r])
                else:
                    zb = blk - 8
                    s.dma("sp", ZTap[:, zb * 2:zb * 2 + 2, t0:t0 + tn], st.t[:, :, 0:tn], [st.r], [s.ZT.r])

    def cmul(self, e, o_re, o_im, a_re, a_im, b_re, b_im, tmp, rs, ws):
        s = self
        s.tt(e, o_re, a_re, b_re, ALU.mult, rs, ws)
        s.tt(e, tmp, a_im, b_im, ALU.mult, rs, ws)
        s.tt(e, o_re, o_re, tmp, ALU.subtract, rs + ws, ws)
        s.tt(e, o_im, a_re, b_im, ALU.mult, rs, ws)
        s.tt(e, tmp, a_im, b_re, ALU.mult, rs, ws)
        s.tt(e, o_im, o_im, tmp, ALU.add, rs + ws, ws)

    def s5_core(self):
        s = self
        PI = float(np.pi)
        cst = s.sb([128, 256], F32, name="s5c")
        s.dma("sp", cst.t[:, :], s.inp["s5_const"].t.ap()[:, :], [], [cst.r])
        selp = cst.t[:, 0:64]
        pm = cst.t[:, 64:66]
        lm = cst.t[:, 66:68]
        I2 = cst.t[:, 68:132]
        L = s.sb([128, 16, 64], F32, name="s5L")
        NST = 11
        apw = s.sb([128, NST, 3, 64], F32, name="s5apw")
        ah0 = s.sb([128, 4, 2, 64], F32, name="s5ah0")
        Bb = s.sb([128, 2, 64, 16], F32, name="s5Bb")
        with s.scope():
            pr_ = s.sb([128, 16, 64], F32, name="s5par")
            P = pr_.t
            R_ = [pr_.r]
            s.dma("sp", P[:, 0, :], s.inp["s5_a_re"].t.ap()[:, :], [], R_)
            s.dma("sp", P[:, 1, :], s.inp["s5_a_im"].t.ap()[:, :], [], R_)
            s.dma("sp", P[:, 2, 0:1], s.inp["s5_log_dt"].t.ap()[:, :], [], R_)
            s.act(P[:, 2, 1:2], P[:, 2, 0:1], AF.Exp, R_, R_)
            s.act(P[:, 3, :], P[:, 0, :], AF.Exp, R_, R_, scale=P[:, 2, 1:2])
            s.ts("dve", P[:, 4, :], P[:, 1, :], P[:, 2, 1:2], None, ALU.mult, None, R_, R_)

            def rangered(off):
                s.ts("dve", P[:, 5, :], P[:, 4, :], off, None, ALU.add, None, R_, R_)
                s.cp("dve", P[:, 15, :], P[:, 5, :], R_, R_)
                for j in range(1, 8):
                    s.ts("dve", P[:, 11, :], P[:, 15, :], 2 * PI * j - PI, -2 * PI, ALU.is_ge, ALU.mult, R_, R_)
                    s.tt("dve", P[:, 5, :], P[:, 5, :], P[:, 11, :], ALU.add, R_, R_)
            rangered(0.0)
            s.act(P[:, 6, :], P[:, 5, :], AF.Sin, R_, R_)
            rangered(0.5 * PI)
            s.act(P[:, 7, :], P[:, 5, :], AF.Sin, R_, R_)
            s.tt("dve", P[:, 8, :], P[:, 3, :], P[:, 7, :], ALU.mult, R_, R_)
            s.tt("dve", P[:, 9, :], P[:, 3, :], P[:, 6, :], ALU.mult, R_, R_)
            s.ts("dve", P[:, 10, :], P[:, 8, :], -1.0, None, ALU.add, None, R_, R_)
            s.tt("dve", P[:, 11, :], P[:, 0, :], P[:, 0, :], ALU.mult, R_, R_)
            s.tt("dve", P[:, 12, :], P[:, 1, :], P[:, 1, :], ALU.mult, R_, R_)
            s.tt("dve", P[:, 11, :], P[:, 11, :], P[:, 12, :], ALU.add, R_, R_)
            s.op("dve", lambda g: g.reciprocal(P[:, 12, :], P[:, 11, :]), R_, R_)
            s.tt("dve", P[:, 13, :], P[:, 10, :], P[:, 0, :], ALU.mult, R_, R_)
            s.tt("dve", P[:, 14, :], P[:, 9, :], P[:, 1, :], ALU.mult, R_, R_)
            s.tt("dve", P[:, 13, :], P[:, 13, :], P[:, 14, :], ALU.add, R_, R_)
            s.tt("dve", P[:, 13, :], P[:, 13, :], P[:, 12, :], ALU.mult, R_, R_)
            s.tt("dve", P[:, 14, :], P[:, 9, :], P[:, 0, :], ALU.mult, R_, R_)
            s.tt("dve", P[:, 15, :], P[:, 10, :], P[:, 1, :], ALU.mult, R_, R_)
            s.tt("dve", P[:, 14, :], P[:, 14, :], P[:, 15, :], ALU.subtract, R_, R_)
            s.tt("dve", P[:, 14, :], P[:, 14, :], P[:, 12, :], ALU.mult, R_, R_)
            st0 = s.sb([128, 4, 128], F32, name="s5st")
            s.dma("sp", st0.t[:, :, :], s.inp["s5_state"].t.ap().rearrange("b g x -> g b x"), [], [st0.r])
            dm = s.sb([128, 2, 64], F32, name="s5dm")

            def to_lanes(src_ap, src_r, dst_idx):
                s.tt("dve", dm.t[:, :, :], src_ap.unsqueeze(1).to_broadcast([128, 2, 64]),
                     pm.unsqueeze(2).to_broadcast([128, 2, 64]), ALU.mult, src_r + [cst.r], [dm.r])
                ps = s.ps()
                s.mm(ps.t[:, 0:64], dm.t[:, :, :].rearrange("p a b -> p (a b)"), selp, True, True, [dm.r, cst.r], [ps.r])
                s.cp("dve", L.t[:, dst_idx, :], ps.t[:, 0:64], [ps.r], [L.r])

            for i, k in enumerate((8, 9, 13, 14)):
                to_lanes(P[:, k, :], R_, i)
            for b in range(4):
                for c in range(2):
                    to_lanes(st0.t[:, b, :].rearrange("p (x c) -> p x c", c=2)[:, :, c], [st0.r], 4 + b * 2 + c)
            tmpl = s.sb([128, 64], F32, name="s5tmpl")
            s.cp("dve", apw.t[:, 0, 0, :], L.t[:, 0, :], [L.r], [apw.r])
            s.cp("dve", apw.t[:, 0, 1, :], L.t[:, 1, :], [L.r], [apw.r])
            for k in range(1, NST):
                s.cmul("dve", apw.t[:, k, 0, :], apw.t[:, k, 1, :], apw.t[:, k - 1, 0, :], apw.t[:, k - 1, 1, :],
                       apw.t[:, k - 1, 0, :], apw.t[:, k - 1, 1, :], tmpl.t[:, :], [apw.r], [apw.r, tmpl.r])
            for k in range(NST):
                s.ts("dve", apw.t[:, k, 2, :], apw.t[:, k, 1, :], -1.0, None, ALU.mult, None, [apw.r], [apw.r])
            for b in range(4):
                s.cmul("dve", ah0.t[:, b, 0, :], ah0.t[:, b, 1, :], L.t[:, 0, :], L.t[:, 1, :], L.t[:, 4 + b * 2, :],
                       L.t[:, 5 + b * 2, :], tmpl.t[:, :], [L.r], [ah0.r, tmpl.r])
            Bn = s.sb([128, 2, 64, 16], F32, name="s5Bn")
            tmpB = s.sb([128, 64, 16], F32, name="s5tmpB")
            for c, nm in enumerate(("s5_b_re", "s5_b_im")):
                bap = s.inp[nm].t.ap()
                for q8 in range(8):
                    srcap = bass.AP(bap.tensor, q8 * 8 * 2048, [[16, 128], [2048, 8], [1, 16]])
                    s.dma("sp", Bn.t[:, c, q8 * 8:(q8 + 1) * 8, :], srcap, [], [Bn.r])
            cre_b = L.t[:, 2, :].unsqueeze(2).to_broadcast([128, 64, 16])
            cim_b = L.t[:, 3, :].unsqueeze(2).to_broadcast([128, 64, 16])
            s.cmul("dve", Bb.t[:, 0, :, :], Bb.t[:, 1, :, :], cre_b, cim_b, Bn.t[:, 0, :, :], Bn.t[:, 1, :, :],
                   tmpB.t[:, :, :], [L.r, Bn.r], [Bb.r, tmpB.r])
        Ct = s.sb([128, 2, 16, 64], F32, name="s5Ct")
        for c, nm in enumerate(("s5_c_re", "s5_c_im")):
            cap = s.inp[nm].t.ap()
            for h2 in range(2):
                srcap = bass.AP(cap.tensor, h2 * 8 * 8192, [[64, 128], [8192, 8], [1, 64]])
                s.dma("sp", Ct.t[:, c, h2 * 8:(h2 + 1) * 8, :], srcap, [], [Ct.r])
        s.ts("dve", Ct.t[:, 1, :, :], Ct.t[:, 1, :, :], -1.0, None, ALU.mult, None, [Ct.r], [Ct.r])
        pm8 = s.sb([128, 2], F32, name="s5pm8")
        s.dma("sp", pm8.t[:, :], s.inp["s5_const"].t.ap()[:, 132:134], [], [pm8.r])
        dfm = s.sb([128, 16], F32, name="s5d")
        s.dma("sp", dfm.t[:, :], s.inp["s5_d_fm"].t.ap()[:, :], [], [dfm.r])
        CTp = s.sb([128, 4, 2, 128], F32, name="s5CTp")
        s.memset("pool", CTp.t[:, :, :, :], 0.0, [CTp.r])
        Ssrc = [s.sb([128, 128], F32, name="s5S") for _ in range(2)]
        for t_ in Ssrc:
            s.memset("pool", t_.t[:, :], 0.0, [t_.r])
        BTp = s.pool([128, 2, 128], BF16, 2)
        Cdm = s.sb([128, 2, 64], F32, name="s5Cdm")
        Hb = [s.sb([128, 2, TT], F32, name="s5H") for _ in range(5)]
        Hl = s.sb([128, 5, 2, 64], F32, name="s5Hl")
        ug_p = s.pool([128, TT], BF16, 2)
        yst = s.pool([128, TT], BF16, 2)
        gtmp = s.pool([128, 512], F32, 2)
        ptmp = s.sb([128, SEQ], F32, name="s5ptmp")
        UTap = s.UT.t.ap()
        Y1ap = s.Y1T.t.ap()
        Tm = Hb[4]
        LP = SEQ
        for gc in range(16):
            ug = ug_p[gc % 2]
            s.dma("sp", ug.t[:, :], UTap[:, gc, :], [s.UT.r], [ug.r])
            for pr in range(4):
                pair = gc * 4 + pr
                X = Hb[pr]
                bt = BTp[pair % 2]
                for c in range(2):
                    S_ = Ssrc[c]
                    if pr > 0:
                        pc = (2 * (pr - 1)) * 16
                        s.memset("pool", S_.t[:, pc:pc + 32], 0.0, [S_.r])
                    elif gc > 0:
                        s.memset("pool", S_.t[:, 96:128], 0.0, [S_.r])
                    c0 = 2 * pr * 16
                    s.cp("dve", S_.t[0:64, c0:c0 + 16], Bb.t[0:64, c, pair, :], [Bb.r], [S_.r])
                    s.cp("dve", S_.t[64:128, c0 + 16:c0 + 32], Bb.t[64:128, c, pair, :], [Bb.r], [S_.r])
                    ps = s.ps()
                    s.tp(ps.t[:, 0:128], S_.t[:, :], s.idf.t[:, :], [S_.r, s.idf.r], [ps.r])
                    s.cp("act", bt.t[:, c, :], ps.t[:, 0:128], [ps.r], [bt.r])
                for (t0, tn) in TBS:
                    for c in range(2):
                        ps = s.ps()
                        s.mm(ps.t[:, 0:tn], bt.t[:, c, :], ug.t[:, t0:t0 + tn], True, True, [bt.r, ug.r], [ps.r])
                        s.cp("act" if c else "dve", Tm.t[:, c, t0:t0 + tn], ps.t[:, 0:tn], [ps.r], [Tm.r])
                for c in range(2):
                    sv = Tm.t[:, c, LP:TT].rearrange("p (b t) -> p b t", b=4)[:, :, 0]
                    s.tt("dve", sv, sv, ah0.t[:, :, c, pair], ALU.add, [Tm.r, ah0.r], [Tm.r])
                src, dst = Tm, X
                for k in range(NST):
                    sh = 1 << k
                    ar = apw.t[:, k, 0, pair:pair + 1]
                    ai = apw.t[:, k, 1, pair:pair + 1]
                    nai = apw.t[:, k, 2, pair:pair + 1]
                    for c, e in ((0, "dve"), (1, "pool")):
                        oth = 1 - c
                        s.cp(e, dst.t[:, c, 0:sh], src.t[:, c, 0:sh], [src.r], [dst.r])
                        if e == "dve":
                            s.stt(e, dst.t[:, c, sh:LP], src.t[:, c, 0:LP - sh], ar, src.t[:, c, sh:LP], ALU.mult, ALU.add,
                                  [src.r, apw.r], [dst.r])
                            s.stt(e, dst.t[:, c, sh:LP], src.t[:, oth, 0:LP - sh], nai if c == 0 else ai,
                                  dst.t[:, c, sh:LP], ALU.mult, ALU.add, [src.r, apw.r, dst.r], [dst.r])
                        else:
                            s.ts(e, dst.t[:, c, sh:LP], src.t[:, c, 0:LP - sh], ar, None, ALU.mult, None,
                                 [src.r, apw.r], [dst.r])
                            s.tt(e, dst.t[:, c, sh:LP], dst.t[:, c, sh:LP], src.t[:, c, sh:LP], ALU.add,
                                 [src.r, dst.r], [dst.r])
                            s.ts(e, ptmp.t[:, 0:LP - sh], src.t[:, oth, 0:LP - sh], nai if c == 0 else ai, None, ALU.mult,
                                 None, [src.r, apw.r], [ptmp.r])
                            s.tt(e, dst.t[:, c, sh:LP], dst.t[:, c, sh:LP], ptmp.t[:, 0:LP - sh], ALU.add,
                                 [ptmp.r, dst.r], [dst.r])
                        sv = src.t[:, c, LP:TT].rearrange("p (b t) -> p b t", b=4)
                        so = src.t[:, oth, LP:TT].rearrange("p (b t) -> p b t", b=4)
                        dv = dst.t[:, c, LP:TT].rearrange("p (b t) -> p b t", b=4)
                        if sh < 4:
                            s.cp("dve", dv[:, :, 0:sh], sv[:, :, 0:sh], [src.r], [dst.r])
                            s.stt("dve", dv[:, :, sh:4], sv[:, :, 0:4 - sh], ar, sv[:, :, sh:4], ALU.mult, ALU.add,
                                  [src.r, apw.r], [dst.r])
                            s.stt("dve", dv[:, :, sh:4], so[:, :, 0:4 - sh], nai if c == 0 else ai, dv[:, :, sh:4],
                                  ALU.mult, ALU.add, [src.r, apw.r, dst.r], [dst.r])
                        else:
                            s.cp(e, dv, sv, [src.r], [dst.r])
                    src, dst = dst, src
                for c in range(2):
                    s.cp("dve", Hl.t[:, 0, c, pair:pair + 1], X.t[:, c, LP - 1:LP], [X.r], [Hl.r])
                    s.cp("dve", Hl.t[:, 1:5, c, pair], X.t[:, c, LP:TT].rearrange("p (b t) -> p b t", b=4)[:, :, 3],
                         [X.r], [Hl.r])
            for c in range(2):
                s.tt("dve", Cdm.t[:, :, :], Ct.t[:, c, gc, :].unsqueeze(1).to_broadcast([128, 2, 64]),
                     pm8.t[:, :].unsqueeze(2).to_broadcast([128, 2, 64]), ALU.mult, [Ct.r, pm8.r], [Cdm.r])
                ps = s.ps()
                s.tp(ps.t[:, 0:128], Cdm.t[:, :, :].rearrange("p a b -> p (a b)"), s.idf.t[:, :], [Cdm.r, s.idf.r], [ps.r])
                for pr in range(4):
                    s.cp("dve", CTp.t[:, pr, c, pr * 32:(pr + 1) * 32], ps.t[:, pr * 32:(pr + 1) * 32], [ps.r], [CTp.r])
            ys = yst[gc % 2]
            for (t0, tn) in TBS:
                ps = s.ps()
                n = 0
                for pr in range(4):
                    for c in range(2):
                        s.mm(ps.t[:, 0:tn], CTp.t[:, pr, c, :], Hb[pr].t[:, c, t0:t0 + tn], n == 0, n == 7,
                             [CTp.r, Hb[pr].r], [ps.r])
                        n += 1
                g1 = gtmp[0]
                g2 = gtmp[1]
                s.stt("dve", g1.t[:, 0:tn], ug.t[:, t0:t0 + tn], dfm.t[:, gc:gc + 1], ps.t[:, 0:tn], ALU.mult, ALU.add,
                      [ug.r, dfm.r, ps.r], [g1.r])
                s.act(g2.t[:, 0:tn], g1.t[:, 0:tn], AF.Square, [g1.r], [g2.r])
                s.ts("dve", g2.t[:, 0:tn], g2.t[:, 0:tn], 0.044715, 1.0, ALU.mult, ALU.add, [g2.r], [g2.r])
                s.tt("dve", g2.t[:, 0:tn], g2.t[:, 0:tn], g1.t[:, 0:tn], ALU.mult, [g1.r, g2.r], [g2.r])
                s.act(g2.t[:, 0:tn], g2.t[:, 0:tn], AF.Sigmoid, [g2.r], [g2.r], scale=1.5957691216057308)
                s.tt("dve", ys.t[:, t0:t0 + tn], g2.t[:, 0:tn], g1.t[:, 0:tn], ALU.mult, [g1.r, g2.r], [ys.r])
            s.dma("sp", Y1ap[:, gc, :], ys.t[:, :], [ys.r], [s.Y1T.r])
        lh = s.sb([128, 64, 2], F32, name="s5lh")
        og = s.sb([128, 5, 64, 2], F32, name="s5og")
        for w in range(5):
            for c in range(2):
                s.tt("dve", lh.t[:, :, :], Hl.t[:, w, c, :].unsqueeze(2).to_broadcast([128, 64, 2]),
                     lm.unsqueeze(1).to_broadcast([128, 64, 2]), ALU.mult, [Hl.r, cst.r], [lh.r])
                ps = s.ps()
                s.mm(ps.t[:, 0:64], lh.t[:, :, :].rearrange("p a b -> p (a b)"), I2, True, True, [lh.r, cst.r], [ps.r])
                s.cp("dve", og.t[:, w, :, c], ps.t[:, 0:64], [ps.r], [og.r])
        s.dma("sp", s.out["s5_p"].t.ap()[:, :], og.t[:, 0, :, :].rearrange("p a b -> p (a b)"), [og.r], [s.out["s5_p"].r])
        s.dma("sp", s.out["s5_s"].t.ap().rearrange("b g x -> g b x"),
              og.t[:, 1:5, :, :].rearrange("p w a b -> p w (a b)"), [og.r], [s.out["s5_s"].r])

    def s5_glu(self, y1T):
        s = self
        W = s.inp["s5_w_glu"].t.ap()
        s.dma("sp", y1T.t[:, :, :], s.Y1T.t.ap()[:, :, :], [s.Y1T.r], [y1T.r])
        zst = s.pool([128, 2, 512], BF16, 2)
        gst = s.pool([128, 2, 512], BF16, 2)
        sg = s.pool([128, 512], BF16, 2)
        KC = list(range(16))
        ZTap = s.ZT.t.ap()
        Gap = s.GTd.t.ap()
        qi = 0
        for blk in range(8):
            wt = s.nxt(s.wp, "wi")
            s.load_w(wt, W, D, blk * 256, 256)
            for (t0, tn) in TBS:
                zt = zst[qi % 2]
                gt = gst[qi % 2]
                qi += 1
                s.dma("sp", zt.t[:, :, 0:tn], ZTap[:, blk * 2:blk * 2 + 2, t0:t0 + tn], [s.ZT.r], [zt.r])
                for j in range(2):
                    ch = blk * 2 + j
                    ps = s.ps()
                    s.ws_mm(ps.t[:, 0:tn], ps.r, wt, (j * 128, j * 128 + 128), y1T, KC, t0, tn, [y1T.r])
                    sgt = sg[j]
                    s.act(sgt.t[:, 0:tn], ps.t[:, 0:tn], AF.Sigmoid, [ps.r], [sgt.r])
                    s.tt("dve", sgt.t[:, 0:tn], sgt.t[:, 0:tn], y1T.t[:, ch, t0:t0 + tn], ALU.mult, [sgt.r, y1T.r], [sgt.r])
                    s.tt("pool", gt.t[:, j, 0:tn], sgt.t[:, 0:tn], zt.t[:, j, 0:tn], ALU.mult, [sgt.r, zt.r], [gt.r])
                s.dma("sp", Gap[:, blk * 2:blk * 2 + 2, t0:t0 + tn], gt.t[:, :, 0:tn], [gt.r], [s.GTd.r])

    def setup_gdn(self):
        s = self
        s.din("gdn_w_in", [D, 12352])
        s.din("gdn_w_out", [4096, D])
        s.din("gdn_cw", [128, 64, 4])
        s.din("gdn_ab", [128, 64])
        s.din("gdn_nw", [128, 128])
        s.din("gdn_state", [4, 32, 128, 128])
        s.din("gdn_cbuf", [12, 8192])
        s.din("gdn_mp", [128, 770])
        s.din("gdn_ms", [16, 580])
        s.dout("gd_p", [32, 128, 128])
        s.dout("gd_s", [4, 32, 128, 128])
        s.dout("gc", [15, 8192])
        s.QN = s.dram("QN", [16, 128, TT], BF16)
        s.KN = s.dram("KN", [16, 128, TT], BF16)
        s.VT = s.dram("VT", [32, 128, TT], BF16)
        s.GBd = s.dram("GBd", [128, NTILE, 64], F32)

    def gdn_proj(self, xT):
        s = self
        W = s.inp["gdn_w_in"].t.ap()
        KC = list(range(16))
        cw = s.sb([128, 64, 4], F32, name="g_cw")
        s.dma("sp", cw.t[:, :, :], s.inp["gdn_cw"].t.ap()[:, :, :], [], [cw.r])
        ab = s.sb([128, 96], F32, name="g_ab")
        s.dma("sp", ab.t[:, 0:64], s.inp["gdn_ab"].t.ap()[:, :], [], [ab.r])
        s.act(ab.t[:, 64:96], ab.t[:, 0:32], AF.Exp, [ab.r], [ab.r])
        onesb = s.sb([128, 128], BF16, name="g_ones")
        s.memset("dve", onesb.t[:, :], 1.0, [onesb.r])
        Tl = s.sb([128, 64, 15], F32, name="g_Tl")
        pre_p = s.pool([128, 3 + SEQ], F32, 2)
        pres_p = s.pool([128, 4, 7], F32, 2)
        cv_p = s.pool([128, TT], F32, 2)
        sq_p = s.pool([128, TT], BF16, 2)
        ob_p = s.pool([128, TT], BF16, 2)
        cb_p = s.pool([12, 128], F32, 2)
        zst = s.pool([128, 2, 512], BF16, 2)
        GB = s.sb([128, NTILE, 64], F32, name="g_GB")
        cbap = s.inp["gdn_cbuf"].t.ap()
        ZTap = s.ZT.t.ap()
        qi = 0
        for blk in range(49):
            wt = s.nxt(s.wp, "wi")
            ncol = 256 if blk < 48 else 64
            s.load_w(wt, W, D, blk * 256, ncol)
            if blk < 32:
                for j in range(2):
                    ch = blk * 2 + j
                    pre = pre_p[ch % 2]
                    prs = pres_p[ch % 2]
                    cv = cv_p[ch % 2]
                    s.memset("pool", pre.t[:, 0:3], 0.0, [pre.r])
                    cb = cb_p[ch % 2]
                    s.dma("sp", cb.t[:, :], cbap[:, ch * 128:(ch + 1) * 128], [], [cb.r])
                    psb = s.ps()
                    s.tp(psb.t[:, 0:12], cb.t[:, :], s.idf.t[0:12, 0:12], [cb.r, s.idf.r], [psb.r])
                    s.cp("act", prs.t[:, :, 0:3], psb.t[:, 0:12].rearrange("p (b r) -> p b r", b=4), [psb.r], [prs.r])
                    for (t0, tn) in TBS:
                        ps = s.ps()
                        s.ws_mm(ps.t[:, 0:tn], ps.r, wt, (j * 128, j * 128 + 128), xT, KC, t0, tn, [xT.r])
                        if t0 < SEQ:
                            s.cp("act", pre.t[:, 3 + t0:3 + t0 + tn], ps.t[:, 0:tn], [ps.r], [pre.r])
                        else:
                            s.cp("act", prs.t[:, :, 3:7], ps.t[:, 0:tn].rearrange("p (b t) -> p b t", b=4), [ps.r], [prs.r])
                    s.ts("dve", cv.t[:, 0:SEQ], pre.t[:, 0:SEQ], cw.t[:, ch, 0:1], None, ALU.mult, None, [pre.r, cw.r], [cv.r])
                    for k in range(1, 4):
                        s.stt("dve", cv.t[:, 0:SEQ], pre.t[:, k:k + SEQ], cw.t[:, ch, k:k + 1], cv.t[:, 0:SEQ], ALU.mult,
                              ALU.add, [pre.r, cw.r, cv.r], [cv.r])
                    cvs = cv.t[:, SEQ:TT].rearrange("p (b t) -> p b t", b=4)
                    s.ts("dve", cvs, prs.t[:, :, 0:4], cw.t[:, ch, 0:1], None, ALU.mult, None, [prs.r, cw.r], [cv.r])
                    for k in range(1, 4):
                        s.stt("dve", cvs, prs.t[:, :, k:k + 4], cw.t[:, ch, k:k + 1], cvs, ALU.mult, ALU.add,
                              [prs.r, cw.r, cv.r], [cv.r])
                    s.cp("pool", Tl.t[:, ch, 0:3], pre.t[:, SEQ:SEQ + 3], [pre.r], [Tl.r])
                    s.cp("pool", Tl.t[:, ch, 3:15].rearrange("p (b r) -> p b r", b=4), prs.t[:, :, 4:7], [prs.r], [Tl.r])
                    s.act(cv.t[:, :], cv.t[:, :], AF.Silu, [cv.r], [cv.r])
                    ob = ob_p[ch % 2]
                    if ch < 32:
                        sq = sq_p[ch % 2]
                        s.act(sq.t[:, :], cv.t[:, :], AF.Square, [cv.r], [sq.r])
                        for (t0, tn) in TBS:
                            ps = s.ps()
                            s.mm(ps.t[:, 0:tn], onesb.t[:, :], sq.t[:, t0:t0 + tn], True, True, [onesb.r, sq.r], [ps.r])
                            g1 = s.nxt(s.ev4, "ev4i")
                            s.ts("dve", g1.t[:, 0:tn], ps.t[:, 0:tn], 1e-6, None, ALU.add, None, [ps.r], [g1.r])
                            s.act(g1.t[:, 0:tn], g1.t[:, 0:tn], AF.Sqrt, [g1.r], [g1.r])
                            s.op("dve", lambda g, g1=g1, tn=tn: g.reciprocal(g1.t[:, 0:tn], g1.t[:, 0:tn]), [g1.r], [g1.r])
                            sc = (128 ** -0.5) if ch < 16 else 1.0
                            s.stt("dve", ob.t[:, t0:t0 + tn], cv.t[:, t0:t0 + tn], sc, g1.t[:, 0:tn], ALU.mult, ALU.mult,
                                  [cv.r, g1.r], [ob.r])
                        dst = s.QN if ch < 16 else s.KN
                        s.dma("sp", dst.t.ap()[ch % 16, :, :], ob.t[:, :], [ob.r], [dst.r])
                    else:
                        s.cp("pool", ob.t[:, :], cv.t[:, :], [cv.r], [ob.r])
                        s.dma("sp", s.VT.t.ap()[ch - 32, :, :], ob.t[:, :], [ob.r], [s.VT.r])
            elif blk < 48:
                zb = blk - 32
                for (t0, tn) in TBS:
                    st = zst[qi % 2]
                    qi += 1
                    for j in range(2):
                        ps = s.ps()
                        s.ws_mm(ps.t[:, 0:tn], ps.r, wt, (j * 128, j * 128 + 128), xT, KC, t0, tn, [xT.r])
                        s.act(st.t[:, j, 0:tn], ps.t[:, 0:tn], AF.Silu, [ps.r], [st.r])
                    s.dma("sp", ZTap[:, zb * 2:zb * 2 + 2, t0:t0 + tn], st.t[:, :, 0:tn], [st.r], [s.ZT.r])
            else:
                for i, (t0, tn) in enumerate(TILES):
                    ps = s.ps()
                    s.as_mm(ps.t[0:tn, 0:64], ps.r, wt, (0, 64), xT, KC, t0, tn, [xT.r])
                    g1 = s.nxt(s.ev4, "ev4i")
                    s.tt("dve", g1.t[0:tn, 0:32], ps.t[0:tn, 0:32], ab.t[0:tn, 32:64], ALU.add, [ps.r, ab.r], [g1.r])
                    s.act(g1.t[0:tn, 0:32], g1.t[0:tn, 0:32], AF.Exp, [g1.r], [g1.r])
                    s.act(g1.t[0:tn, 0:32], g1.t[0:tn, 0:32], AF.Ln, [g1.r], [g1.r], bias=1.0, scale=1.0)
                    s.stt("dve", GB.t[0:tn, i, 0:32], g1.t[0:tn, 0:32], -1.0, ab.t[0:tn, 64:96], ALU.mult, ALU.mult,
                          [g1.r, ab.r], [GB.r])
                    s.act(GB.t[0:tn, i, 32:64], ps.t[0:tn, 32:64], AF.Sigmoid, [ps.r], [GB.r, ps.r])
        s.dma("sp", s.GBd.t.ap()[:, :, :], GB.t[:, :, :], [GB.r], [s.GBd.r])
        gcap = s.out["gc"].t.ap()
        rt_p = s.pool([15, 512], F32, 2)
        for g4 in range(16):
            ps = s.ps()
            for j in range(4):
                ch = g4 * 4 + j
                s.tp(ps.t[0:15, j * 128:(j + 1) * 128], Tl.t[:, ch, :], s.idf.t[:, :], [Tl.r, s.idf.r], [ps.r], inc=(j == 3))
            rt = rt_p[g4 % 2]
            s.cp("act", rt.t[:, :], ps.t[0:15, :], [ps.r], [rt.r])
            s.dma("sp", gcap[:, g4 * 512:(g4 + 1) * 512], rt.t[:, :], [rt.r], [s.out["gc"].r])

    def gdn_core(self):
        s = self
        mp = s.sb([128, 770], F32, name="g_mp")
        ms = s.sb([16, 580], F32, name="g_ms")
        s.dma("sp", mp.t[:, :], s.inp["gdn_mp"].t.ap()[:, :], [], [mp.r])
        s.dma("sp", ms.t[:, :], s.inp["gdn_ms"].t.ap()[:, :], [], [ms.r])
        nw = s.sb([128, 128], F32, name="g_nw")
        s.dma("sp", nw.t[:, :], s.inp["gdn_nw"].t.ap()[:, :], [], [nw.r])
        ones = s.sb([128, 128], F32, name="g_ones32")
        s.memset("dve", ones.t[:, :], 1.0, [ones.r])
        GB = s.sb([128, NTILE, 64], F32, name="g_GB2")
        s.dma("sp", GB.t[:, :, :], s.GBd.t.ap()[:, :, :], [s.GBd.r], [GB.r])
        GX = s.sb([128, NTILE, 3, 32], F32, name="g_GX")
        EC = s.sb([128, NTILE, 4, 32], F32, name="g_EC")

        def geom(i):
            if i < 16:
                M = mp.t
                return dict(np=128, ncs=2, triU=M[:, 0:128], bones=M[:, 128:256], Mb=M[:, 256:384], strict=M[:, 384:512],
                            sel=[M[:, 512:640], M[:, 640:768]], cm=M[:, 768:770], mr=mp.r, nlev=5)
            M = ms.t
            return dict(np=16, ncs=4, triU=M[:, 0:16], bones=M[:, 16:32], Mb=M[:, 32:48], strict=M[:, 48:64],
                        sel=[M[:, 64 + c * 128:64 + (c + 1) * 128] for c in range(4)], cm=M[:, 576:580], mr=ms.r, nlev=1)

        for i in range(NTILE):
            G = geom(i)
            n = G["np"]
            g_ap = GB.t[0:n, i, 0:32]
            ps = s.ps()
            s.mm(ps.t[0:n, 0:32], G["triU"], g_ap, True, True, [G["mr"], GB.r], [ps.r])
            s.mm(ps.t[0:n, 32:64], G["bones"], g_ap, True, True, [G["mr"], GB.r], [ps.r])
            s.cp("dve", GX.t[0:n, i, 0, :], ps.t[0:n, 0:32], [ps.r], [GX.r])
            s.act(GX.t[0:n, i, 1, :], ps.t[0:n, 0:32], AF.Exp, [ps.r], [GX.r, ps.r])
            s.tt("dve", GX.t[0:n, i, 2, :], ps.t[0:n, 32:64], GX.t[0:n, i, 0, :], ALU.subtract, [ps.r, GX.r], [GX.r, ps.r])
            s.act(GX.t[0:n, i, 2, :], GX.t[0:n, i, 2, :], AF.Exp, [GX.r], [GX.r])
            ps2 = s.ps()
            for c in range(G["ncs"]):
                s.mm(ps2.t[:, c * 32:(c + 1) * 32], G["sel"][c], g_ap, True, True, [G["mr"], GB.r], [ps2.r],
                     inc=(c == G["ncs"] - 1))
            s.act(EC.t[:, i, 0:G["ncs"], :], ps2.t[:, 0:32 * G["ncs"]].rearrange("p (c h) -> p c h", h=32), AF.Exp,
                  [ps2.r], [EC.r])
        kq_p = s.pool([128, 2, TT], BF16, 2)
        v_p = s.pool([128, TT], BF16, 2)
        z_p = s.pool([128, TT], BF16, 2)
        gs_p = s.pool([128, TT], BF16, 2)
        W = {}
        for nm in ("Gtri", "D", "Ds", "A", "At", "RT", "B0", "Bt0", "B1", "Bt1", "Kbg", "bV", "u", "wT", "P", "PT", "vn",
                   "oa", "qf", "o2", "kd0", "kd1", "kd2", "kd3", "on"):
            W[nm] = s.pool([128, 128], F32, 2)
        colp = s.pool([128, 16], F32, 2)
        Sp = s.sb([128, 128], F32, name="g_S")
        S0 = [s.sb([128, 128], F32, name="g_S0") for _ in range(4)]
        QNap, KNap, VTap, ZTap, Gap = s.QN.t.ap(), s.KN.t.ap(), s.VT.t.ap(), s.ZT.t.ap(), s.GTd.t.ap()
        stap = s.inp["gdn_state"].t.ap()
        it = 0
        GSTEP = int(os.environ.get("G_STEP", "99"))
        for hq in range(int(os.environ.get("G_NHQ", "16"))):
            kq = kq_p[hq % 2]
            s.dma("sp", kq.t[:, 0, :], KNap[hq, :, :], [s.KN.r], [kq.r])
            s.dma("sp", kq.t[:, 1, :], QNap[hq, :, :], [s.QN.r], [kq.r])
            for h in (2 * hq, 2 * hq + 1):
                vT = v_p[h % 2]
                zT = z_p[h % 2]
                gs = gs_p[h % 2]
                s.dma("sp", vT.t[:, :], VTap[h, :, :], [s.VT.r], [vT.r])
                s.dma("sp", zT.t[:, :], ZTap[:, h, :], [s.ZT.r], [zT.r])
                s.memset("pool", Sp.t[:, :], 0.0, [Sp.r])
                for b in range(4):
                    s.dma("sp", S0[b].t[:, :], stap[b, h, :, :], [], [S0[b].r])
                for i in range(NTILE):
                    if i >= int(os.environ.get("G_NT", "17")) and i < 16:
                        continue
                    if i == 16 and os.environ.get("G_NOS"):
                        continue
                    G = geom(i)
                    n = G["np"]
                    t0 = TILES[i][0]
                    w = {k: v[it % 2] for k, v in W.items()}
                    col = colp[it % 2]
                    it += 1
                    gcol = GB.t[0:n, i, h:h + 1]
                    bcol = GB.t[0:n, i, 32 + h:33 + h]
                    gam = GX.t[0:n, i, 0, h:h + 1]
                    eg = GX.t[0:n, i, 1, h:h + 1]
                    edk = GX.t[0:n, i, 2, h:h + 1]
                    mr = G["mr"]
                    kT = kq.t[:, 0, t0:t0 + n]
                    qT = kq.t[:, 1, t0:t0 + n]
                    s.ts("dve", w["Gtri"].t[0:n, 0:n], G["triU"], gcol, None, ALU.mult, None, [mr, GB.r], [w["Gtri"].r])
                    ps = s.ps()
                    s.mm(ps.t[0:n, 0:n], ones.t[0:n, 0:n], w["Gtri"].t[0:n, 0:n], True, False, [ones.r, w["Gtri"].r], [ps.r],
                         inc=False)
                    s.mm(ps.t[0:n, 0:n], s.idf.t[0:n, 0:n], G["Mb"], False, True, [s.idf.r, mr], [ps.r])
                    s.act(w["D"].t[0:n, 0:n], ps.t[0:n, 0:n], AF.Exp, [ps.r, GX.r], [w["D"].r], bias=gam, scale=-1.0)
                    if GSTEP <= 1:
                        continue
                    psk = s.ps()
                    s.mm(psk.t[0:n, 0:n], kT, kT, True, True, [kq.r], [psk.r])
                    s.tt("pool", w["Ds"].t[0:n, 0:n], w["D"].t[0:n, 0:n], G["strict"], ALU.mult, [w["D"].r, mr], [w["Ds"].r])
                    s.stt("dve", w["A"].t[0:n, 0:n], psk.t[0:n, 0:n], bcol, w["Ds"].t[0:n, 0:n], ALU.mult, ALU.mult,
                          [psk.r, GB.r, w["Ds"].r], [w["A"].r])
                    if GSTEP <= 2:
                        continue
                    pst = s.ps()
                    s.tp(pst.t[0:n, 0:n], w["A"].t[0:n, 0:n], s.idf.t[0:n, 0:n], [w["A"].r, s.idf.r], [pst.r])
                    s.cp("act", w["At"].t[0:n, 0:n], pst.t[0:n, 0:n], [pst.r], [w["At"].r])
                    s.tt("dve", w["RT"].t[0:n, 0:n], s.idf.t[0:n, 0:n], pst.t[0:n, 0:n], ALU.subtract, [pst.r, s.idf.r],
                         [w["RT"].r, pst.r])
                    if GSTEP <= 3:
                        continue
                    Bc, Btc = w["A"], w["At"]
                    for lev in range(1, G["nlev"] + 1):
                        Bn_ = w["B%d" % (lev % 2)]
                        Btn = w["Bt%d" % (lev % 2)]
                        p1 = s.ps()
                        s.mm(p1.t[0:n, 0:n], Btc.t[0:n, 0:n], Bc.t[0:n, 0:n], True, True, [Btc.r, Bc.r], [p1.r])
                        s.cp("act", Bn_.t[0:n, 0:n], p1.t[0:n, 0:n], [p1.r], [Bn_.r])
                        if lev < G["nlev"]:
                            p2 = s.ps()
                            s.mm(p2.t[0:n, 0:n], Bc.t[0:n, 0:n], Btc.t[0:n, 0:n], True, True, [Btc.r, Bc.r], [p2.r])
                            s.cp("pool" if False else "act", Btn.t[0:n, 0:n], p2.t[0:n, 0:n], [p2.r], [Btn.r])
                        p3 = s.ps()
                        s.mm(p3.t[0:n, 0:n], Bn_.t[0:n, 0:n], w["RT"].t[0:n, 0:n], True, True, [Bn_.r, w["RT"].r], [p3.r])
                        s.tt("dve", w["RT"].t[0:n, 0:n], w["RT"].t[0:n, 0:n], p3.t[0:n, 0:n], ALU.add, [p3.r, w["RT"].r],
                             [w["RT"].r])
                        Bc, Btc = Bn_, Btn
                    if GSTEP <= 4:
                        continue
                    s.tt("dve", col.t[0:n, 0:1], bcol, eg, ALU.mult, [GB.r, GX.r], [col.r])
                    for c in range(G["ncs"]):
                        s.tt("dve", col.t[0:n, 1 + c:2 + c], edk, G["cm"][:, c:c + 1], ALU.mult, [GX.r, mr], [col.r])
                        s.tt("dve", col.t[0:n, 5 + c:6 + c], eg, G["cm"][:, c:c + 1], ALU.mult, [GX.r, mr], [col.r])
                        s.ts("dve", col.t[0:n, 9 + c:10 + c], G["cm"][:, c:c + 1], -1.0, None, ALU.mult, None, [mr], [col.r])
                    pkt = s.ps()
                    pktb = pkt.t[:, :].bitcast(BF16)
                    s.tp(pktb[0:n, 0:128], kT, s.idb.t[:, :], [kq.r, s.idb.r], [pkt.r], inc=False)
                    s.tp(pktb[0:n, 128:256], vT.t[:, t0:t0 + n], s.idb.t[:, :], [vT.r, s.idb.r], [pkt.r])
                    s.ts("dve", w["Kbg"].t[0:n, :], pktb[0:n, 0:128], col.t[0:n, 0:1], None, ALU.mult, None, [pkt.r, col.r],
                         [w["Kbg"].r])
                    s.ts("dve", w["bV"].t[0:n, :], pktb[0:n, 128:256], bcol, None, ALU.mult, None, [pkt.r, GB.r],
                         [w["bV"].r, pkt.r])
                    for c in range(G["ncs"]):
                        s.ts("dve", w["kd%d" % c].t[0:n, :], pktb[0:n, 0:128], col.t[0:n, 1 + c:2 + c], None, ALU.mult, None,
                             [pkt.r, col.r], [w["kd%d" % c].r, pkt.r])
                    if GSTEP <= 5:
                        continue
                    pu = s.ps()
                    s.mm(pu.t[0:n, 0:128], w["RT"].t[0:n, 0:n], w["bV"].t[0:n, :], True, True, [w["RT"].r, w["bV"].r], [pu.r])
                    s.cp("act", w["vn"].t[0:n, :], pu.t[0:n, 0:128], [pu.r], [w["vn"].r])
                    pw = s.ps()
                    s.mm(pw.t[:, 0:n], w["Kbg"].t[0:n, :], w["RT"].t[0:n, 0:n], True, True, [w["RT"].r, w["Kbg"].r], [pw.r])
                    s.cp("act", w["wT"].t[:, 0:n], pw.t[:, 0:n], [pw.r], [w["wT"].r])
                    if GSTEP <= 6:
                        continue
                    pq = s.ps()
                    s.mm(pq.t[0:n, 0:n], qT, kT, True, True, [kq.r], [pq.r])
                    s.tt("dve", w["P"].t[0:n, 0:n], pq.t[0:n, 0:n], w["D"].t[0:n, 0:n], ALU.mult, [pq.r, w["D"].r], [w["P"].r])
                    ppt = s.ps()
                    s.tp(ppt.t[0:n, 0:n], w["P"].t[0:n, 0:n], s.idf.t[0:n, 0:n], [w["P"].r, s.idf.r], [ppt.r])
                    s.cp("act", w["PT"].t[0:n, 0:n], ppt.t[0:n, 0:n], [ppt.r], [w["PT"].r])
                    s.cp("pool", w["qf"].t[:, 0:n], qT, [kq.r], [w["qf"].r])
                    if GSTEP <= 7:
                        continue
                    for c in range(G["ncs"]):
                        Sc = Sp if i < 16 else S0[c]
                        p1 = s.ps()
                        s.mm(p1.t[0:n, 0:128], w["wT"].t[:, 0:n], Sc.t[:, :], True, True, [w["wT"].r, Sc.r], [p1.r])
                        s.stt("dve", w["vn"].t[0:n, :], p1.t[0:n, 0:128], col.t[0:n, 9 + c:10 + c], w["vn"].t[0:n, :],
                              ALU.mult, ALU.add, [p1.r, col.r, w["vn"].r], [w["vn"].r])
                        p2 = s.ps()
                        s.mm(p2.t[0:n, 0:128], w["qf"].t[:, 0:n], Sc.t[:, :], True, True, [w["qf"].r, Sc.r], [p2.r])
                        if c == 0:
                            s.ts("dve", w["oa"].t[0:n, :], p2.t[0:n, 0:128], col.t[0:n, 5:6], None, ALU.mult, None,
                                 [p2.r, col.r], [w["oa"].r])
                        else:
                            s.stt("dve", w["oa"].t[0:n, :], p2.t[0:n, 0:128], col.t[0:n, 5 + c:6 + c], w["oa"].t[0:n, :],
                                  ALU.mult, ALU.add, [p2.r, col.r, w["oa"].r], [w["oa"].r])
                        p4 = s.ps()
                        s.mm(p4.t[:, 0:128], w["kd%d" % c].t[0:n, :], w["vn"].t[0:n, :], True, True,
                             [w["kd%d" % c].r, w["vn"].r], [p4.r])
                        s.stt("dve", Sc.t[:, :], Sc.t[:, :], EC.t[:, i, c, h:h + 1], p4.t[:, 0:128], ALU.mult, ALU.add,
                              [Sc.r, EC.r, p4.r], [Sc.r])
                    p5 = s.ps()
                    s.mm(p5.t[0:n, 0:128], w["PT"].t[0:n, 0:n], w["vn"].t[0:n, :], True, True, [w["PT"].r, w["vn"].r], [p5.r])
                    s.tt("dve", w["oa"].t[0:n, :], w["oa"].t[0:n, :], p5.t[0:n, 0:128], ALU.add, [p5.r, w["oa"].r], [w["oa"].r])
                    if GSTEP <= 8:
                        continue
                    s.act(w["o2"].t[0:n, :], w["oa"].t[0:n, :], AF.Square, [w["oa"].r], [w["o2"].r, col.r],
                          accum_out=col.t[0:n, 13:14])
                    s.ts("dve", col.t[0:n, 14:15], col.t[0:n, 13:14], 1.0 / 128, 1e-6, ALU.mult, ALU.add, [col.r], [col.r])
                    s.act(col.t[0:n, 14:15], col.t[0:n, 14:15], AF.Sqrt, [col.r], [col.r])
                    s.op("dve", lambda g, col=col, n=n: g.reciprocal(col.t[0:n, 15:16], col.t[0:n, 14:15]), [col.r], [col.r])
                    s.stt("dve", w["on"].t[0:n, :], w["oa"].t[0:n, :], col.t[0:n, 15:16], nw.t[0:n, :], ALU.mult, ALU.mult,
                          [w["oa"].r, col.r, nw.r], [w["on"].r])
                    pf = s.ps()
                    s.tp(pf.t[:, 0:n], w["on"].t[0:n, :], s.idf.t[0:n, 0:n], [w["on"].r, s.idf.r], [pf.r])
                    s.tt("dve", gs.t[:, t0:t0 + n], pf.t[:, 0:n], zT.t[:, t0:t0 + n], ALU.mult, [pf.r, zT.r], [gs.r])
                s.dma("sp", Gap[:, h, :], gs.t[:, :], [gs.r], [s.GTd.r])
                s.dma("sp", s.out["gd_p"].t.ap()[h, :, :], Sp.t[:, :], [Sp.r], [s.out["gd_p"].r])
                for b in range(4):
                    s.dma("sp", s.out["gd_s"].t.ap()[b, h, :, :], S0[b].t[:, :], [S0[b].r], [s.out["gd_s"].r])

    def setup_dsa(self):
        s = self
        s.din("dsa_w_in", [D, 7312])
        s.din("dsa_w_out", [D, D])
        s.din("d_bias_p", [32, 128, 2048])
        s.din("d_causal", [128, 128])
        s.dout("d_kv", [TT, 1024])
        s.dout("d_ki", [TT, 128])
        s.KT = s.dram("KT", [64, 8, TT], BF16)
        s.VK = s.dram("VK", [128, NTILE, 512], BF16)
        s.KIT = s.dram("KIT", [128, TT], BF16)
        s.WI = s.dram("WI", [128, NTILE, 16], F32)
        s.MK = s.dram("MK", [16, 128, 2048], BF16)

    def dsa_proj(self, xT):
        s = self
        W = s.inp["dsa_w_in"].t.ap()
        KC = list(range(16))
        qst = s.pool([64, 4, 512], BF16, 2)
        zst = s.pool([128, 2, 512], BF16, 2)
        kst = s.pool([64, 4, 512], BF16, 2)
        kvs = s.pool([128, NTILE, 256], F32, 2)
        vbs = s.pool([128, NTILE, 256], BF16, 2)
        QTap, ZTap, KTap, VKap, QIap = s.QT.t.ap(), s.ZT.t.ap(), s.KT.t.ap(), s.VK.t.ap(), s.QN.t.ap()
        okv = s.out["d_kv"].t.ap()
        oki = s.out["d_ki"].t.ap()
        qi = 0
        for blk in range(29):
            wt = s.nxt(s.wp, "wi")
            ncol = 256 if blk < 28 else 144
            s.load_w(wt, W, D, blk * 256, ncol)
            if blk < 8:
                for (t0, tn) in TBS:
                    st = qst[qi % 2]
                    qi += 1
                    for j in range(4):
                        ps = s.ps()
                        s.ws_mm(ps.t[0:64, 0:tn], ps.r, wt, (j * 64, j * 64 + 64), xT, KC, t0, tn, [xT.r])
                        if j % 2:
                            s.act(st.t[:, j, 0:tn], ps.t[0:64, 0:tn], AF.Copy, [ps.r], [st.r], scale=0.125)
                        else:
                            s.ts("dve", st.t[:, j, 0:tn], ps.t[0:64, 0:tn], 0.125, None, ALU.mult, None, [ps.r], [st.r])
                    s.dma("sp", QTap[:, blk * 4:blk * 4 + 4, t0:t0 + tn], st.t[:, :, 0:tn], [st.r], [s.QT.r])
            elif blk < 12:
                kb = blk - 8
                kv = kvs[blk % 2]
                for i, (t0, tn) in enumerate(TILES):
                    ps = s.ps()
                    s.as_mm(ps.t[0:tn, 0:256], ps.r, wt, (0, 256), xT, KC, t0, tn, [xT.r])
                    s.cp(s.ev_eng(), kv.t[0:tn, i, :], ps.t[0:tn, 0:256], [ps.r], [kv.r])
                s.dma("sp", okv[0:2048, kb * 256:(kb + 1) * 256].rearrange("(i p) n -> p i n", p=128), kv.t[:, 0:16, :],
                      [kv.r], [s.out["d_kv"].r])
                s.dma("sp", okv[2048:TT, kb * 256:(kb + 1) * 256], kv.t[0:NS, 16, :], [kv.r], [s.out["d_kv"].r])
                if blk < 10:
                    for (t0, tn) in TBS:
                        st = kst[qi % 2]
                        qi += 1
                        for j in range(4):
                            ps = s.ps()
                            s.ws_mm(ps.t[0:64, 0:tn], ps.r, wt, (j * 64, j * 64 + 64), xT, KC, t0, tn, [xT.r])
                            s.cp(s.ev_eng(), st.t[:, j, 0:tn], ps.t[0:64, 0:tn], [ps.r], [st.r])
                        s.dma("sp", KTap[:, kb * 4:kb * 4 + 4, t0:t0 + tn], st.t[:, :, 0:tn], [st.r], [s.KT.r])
                else:
                    vb = vbs[blk % 2]
                    s.cp("pool", vb.t[:, 0:16, :], kv.t[:, 0:16, :], [kv.r], [vb.r])
                    s.cp("pool", vb.t[0:NS, 16, :], kv.t[0:NS, 16, :], [kv.r], [vb.r])
                    vo = (blk - 10) * 256
                    s.dma("sp", VKap[:, 0:16, vo:vo + 256], vb.t[:, 0:16, :], [vb.r], [s.VK.r])
                    s.dma("sp", VKap[0:NS, 16, vo:vo + 256], vb.t[0:NS, 16, :], [vb.r], [s.VK.r])
            elif blk < 20:
                zb = blk - 12
                for (t0, tn) in TBS:
                    st = zst[qi % 2]
                    qi += 1
                    for j in range(2):
                        ps = s.ps()
                        s.ws_mm(ps.t[:, 0:tn], ps.r, wt, (j * 128, j * 128 + 128), xT, KC, t0, tn, [xT.r])
                        s.act(st.t[:, j, 0:tn], ps.t[:, 0:tn], AF.Silu, [ps.r], [st.r])
                    s.dma("sp", ZTap[:, zb * 2:zb * 2 + 2, t0:t0 + tn], st.t[:, :, 0:tn], [st.r], [s.ZT.r])
            elif blk < 28:
                ib = blk - 20
                for (t0, tn) in TBS:
                    st = zst[qi % 2]
                    qi += 1
                    for j in range(2):
                        ps = s.ps()
                        s.ws_mm(ps.t[:, 0:tn], ps.r, wt, (j * 128, j * 128 + 128), xT, KC, t0, tn, [xT.r])
                        s.cp(s.ev_eng(), st.t[:, j, 0:tn], ps.t[:, 0:tn], [ps.r], [st.r])
                    for j in range(2):
                        s.dma("sp", QIap[ib * 2 + j, :, t0:t0 + tn], st.t[:, j, 0:tn], [st.r], [s.QN.r])
            else:
                for (t0, tn) in TBS:
                    st = zst[qi % 2]
                    qi += 1
                    ps = s.ps()
                    s.ws_mm(ps.t[:, 0:tn], ps.r, wt, (0, 128), xT, KC, t0, tn, [xT.r])
                    s.cp("act", st.t[:, 0, 0:tn], ps.t[:, 0:tn], [ps.r], [st.r])
                    s.dma("sp", s.KIT.t.ap()[:, t0:t0 + tn], st.t[:, 0, 0:tn], [st.r], [s.KIT.r])
                kv = kvs[blk % 2]
                for i, (t0, tn) in enumerate(TILES):
                    ps = s.ps()
                    s.as_mm(ps.t[0:tn, 0:144], ps.r, wt, (0, 144), xT, KC, t0, tn, [xT.r])
                    s.cp("dve", kv.t[0:tn, i, 0:144], ps.t[0:tn, 0:144], [ps.r], [kv.r])
                s.dma("sp", oki[0:2048, :].rearrange("(i p) n -> p i n", p=128), kv.t[:, 0:16, 0:128], [kv.r],
                      [s.out["d_ki"].r])
                s.dma("sp", oki[2048:TT, :], kv.t[0:NS, 16, 0:128], [kv.r], [s.out["d_ki"].r])
                s.dma("sp", s.WI.t.ap()[:, 0:16, :], kv.t[:, 0:16, 128:144], [kv.r], [s.WI.r])
                s.dma("sp", s.WI.t.ap()[0:NS, 16, :], kv.t[0:NS, 16, 128:144], [kv.r], [s.WI.r])

    def dsa_prompt(self):
        s = self
        QIap, MKap, QTap, ZTap, Gap = s.QN.t.ap(), s.MK.t.ap(), s.QT.t.ap(), s.ZT.t.ap(), s.GTd.t.ap()
        kiT = s.sb([128, TT], BF16, name="d_kiT")
        s.dma("sp", kiT.t[:, :], s.KIT.t.ap()[:, :], [s.KIT.r], [kiT.r])
        wi = s.sb([128, NTILE, 16], F32, name="d_wi")
        s.dma("sp", wi.t[:, :, :], s.WI.t.ap()[:, :, :], [s.WI.r], [wi.r])
        s.ts("dve", wi.t[:, :, :], wi.t[:, :, :], (128 ** -0.5) * 0.25, None, ALU.mult, None, [wi.r], [wi.r])
        cneg = s.sb([128, 128], F32, name="d_cneg")
        s.dma("sp", cneg.t[:, :], s.inp["d_causal"].t.ap()[:, :], [], [cneg.r])
        with s.scope():
            qib_p = s.pool([128, 16, 128], BF16, 2)
            I_p = s.pool([128, 2048], F32, 2)
            R_p = s.pool([128, 512], F32, 3)
            jk = s.sb([128, 2048], BF16, name="d_jk")
            mk_p = s.pool([128, 2048], BF16, 2)
            cl = s.pool([128, 8], F32, 2)
            for n in range(16):
                t0 = n * 128
                nk = t0 + 128
                qib = qib_p[n % 2]
                s.dma("sp", qib.t[:, :, :], QIap[:, :, t0:t0 + 128].rearrange("h p t -> p h t"), [s.QN.r], [qib.r])
                I = I_p[n % 2]
                for h in range(16):
                    for c0 in range(0, nk, 512):
                        cn = min(512, nk - c0)
                        ps = s.ps()
                        s.mm(ps.t[:, 0:cn], qib.t[:, h, :], kiT.t[:, c0:c0 + cn], True, True, [qib.r, kiT.r], [ps.r])
                        R = s.nxt(R_p, "dri")
                        s.act(R.t[:, 0:cn], ps.t[:, 0:cn], AF.Relu, [ps.r], [R.r])
                        if h == 0:
                            s.ts("dve", I.t[:, c0:c0 + cn], R.t[:, 0:cn], wi.t[:, n, 0:1], None, ALU.mult, None,
                                 [R.r, wi.r], [I.r])
                        else:
                            s.stt("dve", I.t[:, c0:c0 + cn], R.t[:, 0:cn], wi.t[:, n, h:h + 1], I.t[:, c0:c0 + cn],
                                  ALU.mult, ALU.add, [R.r, wi.r, I.r], [I.r])
                s.tt("dve", I.t[:, t0:nk], I.t[:, t0:nk], cneg.t[:, :], ALU.add, [I.r, cneg.r], [I.r])
                c = cl[n % 2]
                s.memset("dve", c.t[:, 0:1], -256.0, [c.r])
                for itn in range(32):
                    wdt = 256.0 * (0.5 ** itn)
                    s.ts("dve", jk.t[:, 0:nk], I.t[:, 0:nk], c.t[:, 0:1], wdt, ALU.subtract, ALU.is_ge, [I.r, c.r], [jk.r])
                    s.red("dve", c.t[:, 1:2], jk.t[:, 0:nk], ALU.add, [jk.r], [c.r])
                    s.ts("dve", c.t[:, 2:3], c.t[:, 1:2], 256.0, wdt, ALU.is_ge, ALU.mult, [c.r], [c.r])
                    s.tt("dve", c.t[:, 0:1], c.t[:, 0:1], c.t[:, 2:3], ALU.add, [c.r], [c.r])
                mk = mk_p[n % 2]
                s.ts("dve", mk.t[:, 0:nk], I.t[:, 0:nk], c.t[:, 0:1], NEG, ALU.is_lt, ALU.mult, [I.r, c.r], [mk.r])
                s.dma("sp", MKap[n, :, 0:nk], mk.t[:, 0:nk], [mk.r], [s.MK.r])
        with s.scope():
            kT = s.sb([64, 8, SEQ], BF16, name="d_kT")
            s.dma("sp", kT.t[:, :, :], s.KT.t.ap()[:, :, 0:SEQ], [s.KT.r], [kT.r])
            vtok = s.sb([128, 16, 512], BF16, name="d_vtok")
            s.dma("sp", vtok.t[:, :, :], s.VK.t.ap()[:, 0:16, :], [s.VK.r], [vtok.r])
            mk_p = s.pool([128, 2048], BF16, 2)
            tb_p = s.pool([128, 2048], BF16, 3)
            qb_p = s.pool([64, 32, 128], BF16, 2)
            zb_p = s.pool([128, 16, 128], BF16, 2)
            gs_p = s.pool([128, 16, 128], BF16, 2)
            S_p = s.pool([128, 2048], F32, 2)
            E_p = s.pool([128, 2048], BF16, 2)
            P_p = s.pool([128, 2048], BF16, 2)
            PT_p = s.pool([128, 16, 128], BF16, 2)
            smp = s.pool([128, 8], F32, 4)
            Tbap = s.inp["d_bias_p"].t.ap()
            it = 0
            for n in range(16):
                t0 = n * 128
                nk = t0 + 128
                nkt = nk // 128
                mk = mk_p[n % 2]
                qb = qb_p[n % 2]
                zb = zb_p[n % 2]
                gs = gs_p[n % 2]
                s.dma("sp", mk.t[:, 0:nk], MKap[n, :, 0:nk], [s.MK.r], [mk.r])
                s.dma("sp", qb.t[:, :, :], QTap[:, :, t0:t0 + 128], [s.QT.r], [qb.r])
                s.dma("sp", zb.t[:, :, :], ZTap[:, 0:16, t0:t0 + 128], [s.ZT.r], [zb.r])
                po = None
                for h in range(32):
                    kvh = h // 4
                    tb = s.nxt(tb_p, "dtbi")
                    s.dma("pool", tb.t[:, 0:nk], Tbap[h, :, 1920 - t0:2048], [], [tb.r])
                    Sb = S_p[it % 2]
                    E = E_p[it % 2]
                    P = P_p[it % 2]
                    PT = PT_p[it % 2]
                    sm = smp[it % 4]
                    it += 1
                    for c0 in range(0, nk, 512):
                        cn = min(512, nk - c0)
                        ps = s.ps()
                        s.mm(ps.t[:, 0:cn], s.idb.t[:, :], tb.t[:, c0:c0 + cn], True, False, [s.idb.r, tb.r], [ps.r], inc=False)
                        s.mm(ps.t[:, 0:cn], s.idb.t[:, :], mk.t[:, c0:c0 + cn], False, False, [s.idb.r, mk.r], [ps.r],
                             inc=False)
                        s.mm(ps.t[:, 0:cn], qb.t[:, h, :], kT.t[:, kvh, c0:c0 + cn], False, True, [qb.r, kT.r], [ps.r])
                        s.cp("act", Sb.t[:, c0:c0 + cn], ps.t[:, 0:cn], [ps.r], [Sb.r])
                    s.red("dve", sm.t[:, 0:1], Sb.t[:, 0:nk], ALU.max, [Sb.r], [sm.r])
                    s.ts("dve", sm.t[:, 1:2], sm.t[:, 0:1], -1.0, None, ALU.mult, None, [sm.r], [sm.r])
                    s.act(E.t[:, 0:nk], Sb.t[:, 0:nk], AF.Exp, [Sb.r, sm.r], [E.r, sm.r], bias=sm.t[:, 1:2], scale=1.0,
                          accum_out=sm.t[:, 2:3])
                    s.op("dve", lambda g, sm=sm: g.reciprocal(sm.t[:, 3:4], sm.t[:, 2:3]), [sm.r], [sm.r])
                    s.ts("dve", P.t[:, 0:nk], E.t[:, 0:nk], sm.t[:, 3:4], None, ALU.mult, None, [E.r, sm.r], [P.r])
                    for g0 in range(0, nkt, 8):
                        gn = min(8, nkt - g0)
                        pst = s.ps()
                        pstb = pst.t[:, :].bitcast(BF16)
                        for kt in range(gn):
                            s.tp(pstb[:, kt * 128:(kt + 1) * 128], P.t[:, (g0 + kt) * 128:(g0 + kt + 1) * 128], s.idb.t[:, :],
                                 [P.r, s.idb.r], [pst.r], inc=(kt == gn - 1))
                        s.cp(s.ev_eng(), PT.t[:, g0:g0 + gn, :], pstb[:, 0:gn * 128].rearrange("p (a b) -> p a b", a=gn),
                             [pst.r], [PT.r])
                    j = h % 2
                    if j == 0:
                        po = s.ps()
                    for kt in range(nkt):
                        s.mm(po.t[j * 64:(j + 1) * 64, 0:128], vtok.t[:, kt, kvh * 64:(kvh + 1) * 64], PT.t[:, kt, :],
                             kt == 0, kt == nkt - 1, [vtok.r, PT.r], [po.r], inc=(j == 1 and kt == nkt - 1))
                    if j == 1:
                        pr = h // 2
                        s.tt("dve", gs.t[:, pr, :], po.t[:, 0:128], zb.t[:, pr, :], ALU.mult, [po.r, zb.r], [gs.r])
                s.dma("sp", Gap[:, 0:16, t0:t0 + 128], gs.t[:, :, :], [gs.r], [s.GTd.r])

    def setup_dsa_sample(self):
        s = self
        s.din("d_kidx_pool", [NPOOL * 128, 128])
        s.din("d_k_pool", [NPOOL * 128, 512])
        s.din("d_v_pool", [NPOOL * 128, 512])
        s.din("pt_loc", [4, 128], I32)
        s.din("d_iota", [128, 1])
        s.din("d_bias_l", [128, 16, 128])
        s.din("d_bias_31", [128, 128])
        s.din("d_bias_n", [4, 128])
        s.din("d_cneg4", [4, 4])
        s.din("d_selm", [128, 8])
        s.din("d_perm", [128, 128])
        s.OS = s.dram("OS", [NS, D], F32)

    def dsa_sample(self):
        s = self
        iot = s.sb([128, 1], F32, name="ds_iota")
        s.dma("sp", iot.t[:, :], s.inp["d_iota"].t.ap()[:, :], [], [iot.r])
        B31 = s.sb([128, 128], F32, name="ds_b31")
        s.dma("sp", B31.t[:, :], s.inp["d_bias_31"].t.ap()[:, :], [], [B31.r])
        Bl = s.sb([128, 16, 128], F32, name="ds_bl")
        s.dma("sp", Bl.t[:, :, :], s.inp["d_bias_l"].t.ap()[:, :, :], [], [Bl.r])
        cs = s.sb([128, 400], F32, name="ds_cs")
        s.dma("sp", cs.t[0:4, 0:128], s.inp["d_bias_n"].t.ap()[:, :], [], [cs.r])
        s.dma("sp", cs.t[0:4, 128:132], s.inp["d_cneg4"].t.ap()[:, :], [], [cs.r])
        s.dma("sp", cs.t[:, 136:144], s.inp["d_selm"].t.ap()[:, :], [], [cs.r])
        s.dma("sp", cs.t[:, 144:272], s.inp["d_perm"].t.ap()[:, :], [], [cs.r])
        ones = s.sb([128, 128], F32, name="ds_ones")
        s.memset("dve", ones.t[:, :], 1.0, [ones.r])
        ptb = s.sb([128, 128], I32, name="ds_ptb")
        ridx = s.sb([128, 3, 128], I32, name="ds_ridx")
        qiTb = s.sb([128, 16, 4], BF16, name="ds_qi")
        wrow = s.sb([1, 64], F32, name="ds_wrow")
        Wb = s.sb([128, 64], F32, name="ds_Wb")
        IT = s.sb([128, 129, 4], F32, name="ds_IT")
        MT = s.sb([128, 129, 4], F32, name="ds_MT")
        cmpt = s.sb([128, 129, 4], F32, name="ds_cmp")
        lo = s.sb([128, 16], F32, name="ds_lo")
        qT2 = s.sb([128, 32, 4], BF16, name="ds_qT2")
        kn2 = s.sb([128, 8, 4], BF16, name="ds_kn2")
        kin = s.sb([128, 4], BF16, name="ds_kin")
        vnew = s.sb([4, 512], BF16, name="ds_vnew")
        LT = s.sb([128, 129, 128], F32, name="ds_LT")
        ET = s.sb([128, 129, 128], BF16, name="ds_ET")
        kip_p = s.pool([128, 128], F32, 4)
        kvp_p = s.pool([128, 512], F32, 3)
        ktb_p = s.pool([128, 4, 128], BF16, 2)
        R_p = s.pool([128, 512], F32, 2)
        vb_p = s.pool([128, 512], BF16, 2)
        w128 = s.pool([128, 128], F32, 4)
        osl = s.sb([128, 64], F32, name="ds_osl")
        KIpool = s.inp["d_kidx_pool"].t.ap()
        Kpool = s.inp["d_k_pool"].t.ap()
        Vpool = s.inp["d_v_pool"].t.ap()
        QIap, QTap, KTap, VKap, WIap = s.QN.t.ap(), s.QT.t.ap(), s.KT.t.ap(), s.VK.t.ap(), s.WI.t.ap()
        OSap = s.OS.t.ap()
        WSC = (128 ** -0.5) * 0.25
        selm = cs.t[:, 136:144]
        perm = cs.t[:, 144:272]
        DSTEP = int(os.environ.get("DS_STEP", "99"))
        for bi in range(int(os.environ.get("DS_NB", "4"))):
            c0 = SEQ + bi * 4
            s.dma("sp", ptb.t[:, :], s.inp["pt_loc"].t.ap()[bi:bi + 1, :].to_broadcast([128, 128]), [], [ptb.r])
            s.ts("dve", ridx.t[:, 0, :], ptb.t[:, :], 128.0, iot.t[:, 0:1], ALU.mult, ALU.add, [ptb.r, iot.r], [ridx.r])
            s.ts("dve", ridx.t[:, 1, :], ridx.t[:, 0, :], 2.0, None, ALU.mult, None, [ridx.r], [ridx.r])
            s.ts("dve", ridx.t[:, 2, :], ridx.t[:, 0, :], 2.0, 1.0, ALU.mult, ALU.add, [ridx.r], [ridx.r])
            s.dma("sp", qiTb.t[:, :, :], QIap[:, :, c0:c0 + 4].rearrange("h p t -> p h t"), [s.QN.r], [qiTb.r])
            s.dma("sp", wrow.t[0:1, :].rearrange("p (t h) -> p t h", t=4),
                  WIap[bi * 4:bi * 4 + 4, 16, :].unsqueeze(0), [s.WI.r], [wrow.r])
            ps = s.ps()
            s.mm(ps.t[:, 0:64], ones.t[0:1, :], wrow.t[0:1, :], True, True, [ones.r, wrow.r], [ps.r])
            s.ts("dve", Wb.t[:, :].rearrange("p (h t) -> p h t", h=16), ps.t[:, 0:64].rearrange("p (t h) -> p h t", t=4),
                 WSC, None, ALU.mult, None, [ps.r], [Wb.r])
            s.memset("pool", qT2.t[:, :, :], 0.0, [qT2.r])
            for kvh in range(8):
                b0 = (kvh % 2) * 64
                s.dma("sp", qT2.t[b0:b0 + 64, kvh * 4:kvh * 4 + 4, :], QTap[:, kvh * 4:kvh * 4 + 4, c0:c0 + 4], [s.QT.r], [qT2.r])
            for par in range(2):
                s.dma("sp", kn2.t[par * 64:(par + 1) * 64, 0:4, :], KTap[:, par:8:2, c0:c0 + 4], [s.KT.r], [kn2.r])
            s.dma("sp", kin.t[:, :], s.KIT.t.ap()[:, c0:c0 + 4], [s.KIT.r], [kin.r])
            s.dma("sp", vnew.t[0:4, :], VKap[bi * 4:bi * 4 + 4, 16, :], [s.VK.r], [vnew.r])
            qif = qiTb.t[:, :, :].rearrange("p h t -> p (h t)")

            def score_tail(psS, nrow, ncols, dst):
                R = s.nxt(R_p, "dsr")
                npg = ncols // 64
                s.act(R.t[0:nrow, 0:ncols], psS.t[0:nrow, 0:ncols], AF.Relu, [psS.r], [R.r])
                Rv = R.t[0:nrow, 0:ncols].rearrange("p (g c) -> p g c", g=npg)
                s.tt("dve", Rv, Rv, Wb.t[0:nrow, :].unsqueeze(1).to_broadcast([nrow, npg, 64]), ALU.mult, [R.r, Wb.r], [R.r])
                s.red("dve", dst, R.t[0:nrow, 0:ncols].rearrange("p (g h t) -> p g t h", g=npg, h=16), ALU.add, [R.r], [IT.r])

            for j0 in range(0, 128, 8):
                psS = s.ps()
                for half in range(2):
                    pst = s.ps()
                    for jj in range(4):
                        j = j0 + half * 4 + jj
                        kip = s.nxt(kip_p, "dskip")
                        s.dma_gather(kip.t[:, :], KIpool, ridx.t[:, 0, j:j + 1], [ridx.r], [kip.r])
                        s.tp(pst.t[:, jj * 128:(jj + 1) * 128], kip.t[:, :], s.idf.t[:, :], [kip.r, s.idf.r], [pst.r],
                             inc=(jj == 3))
                    ktb = s.nxt(ktb_p, "dsktb")
                    s.cp("act", ktb.t[:, :, :], pst.t[:, :].rearrange("p (a b) -> p a b", a=4), [pst.r], [ktb.r])
                    for jj in range(4):
                        cc = (half * 4 + jj) * 64
                        s.mm(psS.t[:, cc:cc + 64], ktb.t[:, jj, :], qif, True, True, [ktb.r, qiTb.r], [psS.r],
                             inc=(half == 1 and jj == 3))
                score_tail(psS, 128, 512, IT.t[:, j0:j0 + 8, :])
            s.memset("pool", IT.t[:, 128, :], -1e30, [IT.r])
            psN = s.ps()
            s.mm(psN.t[0:4, 0:64], kin.t[:, :], qif, True, True, [kin.r, qiTb.r], [psN.r])
            score_tail(psN, 4, 64, IT.t[0:4, 128:129, :])
            s.tt("dve", IT.t[0:4, 128, :], IT.t[0:4, 128, :], cs.t[0:4, 128:132], ALU.add, [IT.r, cs.r], [IT.r])
            if DSTEP <= 1:
                continue
            s.memset("dve", lo.t[:, 0:4], -256.0, [lo.r])
            for itn in range(32):
                wdt = 256.0 * (0.5 ** itn)
                s.ts("dve", lo.t[:, 4:8], lo.t[:, 0:4], wdt, None, ALU.add, None, [lo.r], [lo.r])
                s.tt("dve", cmpt.t[:, :, :], IT.t[:, :, :], lo.t[:, 4:8].unsqueeze(1).to_broadcast([128, 129, 4]), ALU.is_ge,
                     [IT.r, lo.r], [cmpt.r])
                s.red("dve", lo.t[:, 8:12], cmpt.t[:, :, :].rearrange("p j t -> p t j"), ALU.add, [cmpt.r], [lo.r])
                pc = s.ps()
                s.mm(pc.t[:, 0:4], ones.t[:, :], lo.t[:, 8:12], True, True, [ones.r, lo.r], [pc.r])
                s.ts("dve", lo.t[:, 12:16], pc.t[:, 0:4], 256.0, wdt, ALU.is_ge, ALU.mult, [pc.r], [lo.r])
                s.tt("dve", lo.t[:, 0:4], lo.t[:, 0:4], lo.t[:, 12:16], ALU.add, [lo.r], [lo.r])
            s.tt("dve", MT.t[:, :, :], IT.t[:, :, :], lo.t[:, 0:4].unsqueeze(1).to_broadcast([128, 129, 4]), ALU.is_lt,
                 [IT.r, lo.r], [MT.r])
            s.ts("dve", MT.t[:, :, :], MT.t[:, :, :], NEG, None, ALU.mult, None, [MT.r], [MT.r])
            if DSTEP <= 2:
                continue
            for j0 in range(0, 128, 4):
                psL = s.ps()
                for jj in range(4):
                    j = j0 + jj
                    kvp = s.nxt(kvp_p, "dskvp")
                    s.dma_gather(kvp.t[:, :], Kpool, ridx.t[:, 0, j:j + 1], [ridx.r], [kvp.r])
                    psK = s.ps()
                    for q4 in range(4):
                        s.tp(psK.t[:, q4 * 128:(q4 + 1) * 128], kvp.t[:, q4 * 128:(q4 + 1) * 128], s.idf.t[:, :],
                             [kvp.r, s.idf.r], [psK.r], inc=(q4 == 3))
                    ktb = s.nxt(ktb_p, "dsktb")
                    s.cp("act", ktb.t[:, :, :], psK.t[:, :].rearrange("p (a b) -> p a b", a=4), [psK.r], [ktb.r])
                    for kvh in range(8):
                        s.mm(psL.t[:, jj * 128 + kvh * 16:jj * 128 + kvh * 16 + 16], ktb.t[:, kvh // 2, :],
                             qT2.t[:, kvh * 4:kvh * 4 + 4, :].rearrange("p g t -> p (g t)"), True, True,
                             [ktb.r, qT2.r], [psL.r], inc=(jj == 3 and kvh == 7))
                pv = psL.t[:, :].rearrange("p (a b) -> p a b", a=4)
                if j0 < 112:
                    bias_ap = B31.t[:, :].unsqueeze(1).to_broadcast([128, 4, 128])
                    br = B31.r
                else:
                    bias_ap = Bl.t[:, j0 - 112:j0 - 108, :]
                    br = Bl.r
                s.tt("dve", LT.t[:, j0:j0 + 4, :], pv, bias_ap, ALU.add, [psL.r, br], [LT.r])
                s.tt("dve", LT.t[:, j0:j0 + 4, :].rearrange("p a (h t) -> p a h t", t=4),
                     LT.t[:, j0:j0 + 4, :].rearrange("p a (h t) -> p a h t", t=4),
                     MT.t[:, j0:j0 + 4, :].unsqueeze(2).to_broadcast([128, 4, 32, 4]), ALU.add, [LT.r, MT.r], [LT.r])
            s.memset("pool", LT.t[:, 128, :], NEG, [LT.r])
            psLn = s.ps()
            for kvh in range(8):
                s.mm(psLn.t[0:4, kvh * 16:kvh * 16 + 16], kn2.t[:, kvh // 2, :],
                     qT2.t[:, kvh * 4:kvh * 4 + 4, :].rearrange("p g t -> p (g t)"), True, True, [kn2.r, qT2.r],
                     [psLn.r], inc=(kvh == 7))
            s.tt("dve", LT.t[0:4, 128, :], psLn.t[0:4, 0:128], cs.t[0:4, 0:128], ALU.add, [psLn.r, cs.r], [LT.r])
            s.tt("dve", LT.t[0:4, 128, :].rearrange("p (h t) -> p h t", t=4), LT.t[0:4, 128, :].rearrange("p (h t) -> p h t", t=4),
                 MT.t[0:4, 128, :].unsqueeze(1).to_broadcast([4, 32, 4]), ALU.add, [LT.r, MT.r], [LT.r])
            if DSTEP <= 3:
                continue
            pm = s.nxt(w128, "dsw")
            s.red("dve", pm.t[:, :], LT.t[:, :, :].rearrange("p j c -> p c j"), ALU.max, [LT.r], [pm.r])
            pt_ = s.ps()
            s.tp(pt_.t[:, 0:128], pm.t[:, :], s.idf.t[:, :], [pm.r, s.idf.r], [pt_.r])
            s.red("dve", lo.t[:, 4:5], pt_.t[:, 0:128], ALU.max, [pt_.r], [lo.r])
            dmx = s.nxt(w128, "dsw")
            s.ts("dve", dmx.t[:, :], s.idf.t[:, :], lo.t[:, 4:5], None, ALU.mult, None, [s.idf.r, lo.r], [dmx.r])
            pb = s.ps()
            s.mm(pb.t[:, 0:128], ones.t[:, :], dmx.t[:, :], True, True, [ones.r, dmx.r], [pb.r])
            mxb = s.nxt(w128, "dsw")
            s.cp("act", mxb.t[:, :], pb.t[:, 0:128], [pb.r], [mxb.r])
            s.tt("dve", LT.t[:, :, :], LT.t[:, :, :], mxb.t[:, :].unsqueeze(1).to_broadcast([128, 129, 128]), ALU.subtract,
                 [LT.r, mxb.r], [LT.r])
            s.act(ET.t[:, :, :], LT.t[:, :, :], AF.Exp, [LT.r], [ET.r])
            ets = s.nxt(w128, "dsw")
            s.red("dve", ets.t[:, :], ET.t[:, :, :].rearrange("p j c -> p c j"), ALU.add, [ET.r], [ets.r])
            pd = s.ps()
            s.mm(pd.t[:, 0:1], ets.t[:, :], ones.t[:, 0:1], True, True, [ets.r, ones.r], [pd.r])
            s.op("dve", lambda g, pd=pd: g.reciprocal(lo.t[:, 5:6], pd.t[:, 0:1]), [pd.r], [lo.r])
            if DSTEP <= 4:
                continue
            po = s.ps()
            for j in range(128):
                kvp = s.nxt(kvp_p, "dskvp")
                s.dma_gather(kvp.t[:, :], Vpool, ridx.t[:, 0, j:j + 1], [ridx.r], [kvp.r])
                vb = s.nxt(vb_p, "dsvb")
                s.cp("act" if j % 2 else "pool", vb.t[:, :], kvp.t[:, :], [kvp.r], [vb.r])
                s.mm(po.t[:, :], ET.t[:, j, :], vb.t[:, :], j == 0, False, [ET.r, vb.r], [po.r], inc=False)
            s.mm(po.t[:, :], ET.t[0:4, 128, :], vnew.t[0:4, :], False, True, [ET.r, vnew.r], [po.r])
            R = s.nxt(R_p, "dsr")
            s.tt("dve", R.t[:, :].rearrange("p (k d) -> p k d", k=8), po.t[:, :].rearrange("p (k d) -> p k d", k=8),
                 selm.unsqueeze(2).to_broadcast([128, 8, 64]), ALU.mult, [po.r, cs.r], [R.r])
            s.red("dve", osl.t[:, :], R.t[:, :].rearrange("p (k d) -> p d k", k=8), ALU.add, [R.r], [osl.r])
            s.ts("dve", osl.t[:, :], osl.t[:, :], lo.t[:, 5:6], None, ALU.mult, None, [osl.r, lo.r], [osl.r])
            pp = s.ps()
            s.mm(pp.t[:, 0:64], perm, osl.t[:, :], True, True, [cs.r, osl.r], [pp.r])
            op_ = s.nxt(w128, "dsw")
            s.cp("act", op_.t[:, 0:64], pp.t[:, 0:64], [pp.r], [op_.r])
            for t in range(4):
                s.dma("sp", OSap[bi * 4 + t:bi * 4 + t + 1, :].rearrange("o (h d) -> (o h) d", d=64),
                      op_.t[t * 32:(t + 1) * 32, 0:64], [op_.r], [s.OS.r])
        ot = s.nxt(s.tokp, "toki")
        s.dma("sp", ot.t[0:NS, :], OSap[:, :], [s.OS.r], [ot.r])
        zs = s.sb([128, 16, NS], BF16, name="ds_zs")
        gss = s.sb([128, 16, NS], BF16, name="ds_gss")
        s.dma("sp", zs.t[:, :, :], s.ZT.t.ap()[:, 0:16, SEQ:TT], [s.ZT.r], [zs.r])
        for g in range(0, 16, 4):
            ps = s.ps()
            for j in range(4):
                cc = (g + j) * 128
                s.tp(ps.t[:, j * 16:j * 16 + 16], ot.t[0:NS, cc:cc + 128], s.idf.t[0:NS, 0:NS], [ot.r, s.idf.r], [ps.r],
                     inc=(j == 3))
            s.tt("dve", gss.t[:, g:g + 4, :], ps.t[:, 0:64].rearrange("p (a b) -> p a b", a=4), zs.t[:, g:g + 4, :], ALU.mult,
                 [ps.r, zs.r], [gss.r])
        s.dma("sp", s.GTd.t.ap()[:, 0:16, SEQ:TT], gss.t[:, :, :], [gss.r], [s.GTd.r])

    def build(self):
        s = self
        s.setup()
        s.setup_swa()
        s.setup_s5()
        s.setup_gdn()
        s.setup_dsa()
        s.setup_dsa_sample()
        Xcur = s.inp["xin"]
        for li in range(s.n_layers):
            if os.environ.get("DS_DEBUG") and li < 3:
                continue
            if li == 0:
                with s.scope():
                    kT = s.sb([64, 4, TT], BF16, name="a_kT")
                    vtok = s.sb([128, NTILE, 256], BF16, name="a_vtok")
                    kv32 = s.sb([128, 2, 512], F32, name="a_kv32")
                    with s.scope():
                        xT = s.sb([128, 16, TT], BF16, name="BIG")
                        s.load_input(xT)
                        if s.stop == "load":
                            s.dma("sp", s.XTd.t.ap()[:, :, :], xT.t[:, :, :], [xT.r], [s.XTd.r])
                        else:
                            s.swa_proj(xT, kT, vtok, kv32)
                    if s.stop not in ("load", "proj"):
                        with s.scope():
                            s.swa_attn(kT, vtok)
                if s.stop in ("load", "proj", "attn"):
                    break
                nkc = 16
                Wo = s.inp["a_w_out"].t.ap()
            elif li == 1:
                with s.scope():
                    xT = s.sb([128, 16, TT], BF16, name="BIG")
                    s.dma("sp", xT.t[:, :, :], s.XTd.t.ap()[:, :, :], [s.XTd.r], [xT.r])
                    s.s5_proj(xT)
                with s.scope():
                    s.s5_core()
                with s.scope():
                    y1T = s.sb([128, 16, TT], BF16, name="BIG")
                    s.s5_glu(y1T)
                nkc = 16
                Wo = s.inp["s5_w_out"].t.ap()
            elif li == 2:
                with s.scope():
                    xT = s.sb([128, 16, TT], BF16, name="BIG")
                    s.dma("sp", xT.t[:, :, :], s.XTd.t.ap()[:, :, :], [s.XTd.r], [xT.r])
                    s.gdn_proj(xT)
                if s.stop == "gproj":
                    break
                with s.scope():
                    s.gdn_core()
                nkc = 32
                Wo = s.inp["gdn_w_out"].t.ap()
            elif li == 3:
                with s.scope():
                    xT = s.sb([128, 16, TT], BF16, name="BIG")
                    s.dma("sp", xT.t[:, :, :], s.XTd.t.ap()[:, :, :], [s.XTd.r], [xT.r])
                    s.dsa_proj(xT)
                if s.stop == "dproj":
                    break
                with s.scope():
                    s.dsa_prompt()
                if s.stop != "dprompt":
                    with s.scope():
                        s.dsa_sample()
                nkc = 16
                Wo = s.inp["dsa_w_out"].t.ap()
            last = (li == s.n_layers - 1)
            Xnext = s.out["y"] if last else s.X[li % 2]
            with s.scope():
                s.out_proj(li, nkc, Wo, Xcur)
            if s.stop == "outproj":
                break
            with s.scope():
                xnT = s.sb([128, 16, TT], BF16, name="BIG")
                s.ln_pass(li, xnT)
                if s.stop != "ln":
                    s.ple_stage(li, xnT, Xnext, last)
            Xcur = Xnext
        s.finish()
        return s.nc


def _rel_bucket_np(dist):
    n = np.maximum(dist, 0)
    exact = 16
    with np.errstate(divide="ignore"):
        logb = exact + (np.log(np.maximum(n, exact).astype(np.float32) / np.float32(exact))
                        / np.float32(np.log(2048 / exact)) * (32 - exact)).astype(np.int32)
    return np.where(n < exact, n, np.minimum(logb, 31)).astype(np.int64)


def _static_tables():
    q = np.arange(128)[:, None]
    k = np.arange(256)[None, :]
    dist = q + 128 - k
    bkt_p = _rel_bucket_np(dist)
    mask_p = np.where((dist >= 0) & (dist < 128), 0.0, NEG).astype(np.float32)
    tok = np.arange(4)
    bkt_s = np.zeros((4, 4, 144), np.int64)
    mask_s = np.full((4, 4, 144), NEG, np.float32)
    for bi in range(4):
        for i in range(4):
            for j in range(128):
                d = 128 + i - j
                bkt_s[i, bi, j] = _rel_bucket_np(np.array(d))
                if 0 <= d < 128:
                    mask_s[i, bi, j] = 0.0
            for jj in range(16):
                bj, i2 = jj // 4, jj % 4
                d = i - i2
                bkt_s[i, bi, 128 + jj] = _rel_bucket_np(np.array(d))
                if bj == bi and d >= 0:
                    mask_s[i, bi, 128 + jj] = 0.0
    return bkt_p, mask_p, bkt_s, mask_s


_PROG_CACHE = {}


def _get_prog(n_layers=4, debug=False):
    key = (n_layers, debug)
    if key not in _PROG_CACHE:
        p = Prog(n_layers, debug)
        p.build()
        _PROG_CACHE[key] = p
    return _PROG_CACHE[key]


def make_in_maps(inputs):
    f = lambda a: np.ascontiguousarray(np.asarray(a, dtype=np.float32))
    x_prompt = f(inputs["x_prompt"])
    x_sample = f(inputs["x_sample"])
    p_prompt = f(inputs["p_prompt"])
    p_sample = f(inputs["p_sample"])
    rel_bias = f(inputs["rel_bias"])
    bkt_p, mask_p, bkt_s, mask_s = _static_tables()
    ident = np.eye(128, dtype=np.float32)
    a_bias_p = np.ascontiguousarray(rel_bias[bkt_p].transpose(0, 2, 1))
    bs = rel_bias[bkt_s]
    hord = np.array([kvh * 8 + g2 * 2 + par for kvh in range(4) for par in range(2) for g2 in range(4)])
    bs = bs[..., hord]
    a_bias_s = bs.transpose(3, 0, 1, 2).reshape(2, 64, 4, 144)
    a_bias_s = np.ascontiguousarray(a_bias_s.transpose(1, 0, 2, 3).reshape(64, 1152))
    a_mask_s = np.broadcast_to(mask_s[None], (32, 4, 4, 144)).reshape(2, 64, 4, 144)
    a_mask_s = np.ascontiguousarray(a_mask_s.transpose(1, 0, 2, 3).reshape(64, 1152))
    sinks = f(inputs["a_sinks"])[0]
    a_sink_p = np.ascontiguousarray(np.broadcast_to(sinks[None, :], (128, 32)))
    a_sink_s = np.ascontiguousarray(np.repeat(sinks[hord], 4).reshape(2, 64).T)
    shared = dict(
        ident=ident, ln_g=f(inputs["ln_g"]), ln_b=f(inputs["ln_b"]), ple_gate_w=f(inputs["ple_gate_w"]),
        ple_w=f(inputs["ple_w"]), a_w_in=f(inputs["a_w_in"])[0], a_w_out=f(inputs["a_w_out"])[0],
        a_bias_p=a_bias_p, a_mask_p=mask_p, a_bias_s=a_bias_s, a_mask_s=a_mask_s, a_sink_p=a_sink_p,
        a_sink_s=a_sink_s,
    )
    if "s5_w_in" in inputs:
        g = np.arange(128)
        cst = np.zeros((128, 256), np.float32)
        cst[g, g // 2] = 1.0
        cst[g, 64 + g % 2] = 1.0
        cst[g, 66 + g // 64] = 1.0
        cst[g, 68 + g % 64] = 1.0
        cst[g, 132 + (g // 16) % 2] = 1.0
        shared.update(
            s5_w_in=f(inputs["s5_w_in"])[0], s5_w_glu=f(inputs["s5_w_glu"])[0], s5_w_out=f(inputs["s5_w_out"])[0],
            s5_a_re=f(inputs["s5_a_re"])[0], s5_a_im=f(inputs["s5_a_im"])[0],
            s5_log_dt=f(inputs["s5_log_dt"])[0].reshape(128, 1), s5_b_re=f(inputs["s5_b_re"])[0],
            s5_b_im=f(inputs["s5_b_im"])[0], s5_c_re=f(inputs["s5_c_re"])[0], s5_c_im=f(inputs["s5_c_im"])[0],
            s5_d_fm=np.ascontiguousarray(f(inputs["s5_d"])[0].reshape(16, 128).T), s5_const=cst)
        state_s5 = f(inputs["state_s5"])[0].reshape(32, 128, 128)
    if "gdn_w_in" in inputs:
        def masks(n, cs):
            i = np.arange(n)
            same = (i[:, None] // cs) == (i[None, :] // cs)
            triU = (same & (i[:, None] <= i[None, :])).astype(np.float32)
            bones = same.astype(np.float32)
            Mb = np.where(same & (i[None, :] <= i[:, None]), 0.0, 30000.0).astype(np.float32)
            strict = (same & (i[None, :] < i[:, None])).astype(np.float32)
            ncs = n // cs
            sel = [np.broadcast_to(((i // cs) == c)[:, None], (n, 128)).astype(np.float32) for c in range(ncs)]
            cm = np.stack([((i // cs) == c) for c in range(ncs)], 1).astype(np.float32)
            return np.ascontiguousarray(np.concatenate([triU, bones, Mb, strict] + sel + [cm], 1))
        cwv = f(inputs["gdn_conv_w"])[0]
        shared.update(
            gdn_w_in=f(inputs["gdn_w_in"])[0], gdn_w_out=f(inputs["gdn_w_out"])[0],
            gdn_cw=np.ascontiguousarray(cwv.reshape(4, 64, 128).transpose(2, 1, 0)),
            gdn_ab=np.ascontiguousarray(np.broadcast_to(
                np.concatenate([f(inputs["gdn_a_log"])[0], f(inputs["gdn_dt_bias"])[0]])[None], (128, 64))),
            gdn_nw=np.ascontiguousarray(np.broadcast_to(f(inputs["gdn_norm_w"])[0][None], (128, 128))),
            gdn_mp=masks(128, 64), gdn_ms=masks(16, 4))
        state_gdn = f(inputs["state_gdn"])[0]
        state_gc = f(inputs["state_gdn_conv"])[0]
    if "dsa_w_in" in inputs:
        qq = np.arange(128)[:, None]
        uu = np.arange(2048)[None, :]
        bk = _rel_bucket_np(qq + 1920 - uu)
        d_bias_p = np.ascontiguousarray(rel_bias[bk].transpose(2, 0, 1))
        d_causal = np.where(np.arange(128)[None, :] > np.arange(128)[:, None], -1e30, 0.0).astype(np.float32)
        shared.update(dsa_w_in=f(inputs["dsa_w_in"])[0], dsa_w_out=f(inputs["dsa_w_out"])[0], d_bias_p=d_bias_p,
                      d_causal=d_causal)
    if "cache_d_kv" in inputs:
        cc = np.arange(128)
        head_c = (cc // 16) * 4 + (cc // 4) % 4
        t_c = cc % 4
        pp_ = np.arange(128)[:, None, None]
        jj_ = np.arange(16)[None, :, None]
        dist_l = 16384 + t_c[None, None, :] - ((112 + jj_) * 128 + pp_)
        d_bias_l = np.ascontiguousarray(rel_bias[_rel_bucket_np(dist_l), head_c[None, None, :]])
        d_bias_31 = np.ascontiguousarray(np.broadcast_to(rel_bias[31, head_c][None, :], (128, 128)))
        dist_n = t_c[None, :] - np.arange(4)[:, None]
        d_bias_n = np.ascontiguousarray(rel_bias[_rel_bucket_np(dist_n), head_c[None, :]])
        d_cneg4 = np.where(np.arange(4)[:, None] > np.arange(4)[None, :], -1e30, 0.0).astype(np.float32)
        d_selm = (np.arange(8)[None, :] == (cc // 16)[:, None]).astype(np.float32)
        d_perm = np.zeros((128, 128), np.float32)
        d_perm[cc, t_c * 32 + head_c] = 1.0
        shared.update(
            d_kidx_pool=f(inputs["cache_d_kidx"])[0].reshape(5120 * 128, 128)[:NPOOL * 128],
            d_k_pool=np.ascontiguousarray(f(inputs["cache_d_kv"])[0].reshape(5120 * 128, 2, 512)[:NPOOL * 128, 0, :]),
            d_v_pool=np.ascontiguousarray(f(inputs["cache_d_kv"])[0].reshape(5120 * 128, 2, 512)[:NPOOL * 128, 1, :]),
            d_iota=np.arange(128, dtype=np.float32).reshape(128, 1), d_bias_l=d_bias_l, d_bias_31=d_bias_31,
            d_bias_n=d_bias_n, d_cneg4=d_cneg4, d_selm=d_selm, d_perm=d_perm)
        page_table = np.ascontiguousarray(np.asarray(inputs["page_table"], dtype=np.int32) % NPOOL)
    cache_a = f(inputs["cache_a_kv"])[0].reshape(32, 128, 512)
    maps = []
    for c in range(8):
        b = c % 4
        m = dict(shared)
        m["xin"] = np.ascontiguousarray(np.concatenate([x_prompt[b], x_sample[4 * c:4 * c + 4].reshape(NS, D)], 0))
        m["p_all"] = np.ascontiguousarray(
            np.concatenate([p_prompt[:, b], p_sample[:, 4 * c:4 * c + 4].reshape(4, NS, 256)], 1))
        m["a_cache"] = np.ascontiguousarray(cache_a[4 * c:4 * c + 4])
        if "s5_w_in" in inputs:
            m["s5_state"] = np.ascontiguousarray(state_s5[4 * c:4 * c + 4])
        if "cache_d_kv" in inputs:
            m["pt_loc"] = np.ascontiguousarray(page_table[4 * c:4 * c + 4])
        if "gdn_w_in" in inputs:
            m["gdn_state"] = np.ascontiguousarray(state_gdn[4 * c:4 * c + 4])
            m["gdn_cbuf"] = np.ascontiguousarray(state_gc[4 * c:4 * c + 4].reshape(12, 8192))
        maps.append(m)
    return maps


def kernel(**inputs):
    prog = _get_prog()
    maps = make_in_maps(inputs)
    maps = [{k: v for k, v in m.items() if k in prog.inp} for m in maps]
    res = run_bass_kernel_spmd(prog.nc, maps, core_ids=list(range(8)))
    R = res.results
    y_prompt = np.stack([R[b]["y"][:SEQ] for b in range(4)])
    y_sample = np.concatenate([R[c]["y"][SEQ:].reshape(4, 4, D) for c in range(8)], 0)
    a_kv_p = np.stack([R[b]["a_kv_p"].reshape(128, 2, 4, 64) for b in range(4)])[None]
    a_kv_s = np.concatenate([R[c]["a_kv_s"].reshape(4, 128, 2, 4, 64) for c in range(8)], 0)[None]
    z = lambda *sh: np.zeros(sh, np.float32)
    gd_p = np.stack([R[b]["gd_p"] for b in range(4)])[None]
    gd_s = np.concatenate([R[c]["gd_s"] for c in range(8)], 0)[None]
    gc_p = np.stack([R[b]["gc"][0:3] for b in range(4)])[None]
    gc_s = np.concatenate([R[c]["gc"][3:15].reshape(4, 3, 8192) for c in range(8)], 0)[None]
    dkv_p = np.stack([R[b]["d_kv"][:SEQ].reshape(SEQ, 2, 8, 64) for b in range(4)])[None]
    dkv_s = np.concatenate([R[c]["d_kv"][SEQ:].reshape(4, 4, 2, 8, 64) for c in range(8)], 0)[None]
    dki_p = np.stack([R[b]["d_ki"][:SEQ] for b in range(4)])[None]
    dki_s = np.concatenate([R[c]["d_ki"][SEQ:].reshape(4, 4, 128) for c in range(8)], 0)[None]
    s5_p = np.stack([R[b]["s5_p"].reshape(128, 64, 2) for b in range(4)])[None]
    s5_s = np.concatenate([R[c]["s5_s"].reshape(4, 128, 64, 2) for c in range(8)], 0)[None]
    return (y_prompt, y_sample, a_kv_p, a_kv_s,
            s5_p, s5_s, gd_p, gd_s, gc_p, gc_s, dkv_p, dkv_s, dki_p, dki_s)
```

```python
import os
import numpy as np
import concourse.bass as bass
import concourse.mybir as mybir
from concourse.bass_utils import run_bass_kernel_spmd
from contextlib import ExitStack

F32, BF16, I32 = mybir.dt.float32, mybir.dt.bfloat16, mybir.dt.int32
AF = mybir.ActivationFunctionType
ALU = mybir.AluOpType
AX = mybir.AxisListType

D = 2048
SEQ = 2048
NS = 16
TT = SEQ + NS
NTILE = 17
ALPHA = 8 ** 0.25
LN_EPS = 1e-5
NEG = -30000.0
NPOOL = int(os.environ.get('DS_NPOOL', '5120'))
TBS = [(0, 512), (512, 512), (1024, 512), (1536, 512), (2048, NS)]
TILES = [(i * 128, 128) for i in range(16)] + [(2048, NS)]


class Res:
    __slots__ = ("w", "r")

    def __init__(self):
        self.w = None
        self.r = {}


class T:
    def __init__(self, t, n=1):
        self.t = t
        self.rs = [Res() for _ in range(n)]

    @property
    def r(self):
        return self.rs[0]


class KB:
    LIM = 30000
    DLIM = 1800
    NSLOT = 8

    def __init__(self):
        self.nc = bass.Bass("TRN2", target_bir_lowering=False)
        self.es = ExitStack()
        nc = self.nc
        self.eng = {"pe": nc.tensor, "act": nc.scalar, "dve": nc.vector, "pool": nc.gpsimd, "sp": nc.sync}
        self.nsem = 0
        self.csem = {}
        self.ccnt = {}
        self.seen = {e: {} for e in self.eng}
        for e in ("pe", "act", "dve", "pool"):
            self.csem[e] = self._newsem()
            self.ccnt[e] = 0
        self.dq = {}
        for q in ("sp", "pool", "act"):
            self.dq[q] = dict(j=0, sems=[self._newsem() for _ in range(self.NSLOT)], cnt=[0] * self.NSLOT,
                              last=[None] * self.NSLOT)
        self.pend_r = []
        self.pend_w = []
        self.ntens = 0
        self.psb = []
        self.psi = 0
        self.scopes = []

    def _newsem(self):
        h = self.es.enter_context(self.nc.semaphore("sem%d" % self.nsem))
        self.nsem += 1
        return (self.nsem - 1, h)

    def _wait(self, e, ev):
        if ev is None:
            return
        sid, h, v, _ = ev
        if self.seen[e].get(sid, 0) >= v:
            return
        self.eng[e].wait_ge(h, v)
        self.seen[e][sid] = v

    @staticmethod
    def _deps(reads, writes):
        evs = []
        for r in reads:
            if r.w is not None:
                evs.append(r.w)
        for w in writes:
            if w.w is not None:
                evs.append(w.w)
            evs.extend(w.r.values())
        return evs

    @staticmethod
    def _commit(ev, reads, writes):
        for r in reads:
            r.r[ev[0]] = ev
        for w in writes:
            w.w = ev
            w.r = {}

    def op(self, e, fn, reads=(), writes=(), inc=True):
        reads = [x for x in reads]
        writes = [x for x in writes]
        for ev in self._deps(reads, writes):
            if e == "pe" and ev[3] == "pe":
                continue
            self._wait(e, ev)
        ins = fn(self.eng[e])
        if e == "pe" and not inc:
            self.pend_r += reads
            self.pend_w += writes
            return ins
        if self.ccnt[e] >= self.LIM:
            self.csem[e] = self._newsem()
            self.ccnt[e] = 0
        self.ccnt[e] += 1
        ins.then_inc(self.csem[e][1], 1)
        ev = (self.csem[e][0], self.csem[e][1], self.ccnt[e], e)
        if e == "pe":
            reads = reads + self.pend_r
            writes = writes + self.pend_w
            self.pend_r = []
            self.pend_w = []
        self._commit(ev, reads, writes)
        return ins

    def dma(self, q, out, in_, reads=(), writes=(), **kw):
        d = self.dq[q]
        slot = d["j"] % self.NSLOT
        d["j"] += 1
        self._wait(q, d["last"][slot])
        for ev in self._deps(reads, writes):
            self._wait(q, ev)
        if d["cnt"][slot] >= self.DLIM:
            d["sems"][slot] = self._newsem()
            d["cnt"][slot] = 0
        ins = self.eng[q].dma_start(out=out, in_=in_, **kw)
        d["cnt"][slot] += 1
        sid, h = d["sems"][slot]
        ins.then_inc(h, 16)
        ev = (sid, h, 16 * d["cnt"][slot], "dma_" + q)
        d["last"][slot] = ev
        self._commit(ev, list(reads), list(writes))
        return ins

    def dma_gather(self, out, in_, idx_ap, reads=(), writes=()):
        q = "pool"
        d = self.dq[q]
        slot = d["j"] % self.NSLOT
        d["j"] += 1
        self._wait(q, d["last"][slot])
        for ev in self._deps(reads, writes):
            self._wait(q, ev)
        if d["cnt"][slot] >= self.DLIM:
            d["sems"][slot] = self._newsem()
            d["cnt"][slot] = 0
        ins = self.nc.gpsimd.indirect_dma_start(out=out, out_offset=None, in_=in_,
                                                in_offset=bass.IndirectOffsetOnAxis(ap=idx_ap, axis=0))
        d["cnt"][slot] += 1
        sid, h = d["sems"][slot]
        ins.then_inc(h, 16)
        ev = (sid, h, 16 * d["cnt"][slot], "dma_" + q)
        d["last"][slot] = ev
        self._commit(ev, list(reads), list(writes))
        return ins

    def finish(self):
        for q in self.dq:
            for ev in self.dq[q]["last"]:
                self._wait("sp", ev)
        for e in ("pe", "act", "dve", "pool"):
            if self.ccnt[e] > 0:
                self._wait("sp", (self.csem[e][0], self.csem[e][1], self.ccnt[e], e))
        self.es.close()

    def barrier(self):
        evs = []
        for q in self.dq:
            evs += [ev for ev in self.dq[q]["last"] if ev is not None]
        for e in ("pe", "act", "dve", "pool"):
            if self.ccnt[e] > 0:
                evs.append((self.csem[e][0], self.csem[e][1], self.ccnt[e], e))
        for e in self.eng:
            for ev in evs:
                self._wait(e, ev)

    class _Scope:
        def __init__(self, kb):
            self.kb = kb

        def __enter__(self):
            self.kb.scopes.append(ExitStack())
            return self

        def __exit__(self, *a):
            self.kb.barrier()
            self.kb.scopes.pop().close()
            return False

    def scope(self):
        return KB._Scope(self)

    def sb(self, shape, dt, n=1, name=None):
        self.ntens += 1
        st = self.scopes[-1] if self.scopes else self.es
        t = st.enter_context(self.nc.sbuf_tensor("%s_%d" % (name or "sb", self.ntens), list(shape), dt))
        return T(t, n)

    def pool(self, shape, dt, bufs):
        return [self.sb(shape, dt) for _ in range(bufs)]

    def dram(self, name, shape, dt, kind="Internal", n=1):
        t = self.nc.dram_tensor(name, list(shape), dt, kind=kind)
        return T(t, n)

    def init_psum(self):
        for i in range(8):
            t = self.es.enter_context(self.nc.psum_tensor("psb%d" % i, [128, 512], F32))
            self.psb.append(T(t))

    def ps(self):
        p = self.psb[self.psi % 8]
        self.psi += 1
        return p

    def mm(self, out, lhsT, rhs, start, stop, reads, writes, inc=None):
        if inc is None:
            inc = stop
        return self.op("pe", lambda e: e.matmul(out, lhsT, rhs, start=start, stop=stop), reads, writes, inc=inc)

    def tp(self, out, in_, ident, reads, writes, inc=True):
        return self.op("pe", lambda e: e.transpose(out, in_, ident), reads, writes, inc=inc)

    def act(self, out, in_, func, reads, writes, **kw):
        return self.op("act", lambda e: e.activation(out=out, in_=in_, func=func, **kw), reads, writes)

    def tt(self, e, out, in0, in1, op, reads, writes):
        return self.op(e, lambda g: g.tensor_tensor(out=out, in0=in0, in1=in1, op=op), reads, writes)

    def ts(self, e, out, in0, s1, s2, op0, op1, reads, writes, **kw):
        if op1 is None:
            return self.op(e, lambda g: g.tensor_scalar(out=out, in0=in0, scalar1=s1, scalar2=None, op0=op0, **kw),
                           reads, writes)
        return self.op(e, lambda g: g.tensor_scalar(out=out, in0=in0, scalar1=s1, scalar2=s2, op0=op0, op1=op1, **kw),
                       reads, writes)

    def stt(self, e, out, in0, scalar, in1, op0, op1, reads, writes):
        return self.op(e, lambda g: g.scalar_tensor_tensor(out=out, in0=in0, scalar=scalar, in1=in1, op0=op0, op1=op1),
                       reads, writes)

    def red(self, e, out, in_, op, reads, writes):
        return self.op(e, lambda g: g.tensor_reduce(out=out, in_=in_, axis=AX.X, op=op), reads, writes)

    def cp(self, e, out, in_, reads, writes):
        if e == "act":
            return self.act(out, in_, AF.Copy, reads, writes)
        return self.op(e, lambda g: g.tensor_copy(out, in_), reads, writes)

    def memset(self, e, ap, val, writes):
        return self.op(e, lambda g: g.memset(ap, val), [], writes)


class Prog(KB):
    def __init__(self, n_layers=4, debug=False, stop=None):
        super().__init__()
        self.stop = stop
        self.n_layers = n_layers
        self.debug = debug
        self.inp = {}
        self.out = {}
        self.evi = 0

    def din(self, name, shape, dt=F32):
        t = self.nc.dram_tensor(name, list(shape), dt, kind="ExternalInput")
        self.inp[name] = T(t)
        return self.inp[name]

    def dout(self, name, shape, dt=F32):
        t = self.nc.dram_tensor(name, list(shape), dt, kind="ExternalOutput")
        self.out[name] = T(t)
        return self.out[name]

    def ev_eng(self):
        self.evi += 1
        return "act" if self.evi % 2 else "dve"

    def nxt(self, pool, attr):
        i = getattr(self, attr, 0)
        setattr(self, attr, i + 1)
        return pool[i % len(pool)]

    def load_w(self, wt, wap, K, n0, nw, c0=0):
        kc = K // 128
        for k0 in range(0, kc, 4):
            k1 = min(kc, k0 + 4)
            src = wap[k0 * 128:k1 * 128, n0:n0 + nw].rearrange("(kc p) n -> p kc n", p=128)
            self.dma("pool", wt.t[:, k0:k1, c0:c0 + nw], src, [], [wt.r])

    def ws_mm(self, ps_ap, psr, wt, cols, xT, kcs, t0, tn, xr):
        n = len(kcs)
        for i, kc in enumerate(kcs):
            self.mm(ps_ap, wt.t[:, kc, cols[0]:cols[1]], xT.t[:, kc, t0:t0 + tn], i == 0, i == n - 1,
                    [wt.r] + xr, [psr])

    def as_mm(self, ps_ap, psr, wt, cols, xT, kcs, t0, tn, xr):
        n = len(kcs)
        for i, kc in enumerate(kcs):
            self.mm(ps_ap, xT.t[:, kc, t0:t0 + tn], wt.t[:, kc, cols[0]:cols[1]], i == 0, i == n - 1,
                    [wt.r] + xr, [psr])

    def ld_grp(self, xs, dap, dres, n0, grp):
        full = [i for i in grp if i < 16]
        a, b = full[0], full[-1] + 1
        self.dma("sp", xs.t[:, 0:b - a, :], dap[a * 128:b * 128, n0:n0 + 256].rearrange("(i p) n -> p i n", p=128),
                 [dres], [xs.r])
        if 16 in grp:
            self.dma("sp", xs.t[0:NS, grp.index(16), :], dap[2048:TT, n0:n0 + 256], [dres], [xs.r])

    def st_grp(self, xs, dap, dres, n0, grp):
        full = [i for i in grp if i < 16]
        a, b = full[0], full[-1] + 1
        self.dma("sp", dap[a * 128:b * 128, n0:n0 + 256].rearrange("(i p) n -> p i n", p=128), xs.t[:, 0:b - a, :],
                 [xs.r], [dres])
        if 16 in grp:
            self.dma("sp", dap[2048:TT, n0:n0 + 256], xs.t[0:NS, grp.index(16), :], [xs.r], [dres])

    def setup(self):
        s = self
        s.init_psum()
        s.din("xin", [TT, D])
        s.din("p_all", [4, TT, 256])
        s.din("ident", [128, 128])
        s.din("ln_g", [4, D])
        s.din("ln_b", [4, D])
        s.din("ple_gate_w", [4, D, D])
        s.din("ple_w", [4, 256, D])
        s.dout("y", [TT, D])
        s.X = [s.dram("Xtok0", [TT, D], F32), s.dram("Xtok1", [TT, D], F32)]
        s.Rt = s.dram("Rt", [TT, D], F32)
        s.XN = s.dram("XN", [TT, D], F32)
        s.GTd = s.dram("GTd", [128, 32, TT], BF16)
        s.XTd = s.dram("XTd", [128, 16, TT], BF16)
        s.wp = s.pool([128, 16, 256], BF16, 2)
        s.idf = s.sb([128, 128], F32, name="idf")
        s.idb = s.sb([128, 128], BF16, name="idb")
        s.dma("sp", s.idf.t[:], s.inp["ident"].t.ap()[:, :], [], [s.idf.r])
        s.cp("dve", s.idb.t[:], s.idf.t[:], [s.idf.r], [s.idb.r])
        s.tokp = s.pool([128, 2048], F32, 2)
        s.stg = s.pool([128, 9, 256], F32, 2)
        s.sm = [s.sb([128, 16], F32, name="sm") for i in range(4)]
        s.ev4 = s.pool([128, 512], F32, 2)

    GRPS = [list(range(0, 9)), list(range(9, 17))]

    def load_input(self, xT):
        s = self
        xin = s.inp["xin"].t.ap()
        for i, (t0, tn) in enumerate(TILES):
            xt = s.nxt(s.tokp, "toki")
            s.dma("sp", xt.t[0:tn, :], xin[t0:t0 + tn, :], [], [xt.r])
            s.transpose_tile(xt, tn, xT, t0, 16)

    def transpose_tile(self, xt, tn, xT, t0, nchunk, c0=0, src_c0=0):
        s = self
        for g in range(0, nchunk, 4):
            ps = s.ps()
            ng = min(4, nchunk - g)
            for j in range(ng):
                c = src_c0 + (g + j) * 128
                s.tp(ps.t[:, j * 128:j * 128 + tn], xt.t[0:tn, c:c + 128], s.idf.t[0:tn, 0:tn],
                     [xt.r, s.idf.r], [ps.r], inc=(j == ng - 1))
            s.cp(s.ev_eng(), xT.t[:, c0 + g:c0 + g + ng, t0:t0 + tn],
                 ps.t[:, 0:ng * 128].rearrange("p (a b) -> p a b", a=ng)[:, :, 0:tn], [ps.r], [xT.r])

    def out_proj(self, li, nkc, Wo, Xcur):
        s = self
        Xap = Xcur.t.ap()
        Rap = s.Rt.t.ap()
        Gap = s.GTd.t.ap()
        gtp = s.pool([128, nkc, 128], BF16, 2)
        for nb in range(8):
            n0 = nb * 256
            wts = []
            for k0 in range(0, nkc, 16):
                wt = s.nxt(s.wp, "wi")
                s.load_w(wt, Wo[k0 * 128:(k0 + 16) * 128, :], 2048, n0, 256)
                wts.append(wt)
            for grp in s.GRPS:
                xs = s.nxt(s.stg, "stgi")
                s.ld_grp(xs, Xap, Xcur.r, n0, grp)
                for li_, i in enumerate(grp):
                    t0, tn = TILES[i]
                    gt = s.nxt(gtp, "gti")
                    s.dma("sp", gt.t[:, :, 0:tn], Gap[:, 0:nkc, t0:t0 + tn], [s.GTd.r], [gt.r])
                    ps = s.ps()
                    for kc in range(nkc):
                        wt = wts[kc // 16]
                        s.mm(ps.t[0:tn, 0:256], gt.t[:, kc, 0:tn], wt.t[:, kc % 16, :], kc == 0, kc == nkc - 1,
                             [gt.r, wt.r], [ps.r])
                    s.stt("dve", xs.t[0:tn, li_, :], xs.t[0:tn, li_, :], ALPHA, ps.t[0:tn, 0:256], ALU.mult, ALU.add,
                          [xs.r, ps.r], [xs.r])
                s.st_grp(xs, Rap, s.Rt.r, n0, grp)

    def ln_pass(self, li, xnT):
        s = self
        gb = s.sb([128, 2, D], F32, name="gb")
        gap = s.inp["ln_g"].t.ap()
        bap = s.inp["ln_b"].t.ap()
        s.dma("sp", gb.t[:, 0, :], gap[li:li + 1, :].to_broadcast([128, D]), [], [gb.r])
        s.dma("sp", gb.t[:, 1, :], bap[li:li + 1, :].to_broadcast([128, D]), [], [gb.r])
        Rap = s.Rt.t.ap()
        XNap = s.XN.t.ap()
        for i, (t0, tn) in enumerate(TILES):
            rt = s.nxt(s.tokp, "toki")
            s.dma("sp", rt.t[0:tn, :], Rap[t0:t0 + tn, :], [s.Rt.r], [rt.r])
            sm = s.nxt(s.sm, "smi")
            jk = s.nxt(s.tokp, "toki")
            s.red("dve", sm.t[0:tn, 0:1], rt.t[0:tn, :], ALU.add, [rt.r], [sm.r])
            s.ts("dve", sm.t[0:tn, 1:2], sm.t[0:tn, 0:1], -1.0 / D, None, ALU.mult, None, [sm.r], [sm.r])
            s.act(jk.t[0:tn, :], rt.t[0:tn, :], AF.Square, [rt.r, sm.r], [jk.r, sm.r], bias=sm.t[0:tn, 1:2], scale=1.0,
                  accum_out=sm.t[0:tn, 2:3])
            s.ts("dve", sm.t[0:tn, 3:4], sm.t[0:tn, 2:3], 1.0 / D, LN_EPS, ALU.mult, ALU.add, [sm.r], [sm.r])
            s.act(sm.t[0:tn, 5:6], sm.t[0:tn, 3:4], AF.Sqrt, [sm.r], [sm.r])
            s.op("dve", lambda g: g.reciprocal(sm.t[0:tn, 4:5], sm.t[0:tn, 5:6]), [sm.r], [sm.r])
            s.ts("dve", jk.t[0:tn, :], rt.t[0:tn, :], sm.t[0:tn, 1:2], sm.t[0:tn, 4:5], ALU.add, ALU.mult,
                 [rt.r, sm.r], [jk.r])
            s.tt("dve", jk.t[0:tn, :], jk.t[0:tn, :], gb.t[0:tn, 0, :], ALU.mult, [jk.r, gb.r], [jk.r])
            s.tt("dve", jk.t[0:tn, :], jk.t[0:tn, :], gb.t[0:tn, 1, :], ALU.add, [jk.r, gb.r], [jk.r])
            s.dma("sp", XNap[t0:t0 + tn, :], jk.t[0:tn, :], [jk.r], [s.XN.r])
            s.transpose_tile(jk, tn, xnT, t0, 16)

    def ple_stage(self, li, xnT, Xnext, last):
        s = self
        pT = s.sb([128, 2, TT], BF16, name="pT")
        xst_p = s.pool([128, 2, TT], BF16, 2)
        wpl_p = s.pool([128, 512], BF16, 2)
        pap = s.inp["p_all"].t.ap()
        for grp in s.GRPS:
            pst = s.nxt(s.stg, "stgi")
            s.ld_grp(pst, pap[li], Res(), 0, grp)
            for li_, i in enumerate(grp):
                t0, tn = TILES[i]
                ps = s.ps()
                for j in range(2):
                    s.tp(ps.t[:, j * 128:j * 128 + tn], pst.t[0:tn, li_, j * 128:(j + 1) * 128], s.idf.t[0:tn, 0:tn],
                         [pst.r, s.idf.r], [ps.r], inc=(j == 1))
                s.cp(s.ev_eng(), pT.t[:, :, t0:t0 + tn], ps.t[:, 0:256].rearrange("p (a b) -> p a b", a=2)[:, :, 0:tn],
                     [ps.r], [pT.r])
        Wg = s.inp["ple_gate_w"].t.ap()
        Wp = s.inp["ple_w"].t.ap()
        XNap = s.XN.t.ap()
        Xap = Xnext.t.ap()
        XTap = s.XTd.t.ap()
        for nb in range(8):
            n0 = nb * 256
            wg = s.nxt(s.wp, "wi")
            s.load_w(wg, Wg[li], D, n0, 256)
            wpl = s.nxt(wpl_p, "wpli")
            wplb = wpl.t[:, :]
            s.dma("pool", wplb[:, 0:512].rearrange("p (a b) -> p a b", a=2),
                  Wp[li][:, n0:n0 + 256].rearrange("(kc p) n -> p kc n", p=128), [], [wpl.r])
            xst = s.nxt(xst_p, "xsti")
            for grp in s.GRPS:
                xs = s.nxt(s.stg, "stgi")
                s.ld_grp(xs, XNap, s.XN.r, n0, grp)
                for li_, i in enumerate(grp):
                    t0, tn = TILES[i]
                    psg = s.ps()
                    s.as_mm(psg.t[0:tn, 0:256], psg.r, wg, (0, 256), xnT, list(range(16)), t0, tn, [xnT.r])
                    for kc in range(2):
                        s.mm(psg.t[0:tn, 256:512], pT.t[:, kc, t0:t0 + tn], wplb[:, kc * 256:(kc + 1) * 256], kc == 0,
                             kc == 1, [pT.r, wpl.r], [psg.r])
                    gt = s.nxt(s.ev4, "ev4i")
                    s.act(gt.t[0:tn, 0:256], psg.t[0:tn, 0:256], AF.Sigmoid, [psg.r], [gt.r])
                    s.tt("dve", gt.t[0:tn, 0:256], gt.t[0:tn, 0:256], psg.t[0:tn, 256:512], ALU.mult, [gt.r, psg.r], [gt.r])
                    s.tt("pool", xs.t[0:tn, li_, :], xs.t[0:tn, li_, :], gt.t[0:tn, 0:256], ALU.add, [xs.r, gt.r], [xs.r])
                    if not last:
                        ps = s.ps()
                        for j in range(2):
                            s.tp(ps.t[:, j * 128:j * 128 + tn], xs.t[0:tn, li_, j * 128:(j + 1) * 128],
                                 s.idf.t[0:tn, 0:tn], [xs.r, s.idf.r], [ps.r], inc=(j == 1))
                        s.cp(s.ev_eng(), xst.t[:, :, t0:t0 + tn],
                             ps.t[:, 0:256].rearrange("p (a b) -> p a b", a=2)[:, :, 0:tn], [ps.r], [xst.r])
                s.st_grp(xs, Xap, Xnext.r, n0, grp)
            if not last:
                s.dma("sp", XTap[:, nb * 2:nb * 2 + 2, :], xst.t[:, :, :], [xst.r], [s.XTd.r])

    def setup_swa(self):
        s = self
        s.din("a_w_in", [D, 4608])
        s.din("a_w_out", [D, D])
        s.din("a_bias_p", [128, 32, 256])
        s.din("a_mask_p", [128, 256])
        s.din("a_bias_s", [64, 1152])
        s.din("a_mask_s", [64, 1152])
        s.din("a_sink_p", [128, 32])
        s.din("a_sink_s", [64, 2])
        s.din("a_cache", [4, 128, 512])
        s.dout("a_kv_p", [128, 512])
        s.dout("a_kv_s", [4, 128, 512])
        s.QT = s.dram("QT", [64, 32, TT], BF16)
        s.ZT = s.dram("ZT", [128, 32, TT], BF16)

    def swa_proj(self, xT, kT, vtok, kv32):
        s = self
        W = s.inp["a_w_in"].t.ap()
        qst = s.pool([64, 4, 512], BF16, 2)
        zst = s.pool([128, 2, 512], BF16, 2)
        qi = 0
        QTap = s.QT.t.ap()
        ZTap = s.ZT.t.ap()
        KC = list(range(16))
        for blk in range(int(os.environ.get("NBLK", "18"))):
            wt = s.nxt(s.wp, "wi")
            s.load_w(wt, W, D, blk * 256, 256)
            if blk < 8:
                for (t0, tn) in TBS:
                    st = qst[qi % 2]
                    qi += 1
                    for j in range(4):
                        ps = s.ps()
                        s.ws_mm(ps.t[0:64, 0:tn], ps.r, wt, (j * 64, j * 64 + 64), xT, KC, t0, tn, [xT.r])
                        if j % 2:
                            s.act(st.t[:, j, 0:tn], ps.t[0:64, 0:tn], AF.Copy, [ps.r], [st.r], scale=0.125)
                        else:
                            s.ts("dve", st.t[:, j, 0:tn], ps.t[0:64, 0:tn], 0.125, None, ALU.mult, None, [ps.r], [st.r])
                    s.dma("sp", QTap[:, blk * 4:blk * 4 + 4, t0:t0 + tn], st.t[:, :, 0:tn], [st.r], [s.QT.r])
            elif blk == 8:
                for (t0, tn) in TBS:
                    for j in range(4):
                        ps = s.ps()
                        s.ws_mm(ps.t[0:64, 0:tn], ps.r, wt, (j * 64, j * 64 + 64), xT, KC, t0, tn, [xT.r])
                        s.cp(s.ev_eng(), kT.t[:, j, t0:t0 + tn], ps.t[0:64, 0:tn], [ps.r], [kT.r])
                for ii, i in enumerate((15, 16)):
                    t0, tn = TILES[i]
                    ps = s.ps()
                    s.as_mm(ps.t[0:tn, 0:256], ps.r, wt, (0, 256), xT, KC, t0, tn, [xT.r])
                    s.cp("dve", kv32.t[0:tn, ii, 0:256], ps.t[0:tn, 0:256], [ps.r], [kv32.r])
            elif blk == 9:
                for i, (t0, tn) in enumerate(TILES):
                    ps = s.ps()
                    s.as_mm(ps.t[0:tn, 0:256], ps.r, wt, (0, 256), xT, KC, t0, tn, [xT.r])
                    s.cp("act", vtok.t[0:tn, i, :], ps.t[0:tn, 0:256], [ps.r], [vtok.r])
                    if i >= 15:
                        s.cp("dve", kv32.t[0:tn, i - 15, 256:512], ps.t[0:tn, 0:256], [ps.r], [kv32.r, ps.r])
            else:
                zb = blk - 10
                for (t0, tn) in TBS:
                    st = zst[qi % 2]
                    qi += 1
                    for j in range(2):
                        ps = s.ps()
                        s.ws_mm(ps.t[:, 0:tn], ps.r, wt, (j * 128, j * 128 + 128), xT, KC, t0, tn, [xT.r])
                        s.act(st.t[:, j, 0:tn], ps.t[:, 0:tn], AF.Silu, [ps.r], [st.r])
                    s.dma("sp", ZTap[:, zb * 2:zb * 2 + 2, t0:t0 + tn], st.t[:, :, 0:tn], [st.r], [s.ZT.r])
        if os.environ.get("SKIPOUT"):
            return
        s.dma("sp", s.out["a_kv_p"].t.ap()[:, :], kv32.t[:, 0, :], [kv32.r], [s.out["a_kv_p"].r])
        cache = s.inp["a_cache"].t.ap()
        oks = s.out["a_kv_s"].t.ap()
        for bi in range(4):
            s.dma("sp", oks[bi, 0:124, :], cache[bi, 4:128, :], [], [s.out["a_kv_s"].r])
            s.dma("sp", oks[bi, 124:128, :], kv32.t[bi * 4:bi * 4 + 4, 1, :], [kv32.r], [s.out["a_kv_s"].r])

    def swa_attn(self, kT, vtok):
        s = self
        QTap = s.QT.t.ap()
        ZTap = s.ZT.t.ap()
        Gap = s.GTd.t.ap()
        cache = s.inp["a_cache"].t.ap()
        bm = s.sb([128, 32, 256], BF16, name="a_bm")
        bs = s.sb([64, 2, 4, 144], BF16, name="a_bs")
        sink = s.sb([128, 34], F32, name="a_sink")
        tmpb = s.nxt(s.tokp, "toki")
        s.dma("sp", sink.t[:, 0:32], s.inp["a_sink_p"].t.ap()[:, :], [], [sink.r])
        s.dma("sp", sink.t[0:64, 32:34], s.inp["a_sink_s"].t.ap()[:, :], [], [sink.r])
        mk = s.nxt(s.ev4, "ev4i")
        s.dma("sp", mk.t[:, 0:256], s.inp["a_mask_p"].t.ap()[:, :], [], [mk.r])
        for hq in range(4):
            tmpb = s.nxt(s.tokp, "toki")
            s.dma("sp", tmpb.t[:, :].rearrange("p (a b) -> p a b", a=8), s.inp["a_bias_p"].t.ap()[:, hq * 8:hq * 8 + 8, :],
                  [], [tmpb.r])
            s.tt("dve", bm.t[:, hq * 8:hq * 8 + 8, :], tmpb.t[:, :].rearrange("p (a b) -> p a b", a=8),
                 mk.t[:, 0:256].unsqueeze(1).to_broadcast([128, 8, 256]), ALU.add, [tmpb.r, mk.r], [bm.r])
        tmps = s.nxt(s.tokp, "toki")
        s.dma("sp", tmps.t[0:64, 0:1152], s.inp["a_bias_s"].t.ap()[:, :], [], [tmps.r])
        s.dma("sp", tmps.t[0:64, 1152:2304 - 256], s.inp["a_mask_s"].t.ap()[:, 0:896], [], [tmps.r])
        tmps2 = s.nxt(s.ev4, "ev4i")
        s.dma("sp", tmps2.t[0:64, 0:256], s.inp["a_mask_s"].t.ap()[:, 896:1152], [], [tmps2.r])
        bsf = bs.t[:, :, :, :].rearrange("p a b c -> p (a b c)")
        s.tt("dve", bsf[:, 0:896], tmps.t[0:64, 0:896], tmps.t[0:64, 1152:2048], ALU.add, [tmps.r], [bs.r])
        s.tt("dve", bsf[:, 896:1152], tmps.t[0:64, 896:1152], tmps2.t[0:64, 0:256], ALU.add, [tmps.r, tmps2.r], [bs.r])
        qb_p = s.pool([64, 32, 128], BF16, 2)
        zb_p = s.pool([128, 16, 128], BF16, 2)
        gs_p = s.pool([128, 16, 128], BF16, 2)
        Ep = s.pool([128, 2, 256], BF16, 2)
        Pp = s.pool([128, 2, 256], BF16, 2)
        PTp = s.pool([128, 4, 128], BF16, 2)
        smp = s.pool([128, 16], F32, 4)
        it = 0
        for n in range(16):
            t0 = n * 128
            qb = qb_p[n % 2]
            zb = zb_p[n % 2]
            gs = gs_p[n % 2]
            s.dma("sp", qb.t[:, :, :], QTap[:, :, t0:t0 + 128], [s.QT.r], [qb.r])
            s.dma("sp", zb.t[:, :, :], ZTap[:, 0:16, t0:t0 + 128], [s.ZT.r], [zb.r])
            nk = 128 if n == 0 else 256
            k0 = 0 if n == 0 else t0 - 128
            bo = 128 if n == 0 else 0
            nkt = nk // 128
            for pr in range(16):
                kvh = pr // 4
                E = Ep[it % 2]
                P = Pp[it % 2]
                PT = PTp[it % 2]
                sm = smp[it % 4]
                it += 1
                ps = s.ps()
                for j in range(2):
                    h = pr * 2 + j
                    s.mm(ps.t[:, j * 256:j * 256 + nk], s.idb.t[:, :], bm.t[:, h, bo:bo + nk], True, False,
                         [s.idb.r, bm.r], [ps.r], inc=False)
                    s.mm(ps.t[:, j * 256:j * 256 + nk], qb.t[:, h, :], kT.t[:, kvh, k0:k0 + nk], False, True,
                         [qb.r, kT.r], [ps.r], inc=(j == 1))
                pv = ps.t[:, :].rearrange("p (a b) -> p a b", a=2)[:, :, 0:nk]
                s.red("dve", sm.t[:, 0:2], pv, ALU.max, [ps.r], [sm.r])
                s.tt("dve", sm.t[:, 0:2], sm.t[:, 0:2], sink.t[:, pr * 2:pr * 2 + 2], ALU.max, [sm.r, sink.r], [sm.r])
                s.ts("dve", sm.t[:, 2:4], sm.t[:, 0:2], -1.0, None, ALU.mult, None, [sm.r], [sm.r])
                for j in range(2):
                    s.act(E.t[:, j, 0:nk], ps.t[:, j * 256:j * 256 + nk], AF.Exp, [ps.r, sm.r], [E.r, sm.r],
                          bias=sm.t[:, 2 + j:3 + j], scale=1.0, accum_out=sm.t[:, 4 + j:5 + j])
                s.tt("dve", sm.t[:, 6:8], sink.t[:, pr * 2:pr * 2 + 2], sm.t[:, 0:2], ALU.subtract, [sm.r, sink.r], [sm.r])
                s.act(sm.t[:, 8:10], sm.t[:, 6:8], AF.Exp, [sm.r], [sm.r])
                s.tt("dve", sm.t[:, 10:12], sm.t[:, 8:10], sm.t[:, 4:6], ALU.add, [sm.r], [sm.r])
                s.op("dve", lambda g: g.reciprocal(sm.t[:, 12:14], sm.t[:, 10:12]), [sm.r], [sm.r])
                s.tt("dve", P.t[:, :, 0:nk], E.t[:, :, 0:nk], sm.t[:, 12:14].unsqueeze(2).to_broadcast([128, 2, nk]),
                     ALU.mult, [E.r, sm.r], [P.r])
                pst = s.ps()
                pstb = pst.t[:, :].bitcast(BF16)
                for j in range(2):
                    for kt in range(nkt):
                        ix = j * nkt + kt
                        s.tp(pstb[:, ix * 128:(ix + 1) * 128], P.t[:, j, kt * 128:(kt + 1) * 128], s.idb.t[:, :],
                             [P.r, s.idb.r], [pst.r], inc=(ix == 2 * nkt - 1))
                s.cp(s.ev_eng(), PT.t[:, 0:2 * nkt, :], pstb[:, 0:2 * nkt * 128].rearrange("p (a b) -> p a b", a=2 * nkt),
                     [pst.r], [PT.r])
                po = s.ps()
                for j in range(2):
                    for kt in range(nkt):
                        ktile = n if n == 0 else n - 1 + kt
                        s.mm(po.t[j * 64:(j + 1) * 64, 0:128], vtok.t[:, ktile, kvh * 64:(kvh + 1) * 64],
                             PT.t[:, j * nkt + kt, :], kt == 0, kt == nkt - 1, [vtok.r, PT.r], [po.r],
                             inc=(j == 1 and kt == nkt - 1))
                s.tt("dve", gs.t[:, pr, :], po.t[:, 0:128], zb.t[:, pr, :], ALU.mult, [po.r, zb.r], [gs.r])
            s.dma("sp", Gap[:, 0:16, t0:t0 + 128], gs.t[:, :, :], [gs.r], [s.GTd.r])
        qs = s.sb([64, 32, NS], BF16, name="a_qs")
        zs = s.sb([128, 16, NS], BF16, name="a_zs")
        gss = s.sb([128, 16, NS], BF16, name="a_gss")
        s.dma("sp", qs.t[:, :, :], QTap[:, :, 2048:TT], [s.QT.r], [qs.r])
        qs2 = s.sb([64, 4, 128], BF16, name="a_qs2")
        for kvh in range(4):
            for par in range(2):
                o_ = qs2.t[:, :, kvh * 32 + par * 16:kvh * 32 + par * 16 + 16].rearrange("p b (g t) -> p g b t", g=4)
                i_ = qs.t[:, kvh * 8 + par:kvh * 8 + 8:2, :].rearrange("p g (b t) -> p g b t", b=4)
                s.cp("dve", o_, i_, [qs.r], [qs2.r])
        s.dma("sp", zs.t[:, :, :], ZTap[:, 0:16, 2048:TT], [s.ZT.r], [zs.r])
        cst = s.pool([128, 512], F32, 2)
        cbf = s.pool([128, 512], BF16, 2)
        kTs = s.pool([64, 4, 144], BF16, 2)
        Es = s.pool([64, 2, 144], BF16, 2)
        PTs = s.pool([128, 2, 128], BF16, 2)
        for bi in range(4):
            c32 = cst[bi % 2]
            cb = cbf[bi % 2]
            kts = kTs[bi % 2]
            E = Es[bi % 2]
            PT = PTs[bi % 2]
            sm = smp[bi % 4]
            s.dma("sp", c32.t[:, :], cache[bi, :, :], [], [c32.r])
            s.cp("dve", cb.t[:, :], c32.t[:, :], [c32.r], [cb.r])
            pst = s.ps()
            pstb = pst.t[:, :].bitcast(BF16)
            for kvh in range(4):
                s.tp(pstb[0:64, kvh * 128:(kvh + 1) * 128], cb.t[:, kvh * 64:(kvh + 1) * 64], s.idb.t[:, :],
                     [cb.r, s.idb.r], [pst.r], inc=(kvh == 3))
            s.cp("act", kts.t[:, :, 0:128], pstb[0:64, 0:512].rearrange("p (a b) -> p a b", a=4), [pst.r], [kts.r])
            s.cp("dve", kts.t[:, :, 128:144], kT.t[:, :, 2048:TT], [kT.r], [kts.r])
            pst2 = s.ps()
            pst2b = pst2.t[:, :].bitcast(BF16)
            for half in range(2):
                sm = smp[(bi * 2 + half) % 4]
                ps = s.ps()
                s.mm(ps.t[0:64, 0:144], s.idb.t[0:64, 0:64], bs.t[:, half, bi, :], True, False, [s.idb.r, bs.r], [ps.r],
                     inc=False)
                for k2 in range(2):
                    kvh = half * 2 + k2
                    s.mm(ps.t[k2 * 32:(k2 + 1) * 32, 0:144], qs2.t[:, bi, kvh * 32:(kvh + 1) * 32],
                         kts.t[:, kvh, :], False, k2 == 1, [qs2.r, kts.r], [ps.r], inc=(k2 == 1))
                sk = sink.t[0:64, 32 + half:33 + half]
                s.red("dve", sm.t[0:64, 0:1], ps.t[0:64, 0:144], ALU.max, [ps.r], [sm.r])
                s.tt("dve", sm.t[0:64, 0:1], sm.t[0:64, 0:1], sk, ALU.max, [sm.r, sink.r], [sm.r])
                s.ts("dve", sm.t[0:64, 2:3], sm.t[0:64, 0:1], -1.0, None, ALU.mult, None, [sm.r], [sm.r])
                s.act(E.t[0:64, half, :], ps.t[0:64, 0:144], AF.Exp, [ps.r, sm.r], [E.r, sm.r], bias=sm.t[0:64, 2:3],
                      scale=1.0, accum_out=sm.t[0:64, 4:5])
                s.tt("dve", sm.t[0:64, 6:7], sk, sm.t[0:64, 0:1], ALU.subtract, [sm.r, sink.r], [sm.r])
                s.act(sm.t[0:64, 8:9], sm.t[0:64, 6:7], AF.Exp, [sm.r], [sm.r])
                s.tt("dve", sm.t[0:64, 10:11], sm.t[0:64, 8:9], sm.t[0:64, 4:5], ALU.add, [sm.r], [sm.r])
                s.op("dve", lambda g: g.reciprocal(sm.t[0:64, 12:13], sm.t[0:64, 10:11]), [sm.r], [sm.r])
                s.ts("dve", E.t[0:64, half, :], E.t[0:64, half, :], sm.t[0:64, 12:13], None, ALU.mult, None,
                     [E.r, sm.r], [E.r])
                s.tp(pst2b[:, half * 64:(half + 1) * 64], E.t[0:64, half, 0:128], s.idb.t[0:64, 0:64],
                     [E.r, s.idb.r], [pst2.r], inc=False)
                s.tp(pst2b[0:16, 128 + half * 64:128 + (half + 1) * 64], E.t[0:64, half, 128:144], s.idb.t[0:64, 0:64],
                     [E.r, s.idb.r], [pst2.r], inc=True)
            s.cp("act", PT.t[:, 0, :], pst2b[:, 0:128], [pst2.r], [PT.r])
            s.cp("dve", PT.t[0:16, 1, :], pst2b[0:16, 128:256], [pst2.r], [PT.r, pst2.r])
            po = s.ps()
            for kvh in range(4):
                for par in range(2):
                    r0 = kvh * 32 + par * 16
                    rows = PT.t[:, 0, r0:r0 + 16]
                    rows2 = PT.t[0:16, 1, r0:r0 + 16]
                    o_ap = po.t[par * 64:(par + 1) * 64, kvh * 16:(kvh + 1) * 16]
                    s.mm(o_ap, cb.t[:, 256 + kvh * 64:256 + (kvh + 1) * 64], rows, True, False, [cb.r, PT.r], [po.r],
                         inc=False)
                    s.mm(o_ap, vtok.t[0:16, 16, kvh * 64:(kvh + 1) * 64], rows2, False, True, [vtok.r, PT.r], [po.r],
                         inc=(kvh == 3 and par == 1))
            s.tt("dve", gss.t[:, :, bi * 4:bi * 4 + 4],
                 po.t[:, 0:64].rearrange("p (a b) -> p a b", a=16), zs.t[:, :, bi * 4:bi * 4 + 4], ALU.mult,
                 [po.r, zs.r], [gss.r])
        s.dma("sp", Gap[:, 0:16, 2048:TT], gss.t[:, :, :], [gss.r], [s.GTd.r])

    def setup_s5(self):
        s = self
        s.din("s5_w_in", [D, 4096])
        s.din("s5_w_glu", [D, D])
        s.din("s5_w_out", [D, D])
        s.din("s5_a_re", [128, 64])
        s.din("s5_a_im", [128, 64])
        s.din("s5_log_dt", [128, 1])
        s.din("s5_b_re", [128, 64, 16])
        s.din("s5_b_im", [128, 64, 16])
        s.din("s5_c_re", [128, 16, 64])
        s.din("s5_c_im", [128, 16, 64])
        s.din("s5_d_fm", [128, 16])
        s.din("s5_state", [4, 128, 128])
        s.din("s5_const", [128, 256])
        s.dout("s5_p", [128, 128])
        s.dout("s5_s", [4, 128, 128])
        s.UT = s.dram("UT", [128, 16, TT], BF16)
        s.Y1T = s.dram("Y1T", [128, 16, TT], BF16)

    def s5_proj(self, xT):
        s = self
        W = s.inp["s5_w_in"].t.ap()
        ust = s.pool([128, 2, 512], BF16, 2)
        KC = list(range(16))
        UTap = s.UT.t.ap()
        ZTap = s.ZT.t.ap()
        qi = 0
        for blk in range(16):
            wt = s.nxt(s.wp, "wi")
            s.load_w(wt, W, D, blk * 256, 256)
            for (t0, tn) in TBS:
                st = ust[qi % 2]
                qi += 1
                for j in range(2):
                    ps = s.ps()
                    s.ws_mm(ps.t[:, 0:tn], ps.r, wt, (j * 128, j * 128 + 128), xT, KC, t0, tn, [xT.r])
                    if blk < 8:
                        s.cp(s.ev_eng(), st.t[:, j, 0:tn], ps.t[:, 0:tn], [ps.r], [st.r])
                    else:
                        s.act(st.t[:, j, 0:tn], ps.t[:, 0:tn], AF.Silu, [ps.r], [st.r])
                if blk < 8:
                    s.dma("sp", UTap[:, blk * 2:blk * 2 + 2, t0:t0 + tn], st.t[:, :, 0:tn], [st.r], [s.UT.r])
                else:
                    zb = blk - 8
                    s.dma("sp", ZTap[:, zb * 2:zb * 2 + 2, t0:t0 + tn], st.t[:, :, 0:tn], [st.r], [s.ZT.r])

    def cmul(self, e, o_re, o_im, a_re, a_im, b_re, b_im, tmp, rs, ws):
        s = self
        s.tt(e, o_re, a_re, b_re, ALU.mult, rs, ws)
        s.tt(e, tmp, a_im, b_im, ALU.mult, rs, ws)
        s.tt(e, o_re, o_re, tmp, ALU.subtract, rs + ws, ws)
        s.tt(e, o_im, a_re, b_im, ALU.mult, rs, ws)
        s.tt(e, tmp, a_im, b_re, ALU.mult, rs, ws)
        s.tt(e, o_im, o_im, tmp, ALU.add, rs + ws, ws)

    def s5_core(self):
        s = self
        PI = float(np.pi)
        cst = s.sb([128, 256], F32, name="s5c")
        s.dma("sp", cst.t[:, :], s.inp["s5_const"].t.ap()[:, :], [], [cst.r])
        selp = cst.t[:, 0:64]
        pm = cst.t[:, 64:66]
        lm = cst.t[:, 66:68]
        I2 = cst.t[:, 68:132]
        L = s.sb([128, 16, 64], F32, name="s5L")
        NST = 11
        apw = s.sb([128, NST, 3, 64], F32, name="s5apw")
        ah0 = s.sb([128, 4, 2, 64], F32, name="s5ah0")
        Bb = s.sb([128, 2, 64, 16], F32, name="s5Bb")
        with s.scope():
            pr_ = s.sb([128, 16, 64], F32, name="s5par")
            P = pr_.t
            R_ = [pr_.r]
            s.dma("sp", P[:, 0, :], s.inp["s5_a_re"].t.ap()[:, :], [], R_)
            s.dma("sp", P[:, 1, :], s.inp["s5_a_im"].t.ap()[:, :], [], R_)
            s.dma("sp", P[:, 2, 0:1], s.inp["s5_log_dt"].t.ap()[:, :], [], R_)
            s.act(P[:, 2, 1:2], P[:, 2, 0:1], AF.Exp, R_, R_)
            s.act(P[:, 3, :], P[:, 0, :], AF.Exp, R_, R_, scale=P[:, 2, 1:2])
            s.ts("dve", P[:, 4, :], P[:, 1, :], P[:, 2, 1:2], None, ALU.mult, None, R_, R_)

            def rangered(off):
                s.ts("dve", P[:, 5, :], P[:, 4, :], off, None, ALU.add, None, R_, R_)
                s.cp("dve", P[:, 15, :], P[:, 5, :], R_, R_)
                for j in range(1, 8):
                    s.ts("dve", P[:, 11, :], P[:, 15, :], 2 * PI * j - PI, -2 * PI, ALU.is_ge, ALU.mult, R_, R_)
                    s.tt("dve", P[:, 5, :], P[:, 5, :], P[:, 11, :], ALU.add, R_, R_)
            rangered(0.0)
            s.act(P[:, 6, :], P[:, 5, :], AF.Sin, R_, R_)
            rangered(0.5 * PI)
            s.act(P[:, 7, :], P[:, 5, :], AF.Sin, R_, R_)
            s.tt("dve", P[:, 8, :], P[:, 3, :], P[:, 7, :], ALU.mult, R_, R_)
            s.tt("dve", P[:, 9, :], P[:, 3, :], P[:, 6, :], ALU.mult, R_, R_)
            s.ts("dve", P[:, 10, :], P[:, 8, :], -1.0, None, ALU.add, None, R_, R_)
            s.tt("dve", P[:, 11, :], P[:, 0, :], P[:, 0, :], ALU.mult, R_, R_)
            s.tt("dve", P[:, 12, :], P[:, 1, :], P[:, 1, :], ALU.mult, R_, R_)
            s.tt("dve", P[:, 11, :], P[:, 11, :], P[:, 12, :], ALU.add, R_, R_)
            s.op("dve", lambda g: g.reciprocal(P[:, 12, :], P[:, 11, :]), R_, R_)
            s.tt("dve", P[:, 13, :], P[:, 10, :], P[:, 0, :], ALU.mult, R_, R_)
            s.tt("dve", P[:, 14, :], P[:, 9, :], P[:, 1, :], ALU.mult, R_, R_)
            s.tt("dve", P[:, 13, :], P[:, 13, :], P[:, 14, :], ALU.add, R_, R_)
            s.tt("dve", P[:, 13, :], P[:, 13, :], P[:, 12, :], ALU.mult, R_, R_)
            s.tt("dve", P[:, 14, :], P[:, 9, :], P[:, 0, :], ALU.mult, R_, R_)
            s.tt("dve", P[:, 15, :], P[:, 10, :], P[:, 1, :], ALU.mult, R_, R_)
            s.tt("dve", P[:, 14, :], P[:, 14, :], P[:, 15, :], ALU.subtract, R_, R_)
            s.tt("dve", P[:, 14, :], P[:, 14, :], P[:, 12, :], ALU.mult, R_, R_)
            st0 = s.sb([128, 4, 128], F32, name="s5st")
            s.dma("sp", st0.t[:, :, :], s.inp["s5_state"].t.ap().rearrange("b g x -> g b x"), [], [st0.r])
            dm = s.sb([128, 2, 64], F32, name="s5dm")

            def to_lanes(src_ap, src_r, dst_idx):
                s.tt("dve", dm.t[:, :, :], src_ap.unsqueeze(1).to_broadcast([128, 2, 64]),
                     pm.unsqueeze(2).to_broadcast([128, 2, 64]), ALU.mult, src_r + [cst.r], [dm.r])
                ps = s.ps()
                s.mm(ps.t[:, 0:64], dm.t[:, :, :].rearrange("p a b -> p (a b)"), selp, True, True, [dm.r, cst.r], [ps.r])
                s.cp("dve", L.t[:, dst_idx, :], ps.t[:, 0:64], [ps.r], [L.r])

            for i, k in enumerate((8, 9, 13, 14)):
                to_lanes(P[:, k, :], R_, i)
            for b in range(4):
                for c in range(2):
                    to_lanes(st0.t[:, b, :].rearrange("p (x c) -> p x c", c=2)[:, :, c], [st0.r], 4 + b * 2 + c)
            tmpl = s.sb([128, 64], F32, name="s5tmpl")
            s.cp("dve", apw.t[:, 0, 0, :], L.t[:, 0, :], [L.r], [apw.r])
            s.cp("dve", apw.t[:, 0, 1, :], L.t[:, 1, :], [L.r], [apw.r])
            for k in range(1, NST):
                s.cmul("dve", apw.t[:, k, 0, :], apw.t[:, k, 1, :], apw.t[:, k - 1, 0, :], apw.t[:, k - 1, 1, :],
                       apw.t[:, k - 1, 0, :], apw.t[:, k - 1, 1, :], tmpl.t[:, :], [apw.r], [apw.r, tmpl.r])
            for k in range(NST):
                s.ts("dve", apw.t[:, k, 2, :], apw.t[:, k, 1, :], -1.0, None, ALU.mult, None, [apw.r], [apw.r])
            for b in range(4):
                s.cmul("dve", ah0.t[:, b, 0, :], ah0.t[:, b, 1, :], L.t[:, 0, :], L.t[:, 1, :], L.t[:, 4 + b * 2, :],
                       L.t[:, 5 + b * 2, :], tmpl.t[:, :], [L.r], [ah0.r, tmpl.r])
            Bn = s.sb([128, 2, 64, 16], F32, name="s5Bn")
            tmpB = s.sb([128, 64, 16], F32, name="s5tmpB")
            for c, nm in enumerate(("s5_b_re", "s5_b_im")):
                bap = s.inp[nm].t.ap()
                for q8 in range(8):
                    srcap = bass.AP(bap.tensor, q8 * 8 * 2048, [[16, 128], [2048, 8], [1, 16]])
                    s.dma("sp", Bn.t[:, c, q8 * 8:(q8 + 1) * 8, :], srcap, [], [Bn.r])
            cre_b = L.t[:, 2, :].unsqueeze(2).to_broadcast([128, 64, 16])
            cim_b = L.t[:, 3, :].unsqueeze(2).to_broadcast([128, 64, 16])
            s.cmul("dve", Bb.t[:, 0, :, :], Bb.t[:, 1, :, :], cre_b, cim_b, Bn.t[:, 0, :, :], Bn.t[:, 1, :, :],
                   tmpB.t[:, :, :], [L.r, Bn.r], [Bb.r, tmpB.r])
        Ct = s.sb([128, 2, 16, 64], F32, name="s5Ct")
        for c, nm in enumerate(("s5_c_re", "s5_c_im")):
            cap = s.inp[nm].t.ap()
            for h2 in range(2):
                srcap = bass.AP(cap.tensor, h2 * 8 * 8192, [[64, 128], [8192, 8], [1, 64]])
                s.dma("sp", Ct.t[:, c, h2 * 8:(h2 + 1) * 8, :], srcap, [], [Ct.r])
        s.ts("dve", Ct.t[:, 1, :, :], Ct.t[:, 1, :, :], -1.0, None, ALU.mult, None, [Ct.r], [Ct.r])
        pm8 = s.sb([128, 2], F32, name="s5pm8")
        s.dma("sp", pm8.t[:, :], s.inp["s5_const"].t.ap()[:, 132:134], [], [pm8.r])
        dfm = s.sb([128, 16], F32, name="s5d")
        s.dma("sp", dfm.t[:, :], s.inp["s5_d_fm"].t.ap()[:, :], [], [dfm.r])
        CTp = s.sb([128, 4, 2, 128], F32, name="s5CTp")
        s.memset("pool", CTp.t[:, :, :, :], 0.0, [CTp.r])
        Ssrc = [s.sb([128, 128], F32, name="s5S") for _ in range(2)]
        for t_ in Ssrc:
            s.memset("pool", t_.t[:, :], 0.0, [t_.r])
        BTp = s.pool([128, 2, 128], BF16, 2)
        Cdm = s.sb([128, 2, 64], F32, name="s5Cdm")
        Hb = [s.sb([128, 2, TT], F32, name="s5H") for _ in range(5)]
        Hl = s.sb([128, 5, 2, 64], F32, name="s5Hl")
        ug_p = s.pool([128, TT], BF16, 2)
        yst = s.pool([128, TT], BF16, 2)
        gtmp = s.pool([128, 512], F32, 2)
        ptmp = s.sb([128, SEQ], F32, name="s5ptmp")
        UTap = s.UT.t.ap()
        Y1ap = s.Y1T.t.ap()
        Tm = Hb[4]
        LP = SEQ
        for gc in range(16):
            ug = ug_p[gc % 2]
            s.dma("sp", ug.t[:, :], UTap[:, gc, :], [s.UT.r], [ug.r])
            for pr in range(4):
                pair = gc * 4 + pr
                X = Hb[pr]
                bt = BTp[pair % 2]
                for c in range(2):
                    S_ = Ssrc[c]
                    if pr > 0:
                        pc = (2 * (pr - 1)) * 16
                        s.memset("pool", S_.t[:, pc:pc + 32], 0.0, [S_.r])
                    elif gc > 0:
                        s.memset("pool", S_.t[:, 96:128], 0.0, [S_.r])
                    c0 = 2 * pr * 16
                    s.cp("dve", S_.t[0:64, c0:c0 + 16], Bb.t[0:64, c, pair, :], [Bb.r], [S_.r])
                    s.cp("dve", S_.t[64:128, c0 + 16:c0 + 32], Bb.t[64:128, c, pair, :], [Bb.r], [S_.r])
                    ps = s.ps()
                    s.tp(ps.t[:, 0:128], S_.t[:, :], s.idf.t[:, :], [S_.r, s.idf.r], [ps.r])
                    s.cp("act", bt.t[:, c, :], ps.t[:, 0:128], [ps.r], [bt.r])
                for (t0, tn) in TBS:
                    for c in range(2):
                        ps = s.ps()
                        s.mm(ps.t[:, 0:tn], bt.t[:, c, :], ug.t[:, t0:t0 + tn], True, True, [bt.r, ug.r], [ps.r])
                        s.cp("act" if c else "dve", Tm.t[:, c, t0:t0 + tn], ps.t[:, 0:tn], [ps.r], [Tm.r])
                for c in range(2):
                    sv = Tm.t[:, c, LP:TT].rearrange("p (b t) -> p b t", b=4)[:, :, 0]
                    s.tt("dve", sv, sv, ah0.t[:, :, c, pair], ALU.add, [Tm.r, ah0.r], [Tm.r])
                src, dst = Tm, X
                for k in range(NST):
                    sh = 1 << k
                    ar = apw.t[:, k, 0, pair:pair + 1]
                    ai = apw.t[:, k, 1, pair:pair + 1]
                    nai = apw.t[:, k, 2, pair:pair + 1]
                    for c, e in ((0, "dve"), (1, "dve")):
                        oth = 1 - c
                        s.cp(e, dst.t[:, c, 0:sh], src.t[:, c, 0:sh], [src.r], [dst.r])
                        if e == "dve":
                            s.stt(e, dst.t[:, c, sh:LP], src.t[:, c, 0:LP - sh], ar, src.t[:, c, sh:LP], ALU.mult, ALU.add,
                                  [src.r, apw.r], [dst.r])
                            s.stt(e, dst.t[:, c, sh:LP], src.t[:, oth, 0:LP - sh], nai if c == 0 else ai,
                                  dst.t[:, c, sh:LP], ALU.mult, ALU.add, [src.r, apw.r, dst.r], [dst.r])
                        else:
                            s.ts(e, dst.t[:, c, sh:LP], src.t[:, c, 0:LP - sh], ar, None, ALU.mult, None,
                                 [src.r, apw.r], [dst.r])
                            s.tt(e, dst.t[:, c, sh:LP], dst.t[:, c, sh:LP], src.t[:, c, sh:LP], ALU.add,
                                 [src.r, dst.r], [dst.r])
                            s.ts(e, ptmp.t[:, 0:LP - sh], src.t[:, oth, 0:LP - sh], nai if c == 0 else ai, None, ALU.mult,
                                 None, [src.r, apw.r], [ptmp.r])
                            s.tt(e, dst.t[:, c, sh:LP], dst.t[:, c, sh:LP], ptmp.t[:, 0:LP - sh], ALU.add,
                                 [ptmp.r, dst.r], [dst.r])
                        sv = src.t[:, c, LP:TT].rearrange("p (b t) -> p b t", b=4)
                        so = src.t[:, oth, LP:TT].rearrange("p (b t) -> p b t", b=4)
                        dv = dst.t[:, c, LP:TT].rearrange("p (b t) -> p b t", b=4)
                        if sh < 4:
                            s.cp("dve", dv[:, :, 0:sh], sv[:, :, 0:sh], [src.r], [dst.r])
                            s.stt("dve", dv[:, :, sh:4], sv[:, :, 0:4 - sh], ar, sv[:, :, sh:4], ALU.mult, ALU.add,
                                  [src.r, apw.r], [dst.r])
                            s.stt("dve", dv[:, :, sh:4], so[:, :, 0:4 - sh], nai if c == 0 else ai, dv[:, :, sh:4],
                                  ALU.mult, ALU.add, [src.r, apw.r, dst.r], [dst.r])
                        else:
                            s.cp(e, dv, sv, [src.r], [dst.r])
                    src, dst = dst, src
                for c in range(2):
                    s.cp("dve", Hl.t[:, 0, c, pair:pair + 1], X.t[:, c, LP - 1:LP], [X.r], [Hl.r])
                    s.cp("dve", Hl.t[:, 1:5, c, pair], X.t[:, c, LP:TT].rearrange("p (b t) -> p b t", b=4)[:, :, 3],
                         [X.r], [Hl.r])
            for c in range(2):
                s.tt("dve", Cdm.t[:, :, :], Ct.t[:, c, gc, :].unsqueeze(1).to_broadcast([128, 2, 64]),
                     pm8.t[:, :].unsqueeze(2).to_broadcast([128, 2, 64]), ALU.mult, [Ct.r, pm8.r], [Cdm.r])
                ps = s.ps()
                s.tp(ps.t[:, 0:128], Cdm.t[:, :, :].rearrange("p a b -> p (a b)"), s.idf.t[:, :], [Cdm.r, s.idf.r], [ps.r])
                for pr in range(4):
                    s.cp("dve", CTp.t[:, pr, c, pr * 32:(pr + 1) * 32], ps.t[:, pr * 32:(pr + 1) * 32], [ps.r], [CTp.r])
            ys = yst[gc % 2]
            for (t0, tn) in TBS:
                ps = s.ps()
                n = 0
                for pr in range(4):
                    for c in range(2):
                        s.mm(ps.t[:, 0:tn], CTp.t[:, pr, c, :], Hb[pr].t[:, c, t0:t0 + tn], n == 0, n == 7,
                             [CTp.r, Hb[pr].r], [ps.r])
                        n += 1
                g1 = gtmp[0]
                g2 = gtmp[1]
                s.stt("dve", g1.t[:, 0:tn], ug.t[:, t0:t0 + tn], dfm.t[:, gc:gc + 1], ps.t[:, 0:tn], ALU.mult, ALU.add,
                      [ug.r, dfm.r, ps.r], [g1.r])
                s.act(g2.t[:, 0:tn], g1.t[:, 0:tn], AF.Square, [g1.r], [g2.r])
                s.ts("dve", g2.t[:, 0:tn], g2.t[:, 0:tn], 0.044715, 1.0, ALU.mult, ALU.add, [g2.r], [g2.r])
                s.tt("dve", g2.t[:, 0:tn], g2.t[:, 0:tn], g1.t[:, 0:tn], ALU.mult, [g1.r, g2.r], [g2.r])
                s.act(g2.t[:, 0:tn], g2.t[:, 0:tn], AF.Sigmoid, [g2.r], [g2.r], scale=1.5957691216057308)
                s.tt("dve", ys.t[:, t0:t0 + tn], g2.t[:, 0:tn], g1.t[:, 0:tn], ALU.mult, [g1.r, g2.r], [ys.r])
            s.dma("sp", Y1ap[:, gc, :], ys.t[:, :], [ys.r], [s.Y1T.r])
        lh = s.sb([128, 64, 2], F32, name="s5lh")
        og = s.sb([128, 5, 64, 2], F32, name="s5og")
        for w in range(5):
            for c in range(2):
                s.tt("dve", lh.t[:, :, :], Hl.t[:, w, c, :].unsqueeze(2).to_broadcast([128, 64, 2]),
                     lm.unsqueeze(1).to_broadcast([128, 64, 2]), ALU.mult, [Hl.r, cst.r], [lh.r])
                ps = s.ps()
                s.mm(ps.t[:, 0:64], lh.t[:, :, :].rearrange("p a b -> p (a b)"), I2, True, True, [lh.r, cst.r], [ps.r])
                s.cp("dve", og.t[:, w, :, c], ps.t[:, 0:64], [ps.r], [og.r])
        s.dma("sp", s.out["s5_p"].t.ap()[:, :], og.t[:, 0, :, :].rearrange("p a b -> p (a b)"), [og.r], [s.out["s5_p"].r])
        s.dma("sp", s.out["s5_s"].t.ap().rearrange("b g x -> g b x"),
              og.t[:, 1:5, :, :].rearrange("p w a b -> p w (a b)"), [og.r], [s.out["s5_s"].r])

    def s5_glu(self, y1T):
        s = self
        W = s.inp["s5_w_glu"].t.ap()
        s.dma("sp", y1T.t[:, :, :], s.Y1T.t.ap()[:, :, :], [s.Y1T.r], [y1T.r])
        zst = s.pool([128, 2, 512], BF16, 2)
        gst = s.pool([128, 2, 512], BF16, 2)
        sg = s.pool([128, 512], BF16, 2)
        KC = list(range(16))
        ZTap = s.ZT.t.ap()
        Gap = s.GTd.t.ap()
        qi = 0
        for blk in range(8):
            wt = s.nxt(s.wp, "wi")
            s.load_w(wt, W, D, blk * 256, 256)
            for (t0, tn) in TBS:
                zt = zst[qi % 2]
                gt = gst[qi % 2]
                qi += 1
                s.dma("sp", zt.t[:, :, 0:tn], ZTap[:, blk * 2:blk * 2 + 2, t0:t0 + tn], [s.ZT.r], [zt.r])
                for j in range(2):
                    ch = blk * 2 + j
                    ps = s.ps()
                    s.ws_mm(ps.t[:, 0:tn], ps.r, wt, (j * 128, j * 128 + 128), y1T, KC, t0, tn, [y1T.r])
                    sgt = sg[j]
                    s.act(sgt.t[:, 0:tn], ps.t[:, 0:tn], AF.Sigmoid, [ps.r], [sgt.r])
                    s.tt("dve", sgt.t[:, 0:tn], sgt.t[:, 0:tn], y1T.t[:, ch, t0:t0 + tn], ALU.mult, [sgt.r, y1T.r], [sgt.r])
                    s.tt("pool", gt.t[:, j, 0:tn], sgt.t[:, 0:tn], zt.t[:, j, 0:tn], ALU.mult, [sgt.r, zt.r], [gt.r])
                s.dma("sp", Gap[:, blk * 2:blk * 2 + 2, t0:t0 + tn], gt.t[:, :, 0:tn], [gt.r], [s.GTd.r])

    def setup_gdn(self):
        s = self
        s.din("gdn_w_in", [D, 12352])
        s.din("gdn_w_out", [4096, D])
        s.din("gdn_cw", [128, 64, 4])
        s.din("gdn_ab", [128, 64])
        s.din("gdn_nw", [128, 128])
        s.din("gdn_state", [4, 32, 128, 128])
        s.din("gdn_cbuf", [12, 8192])
        s.din("gdn_mp", [128, 770])
        s.din("gdn_ms", [16, 580])
        s.dout("gd_p", [32, 128, 128])
        s.dout("gd_s", [4, 32, 128, 128])
        s.dout("gc", [15, 8192])
        s.QN = s.dram("QN", [16, 128, TT], BF16)
        s.KN = s.dram("KN", [16, 128, TT], BF16)
        s.VT = s.dram("VT", [32, 128, TT], BF16)
        s.GBd = s.dram("GBd", [128, NTILE, 64], F32)

    def gdn_proj(self, xT):
        s = self
        W = s.inp["gdn_w_in"].t.ap()
        KC = list(range(16))
        cw = s.sb([128, 64, 4], F32, name="g_cw")
        s.dma("sp", cw.t[:, :, :], s.inp["gdn_cw"].t.ap()[:, :, :], [], [cw.r])
        ab = s.sb([128, 96], F32, name="g_ab")
        s.dma("sp", ab.t[:, 0:64], s.inp["gdn_ab"].t.ap()[:, :], [], [ab.r])
        s.act(ab.t[:, 64:96], ab.t[:, 0:32], AF.Exp, [ab.r], [ab.r])
        onesb = s.sb([128, 128], BF16, name="g_ones")
        s.memset("dve", onesb.t[:, :], 1.0, [onesb.r])
        Tl = s.sb([128, 64, 15], F32, name="g_Tl")
        pre_p = s.pool([128, 3 + SEQ], F32, 2)
        pres_p = s.pool([128, 4, 7], F32, 2)
        cv_p = s.pool([128, TT], F32, 2)
        sq_p = s.pool([128, TT], BF16, 2)
        ob_p = s.pool([128, TT], BF16, 2)
        cb_p = s.pool([12, 128], F32, 2)
        zst = s.pool([128, 2, 512], BF16, 2)
        GB = s.sb([128, NTILE, 64], F32, name="g_GB")
        cbap = s.inp["gdn_cbuf"].t.ap()
        ZTap = s.ZT.t.ap()
        qi = 0
        for blk in range(49):
            wt = s.nxt(s.wp, "wi")
            ncol = 256 if blk < 48 else 64
            s.load_w(wt, W, D, blk * 256, ncol)
            if blk < 32:
                for j in range(2):
                    ch = blk * 2 + j
                    pre = pre_p[ch % 2]
                    prs = pres_p[ch % 2]
                    cv = cv_p[ch % 2]
                    s.memset("pool", pre.t[:, 0:3], 0.0, [pre.r])
                    cb = cb_p[ch % 2]
                    s.dma("sp", cb.t[:, :], cbap[:, ch * 128:(ch + 1) * 128], [], [cb.r])
                    psb = s.ps()
                    s.tp(psb.t[:, 0:12], cb.t[:, :], s.idf.t[0:12, 0:12], [cb.r, s.idf.r], [psb.r])
                    s.cp("act", prs.t[:, :, 0:3], psb.t[:, 0:12].rearrange("p (b r) -> p b r", b=4), [psb.r], [prs.r])
                    for (t0, tn) in TBS:
                        ps = s.ps()
                        s.ws_mm(ps.t[:, 0:tn], ps.r, wt, (j * 128, j * 128 + 128), xT, KC, t0, tn, [xT.r])
                        if t0 < SEQ:
                            s.cp("act", pre.t[:, 3 + t0:3 + t0 + tn], ps.t[:, 0:tn], [ps.r], [pre.r])
                        else:
                            s.cp("act", prs.t[:, :, 3:7], ps.t[:, 0:tn].rearrange("p (b t) -> p b t", b=4), [ps.r], [prs.r])
                    s.ts("dve", cv.t[:, 0:SEQ], pre.t[:, 0:SEQ], cw.t[:, ch, 0:1], None, ALU.mult, None, [pre.r, cw.r], [cv.r])
                    for k in range(1, 4):
                        s.stt("dve", cv.t[:, 0:SEQ], pre.t[:, k:k + SEQ], cw.t[:, ch, k:k + 1], cv.t[:, 0:SEQ], ALU.mult,
                              ALU.add, [pre.r, cw.r, cv.r], [cv.r])
                    cvs = cv.t[:, SEQ:TT].rearrange("p (b t) -> p b t", b=4)
                    s.ts("dve", cvs, prs.t[:, :, 0:4], cw.t[:, ch, 0:1], None, ALU.mult, None, [prs.r, cw.r], [cv.r])
                    for k in range(1, 4):
                        s.stt("dve", cvs, prs.t[:, :, k:k + 4], cw.t[:, ch, k:k + 1], cvs, ALU.mult, ALU.add,
                              [prs.r, cw.r, cv.r], [cv.r])
                    s.cp("pool", Tl.t[:, ch, 0:3], pre.t[:, SEQ:SEQ + 3], [pre.r], [Tl.r])
                    s.cp("pool", Tl.t[:, ch, 3:15].rearrange("p (b r) -> p b r", b=4), prs.t[:, :, 4:7], [prs.r], [Tl.r])
                    s.act(cv.t[:, :], cv.t[:, :], AF.Silu, [cv.r], [cv.r])
                    ob = ob_p[ch % 2]
                    if ch < 32:
                        sq = sq_p[ch % 2]
                        s.act(sq.t[:, :], cv.t[:, :], AF.Square, [cv.r], [sq.r])
                        for (t0, tn) in TBS:
                            ps = s.ps()
                            s.mm(ps.t[:, 0:tn], onesb.t[:, :], sq.t[:, t0:t0 + tn], True, True, [onesb.r, sq.r], [ps.r])
                            g1 = s.nxt(s.ev4, "ev4i")
                            s.ts("dve", g1.t[:, 0:tn], ps.t[:, 0:tn], 1e-6, None, ALU.add, None, [ps.r], [g1.r])
                            s.act(g1.t[:, 0:tn], g1.t[:, 0:tn], AF.Sqrt, [g1.r], [g1.r])
                            s.op("dve", lambda g, g1=g1, tn=tn: g.reciprocal(g1.t[:, 0:tn], g1.t[:, 0:tn]), [g1.r], [g1.r])
                            sc = (128 ** -0.5) if ch < 16 else 1.0
                            s.stt("dve", ob.t[:, t0:t0 + tn], cv.t[:, t0:t0 + tn], sc, g1.t[:, 0:tn], ALU.mult, ALU.mult,
                                  [cv.r, g1.r], [ob.r])
                        dst = s.QN if ch < 16 else s.KN
                        s.dma("sp", dst.t.ap()[ch % 16, :, :], ob.t[:, :], [ob.r], [dst.r])
                    else:
                        s.cp("pool", ob.t[:, :], cv.t[:, :], [cv.r], [ob.r])
                        s.dma("sp", s.VT.t.ap()[ch - 32, :, :], ob.t[:, :], [ob.r], [s.VT.r])
            elif blk < 48:
                zb = blk - 32
                for (t0, tn) in TBS:
                    st = zst[qi % 2]
                    qi += 1
                    for j in range(2):
                        ps = s.ps()
                        s.ws_mm(ps.t[:, 0:tn], ps.r, wt, (j * 128, j * 128 + 128), xT, KC, t0, tn, [xT.r])
                        s.act(st.t[:, j, 0:tn], ps.t[:, 0:tn], AF.Silu, [ps.r], [st.r])
                    s.dma("sp", ZTap[:, zb * 2:zb * 2 + 2, t0:t0 + tn], st.t[:, :, 0:tn], [st.r], [s.ZT.r])
            else:
                for i, (t0, tn) in enumerate(TILES):
                    ps = s.ps()
                    s.as_mm(ps.t[0:tn, 0:64], ps.r, wt, (0, 64), xT, KC, t0, tn, [xT.r])
                    g1 = s.nxt(s.ev4, "ev4i")
                    s.tt("dve", g1.t[0:tn, 0:32], ps.t[0:tn, 0:32], ab.t[0:tn, 32:64], ALU.add, [ps.r, ab.r], [g1.r])
                    s.act(g1.t[0:tn, 0:32], g1.t[0:tn, 0:32], AF.Exp, [g1.r], [g1.r])
                    s.act(g1.t[0:tn, 0:32], g1.t[0:tn, 0:32], AF.Ln, [g1.r], [g1.r], bias=1.0, scale=1.0)
                    s.stt("dve", GB.t[0:tn, i, 0:32], g1.t[0:tn, 0:32], -1.0, ab.t[0:tn, 64:96], ALU.mult, ALU.mult,
                          [g1.r, ab.r], [GB.r])
                    s.act(GB.t[0:tn, i, 32:64], ps.t[0:tn, 32:64], AF.Sigmoid, [ps.r], [GB.r, ps.r])
        s.dma("sp", s.GBd.t.ap()[:, :, :], GB.t[:, :, :], [GB.r], [s.GBd.r])
        gcap = s.out["gc"].t.ap()
        rt_p = s.pool([15, 512], F32, 2)
        for g4 in range(16):
            ps = s.ps()
            for j in range(4):
                ch = g4 * 4 + j
                s.tp(ps.t[0:15, j * 128:(j + 1) * 128], Tl.t[:, ch, :], s.idf.t[:, :], [Tl.r, s.idf.r], [ps.r], inc=(j == 3))
            rt = rt_p[g4 % 2]
            s.cp("act", rt.t[:, :], ps.t[0:15, :], [ps.r], [rt.r])
            s.dma("sp", gcap[:, g4 * 512:(g4 + 1) * 512], rt.t[:, :], [rt.r], [s.out["gc"].r])

    def gdn_core(self):
        s = self
        mp = s.sb([128, 770], F32, name="g_mp")
        ms = s.sb([16, 580], F32, name="g_ms")
        s.dma("sp", mp.t[:, :], s.inp["gdn_mp"].t.ap()[:, :], [], [mp.r])
        s.dma("sp", ms.t[:, :], s.inp["gdn_ms"].t.ap()[:, :], [], [ms.r])
        nw = s.sb([128, 128], F32, name="g_nw")
        s.dma("sp", nw.t[:, :], s.inp["gdn_nw"].t.ap()[:, :], [], [nw.r])
        ones = s.sb([128, 128], F32, name="g_ones32")
        s.memset("dve", ones.t[:, :], 1.0, [ones.r])
        GB = s.sb([128, NTILE, 64], F32, name="g_GB2")
        s.dma("sp", GB.t[:, :, :], s.GBd.t.ap()[:, :, :], [s.GBd.r], [GB.r])
        GX = s.sb([128, NTILE, 3, 32], F32, name="g_GX")
        EC = s.sb([128, NTILE, 4, 32], F32, name="g_EC")

        def geom(i):
            if i < 16:
                M = mp.t
                return dict(np=128, ncs=2, triU=M[:, 0:128], bones=M[:, 128:256], Mb=M[:, 256:384], strict=M[:, 384:512],
                            sel=[M[:, 512:640], M[:, 640:768]], cm=M[:, 768:770], mr=mp.r, nlev=5)
            M = ms.t
            return dict(np=16, ncs=4, triU=M[:, 0:16], bones=M[:, 16:32], Mb=M[:, 32:48], strict=M[:, 48:64],
                        sel=[M[:, 64 + c * 128:64 + (c + 1) * 128] for c in range(4)], cm=M[:, 576:580], mr=ms.r, nlev=1)

        for i in range(NTILE):
            G = geom(i)
            n = G["np"]
            g_ap = GB.t[0:n, i, 0:32]
            ps = s.ps()
            s.mm(ps.t[0:n, 0:32], G["triU"], g_ap, True, True, [G["mr"], GB.r], [ps.r])
            s.mm(ps.t[0:n, 32:64], G["bones"], g_ap, True, True, [G["mr"], GB.r], [ps.r])
            s.cp("dve", GX.t[0:n, i, 0, :], ps.t[0:n, 0:32], [ps.r], [GX.r])
            s.act(GX.t[0:n, i, 1, :], ps.t[0:n, 0:32], AF.Exp, [ps.r], [GX.r, ps.r])
            s.tt("dve", GX.t[0:n, i, 2, :], ps.t[0:n, 32:64], GX.t[0:n, i, 0, :], ALU.subtract, [ps.r, GX.r], [GX.r, ps.r])
            s.act(GX.t[0:n, i, 2, :], GX.t[0:n, i, 2, :], AF.Exp, [GX.r], [GX.r])
            ps2 = s.ps()
            for c in range(G["ncs"]):
                s.mm(ps2.t[:, c * 32:(c + 1) * 32], G["sel"][c], g_ap, True, True, [G["mr"], GB.r], [ps2.r],
                     inc=(c == G["ncs"] - 1))
            s.act(EC.t[:, i, 0:G["ncs"], :], ps2.t[:, 0:32 * G["ncs"]].rearrange("p (c h) -> p c h", h=32), AF.Exp,
                  [ps2.r], [EC.r])
        kq_p = s.pool([128, 2, TT], BF16, 2)
        v_p = s.pool([128, TT], BF16, 2)
        z_p = s.pool([128, TT], BF16, 2)
        gs_p = s.pool([128, TT], BF16, 2)
        W = {}
        for nm in ("Gtri", "D", "Ds", "A", "At", "RT", "B0", "Bt0", "B1", "Bt1", "Kbg", "bV", "u", "wT", "P", "PT", "vn",
                   "oa", "qf", "o2", "kd0", "kd1", "kd2", "kd3", "on"):
            W[nm] = s.pool([128, 128], F32, 2)
        colp = s.pool([128, 16], F32, 2)
        Sp = s.sb([128, 128], F32, name="g_S")
        S0 = [s.sb([128, 128], F32, name="g_S0") for _ in range(4)]
        QNap, KNap, VTap, ZTap, Gap = s.QN.t.ap(), s.KN.t.ap(), s.VT.t.ap(), s.ZT.t.ap(), s.GTd.t.ap()
        stap = s.inp["gdn_state"].t.ap()
        it = 0
        GSTEP = int(os.environ.get("G_STEP", "99"))
        for hq in range(int(os.environ.get("G_NHQ", "16"))):
            kq = kq_p[hq % 2]
            s.dma("sp", kq.t[:, 0, :], KNap[hq, :, :], [s.KN.r], [kq.r])
            s.dma("sp", kq.t[:, 1, :], QNap[hq, :, :], [s.QN.r], [kq.r])
            for h in (2 * hq, 2 * hq + 1):
                vT = v_p[h % 2]
                zT = z_p[h % 2]
                gs = gs_p[h % 2]
                s.dma("sp", vT.t[:, :], VTap[h, :, :], [s.VT.r], [vT.r])
                s.dma("sp", zT.t[:, :], ZTap[:, h, :], [s.ZT.r], [zT.r])
                s.memset("pool", Sp.t[:, :], 0.0, [Sp.r])
                for b in range(4):
                    s.dma("sp", S0[b].t[:, :], stap[b, h, :, :], [], [S0[b].r])
                for i in range(NTILE):
                    if i >= int(os.environ.get("G_NT", "17")) and i < 16:
                        continue
                    if i == 16 and os.environ.get("G_NOS"):
                        continue
                    G = geom(i)
                    n = G["np"]
                    t0 = TILES[i][0]
                    w = {k: v[it % 2] for k, v in W.items()}
                    col = colp[it % 2]
                    it += 1
                    gcol = GB.t[0:n, i, h:h + 1]
                    bcol = GB.t[0:n, i, 32 + h:33 + h]
                    gam = GX.t[0:n, i, 0, h:h + 1]
                    eg = GX.t[0:n, i, 1, h:h + 1]
                    edk = GX.t[0:n, i, 2, h:h + 1]
                    mr = G["mr"]
                    kT = kq.t[:, 0, t0:t0 + n]
                    qT = kq.t[:, 1, t0:t0 + n]
                    s.ts("dve", w["Gtri"].t[0:n, 0:n], G["triU"], gcol, None, ALU.mult, None, [mr, GB.r], [w["Gtri"].r])
                    ps = s.ps()
                    s.mm(ps.t[0:n, 0:n], ones.t[0:n, 0:n], w["Gtri"].t[0:n, 0:n], True, False, [ones.r, w["Gtri"].r], [ps.r],
                         inc=False)
                    s.mm(ps.t[0:n, 0:n], s.idf.t[0:n, 0:n], G["Mb"], False, True, [s.idf.r, mr], [ps.r])
                    s.act(w["D"].t[0:n, 0:n], ps.t[0:n, 0:n], AF.Exp, [ps.r, GX.r], [w["D"].r], bias=gam, scale=-1.0)
                    if GSTEP <= 1:
                        continue
                    psk = s.ps()
                    s.mm(psk.t[0:n, 0:n], kT, kT, True, True, [kq.r], [psk.r])
                    s.tt("dve", w["Ds"].t[0:n, 0:n], w["D"].t[0:n, 0:n], G["strict"], ALU.mult, [w["D"].r, mr], [w["Ds"].r])
                    s.stt("dve", w["A"].t[0:n, 0:n], psk.t[0:n, 0:n], bcol, w["Ds"].t[0:n, 0:n], ALU.mult, ALU.mult,
                          [psk.r, GB.r, w["Ds"].r], [w["A"].r])
                    if GSTEP <= 2:
                        continue
                    pst = s.ps()
                    s.tp(pst.t[0:n, 0:n], w["A"].t[0:n, 0:n], s.idf.t[0:n, 0:n], [w["A"].r, s.idf.r], [pst.r])
                    s.cp("act", w["At"].t[0:n, 0:n], pst.t[0:n, 0:n], [pst.r], [w["At"].r])
                    s.tt("dve", w["RT"].t[0:n, 0:n], s.idf.t[0:n, 0:n], pst.t[0:n, 0:n], ALU.subtract, [pst.r, s.idf.r],
                         [w["RT"].r, pst.r])
                    if GSTEP <= 3:
                        continue
                    Bc, Btc = w["A"], w["At"]
                    for lev in range(1, G["nlev"] + 1):
                        Bn_ = w["B%d" % (lev % 2)]
                        Btn = w["Bt%d" % (lev % 2)]
                        p1 = s.ps()
                        s.mm(p1.t[0:n, 0:n], Btc.t[0:n, 0:n], Bc.t[0:n, 0:n], True, True, [Btc.r, Bc.r], [p1.r])
                        s.cp("act", Bn_.t[0:n, 0:n], p1.t[0:n, 0:n], [p1.r], [Bn_.r])
                        if lev < G["nlev"]:
                            p2 = s.ps()
                            s.mm(p2.t[0:n, 0:n], Bc.t[0:n, 0:n], Btc.t[0:n, 0:n], True, True, [Btc.r, Bc.r], [p2.r])
                            s.cp("pool" if False else "act", Btn.t[0:n, 0:n], p2.t[0:n, 0:n], [p2.r], [Btn.r])
                        p3 = s.ps()
                        s.mm(p3.t[0:n, 0:n], Bn_.t[0:n, 0:n], w["RT"].t[0:n, 0:n], True, True, [Bn_.r, w["RT"].r], [p3.r])
                        s.tt("dve", w["RT"].t[0:n, 0:n], w["RT"].t[0:n, 0:n], p3.t[0:n, 0:n], ALU.add, [p3.r, w["RT"].r],
                             [w["RT"].r])
                        Bc, Btc = Bn_, Btn
                    if GSTEP <= 4:
                        continue
                    s.tt("dve", col.t[0:n, 0:1], bcol, eg, ALU.mult, [GB.r, GX.r], [col.r])
                    for c in range(G["ncs"]):
                        s.tt("dve", col.t[0:n, 1 + c:2 + c], edk, G["cm"][:, c:c + 1], ALU.mult, [GX.r, mr], [col.r])
                        s.tt("dve", col.t[0:n, 5 + c:6 + c], eg, G["cm"][:, c:c + 1], ALU.mult, [GX.r, mr], [col.r])
                        s.ts("dve", col.t[0:n, 9 + c:10 + c], G["cm"][:, c:c + 1], -1.0, None, ALU.mult, None, [mr], [col.r])
                    pkt = s.ps()
                    pktb = pkt.t[:, :].bitcast(BF16)
                    s.tp(pktb[0:n, 0:128], kT, s.idb.t[:, :], [kq.r, s.idb.r], [pkt.r], inc=False)
                    s.tp(pktb[0:n, 128:256], vT.t[:, t0:t0 + n], s.idb.t[:, :], [vT.r, s.idb.r], [pkt.r])
                    s.ts("dve", w["Kbg"].t[0:n, :], pktb[0:n, 0:128], col.t[0:n, 0:1], None, ALU.mult, None, [pkt.r, col.r],
                         [w["Kbg"].r])
                    s.ts("dve", w["bV"].t[0:n, :], pktb[0:n, 128:256], bcol, None, ALU.mult, None, [pkt.r, GB.r],
                         [w["bV"].r, pkt.r])
                    for c in range(G["ncs"]):
                        s.ts("dve", w["kd%d" % c].t[0:n, :], pktb[0:n, 0:128], col.t[0:n, 1 + c:2 + c], None, ALU.mult, None,
                             [pkt.r, col.r], [w["kd%d" % c].r, pkt.r])
                    if GSTEP <= 5:
                        continue
                    pu = s.ps()
                    s.mm(pu.t[0:n, 0:128], w["RT"].t[0:n, 0:n], w["bV"].t[0:n, :], True, True, [w["RT"].r, w["bV"].r], [pu.r])
                    s.cp("act", w["vn"].t[0:n, :], pu.t[0:n, 0:128], [pu.r], [w["vn"].r])
                    pw = s.ps()
                    s.mm(pw.t[:, 0:n], w["Kbg"].t[0:n, :], w["RT"].t[0:n, 0:n], True, True, [w["RT"].r, w["Kbg"].r], [pw.r])
                    s.cp("act", w["wT"].t[:, 0:n], pw.t[:, 0:n], [pw.r], [w["wT"].r])
                    if GSTEP <= 6:
                        continue
                    pq = s.ps()
                    s.mm(pq.t[0:n, 0:n], qT, kT, True, True, [kq.r], [pq.r])
                    s.tt("dve", w["P"].t[0:n, 0:n], pq.t[0:n, 0:n], w["D"].t[0:n, 0:n], ALU.mult, [pq.r, w["D"].r], [w["P"].r])
                    ppt = s.ps()
                    s.tp(ppt.t[0:n, 0:n], w["P"].t[0:n, 0:n], s.idf.t[0:n, 0:n], [w["P"].r, s.idf.r], [ppt.r])
                    s.cp("act", w["PT"].t[0:n, 0:n], ppt.t[0:n, 0:n], [ppt.r], [w["PT"].r])
                    s.cp("act", w["qf"].t[:, 0:n], qT, [kq.r], [w["qf"].r])
                    if GSTEP <= 7:
                        continue
                    for c in range(G["ncs"]):
                        Sc = Sp if i < 16 else S0[c]
                        p1 = s.ps()
                        s.mm(p1.t[0:n, 0:128], w["wT"].t[:, 0:n], Sc.t[:, :], True, True, [w["wT"].r, Sc.r], [p1.r])
                        s.stt("dve", w["vn"].t[0:n, :], p1.t[0:n, 0:128], col.t[0:n, 9 + c:10 + c], w["vn"].t[0:n, :],
                              ALU.mult, ALU.add, [p1.r, col.r, w["vn"].r], [w["vn"].r])
                        p2 = s.ps()
                        s.mm(p2.t[0:n, 0:128], w["qf"].t[:, 0:n], Sc.t[:, :], True, True, [w["qf"].r, Sc.r], [p2.r])
                        if c == 0:
                            s.ts("dve", w["oa"].t[0:n, :], p2.t[0:n, 0:128], col.t[0:n, 5:6], None, ALU.mult, None,
                                 [p2.r, col.r], [w["oa"].r])
                        else:
                            s.stt("dve", w["oa"].t[0:n, :], p2.t[0:n, 0:128], col.t[0:n, 5 + c:6 + c], w["oa"].t[0:n, :],
                                  ALU.mult, ALU.add, [p2.r, col.r, w["oa"].r], [w["oa"].r])
                        p4 = s.ps()
                        s.mm(p4.t[:, 0:128], w["kd%d" % c].t[0:n, :], w["vn"].t[0:n, :], True, True,
                             [w["kd%d" % c].r, w["vn"].r], [p4.r])
                        s.stt("dve", Sc.t[:, :], Sc.t[:, :], EC.t[:, i, c, h:h + 1], p4.t[:, 0:128], ALU.mult, ALU.add,
                              [Sc.r, EC.r, p4.r], [Sc.r])
                    p5 = s.ps()
                    s.mm(p5.t[0:n, 0:128], w["PT"].t[0:n, 0:n], w["vn"].t[0:n, :], True, True, [w["PT"].r, w["vn"].r], [p5.r])
                    s.tt("dve", w["oa"].t[0:n, :], w["oa"].t[0:n, :], p5.t[0:n, 0:128], ALU.add, [p5.r, w["oa"].r], [w["oa"].r])
                    if GSTEP <= 8:
                        continue
                    s.act(w["o2"].t[0:n, :], w["oa"].t[0:n, :], AF.Square, [w["oa"].r], [w["o2"].r, col.r],
                          accum_out=col.t[0:n, 13:14])
                    s.ts("dve", col.t[0:n, 14:15], col.t[0:n, 13:14], 1.0 / 128, 1e-6, ALU.mult, ALU.add, [col.r], [col.r])
                    s.act(col.t[0:n, 14:15], col.t[0:n, 14:15], AF.Sqrt, [col.r], [col.r])
                    s.op("dve", lambda g, col=col, n=n: g.reciprocal(col.t[0:n, 15:16], col.t[0:n, 14:15]), [col.r], [col.r])
                    s.stt("dve", w["on"].t[0:n, :], w["oa"].t[0:n, :], col.t[0:n, 15:16], nw.t[0:n, :], ALU.mult, ALU.mult,
                          [w["oa"].r, col.r, nw.r], [w["on"].r])
                    pf = s.ps()
                    s.tp(pf.t[:, 0:n], w["on"].t[0:n, :], s.idf.t[0:n, 0:n], [w["on"].r, s.idf.r], [pf.r])
                    s.tt("dve", gs.t[:, t0:t0 + n], pf.t[:, 0:n], zT.t[:, t0:t0 + n], ALU.mult, [pf.r, zT.r], [gs.r])
                s.dma("sp", Gap[:, h, :], gs.t[:, :], [gs.r], [s.GTd.r])
                s.dma("sp", s.out["gd_p"].t.ap()[h, :, :], Sp.t[:, :], [Sp.r], [s.out["gd_p"].r])
                for b in range(4):
                    s.dma("sp", s.out["gd_s"].t.ap()[b, h, :, :], S0[b].t[:, :], [S0[b].r], [s.out["gd_s"].r])

    def setup_dsa(self):
        s = self
        s.din("dsa_w_in", [D, 7312])
        s.din("dsa_w_out", [D, D])
        s.din("d_bias_p", [32, 128, 2048])
        s.din("d_causal", [128, 128])
        s.dout("d_kv", [TT, 1024])
        s.dout("d_ki", [TT, 128])
        s.KT = s.dram("KT", [64, 8, TT], BF16)
        s.VK = s.dram("VK", [128, NTILE, 512], BF16)
        s.KIT = s.dram("KIT", [128, TT], BF16)
        s.WI = s.dram("WI", [128, NTILE, 16], F32)
        s.MK = s.dram("MK", [16, 128, 2048], BF16)

    def dsa_proj(self, xT):
        s = self
        W = s.inp["dsa_w_in"].t.ap()
        KC = list(range(16))
        qst = s.pool([64, 4, 512], BF16, 2)
        zst = s.pool([128, 2, 512], BF16, 2)
        kst = s.pool([64, 4, 512], BF16, 2)
        kvs = s.pool([128, NTILE, 256], F32, 2)
        vbs = s.pool([128, NTILE, 256], BF16, 2)
        QTap, ZTap, KTap, VKap, QIap = s.QT.t.ap(), s.ZT.t.ap(), s.KT.t.ap(), s.VK.t.ap(), s.QN.t.ap()
        okv = s.out["d_kv"].t.ap()
        oki = s.out["d_ki"].t.ap()
        qi = 0
        for blk in range(29):
            wt = s.nxt(s.wp, "wi")
            ncol = 256 if blk < 28 else 144
            s.load_w(wt, W, D, blk * 256, ncol)
            if blk < 8:
                for (t0, tn) in TBS:
                    st = qst[qi % 2]
                    qi += 1
                    for j in range(4):
                        ps = s.ps()
                        s.ws_mm(ps.t[0:64, 0:tn], ps.r, wt, (j * 64, j * 64 + 64), xT, KC, t0, tn, [xT.r])
                        if j % 2:
                            s.act(st.t[:, j, 0:tn], ps.t[0:64, 0:tn], AF.Copy, [ps.r], [st.r], scale=0.125)
                        else:
                            s.ts("dve", st.t[:, j, 0:tn], ps.t[0:64, 0:tn], 0.125, None, ALU.mult, None, [ps.r], [st.r])
                    s.dma("sp", QTap[:, blk * 4:blk * 4 + 4, t0:t0 + tn], st.t[:, :, 0:tn], [st.r], [s.QT.r])
            elif blk < 12:
                kb = blk - 8
                kv = kvs[blk % 2]
                for i, (t0, tn) in enumerate(TILES):
                    ps = s.ps()
                    s.as_mm(ps.t[0:tn, 0:256], ps.r, wt, (0, 256), xT, KC, t0, tn, [xT.r])
                    s.cp(s.ev_eng(), kv.t[0:tn, i, :], ps.t[0:tn, 0:256], [ps.r], [kv.r])
                s.dma("sp", okv[0:2048, kb * 256:(kb + 1) * 256].rearrange("(i p) n -> p i n", p=128), kv.t[:, 0:16, :],
                      [kv.r], [s.out["d_kv"].r])
                s.dma("sp", okv[2048:TT, kb * 256:(kb + 1) * 256], kv.t[0:NS, 16, :], [kv.r], [s.out["d_kv"].r])
                if blk < 10:
                    for (t0, tn) in TBS:
                        st = kst[qi % 2]
                        qi += 1
                        for j in range(4):
                            ps = s.ps()
                            s.ws_mm(ps.t[0:64, 0:tn], ps.r, wt, (j * 64, j * 64 + 64), xT, KC, t0, tn, [xT.r])
                            s.cp(s.ev_eng(), st.t[:, j, 0:tn], ps.t[0:64, 0:tn], [ps.r], [st.r])
                        s.dma("sp", KTap[:, kb * 4:kb * 4 + 4, t0:t0 + tn], st.t[:, :, 0:tn], [st.r], [s.KT.r])
                else:
                    vb = vbs[blk % 2]
                    s.cp("pool", vb.t[:, 0:16, :], kv.t[:, 0:16, :], [kv.r], [vb.r])
                    s.cp("pool", vb.t[0:NS, 16, :], kv.t[0:NS, 16, :], [kv.r], [vb.r])
                    vo = (blk - 10) * 256
                    s.dma("sp", VKap[:, 0:16, vo:vo + 256], vb.t[:, 0:16, :], [vb.r], [s.VK.r])
                    s.dma("sp", VKap[0:NS, 16, vo:vo + 256], vb.t[0:NS, 16, :], [vb.r], [s.VK.r])
            elif blk < 20:
                zb = blk - 12
                for (t0, tn) in TBS:
                    st = zst[qi % 2]
                    qi += 1
                    for j in range(2):
                        ps = s.ps()
                        s.ws_mm(ps.t[:, 0:tn], ps.r, wt, (j * 128, j * 128 + 128), xT, KC, t0, tn, [xT.r])
                        s.act(st.t[:, j, 0:tn], ps.t[:, 0:tn], AF.Silu, [ps.r], [st.r])
                    s.dma("sp", ZTap[:, zb * 2:zb * 2 + 2, t0:t0 + tn], st.t[:, :, 0:tn], [st.r], [s.ZT.r])
            elif blk < 28:
                ib = blk - 20
                for (t0, tn) in TBS:
                    st = zst[qi % 2]
                    qi += 1
                    for j in range(2):
                        ps = s.ps()
                        s.ws_mm(ps.t[:, 0:tn], ps.r, wt, (j * 128, j * 128 + 128), xT, KC, t0, tn, [xT.r])
                        s.cp(s.ev_eng(), st.t[:, j, 0:tn], ps.t[:, 0:tn], [ps.r], [st.r])
                    for j in range(2):
                        s.dma("sp", QIap[ib * 2 + j, :, t0:t0 + tn], st.t[:, j, 0:tn], [st.r], [s.QN.r])
            else:
                for (t0, tn) in TBS:
                    st = zst[qi % 2]
                    qi += 1
                    ps = s.ps()
                    s.ws_mm(ps.t[:, 0:tn], ps.r, wt, (0, 128), xT, KC, t0, tn, [xT.r])
                    s.cp("act", st.t[:, 0, 0:tn], ps.t[:, 0:tn], [ps.r], [st.r])
                    s.dma("sp", s.KIT.t.ap()[:, t0:t0 + tn], st.t[:, 0, 0:tn], [st.r], [s.KIT.r])
                kv = kvs[blk % 2]
                for i, (t0, tn) in enumerate(TILES):
                    ps = s.ps()
                    s.as_mm(ps.t[0:tn, 0:144], ps.r, wt, (0, 144), xT, KC, t0, tn, [xT.r])
                    s.cp("dve", kv.t[0:tn, i, 0:144], ps.t[0:tn, 0:144], [ps.r], [kv.r])
                s.dma("sp", oki[0:2048, :].rearrange("(i p) n -> p i n", p=128), kv.t[:, 0:16, 0:128], [kv.r],
                      [s.out["d_ki"].r])
                s.dma("sp", oki[2048:TT, :], kv.t[0:NS, 16, 0:128], [kv.r], [s.out["d_ki"].r])
                s.dma("sp", s.WI.t.ap()[:, 0:16, :], kv.t[:, 0:16, 128:144], [kv.r], [s.WI.r])
                s.dma("sp", s.WI.t.ap()[0:NS, 16, :], kv.t[0:NS, 16, 128:144], [kv.r], [s.WI.r])

    def dsa_prompt(self):
        s = self
        QIap, MKap, QTap, ZTap, Gap = s.QN.t.ap(), s.MK.t.ap(), s.QT.t.ap(), s.ZT.t.ap(), s.GTd.t.ap()
        kiT = s.sb([128, TT], BF16, name="d_kiT")
        s.dma("sp", kiT.t[:, :], s.KIT.t.ap()[:, :], [s.KIT.r], [kiT.r])
        wi = s.sb([128, NTILE, 16], F32, name="d_wi")
        s.dma("sp", wi.t[:, :, :], s.WI.t.ap()[:, :, :], [s.WI.r], [wi.r])
        s.ts("dve", wi.t[:, :, :], wi.t[:, :, :], (128 ** -0.5) * 0.25, None, ALU.mult, None, [wi.r], [wi.r])
        cneg = s.sb([128, 128], F32, name="d_cneg")
        s.dma("sp", cneg.t[:, :], s.inp["d_causal"].t.ap()[:, :], [], [cneg.r])
        with s.scope():
            qib_p = s.pool([128, 16, 128], BF16, 2)
            I_p = s.pool([128, 2048], F32, 2)
            R_p = s.pool([128, 512], F32, 3)
            jk = s.sb([128, 2048], BF16, name="d_jk")
            mk_p = s.pool([128, 2048], BF16, 2)
            cl = s.pool([128, 8], F32, 2)
            for n in range(16):
                t0 = n * 128
                nk = t0 + 128
                qib = qib_p[n % 2]
                s.dma("sp", qib.t[:, :, :], QIap[:, :, t0:t0 + 128].rearrange("h p t -> p h t"), [s.QN.r], [qib.r])
                I = I_p[n % 2]
                for h in range(16):
                    for c0 in range(0, nk, 512):
                        cn = min(512, nk - c0)
                        ps = s.ps()
                        s.mm(ps.t[:, 0:cn], qib.t[:, h, :], kiT.t[:, c0:c0 + cn], True, True, [qib.r, kiT.r], [ps.r])
                        R = s.nxt(R_p, "dri")
                        s.act(R.t[:, 0:cn], ps.t[:, 0:cn], AF.Relu, [ps.r], [R.r])
                        if h == 0:
                            s.ts("dve", I.t[:, c0:c0 + cn], R.t[:, 0:cn], wi.t[:, n, 0:1], None, ALU.mult, None,
                                 [R.r, wi.r], [I.r])
                        else:
                            s.stt("dve", I.t[:, c0:c0 + cn], R.t[:, 0:cn], wi.t[:, n, h:h + 1], I.t[:, c0:c0 + cn],
                                  ALU.mult, ALU.add, [R.r, wi.r, I.r], [I.r])
                s.tt("dve", I.t[:, t0:nk], I.t[:, t0:nk], cneg.t[:, :], ALU.add, [I.r, cneg.r], [I.r])
                c = cl[n % 2]
                s.memset("dve", c.t[:, 0:1], -256.0, [c.r])
                for itn in range(32):
                    wdt = 256.0 * (0.5 ** itn)
                    s.ts("dve", jk.t[:, 0:nk], I.t[:, 0:nk], c.t[:, 0:1], wdt, ALU.subtract, ALU.is_ge, [I.r, c.r], [jk.r])
                    s.red("dve", c.t[:, 1:2], jk.t[:, 0:nk], ALU.add, [jk.r], [c.r])
                    s.ts("dve", c.t[:, 2:3], c.t[:, 1:2], 256.0, wdt, ALU.is_ge, ALU.mult, [c.r], [c.r])
                    s.tt("dve", c.t[:, 0:1], c.t[:, 0:1], c.t[:, 2:3], ALU.add, [c.r], [c.r])
                mk = mk_p[n % 2]
                s.ts("dve", mk.t[:, 0:nk], I.t[:, 0:nk], c.t[:, 0:1], NEG, ALU.is_lt, ALU.mult, [I.r, c.r], [mk.r])
                s.dma("sp", MKap[n, :, 0:nk], mk.t[:, 0:nk], [mk.r], [s.MK.r])
        with s.scope():
            kT = s.sb([64, 8, SEQ], BF16, name="d_kT")
            s.dma("sp", kT.t[:, :, :], s.KT.t.ap()[:, :, 0:SEQ], [s.KT.r], [kT.r])
            vtok = s.sb([128, 16, 512], BF16, name="d_vtok")
            s.dma("sp", vtok.t[:, :, :], s.VK.t.ap()[:, 0:16, :], [s.VK.r], [vtok.r])
            mk_p = s.pool([128, 2048], BF16, 2)
            tb_p = s.pool([128, 2048], BF16, 3)
            qb_p = s.pool([64, 32, 128], BF16, 2)
            zb_p = s.pool([128, 16, 128], BF16, 2)
            gs_p = s.pool([128, 16, 128], BF16, 2)
            S_p = s.pool([128, 2048], F32, 2)
            E_p = s.pool([128, 2048], BF16, 2)
            P_p = s.pool([128, 2048], BF16, 2)
            PT_p = s.pool([128, 16, 128], BF16, 2)
            smp = s.pool([128, 8], F32, 4)
            Tbap = s.inp["d_bias_p"].t.ap()
            it = 0
            for n in range(16):
                t0 = n * 128
                nk = t0 + 128
                nkt = nk // 128
                mk = mk_p[n % 2]
                qb = qb_p[n % 2]
                zb = zb_p[n % 2]
                gs = gs_p[n % 2]
                s.dma("sp", mk.t[:, 0:nk], MKap[n, :, 0:nk], [s.MK.r], [mk.r])
                s.dma("sp", qb.t[:, :, :], QTap[:, :, t0:t0 + 128], [s.QT.r], [qb.r])
                s.dma("sp", zb.t[:, :, :], ZTap[:, 0:16, t0:t0 + 128], [s.ZT.r], [zb.r])
                po = None
                for h in range(32):
                    kvh = h // 4
                    tb = s.nxt(tb_p, "dtbi")
                    s.dma("pool", tb.t[:, 0:nk], Tbap[h, :, 1920 - t0:2048], [], [tb.r])
                    Sb = S_p[it % 2]
                    E = E_p[it % 2]
                    P = P_p[it % 2]
                    PT = PT_p[it % 2]
                    sm = smp[it % 4]
                    it += 1
                    for c0 in range(0, nk, 512):
                        cn = min(512, nk - c0)
                        ps = s.ps()
                        s.mm(ps.t[:, 0:cn], s.idb.t[:, :], tb.t[:, c0:c0 + cn], True, False, [s.idb.r, tb.r], [ps.r], inc=False)
                        s.mm(ps.t[:, 0:cn], s.idb.t[:, :], mk.t[:, c0:c0 + cn], False, False, [s.idb.r, mk.r], [ps.r],
                             inc=False)
                        s.mm(ps.t[:, 0:cn], qb.t[:, h, :], kT.t[:, kvh, c0:c0 + cn], False, True, [qb.r, kT.r], [ps.r])
                        s.cp("act", Sb.t[:, c0:c0 + cn], ps.t[:, 0:cn], [ps.r], [Sb.r])
                    s.red("dve", sm.t[:, 0:1], Sb.t[:, 0:nk], ALU.max, [Sb.r], [sm.r])
                    s.ts("dve", sm.t[:, 1:2], sm.t[:, 0:1], -1.0, None, ALU.mult, None, [sm.r], [sm.r])
                    s.act(E.t[:, 0:nk], Sb.t[:, 0:nk], AF.Exp, [Sb.r, sm.r], [E.r, sm.r], bias=sm.t[:, 1:2], scale=1.0,
                          accum_out=sm.t[:, 2:3])
                    s.op("dve", lambda g, sm=sm: g.reciprocal(sm.t[:, 3:4], sm.t[:, 2:3]), [sm.r], [sm.r])
                    s.ts("dve", P.t[:, 0:nk], E.t[:, 0:nk], sm.t[:, 3:4], None, ALU.mult, None, [E.r, sm.r], [P.r])
                    for g0 in range(0, nkt, 8):
                        gn = min(8, nkt - g0)
                        pst = s.ps()
                        pstb = pst.t[:, :].bitcast(BF16)
                        for kt in range(gn):
                            s.tp(pstb[:, kt * 128:(kt + 1) * 128], P.t[:, (g0 + kt) * 128:(g0 + kt + 1) * 128], s.idb.t[:, :],
                                 [P.r, s.idb.r], [pst.r], inc=(kt == gn - 1))
                        s.cp(s.ev_eng(), PT.t[:, g0:g0 + gn, :], pstb[:, 0:gn * 128].rearrange("p (a b) -> p a b", a=gn),
                             [pst.r], [PT.r])
                    j = h % 2
                    if j == 0:
                        po = s.ps()
                    for kt in range(nkt):
                        s.mm(po.t[j * 64:(j + 1) * 64, 0:128], vtok.t[:, kt, kvh * 64:(kvh + 1) * 64], PT.t[:, kt, :],
                             kt == 0, kt == nkt - 1, [vtok.r, PT.r], [po.r], inc=(j == 1 and kt == nkt - 1))
                    if j == 1:
                        pr = h // 2
                        s.tt("dve", gs.t[:, pr, :], po.t[:, 0:128], zb.t[:, pr, :], ALU.mult, [po.r, zb.r], [gs.r])
                s.dma("sp", Gap[:, 0:16, t0:t0 + 128], gs.t[:, :, :], [gs.r], [s.GTd.r])

    def setup_dsa_sample(self):
        s = self
        s.din("d_kidx_pool", [NPOOL * 128, 128])
        s.din("d_k_pool", [NPOOL * 128, 512])
        s.din("d_v_pool", [NPOOL * 128, 512])
        s.din("pt_loc", [4, 128], I32)
        s.din("d_iota", [128, 1])
        s.din("d_bias_l", [128, 16, 128])
        s.din("d_bias_31", [128, 128])
        s.din("d_bias_n", [4, 128])
        s.din("d_cneg4", [4, 4])
        s.din("d_selm", [128, 8])
        s.din("d_perm", [128, 128])
        s.OS = s.dram("OS", [NS, D], F32)

    def dsa_sample(self):
        s = self
        iot = s.sb([128, 1], F32, name="ds_iota")
        s.dma("sp", iot.t[:, :], s.inp["d_iota"].t.ap()[:, :], [], [iot.r])
        B31 = s.sb([128, 128], F32, name="ds_b31")
        s.dma("sp", B31.t[:, :], s.inp["d_bias_31"].t.ap()[:, :], [], [B31.r])
        Bl = s.sb([128, 16, 128], F32, name="ds_bl")
        s.dma("sp", Bl.t[:, :, :], s.inp["d_bias_l"].t.ap()[:, :, :], [], [Bl.r])
        cs = s.sb([128, 400], F32, name="ds_cs")
        s.dma("sp", cs.t[0:4, 0:128], s.inp["d_bias_n"].t.ap()[:, :], [], [cs.r])
        s.dma("sp", cs.t[0:4, 128:132], s.inp["d_cneg4"].t.ap()[:, :], [], [cs.r])
        s.dma("sp", cs.t[:, 136:144], s.inp["d_selm"].t.ap()[:, :], [], [cs.r])
        s.dma("sp", cs.t[:, 144:272], s.inp["d_perm"].t.ap()[:, :], [], [cs.r])
        ones = s.sb([128, 128], F32, name="ds_ones")
        s.memset("dve", ones.t[:, :], 1.0, [ones.r])
        ptb = s.sb([128, 128], I32, name="ds_ptb")
        ridx = s.sb([128, 3, 128], I32, name="ds_ridx")
        qiTb = s.sb([128, 16, 4], BF16, name="ds_qi")
        wrow = s.sb([1, 64], F32, name="ds_wrow")
        Wb = s.sb([128, 64], F32, name="ds_Wb")
        IT = s.sb([128, 129, 4], F32, name="ds_IT")
        MT = s.sb([128, 129, 4], F32, name="ds_MT")
        cmpt = s.sb([128, 129, 4], F32, name="ds_cmp")
        lo = s.sb([128, 16], F32, name="ds_lo")
        qT2 = s.sb([128, 32, 4], BF16, name="ds_qT2")
        kn2 = s.sb([128, 8, 4], BF16, name="ds_kn2")
        kin = s.sb([128, 4], BF16, name="ds_kin")
        vnew = s.sb([4, 512], BF16, name="ds_vnew")
        LT = s.sb([128, 129, 128], F32, name="ds_LT")
        ET = s.sb([128, 129, 128], BF16, name="ds_ET")
        kip_p = s.pool([128, 128], F32, 4)
        kvp_p = s.pool([128, 512], F32, 3)
        ktb_p = s.pool([128, 4, 128], BF16, 2)
        R_p = s.pool([128, 512], F32, 2)
        vb_p = s.pool([128, 512], BF16, 2)
        w128 = s.pool([128, 128], F32, 4)
        osl = s.sb([128, 64], F32, name="ds_osl")
        KIpool = s.inp["d_kidx_pool"].t.ap()
        Kpool = s.inp["d_k_pool"].t.ap()
        Vpool = s.inp["d_v_pool"].t.ap()
        QIap, QTap, KTap, VKap, WIap = s.QN.t.ap(), s.QT.t.ap(), s.KT.t.ap(), s.VK.t.ap(), s.WI.t.ap()
        OSap = s.OS.t.ap()
        WSC = (128 ** -0.5) * 0.25
        selm = cs.t[:, 136:144]
        perm = cs.t[:, 144:272]
        DSTEP = int(os.environ.get("DS_STEP", "99"))
        for bi in range(int(os.environ.get("DS_NB", "4"))):
            c0 = SEQ + bi * 4
            s.dma("sp", ptb.t[:, :], s.inp["pt_loc"].t.ap()[bi:bi + 1, :].to_broadcast([128, 128]), [], [ptb.r])
            s.ts("dve", ridx.t[:, 0, :], ptb.t[:, :], 128.0, iot.t[:, 0:1], ALU.mult, ALU.add, [ptb.r, iot.r], [ridx.r])
            s.ts("dve", ridx.t[:, 1, :], ridx.t[:, 0, :], 2.0, None, ALU.mult, None, [ridx.r], [ridx.r])
            s.ts("dve", ridx.t[:, 2, :], ridx.t[:, 0, :], 2.0, 1.0, ALU.mult, ALU.add, [ridx.r], [ridx.r])
            s.dma("sp", qiTb.t[:, :, :], QIap[:, :, c0:c0 + 4].rearrange("h p t -> p h t"), [s.QN.r], [qiTb.r])
            s.dma("sp", wrow.t[0:1, :].rearrange("p (t h) -> p t h", t=4),
                  WIap[bi * 4:bi * 4 + 4, 16, :].unsqueeze(0), [s.WI.r], [wrow.r])
            ps = s.ps()
            s.mm(ps.t[:, 0:64], ones.t[0:1, :], wrow.t[0:1, :], True, True, [ones.r, wrow.r], [ps.r])
            s.ts("dve", Wb.t[:, :].rearrange("p (h t) -> p h t", h=16), ps.t[:, 0:64].rearrange("p (t h) -> p h t", t=4),
                 WSC, None, ALU.mult, None, [ps.r], [Wb.r])
            s.memset("pool", qT2.t[:, :, :], 0.0, [qT2.r])
            for kvh in range(8):
                b0 = (kvh % 2) * 64
                s.dma("sp", qT2.t[b0:b0 + 64, kvh * 4:kvh * 4 + 4, :], QTap[:, kvh * 4:kvh * 4 + 4, c0:c0 + 4], [s.QT.r], [qT2.r])
            for par in range(2):
                s.dma("sp", kn2.t[par * 64:(par + 1) * 64, 0:4, :], KTap[:, par:8:2, c0:c0 + 4], [s.KT.r], [kn2.r])
            s.dma("sp", kin.t[:, :], s.KIT.t.ap()[:, c0:c0 + 4], [s.KIT.r], [kin.r])
            s.dma("sp", vnew.t[0:4, :], VKap[bi * 4:bi * 4 + 4, 16, :], [s.VK.r], [vnew.r])
            qif = qiTb.t[:, :, :].rearrange("p h t -> p (h t)")

            def score_tail(psS, nrow, ncols, dst):
                R = s.nxt(R_p, "dsr")
                npg = ncols // 64
                s.act(R.t[0:nrow, 0:ncols], psS.t[0:nrow, 0:ncols], AF.Relu, [psS.r], [R.r])
                Rv = R.t[0:nrow, 0:ncols].rearrange("p (g c) -> p g c", g=npg)
                s.tt("dve", Rv, Rv, Wb.t[0:nrow, :].unsqueeze(1).to_broadcast([nrow, npg, 64]), ALU.mult, [R.r, Wb.r], [R.r])
                s.red("dve", dst, R.t[0:nrow, 0:ncols].rearrange("p (g h t) -> p g t h", g=npg, h=16), ALU.add, [R.r], [IT.r])

            for j0 in range(0, 128, 8):
                psS = s.ps()
                for half in range(2):
                    pst = s.ps()
                    for jj in range(4):
                        j = j0 + half * 4 + jj
                        kip = s.nxt(kip_p, "dskip")
                        s.dma_gather(kip.t[:, :], KIpool, ridx.t[:, 0, j:j + 1], [ridx.r], [kip.r])
                        s.tp(pst.t[:, jj * 128:(jj + 1) * 128], kip.t[:, :], s.idf.t[:, :], [kip.r, s.idf.r], [pst.r],
                             inc=(jj == 3))
                    ktb = s.nxt(ktb_p, "dsktb")
                    s.cp("act", ktb.t[:, :, :], pst.t[:, :].rearrange("p (a b) -> p a b", a=4), [pst.r], [ktb.r])
                    for jj in range(4):
                        cc = (half * 4 + jj) * 64
                        s.mm(psS.t[:, cc:cc + 64], ktb.t[:, jj, :], qif, True, True, [ktb.r, qiTb.r], [psS.r],
                             inc=(half == 1 and jj == 3))
                score_tail(psS, 128, 512, IT.t[:, j0:j0 + 8, :])
            s.memset("pool", IT.t[:, 128, :], -1e30, [IT.r])
            psN = s.ps()
            s.mm(psN.t[0:4, 0:64], kin.t[:, :], qif, True, True, [kin.r, qiTb.r], [psN.r])
            score_tail(psN, 4, 64, IT.t[0:4, 128:129, :])
            s.tt("dve", IT.t[0:4, 128, :], IT.t[0:4, 128, :], cs.t[0:4, 128:132], ALU.add, [IT.r, cs.r], [IT.r])
            if DSTEP <= 1:
                continue
            s.memset("dve", lo.t[:, 0:4], -256.0, [lo.r])
            for itn in range(32):
                wdt = 256.0 * (0.5 ** itn)
                s.ts("dve", lo.t[:, 4:8], lo.t[:, 0:4], wdt, None, ALU.add, None, [lo.r], [lo.r])
                s.tt("dve", cmpt.t[:, :, :], IT.t[:, :, :], lo.t[:, 4:8].unsqueeze(1).to_broadcast([128, 129, 4]), ALU.is_ge,
                     [IT.r, lo.r], [cmpt.r])
                s.red("dve", lo.t[:, 8:12], cmpt.t[:, :, :].rearrange("p j t -> p t j"), ALU.add, [cmpt.r], [lo.r])
                pc = s.ps()
                s.mm(pc.t[:, 0:4], ones.t[:, :], lo.t[:, 8:12], True, True, [ones.r, lo.r], [pc.r])
                s.ts("dve", lo.t[:, 12:16], pc.t[:, 0:4], 256.0, wdt, ALU.is_ge, ALU.mult, [pc.r], [lo.r])
                s.tt("dve", lo.t[:, 0:4], lo.t[:, 0:4], lo.t[:, 12:16], ALU.add, [lo.r], [lo.r])
            s.tt("dve", MT.t[:, :, :], IT.t[:, :, :], lo.t[:, 0:4].unsqueeze(1).to_broadcast([128, 129, 4]), ALU.is_lt,
                 [IT.r, lo.r], [MT.r])
            s.ts("dve", MT.t[:, :, :], MT.t[:, :, :], NEG, None, ALU.mult, None, [MT.r], [MT.r])
            if DSTEP <= 2:
                continue
            for j0 in range(0, 128, 4):
                psL = s.ps()
                for jj in range(4):
                    j = j0 + jj
                    kvp = s.nxt(kvp_p, "dskvp")
                    s.dma_gather(kvp.t[:, :], Kpool, ridx.t[:, 0, j:j + 1], [ridx.r], [kvp.r])
                    psK = s.ps()
                    for q4 in range(4):
                        s.tp(psK.t[:, q4 * 128:(q4 + 1) * 128], kvp.t[:, q4 * 128:(q4 + 1) * 128], s.idf.t[:, :],
                             [kvp.r, s.idf.r], [psK.r], inc=(q4 == 3))
                    ktb = s.nxt(ktb_p, "dsktb")
                    s.cp("act", ktb.t[:, :, :], psK.t[:, :].rearrange("p (a b) -> p a b", a=4), [psK.r], [ktb.r])
                    for kvh in range(8):
                        s.mm(psL.t[:, jj * 128 + kvh * 16:jj * 128 + kvh * 16 + 16], ktb.t[:, kvh // 2, :],
                             qT2.t[:, kvh * 4:kvh * 4 + 4, :].rearrange("p g t -> p (g t)"), True, True,
                             [ktb.r, qT2.r], [psL.r], inc=(jj == 3 and kvh == 7))
                pv = psL.t[:, :].rearrange("p (a b) -> p a b", a=4)
                if j0 < 112:
                    bias_ap = B31.t[:, :].unsqueeze(1).to_broadcast([128, 4, 128])
                    br = B31.r
                else:
                    bias_ap = Bl.t[:, j0 - 112:j0 - 108, :]
                    br = Bl.r
                s.tt("dve", LT.t[:, j0:j0 + 4, :], pv, bias_ap, ALU.add, [psL.r, br], [LT.r])
                s.tt("dve", LT.t[:, j0:j0 + 4, :].rearrange("p a (h t) -> p a h t", t=4),
                     LT.t[:, j0:j0 + 4, :].rearrange("p a (h t) -> p a h t", t=4),
                     MT.t[:, j0:j0 + 4, :].unsqueeze(2).to_broadcast([128, 4, 32, 4]), ALU.add, [LT.r, MT.r], [LT.r])
            s.memset("pool", LT.t[:, 128, :], NEG, [LT.r])
            psLn = s.ps()
            for kvh in range(8):
                s.mm(psLn.t[0:4, kvh * 16:kvh * 16 + 16], kn2.t[:, kvh // 2, :],
                     qT2.t[:, kvh * 4:kvh * 4 + 4, :].rearrange("p g t -> p (g t)"), True, True, [kn2.r, qT2.r],
                     [psLn.r], inc=(kvh == 7))
            s.tt("dve", LT.t[0:4, 128, :], psLn.t[0:4, 0:128], cs.t[0:4, 0:128], ALU.add, [psLn.r, cs.r], [LT.r])
            s.tt("dve", LT.t[0:4, 128, :].rearrange("p (h t) -> p h t", t=4), LT.t[0:4, 128, :].rearrange("p (h t) -> p h t", t=4),
                 MT.t[0:4, 128, :].unsqueeze(1).to_broadcast([4, 32, 4]), ALU.add, [LT.r, MT.r], [LT.r])
            if DSTEP <= 3:
                continue
            pm = s.nxt(w128, "dsw")
            s.red("dve", pm.t[:, :], LT.t[:, :, :].rearrange("p j c -> p c j"), ALU.max, [LT.r], [pm.r])
            pt_ = s.ps()
            s.tp(pt_.t[:, 0:128], pm.t[:, :], s.idf.t[:, :], [pm.r, s.idf.r], [pt_.r])
            s.red("dve", lo.t[:, 4:5], pt_.t[:, 0:128], ALU.max, [pt_.r], [lo.r])
            dmx = s.nxt(w128, "dsw")
            s.ts("dve", dmx.t[:, :], s.idf.t[:, :], lo.t[:, 4:5], None, ALU.mult, None, [s.idf.r, lo.r], [dmx.r])
            pb = s.ps()
            s.mm(pb.t[:, 0:128], ones.t[:, :], dmx.t[:, :], True, True, [ones.r, dmx.r], [pb.r])
            mxb = s.nxt(w128, "dsw")
            s.cp("act", mxb.t[:, :], pb.t[:, 0:128], [pb.r], [mxb.r])
            s.tt("dve", LT.t[:, :, :], LT.t[:, :, :], mxb.t[:, :].unsqueeze(1).to_broadcast([128, 129, 128]), ALU.subtract,
                 [LT.r, mxb.r], [LT.r])
            s.act(ET.t[:, :, :], LT.t[:, :, :], AF.Exp, [LT.r], [ET.r])
            ets = s.nxt(w128, "dsw")
            s.red("dve", ets.t[:, :], ET.t[:, :, :].rearrange("p j c -> p c j"), ALU.add, [ET.r], [ets.r])
            pd = s.ps()
            s.mm(pd.t[:, 0:1], ets.t[:, :], ones.t[:, 0:1], True, True, [ets.r, ones.r], [pd.r])
            s.op("dve", lambda g, pd=pd: g.reciprocal(lo.t[:, 5:6], pd.t[:, 0:1]), [pd.r], [lo.r])
            if DSTEP <= 4:
                continue
            po = s.ps()
            for j in range(128):
                kvp = s.nxt(kvp_p, "dskvp")
                s.dma_gather(kvp.t[:, :], Vpool, ridx.t[:, 0, j:j + 1], [ridx.r], [kvp.r])
                vb = s.nxt(vb_p, "dsvb")
                s.cp("act" if j % 2 else "dve", vb.t[:, :], kvp.t[:, :], [kvp.r], [vb.r])
                s.mm(po.t[:, :], ET.t[:, j, :], vb.t[:, :], j == 0, False, [ET.r, vb.r], [po.r], inc=False)
            s.mm(po.t[:, :], ET.t[0:4, 128, :], vnew.t[0:4, :], False, True, [ET.r, vnew.r], [po.r])
            R = s.nxt(R_p, "dsr")
            s.tt("dve", R.t[:, :].rearrange("p (k d) -> p k d", k=8), po.t[:, :].rearrange("p (k d) -> p k d", k=8),
                 selm.unsqueeze(2).to_broadcast([128, 8, 64]), ALU.mult, [po.r, cs.r], [R.r])
            s.red("dve", osl.t[:, :], R.t[:, :].rearrange("p (k d) -> p d k", k=8), ALU.add, [R.r], [osl.r])
            s.ts("dve", osl.t[:, :], osl.t[:, :], lo.t[:, 5:6], None, ALU.mult, None, [osl.r, lo.r], [osl.r])
            pp = s.ps()
            s.mm(pp.t[:, 0:64], perm, osl.t[:, :], True, True, [cs.r, osl.r], [pp.r])
            op_ = s.nxt(w128, "dsw")
            s.cp("act", op_.t[:, 0:64], pp.t[:, 0:64], [pp.r], [op_.r])
            for t in range(4):
                s.dma("sp", OSap[bi * 4 + t:bi * 4 + t + 1, :].rearrange("o (h d) -> (o h) d", d=64),
                      op_.t[t * 32:(t + 1) * 32, 0:64], [op_.r], [s.OS.r])
        ot = s.nxt(s.tokp, "toki")
        s.dma("sp", ot.t[0:NS, :], OSap[:, :], [s.OS.r], [ot.r])
        zs = s.sb([128, 16, NS], BF16, name="ds_zs")
        gss = s.sb([128, 16, NS], BF16, name="ds_gss")
        s.dma("sp", zs.t[:, :, :], s.ZT.t.ap()[:, 0:16, SEQ:TT], [s.ZT.r], [zs.r])
        for g in range(0, 16, 4):
            ps = s.ps()
            for j in range(4):
                cc = (g + j) * 128
                s.tp(ps.t[:, j * 16:j * 16 + 16], ot.t[0:NS, cc:cc + 128], s.idf.t[0:NS, 0:NS], [ot.r, s.idf.r], [ps.r],
                     inc=(j == 3))
            s.tt("dve", gss.t[:, g:g + 4, :], ps.t[:, 0:64].rearrange("p (a b) -> p a b", a=4), zs.t[:, g:g + 4, :], ALU.mult,
                 [ps.r, zs.r], [gss.r])
        s.dma("sp", s.GTd.t.ap()[:, 0:16, SEQ:TT], gss.t[:, :, :], [gss.r], [s.GTd.r])

    def build(self):
        s = self
        s.setup()
        s.setup_swa()
        s.setup_s5()
        s.setup_gdn()
        s.setup_dsa()
        s.setup_dsa_sample()
        Xcur = s.inp["xin"]
        for li in range(s.n_layers):
            if os.environ.get("DS_DEBUG") and li < 3:
                continue
            if li == 0:
                with s.scope():
                    kT = s.sb([64, 4, TT], BF16, name="a_kT")
                    vtok = s.sb([128, NTILE, 256], BF16, name="a_vtok")
                    kv32 = s.sb([128, 2, 512], F32, name="a_kv32")
                    with s.scope():
                        xT = s.sb([128, 16, TT], BF16, name="BIG")
                        s.load_input(xT)
                        if s.stop == "load":
                            s.dma("sp", s.XTd.t.ap()[:, :, :], xT.t[:, :, :], [xT.r], [s.XTd.r])
                        else:
                            s.swa_proj(xT, kT, vtok, kv32)
                    if s.stop not in ("load", "proj"):
                        with s.scope():
                            s.swa_attn(kT, vtok)
                if s.stop in ("load", "proj", "attn"):
                    break
                nkc = 16
                Wo = s.inp["a_w_out"].t.ap()
            elif li == 1:
                with s.scope():
                    xT = s.sb([128, 16, TT], BF16, name="BIG")
                    s.dma("sp", xT.t[:, :, :], s.XTd.t.ap()[:, :, :], [s.XTd.r], [xT.r])
                    s.s5_proj(xT)
                with s.scope():
                    s.s5_core()
                with s.scope():
                    y1T = s.sb([128, 16, TT], BF16, name="BIG")
                    s.s5_glu(y1T)
                nkc = 16
                Wo = s.inp["s5_w_out"].t.ap()
            elif li == 2:
                with s.scope():
                    xT = s.sb([128, 16, TT], BF16, name="BIG")
                    s.dma("sp", xT.t[:, :, :], s.XTd.t.ap()[:, :, :], [s.XTd.r], [xT.r])
                    s.gdn_proj(xT)
                if s.stop == "gproj":
                    break
                with s.scope():
                    s.gdn_core()
                nkc = 32
                Wo = s.inp["gdn_w_out"].t.ap()
            elif li == 3:
                with s.scope():
                    xT = s.sb([128, 16, TT], BF16, name="BIG")
                    s.dma("sp", xT.t[:, :, :], s.XTd.t.ap()[:, :, :], [s.XTd.r], [xT.r])
                    s.dsa_proj(xT)
                if s.stop == "dproj":
                    break
                with s.scope():
                    s.dsa_prompt()
                if s.stop != "dprompt":
                    with s.scope():
                        s.dsa_sample()
                nkc = 16
                Wo = s.inp["dsa_w_out"].t.ap()
            last = (li == s.n_layers - 1)
            Xnext = s.out["y"] if last else s.X[li % 2]
            with s.scope():
                s.out_proj(li, nkc, Wo, Xcur)
            if s.stop == "outproj":
                break
            with s.scope():
                xnT = s.sb([128, 16, TT], BF16, name="BIG")
                s.ln_pass(li, xnT)
                if s.stop != "ln":
                    s.ple_stage(li, xnT, Xnext, last)
            Xcur = Xnext
        s.finish()
        return s.nc


def _rel_bucket_np(dist):
    n = np.maximum(dist, 0)
    exact = 16
    with np.errstate(divide="ignore"):
        logb = exact + (np.log(np.maximum(n, exact).astype(np.float32) / np.float32(exact))
                        / np.float32(np.log(2048 / exact)) * (32 - exact)).astype(np.int32)
    return np.where(n < exact, n, np.minimum(logb, 31)).astype(np.int64)


def _static_tables():
    q = np.arange(128)[:, None]
    k = np.arange(256)[None, :]
    dist = q + 128 - k
    bkt_p = _rel_bucket_np(dist)
    mask_p = np.where((dist >= 0) & (dist < 128), 0.0, NEG).astype(np.float32)
    tok = np.arange(4)
    bkt_s = np.zeros((4, 4, 144), np.int64)
    mask_s = np.full((4, 4, 144), NEG, np.float32)
    for bi in range(4):
        for i in range(4):
            for j in range(128):
                d = 128 + i - j
                bkt_s[i, bi, j] = _rel_bucket_np(np.array(d))
                if 0 <= d < 128:
                    mask_s[i, bi, j] = 0.0
            for jj in range(16):
                bj, i2 = jj // 4, jj % 4
                d = i - i2
                bkt_s[i, bi, 128 + jj] = _rel_bucket_np(np.array(d))
                if bj == bi and d >= 0:
                    mask_s[i, bi, 128 + jj] = 0.0
    return bkt_p, mask_p, bkt_s, mask_s


_PROG_CACHE = {}


def _get_prog(n_layers=4, debug=False):
    key = (n_layers, debug)
    if key not in _PROG_CACHE:
        p = Prog(n_layers, debug)
        p.build()
        _PROG_CACHE[key] = p
    return _PROG_CACHE[key]


def make_in_maps(inputs):
    f = lambda a: np.ascontiguousarray(np.asarray(a, dtype=np.float32))
    x_prompt = f(inputs["x_prompt"])
    x_sample = f(inputs["x_sample"])
    p_prompt = f(inputs["p_prompt"])
    p_sample = f(inputs["p_sample"])
    rel_bias = f(inputs["rel_bias"])
    bkt_p, mask_p, bkt_s, mask_s = _static_tables()
    ident = np.eye(128, dtype=np.float32)
    a_bias_p = np.ascontiguousarray(rel_bias[bkt_p].transpose(0, 2, 1))
    bs = rel_bias[bkt_s]
    hord = np.array([kvh * 8 + g2 * 2 + par for kvh in range(4) for par in range(2) for g2 in range(4)])
    bs = bs[..., hord]
    a_bias_s = bs.transpose(3, 0, 1, 2).reshape(2, 64, 4, 144)
    a_bias_s = np.ascontiguousarray(a_bias_s.transpose(1, 0, 2, 3).reshape(64, 1152))
    a_mask_s = np.broadcast_to(mask_s[None], (32, 4, 4, 144)).reshape(2, 64, 4, 144)
    a_mask_s = np.ascontiguousarray(a_mask_s.transpose(1, 0, 2, 3).reshape(64, 1152))
    sinks = f(inputs["a_sinks"])[0]
    a_sink_p = np.ascontiguousarray(np.broadcast_to(sinks[None, :], (128, 32)))
    a_sink_s = np.ascontiguousarray(np.repeat(sinks[hord], 4).reshape(2, 64).T)
    shared = dict(
        ident=ident, ln_g=f(inputs["ln_g"]), ln_b=f(inputs["ln_b"]), ple_gate_w=f(inputs["ple_gate_w"]),
        ple_w=f(inputs["ple_w"]), a_w_in=f(inputs["a_w_in"])[0], a_w_out=f(inputs["a_w_out"])[0],
        a_bias_p=a_bias_p, a_mask_p=mask_p, a_bias_s=a_bias_s, a_mask_s=a_mask_s, a_sink_p=a_sink_p,
        a_sink_s=a_sink_s,
    )
    if "s5_w_in" in inputs:
        g = np.arange(128)
        cst = np.zeros((128, 256), np.float32)
        cst[g, g // 2] = 1.0
        cst[g, 64 + g % 2] = 1.0
        cst[g, 66 + g // 64] = 1.0
        cst[g, 68 + g % 64] = 1.0
        cst[g, 132 + (g // 16) % 2] = 1.0
        shared.update(
            s5_w_in=f(inputs["s5_w_in"])[0], s5_w_glu=f(inputs["s5_w_glu"])[0], s5_w_out=f(inputs["s5_w_out"])[0],
            s5_a_re=f(inputs["s5_a_re"])[0], s5_a_im=f(inputs["s5_a_im"])[0],
            s5_log_dt=f(inputs["s5_log_dt"])[0].reshape(128, 1), s5_b_re=f(inputs["s5_b_re"])[0],
            s5_b_im=f(inputs["s5_b_im"])[0], s5_c_re=f(inputs["s5_c_re"])[0], s5_c_im=f(inputs["s5_c_im"])[0],
            s5_d_fm=np.ascontiguousarray(f(inputs["s5_d"])[0].reshape(16, 128).T), s5_const=cst)
        state_s5 = f(inputs["state_s5"])[0].reshape(32, 128, 128)
    if "gdn_w_in" in inputs:
        def masks(n, cs):
            i = np.arange(n)
            same = (i[:, None] // cs) == (i[None, :] // cs)
            triU = (same & (i[:, None] <= i[None, :])).astype(np.float32)
            bones = same.astype(np.float32)
            Mb = np.where(same & (i[None, :] <= i[:, None]), 0.0, 30000.0).astype(np.float32)
            strict = (same & (i[None, :] < i[:, None])).astype(np.float32)
            ncs = n // cs
            sel = [np.broadcast_to(((i // cs) == c)[:, None], (n, 128)).astype(np.float32) for c in range(ncs)]
            cm = np.stack([((i // cs) == c) for c in range(ncs)], 1).astype(np.float32)
            return np.ascontiguousarray(np.concatenate([triU, bones, Mb, strict] + sel + [cm], 1))
        cwv = f(inputs["gdn_conv_w"])[0]
        shared.update(
            gdn_w_in=f(inputs["gdn_w_in"])[0], gdn_w_out=f(inputs["gdn_w_out"])[0],
            gdn_cw=np.ascontiguousarray(cwv.reshape(4, 64, 128).transpose(2, 1, 0)),
            gdn_ab=np.ascontiguousarray(np.broadcast_to(
                np.concatenate([f(inputs["gdn_a_log"])[0], f(inputs["gdn_dt_bias"])[0]])[None], (128, 64))),
            gdn_nw=np.ascontiguousarray(np.broadcast_to(f(inputs["gdn_norm_w"])[0][None], (128, 128))),
            gdn_mp=masks(128, 64), gdn_ms=masks(16, 4))
        state_gdn = f(inputs["state_gdn"])[0]
        state_gc = f(inputs["state_gdn_conv"])[0]
    if "dsa_w_in" in inputs:
        qq = np.arange(128)[:, None]
        uu = np.arange(2048)[None, :]
        bk = _rel_bucket_np(qq + 1920 - uu)
        d_bias_p = np.ascontiguousarray(rel_bias[bk].transpose(2, 0, 1))
        d_causal = np.where(np.arange(128)[None, :] > np.arange(128)[:, None], -1e30, 0.0).astype(np.float32)
        shared.update(dsa_w_in=f(inputs["dsa_w_in"])[0], dsa_w_out=f(inputs["dsa_w_out"])[0], d_bias_p=d_bias_p,
                      d_causal=d_causal)
    if "cache_d_kv" in inputs:
        cc = np.arange(128)
        head_c = (cc // 16) * 4 + (cc // 4) % 4
        t_c = cc % 4
        pp_ = np.arange(128)[:, None, None]
        jj_ = np.arange(16)[None, :, None]
        dist_l = 16384 + t_c[None, None, :] - ((112 + jj_) * 128 + pp_)
        d_bias_l = np.ascontiguousarray(rel_bias[_rel_bucket_np(dist_l), head_c[None, None, :]])
        d_bias_31 = np.ascontiguousarray(np.broadcast_to(rel_bias[31, head_c][None, :], (128, 128)))
        dist_n = t_c[None, :] - np.arange(4)[:, None]
        d_bias_n = np.ascontiguousarray(rel_bias[_rel_bucket_np(dist_n), head_c[None, :]])
        d_cneg4 = np.where(np.arange(4)[:, None] > np.arange(4)[None, :], -1e30, 0.0).astype(np.float32)
        d_selm = (np.arange(8)[None, :] == (cc // 16)[:, None]).astype(np.float32)
        d_perm = np.zeros((128, 128), np.float32)
        d_perm[cc, t_c * 32 + head_c] = 1.0
        shared.update(
            d_kidx_pool=f(inputs["cache_d_kidx"])[0].reshape(5120 * 128, 128)[:NPOOL * 128],
            d_k_pool=np.ascontiguousarray(f(inputs["cache_d_kv"])[0].reshape(5120 * 128, 2, 512)[:NPOOL * 128, 0, :]),
            d_v_pool=np.ascontiguousarray(f(inputs["cache_d_kv"])[0].reshape(5120 * 128, 2, 512)[:NPOOL * 128, 1, :]),
            d_iota=np.arange(128, dtype=np.float32).reshape(128, 1), d_bias_l=d_bias_l, d_bias_31=d_bias_31,
            d_bias_n=d_bias_n, d_cneg4=d_cneg4, d_selm=d_selm, d_perm=d_perm)
        page_table = np.ascontiguousarray(np.asarray(inputs["page_table"], dtype=np.int32) % NPOOL)
    cache_a = f(inputs["cache_a_kv"])[0].reshape(32, 128, 512)
    maps = []
    for c in range(8):
        b = c % 4
        m = dict(shared)
        m["xin"] = np.ascontiguousarray(np.concatenate([x_prompt[b], x_sample[4 * c:4 * c + 4].reshape(NS, D)], 0))
        m["p_all"] = np.ascontiguousarray(
            np.concatenate([p_prompt[:, b], p_sample[:, 4 * c:4 * c + 4].reshape(4, NS, 256)], 1))
        m["a_cache"] = np.ascontiguousarray(cache_a[4 * c:4 * c + 4])
        if "s5_w_in" in inputs:
            m["s5_state"] = np.ascontiguousarray(state_s5[4 * c:4 * c + 4])
        if "cache_d_kv" in inputs:
            m["pt_loc"] = np.ascontiguousarray(page_table[4 * c:4 * c + 4])
        if "gdn_w_in" in inputs:
            m["gdn_state"] = np.ascontiguousarray(state_gdn[4 * c:4 * c + 4])
            m["gdn_cbuf"] = np.ascontiguousarray(state_gc[4 * c:4 * c + 4].reshape(12, 8192))
        maps.append(m)
    return maps


def kernel(**inputs):
    prog = _get_prog()
    maps = make_in_maps(inputs)
    maps = [{k: v for k, v in m.items() if k in prog.inp} for m in maps]
    res = run_bass_kernel_spmd(prog.nc, maps, core_ids=list(range(8)))
    R = res.results
    y_prompt = np.stack([R[b]["y"][:SEQ] for b in range(4)])
    y_sample = np.concatenate([R[c]["y"][SEQ:].reshape(4, 4, D) for c in range(8)], 0)
    a_kv_p = np.stack([R[b]["a_kv_p"].reshape(128, 2, 4, 64) for b in range(4)])[None]
    a_kv_s = np.concatenate([R[c]["a_kv_s"].reshape(4, 128, 2, 4, 64) for c in range(8)], 0)[None]
    z = lambda *sh: np.zeros(sh, np.float32)
    gd_p = np.stack([R[b]["gd_p"] for b in range(4)])[None]
    gd_s = np.concatenate([R[c]["gd_s"] for c in range(8)], 0)[None]
    gc_p = np.stack([R[b]["gc"][0:3] for b in range(4)])[None]
    gc_s = np.concatenate([R[c]["gc"][3:15].reshape(4, 3, 8192) for c in range(8)], 0)[None]
    dkv_p = np.stack([R[b]["d_kv"][:SEQ].reshape(SEQ, 2, 8, 64) for b in range(4)])[None]
    dkv_s = np.concatenate([R[c]["d_kv"][SEQ:].reshape(4, 4, 2, 8, 64) for c in range(8)], 0)[None]
    dki_p = np.stack([R[b]["d_ki"][:SEQ] for b in range(4)])[None]
    dki_s = np.concatenate([R[c]["d_ki"][SEQ:].reshape(4, 4, 128) for c in range(8)], 0)[None]
    s5_p = np.stack([R[b]["s5_p"].reshape(128, 64, 2) for b in range(4)])[None]
    s5_s = np.concatenate([R[c]["s5_s"].reshape(4, 128, 64, 2) for c in range(8)], 0)[None]
    return (y_prompt, y_sample, a_kv_p, a_kv_s,
            s5_p, s5_s, gd_p, gd_s, gc_p, gc_s, dkv_p, dkv_s, dki_p, dki_s)
```

```python
import os
import numpy as np
import concourse.bass as bass
import concourse.mybir as mybir
from concourse.bass_utils import run_bass_kernel_spmd
from contextlib import ExitStack

F32, BF16, I32 = mybir.dt.float32, mybir.dt.bfloat16, mybir.dt.int32
AF = mybir.ActivationFunctionType
ALU = mybir.AluOpType
AX = mybir.AxisListType

D = 2048
SEQ = 2048
NS = 16
TT = SEQ + NS
NTILE = 17
ALPHA = 8 ** 0.25
LN_EPS = 1e-5
NEG = -30000.0
NPOOL = int(os.environ.get('DS_NPOOL', '5120'))
TBS = [(0, 512), (512, 512), (1024, 512), (1536, 512), (2048, NS)]
TILES = [(i * 128, 128) for i in range(16)] + [(2048, NS)]


class Res:
    __slots__ = ("w", "r")

    def __init__(self):
        self.w = None
        self.r = {}


class T:
    def __init__(self, t, n=1):
        self.t = t
        self.rs = [Res() for _ in range(n)]

    @property
    def r(self):
        return self.rs[0]


class KB:
    LIM = 30000
    DLIM = 1800
    NSLOT = 8

    def __init__(self):
        self.nc = bass.Bass("TRN2", target_bir_lowering=False)
        self.es = ExitStack()
        nc = self.nc
        self.eng = {"pe": nc.tensor, "act": nc.scalar, "dve": nc.vector, "pool": nc.gpsimd, "sp": nc.sync}
        self.nsem = 0
        self.csem = {}
        self.ccnt = {}
        self.seen = {e: {} for e in self.eng}
        for e in ("pe", "act", "dve", "pool"):
            self.csem[e] = self._newsem()
            self.ccnt[e] = 0
        self.dq = {}
        for q in ("sp", "pool", "act"):
            self.dq[q] = dict(j=0, sems=[self._newsem() for _ in range(self.NSLOT)], cnt=[0] * self.NSLOT,
                              last=[None] * self.NSLOT)
        self.pend_r = []
        self.pend_w = []
        self.ntens = 0
        self.psb = []
        self.psi = 0
        self.scopes = []

    def _newsem(self):
        h = self.es.enter_context(self.nc.semaphore("sem%d" % self.nsem))
        self.nsem += 1
        return (self.nsem - 1, h)

    def _wait(self, e, ev):
        if ev is None:
            return
        sid, h, v, _ = ev
        if self.seen[e].get(sid, 0) >= v:
            return
        self.eng[e].wait_ge(h, v)
        self.seen[e][sid] = v

    @staticmethod
    def _deps(reads, writes):
        evs = []
        for r in reads:
            if r.w is not None:
                evs.append(r.w)
        for w in writes:
            if w.w is not None:
                evs.append(w.w)
            evs.extend(w.r.values())
        return evs

    @staticmethod
    def _commit(ev, reads, writes):
        for r in reads:
            r.r[ev[0]] = ev
        for w in writes:
            w.w = ev
            w.r = {}

    def op(self, e, fn, reads=(), writes=(), inc=True):
        reads = [x for x in reads]
        writes = [x for x in writes]
        for ev in self._deps(reads, writes):
            if e == "pe" and ev[3] == "pe":
                continue
            self._wait(e, ev)
        ins = fn(self.eng[e])
        if e == "pe" and not inc:
            self.pend_r += reads
            self.pend_w += writes
            return ins
        if self.ccnt[e] >= self.LIM:
            self.csem[e] = self._newsem()
            self.ccnt[e] = 0
        self.ccnt[e] += 1
        ins.then_inc(self.csem[e][1], 1)
        ev = (self.csem[e][0], self.csem[e][1], self.ccnt[e], e)
        if e == "pe":
            reads = reads + self.pend_r
            writes = writes + self.pend_w
            self.pend_r = []
            self.pend_w = []
        self._commit(ev, reads, writes)
        return ins

    def dma(self, q, out, in_, reads=(), writes=(), **kw):
        d = self.dq[q]
        slot = d["j"] % self.NSLOT
        d["j"] += 1
        self._wait(q, d["last"][slot])
        for ev in self._deps(reads, writes):
            self._wait(q, ev)
        if d["cnt"][slot] >= self.DLIM:
            d["sems"][slot] = self._newsem()
            d["cnt"][slot] = 0
        ins = self.eng[q].dma_start(out=out, in_=in_, **kw)
        d["cnt"][slot] += 1
        sid, h = d["sems"][slot]
        ins.then_inc(h, 16)
        ev = (sid, h, 16 * d["cnt"][slot], "dma_" + q)
        d["last"][slot] = ev
        self._commit(ev, list(reads), list(writes))
        return ins

    def dma_gather(self, out, in_, idx_ap, reads=(), writes=()):
        q = "pool"
        d = self.dq[q]
        slot = d["j"] % self.NSLOT
        d["j"] += 1
        self._wait(q, d["last"][slot])
        for ev in self._deps(reads, writes):
            self._wait(q, ev)
        if d["cnt"][slot] >= self.DLIM:
            d["sems"][slot] = self._newsem()
            d["cnt"][slot] = 0
        ins = self.nc.gpsimd.indirect_dma_start(out=out, out_offset=None, in_=in_,
                                                in_offset=bass.IndirectOffsetOnAxis(ap=idx_ap, axis=0))
        d["cnt"][slot] += 1
        sid, h = d["sems"][slot]
        ins.then_inc(h, 16)
        ev = (sid, h, 16 * d["cnt"][slot], "dma_" + q)
        d["last"][slot] = ev
        self._commit(ev, list(reads), list(writes))
        return ins

    def finish(self):
        for q in self.dq:
            for ev in self.dq[q]["last"]:
                self._wait("sp", ev)
        for e in ("pe", "act", "dve", "pool"):
            if self.ccnt[e] > 0:
                self._wait("sp", (self.csem[e][0], self.csem[e][1], self.ccnt[e], e))
        self.es.close()

    def barrier(self):
        evs = []
        for q in self.dq:
            evs += [ev for ev in self.dq[q]["last"] if ev is not None]
        for e in ("pe", "act", "dve", "pool"):
            if self.ccnt[e] > 0:
                evs.append((self.csem[e][0], self.csem[e][1], self.ccnt[e], e))
        for e in self.eng:
            for ev in evs:
                self._wait(e, ev)

    class _Scope:
        def __init__(self, kb):
            self.kb = kb

        def __enter__(self):
            self.kb.scopes.append(ExitStack())
            return self

        def __exit__(self, *a):
            self.kb.barrier()
            self.kb.scopes.pop().close()
            return False

    def scope(self):
        return KB._Scope(self)

    def sb(self, shape, dt, n=1, name=None):
        self.ntens += 1
        st = self.scopes[-1] if self.scopes else self.es
        t = st.enter_context(self.nc.sbuf_tensor("%s_%d" % (name or "sb", self.ntens), list(shape), dt))
        return T(t, n)

    def pool(self, shape, dt, bufs):
        return [self.sb(shape, dt) for _ in range(bufs)]

    def dram(self, name, shape, dt, kind="Internal", n=1):
        t = self.nc.dram_tensor(name, list(shape), dt, kind=kind)
        return T(t, n)

    def init_psum(self):
        for i in range(8):
            t = self.es.enter_context(self.nc.psum_tensor("psb%d" % i, [128, 512], F32))
            self.psb.append(T(t))

    def ps(self):
        p = self.psb[self.psi % 8]
        self.psi += 1
        return p

    def mm(self, out, lhsT, rhs, start, stop, reads, writes, inc=None):
        if inc is None:
            inc = stop
        return self.op("pe", lambda e: e.matmul(out, lhsT, rhs, start=start, stop=stop), reads, writes, inc=inc)

    def tp(self, out, in_, ident, reads, writes, inc=True):
        return self.op("pe", lambda e: e.transpose(out, in_, ident), reads, writes, inc=inc)

    def act(self, out, in_, func, reads, writes, **kw):
        return self.op("act", lambda e: e.activation(out=out, in_=in_, func=func, **kw), reads, writes)

    def tt(self, e, out, in0, in1, op, reads, writes):
        return self.op(e, lambda g: g.tensor_tensor(out=out, in0=in0, in1=in1, op=op), reads, writes)

    def ts(self, e, out, in0, s1, s2, op0, op1, reads, writes, **kw):
        if op1 is None:
            return self.op(e, lambda g: g.tensor_scalar(out=out, in0=in0, scalar1=s1, scalar2=None, op0=op0, **kw),
                           reads, writes)
        return self.op(e, lambda g: g.tensor_scalar(out=out, in0=in0, scalar1=s1, scalar2=s2, op0=op0, op1=op1, **kw),
                       reads, writes)

    def stt(self, e, out, in0, scalar, in1, op0, op1, reads, writes):
        return self.op(e, lambda g: g.scalar_tensor_tensor(out=out, in0=in0, scalar=scalar, in1=in1, op0=op0, op1=op1),
                       reads, writes)

    def red(self, e, out, in_, op, reads, writes):
        return self.op(e, lambda g: g.tensor_reduce(out=out, in_=in_, axis=AX.X, op=op), reads, writes)

    def cp(self, e, out, in_, reads, writes):
        if e == "act":
            return self.act(out, in_, AF.Copy, reads, writes)
        return self.op(e, lambda g: g.tensor_copy(out, in_), reads, writes)

    def memset(self, e, ap, val, writes):
        return self.op(e, lambda g: g.memset(ap, val), [], writes)


class Prog(KB):
    def __init__(self, n_layers=4, debug=False, stop=None):
        super().__init__()
        self.stop = stop
        self.n_layers = n_layers
        self.debug = debug
        self.inp = {}
        self.out = {}
        self.evi = 0

    def din(self, name, shape, dt=F32):
        t = self.nc.dram_tensor(name, list(shape), dt, kind="ExternalInput")
        self.inp[name] = T(t)
        return self.inp[name]

    def dout(self, name, shape, dt=F32):
        t = self.nc.dram_tensor(name, list(shape), dt, kind="ExternalOutput")
        self.out[name] = T(t)
        return self.out[name]

    def ev_eng(self):
        self.evi += 1
        return "act" if self.evi % 2 else "dve"

    def nxt(self, pool, attr):
        i = getattr(self, attr, 0)
        setattr(self, attr, i + 1)
        return pool[i % len(pool)]

    def load_w(self, wt, wap, K, n0, nw, c0=0):
        kc = K // 128
        for k0 in range(0, kc, 4):
            k1 = min(kc, k0 + 4)
            src = wap[k0 * 128:k1 * 128, n0:n0 + nw].rearrange("(kc p) n -> p kc n", p=128)
            self.dma("pool", wt.t[:, k0:k1, c0:c0 + nw], src, [], [wt.r])

    def ws_mm(self, ps_ap, psr, wt, cols, xT, kcs, t0, tn, xr):
        n = len(kcs)
        for i, kc in enumerate(kcs):
            self.mm(ps_ap, wt.t[:, kc, cols[0]:cols[1]], xT.t[:, kc, t0:t0 + tn], i == 0, i == n - 1,
                    [wt.r] + xr, [psr])

    def as_mm(self, ps_ap, psr, wt, cols, xT, kcs, t0, tn, xr):
        n = len(kcs)
        for i, kc in enumerate(kcs):
            self.mm(ps_ap, xT.t[:, kc, t0:t0 + tn], wt.t[:, kc, cols[0]:cols[1]], i == 0, i == n - 1,
                    [wt.r] + xr, [psr])

    def ld_grp(self, xs, dap, dres, n0, grp):
        full = [i for i in grp if i < 16]
        a, b = full[0], full[-1] + 1
        self.dma("sp", xs.t[:, 0:b - a, :], dap[a * 128:b * 128, n0:n0 + 256].rearrange("(i p) n -> p i n", p=128),
                 [dres], [xs.r])
        if 16 in grp:
            self.dma("sp", xs.t[0:NS, grp.index(16), :], dap[2048:TT, n0:n0 + 256], [dres], [xs.r])

    def st_grp(self, xs, dap, dres, n0, grp):
        full = [i for i in grp if i < 16]
        a, b = full[0], full[-1] + 1
        self.dma("sp", dap[a * 128:b * 128, n0:n0 + 256].rearrange("(i p) n -> p i n", p=128), xs.t[:, 0:b - a, :],
                 [xs.r], [dres])
        if 16 in grp:
            self.dma("sp", dap[2048:TT, n0:n0 + 256], xs.t[0:NS, grp.index(16), :], [xs.r], [dres])

    def setup(self):
        s = self
        s.init_psum()
        s.din("xin", [TT, D])
        s.din("p_all", [4, TT, 256])
        s.din("ident", [128, 128])
        s.din("ln_g", [4, D])
        s.din("ln_b", [4, D])
        s.din("ple_gate_w", [4, D, D])
        s.din("ple_w", [4, 256, D])
        s.dout("y", [TT, D])
        s.X = [s.dram("Xtok0", [TT, D], F32), s.dram("Xtok1", [TT, D], F32)]
        s.Rt = s.dram("Rt", [TT, D], F32)
        s.XN = s.dram("XN", [TT, D], F32)
        s.GTd = s.dram("GTd", [128, 32, TT], BF16)
        s.XTd = s.dram("XTd", [128, 16, TT], BF16)
        s.wp = s.pool([128, 16, 256], BF16, 2)
        s.idf = s.sb([128, 128], F32, name="idf")
        s.idb = s.sb([128, 128], BF16, name="idb")
        s.dma("sp", s.idf.t[:], s.inp["ident"].t.ap()[:, :], [], [s.idf.r])
        s.cp("dve", s.idb.t[:], s.idf.t[:], [s.idf.r], [s.idb.r])
        s.tokp = s.pool([128, 2048], F32, 2)
        s.stg = s.pool([128, 9, 256], F32, 2)
        s.sm = [s.sb([128, 16], F32, name="sm") for i in range(4)]
        s.ev4 = s.pool([128, 512], F32, 2)

    GRPS = [list(range(0, 9)), list(range(9, 17))]

    def load_input(self, xT):
        s = self
        xin = s.inp["xin"].t.ap()
        for i, (t0, tn) in enumerate(TILES):
            xt = s.nxt(s.tokp, "toki")
            s.dma("sp", xt.t[0:tn, :], xin[t0:t0 + tn, :], [], [xt.r])
            s.transpose_tile(xt, tn, xT, t0, 16)

    def transpose_tile(self, xt, tn, xT, t0, nchunk, c0=0, src_c0=0):
        s = self
        for g in range(0, nchunk, 4):
            ps = s.ps()
            ng = min(4, nchunk - g)
            for j in range(ng):
                c = src_c0 + (g + j) * 128
                s.tp(ps.t[:, j * 128:j * 128 + tn], xt.t[0:tn, c:c + 128], s.idf.t[0:tn, 0:tn],
                     [xt.r, s.idf.r], [ps.r], inc=(j == ng - 1))
            s.cp(s.ev_eng(), xT.t[:, c0 + g:c0 + g + ng, t0:t0 + tn],
                 ps.t[:, 0:ng * 128].rearrange("p (a b) -> p a b", a=ng)[:, :, 0:tn], [ps.r], [xT.r])

    def out_proj(self, li, nkc, Wo, Xcur):
        s = self
        Xap = Xcur.t.ap()
        Rap = s.Rt.t.ap()
        Gap = s.GTd.t.ap()
        gtp = s.pool([128, nkc, 128], BF16, 2)
        for nb in range(8):
            n0 = nb * 256
            wts = []
            for k0 in range(0, nkc, 16):
                wt = s.nxt(s.wp, "wi")
                s.load_w(wt, Wo[k0 * 128:(k0 + 16) * 128, :], 2048, n0, 256)
                wts.append(wt)
            for grp in s.GRPS:
                xs = s.nxt(s.stg, "stgi")
                s.ld_grp(xs, Xap, Xcur.r, n0, grp)
                for li_, i in enumerate(grp):
                    t0, tn = TILES[i]
                    gt = s.nxt(gtp, "gti")
                    s.dma("sp", gt.t[:, :, 0:tn], Gap[:, 0:nkc, t0:t0 + tn], [s.GTd.r], [gt.r])
                    ps = s.ps()
                    for kc in range(nkc):
                        wt = wts[kc // 16]
                        s.mm(ps.t[0:tn, 0:256], gt.t[:, kc, 0:tn], wt.t[:, kc % 16, :], kc == 0, kc == nkc - 1,
                             [gt.r, wt.r], [ps.r])
                    s.stt("dve", xs.t[0:tn, li_, :], xs.t[0:tn, li_, :], ALPHA, ps.t[0:tn, 0:256], ALU.mult, ALU.add,
                          [xs.r, ps.r], [xs.r])
                s.st_grp(xs, Rap, s.Rt.r, n0, grp)

    def ln_pass(self, li, xnT):
        s = self
        gb = s.sb([128, 2, D], F32, name="gb")
        gap = s.inp["ln_g"].t.ap()
        bap = s.inp["ln_b"].t.ap()
        s.dma("sp", gb.t[:, 0, :], gap[li:li + 1, :].to_broadcast([128, D]), [], [gb.r])
        s.dma("sp", gb.t[:, 1, :], bap[li:li + 1, :].to_broadcast([128, D]), [], [gb.r])
        Rap = s.Rt.t.ap()
        XNap = s.XN.t.ap()
        for i, (t0, tn) in enumerate(TILES):
            rt = s.nxt(s.tokp, "toki")
            s.dma("sp", rt.t[0:tn, :], Rap[t0:t0 + tn, :], [s.Rt.r], [rt.r])
            sm = s.nxt(s.sm, "smi")
            jk = s.nxt(s.tokp, "toki")
            s.red("dve", sm.t[0:tn, 0:1], rt.t[0:tn, :], ALU.add, [rt.r], [sm.r])
            s.ts("dve", sm.t[0:tn, 1:2], sm.t[0:tn, 0:1], -1.0 / D, None, ALU.mult, None, [sm.r], [sm.r])
            s.act(jk.t[0:tn, :], rt.t[0:tn, :], AF.Square, [rt.r, sm.r], [jk.r, sm.r], bias=sm.t[0:tn, 1:2], scale=1.0,
                  accum_out=sm.t[0:tn, 2:3])
            s.ts("dve", sm.t[0:tn, 3:4], sm.t[0:tn, 2:3], 1.0 / D, LN_EPS, ALU.mult, ALU.add, [sm.r], [sm.r])
            s.act(sm.t[0:tn, 5:6], sm.t[0:tn, 3:4], AF.Sqrt, [sm.r], [sm.r])
            s.op("dve", lambda g: g.reciprocal(sm.t[0:tn, 4:5], sm.t[0:tn, 5:6]), [sm.r], [sm.r])
            s.ts("dve", jk.t[0:tn, :], rt.t[0:tn, :], sm.t[0:tn, 1:2], sm.t[0:tn, 4:5], ALU.add, ALU.mult,
                 [rt.r, sm.r], [jk.r])
            s.tt("dve", jk.t[0:tn, :], jk.t[0:tn, :], gb.t[0:tn, 0, :], ALU.mult, [jk.r, gb.r], [jk.r])
            s.tt("dve", jk.t[0:tn, :], jk.t[0:tn, :], gb.t[0:tn, 1, :], ALU.add, [jk.r, gb.r], [jk.r])
            s.dma("sp", XNap[t0:t0 + tn, :], jk.t[0:tn, :], [jk.r], [s.XN.r])
            s.transpose_tile(jk, tn, xnT, t0, 16)

    def ple_stage(self, li, xnT, Xnext, last):
        s = self
        pT = s.sb([128, 2, TT], BF16, name="pT")
        xst_p = s.pool([128, 2, TT], BF16, 2)
        wpl_p = s.pool([128, 512], BF16, 2)
        pap = s.inp["p_all"].t.ap()
        for grp in s.GRPS:
            pst = s.nxt(s.stg, "stgi")
            s.ld_grp(pst, pap[li], Res(), 0, grp)
            for li_, i in enumerate(grp):
                t0, tn = TILES[i]
                ps = s.ps()
                for j in range(2):
                    s.tp(ps.t[:, j * 128:j * 128 + tn], pst.t[0:tn, li_, j * 128:(j + 1) * 128], s.idf.t[0:tn, 0:tn],
                         [pst.r, s.idf.r], [ps.r], inc=(j == 1))
                s.cp(s.ev_eng(), pT.t[:, :, t0:t0 + tn], ps.t[:, 0:256].rearrange("p (a b) -> p a b", a=2)[:, :, 0:tn],
                     [ps.r], [pT.r])
        Wg = s.inp["ple_gate_w"].t.ap()
        Wp = s.inp["ple_w"].t.ap()
        XNap = s.XN.t.ap()
        Xap = Xnext.t.ap()
        XTap = s.XTd.t.ap()
        for nb in range(8):
            n0 = nb * 256
            wg = s.nxt(s.wp, "wi")
            s.load_w(wg, Wg[li], D, n0, 256)
            wpl = s.nxt(wpl_p, "wpli")
            wplb = wpl.t[:, :]
            s.dma("pool", wplb[:, 0:512].rearrange("p (a b) -> p a b", a=2),
                  Wp[li][:, n0:n0 + 256].rearrange("(kc p) n -> p kc n", p=128), [], [wpl.r])
            xst = s.nxt(xst_p, "xsti")
            for grp in s.GRPS:
                xs = s.nxt(s.stg, "stgi")
                s.ld_grp(xs, XNap, s.XN.r, n0, grp)
                for li_, i in enumerate(grp):
                    t0, tn = TILES[i]
                    psg = s.ps()
                    s.as_mm(psg.t[0:tn, 0:256], psg.r, wg, (0, 256), xnT, list(range(16)), t0, tn, [xnT.r])
                    for kc in range(2):
                        s.mm(psg.t[0:tn, 256:512], pT.t[:, kc, t0:t0 + tn], wplb[:, kc * 256:(kc + 1) * 256], kc == 0,
                             kc == 1, [pT.r, wpl.r], [psg.r])
                    gt = s.nxt(s.ev4, "ev4i")
                    s.act(gt.t[0:tn, 0:256], psg.t[0:tn, 0:256], AF.Sigmoid, [psg.r], [gt.r])
                    s.tt("dve", gt.t[0:tn, 0:256], gt.t[0:tn, 0:256], psg.t[0:tn, 256:512], ALU.mult, [gt.r, psg.r], [gt.r])
                    s.tt("dve", xs.t[0:tn, li_, :], xs.t[0:tn, li_, :], gt.t[0:tn, 0:256], ALU.add, [xs.r, gt.r], [xs.r])
                    if not last:
                        ps = s.ps()
                        for j in range(2):
                            s.tp(ps.t[:, j * 128:j * 128 + tn], xs.t[0:tn, li_, j * 128:(j + 1) * 128],
                                 s.idf.t[0:tn, 0:tn], [xs.r, s.idf.r], [ps.r], inc=(j == 1))
                        s.cp(s.ev_eng(), xst.t[:, :, t0:t0 + tn],
                             ps.t[:, 0:256].rearrange("p (a b) -> p a b", a=2)[:, :, 0:tn], [ps.r], [xst.r])
                s.st_grp(xs, Xap, Xnext.r, n0, grp)
            if not last:
                s.dma("sp", XTap[:, nb * 2:nb * 2 + 2, :], xst.t[:, :, :], [xst.r], [s.XTd.r])

    def setup_swa(self):
        s = self
        s.din("a_w_in", [D, 4608])
        s.din("a_w_out", [D, D])
        s.din("a_bias_p", [128, 32, 256])
        s.din("a_mask_p", [128, 256])
        s.din("a_bias_s", [64, 1152])
        s.din("a_mask_s", [64, 1152])
        s.din("a_sink_p", [128, 32])
        s.din("a_sink_s", [64, 2])
        s.din("a_cache", [4, 128, 512])
        s.dout("a_kv_p", [128, 512])
        s.dout("a_kv_s", [4, 128, 512])
        s.QT = s.dram("QT", [64, 32, TT], BF16)
        s.ZT = s.dram("ZT", [128, 32, TT], BF16)

    def swa_proj(self, xT, kT, vtok, kv32):
        s = self
        W = s.inp["a_w_in"].t.ap()
        qst = s.pool([64, 4, 512], BF16, 2)
        zst = s.pool([128, 2, 512], BF16, 2)
        qi = 0
        QTap = s.QT.t.ap()
        ZTap = s.ZT.t.ap()
        KC = list(range(16))
        for blk in range(int(os.environ.get("NBLK", "18"))):
            wt = s.nxt(s.wp, "wi")
            s.load_w(wt, W, D, blk * 256, 256)
            if blk < 8:
                for (t0, tn) in TBS:
                    st = qst[qi % 2]
                    qi += 1
                    for j in range(4):
                        ps = s.ps()
                        s.ws_mm(ps.t[0:64, 0:tn], ps.r, wt, (j * 64, j * 64 + 64), xT, KC, t0, tn, [xT.r])
                        if j % 2:
                            s.act(st.t[:, j, 0:tn], ps.t[0:64, 0:tn], AF.Copy, [ps.r], [st.r], scale=0.125)
                        else:
                            s.ts("dve", st.t[:, j, 0:tn], ps.t[0:64, 0:tn], 0.125, None, ALU.mult, None, [ps.r], [st.r])
                    s.dma("sp", QTap[:, blk * 4:blk * 4 + 4, t0:t0 + tn], st.t[:, :, 0:tn], [st.r], [s.QT.r])
            elif blk == 8:
                for (t0, tn) in TBS:
                    for j in range(4):
                        ps = s.ps()
                        s.ws_mm(ps.t[0:64, 0:tn], ps.r, wt, (j * 64, j * 64 + 64), xT, KC, t0, tn, [xT.r])
                        s.cp(s.ev_eng(), kT.t[:, j, t0:t0 + tn], ps.t[0:64, 0:tn], [ps.r], [kT.r])
                for ii, i in enumerate((15, 16)):
                    t0, tn = TILES[i]
                    ps = s.ps()
                    s.as_mm(ps.t[0:tn, 0:256], ps.r, wt, (0, 256), xT, KC, t0, tn, [xT.r])
                    s.cp("dve", kv32.t[0:tn, ii, 0:256], ps.t[0:tn, 0:256], [ps.r], [kv32.r])
            elif blk == 9:
                for i, (t0, tn) in enumerate(TILES):
                    ps = s.ps()
                    s.as_mm(ps.t[0:tn, 0:256], ps.r, wt, (0, 256), xT, KC, t0, tn, [xT.r])
                    s.cp("act", vtok.t[0:tn, i, :], ps.t[0:tn, 0:256], [ps.r], [vtok.r])
                    if i >= 15:
                        s.cp("dve", kv32.t[0:tn, i - 15, 256:512], ps.t[0:tn, 0:256], [ps.r], [kv32.r, ps.r])
            else:
                zb = blk - 10
                for (t0, tn) in TBS:
                    st = zst[qi % 2]
                    qi += 1
                    for j in range(2):
                        ps = s.ps()
                        s.ws_mm(ps.t[:, 0:tn], ps.r, wt, (j * 128, j * 128 + 128), xT, KC, t0, tn, [xT.r])
                        s.act(st.t[:, j, 0:tn], ps.t[:, 0:tn], AF.Silu, [ps.r], [st.r])
                    s.dma("sp", ZTap[:, zb * 2:zb * 2 + 2, t0:t0 + tn], st.t[:, :, 0:tn], [st.r], [s.ZT.r])
        if os.environ.get("SKIPOUT"):
            return
        s.dma("sp", s.out["a_kv_p"].t.ap()[:, :], kv32.t[:, 0, :], [kv32.r], [s.out["a_kv_p"].r])
        cache = s.inp["a_cache"].t.ap()
        oks = s.out["a_kv_s"].t.ap()
        for bi in range(4):
            s.dma("sp", oks[bi, 0:124, :], cache[bi, 4:128, :], [], [s.out["a_kv_s"].r])
            s.dma("sp", oks[bi, 124:128, :], kv32.t[bi * 4:bi * 4 + 4, 1, :], [kv32.r], [s.out["a_kv_s"].r])

    def swa_attn(self, kT, vtok):
        s = self
        QTap = s.QT.t.ap()
        ZTap = s.ZT.t.ap()
        Gap = s.GTd.t.ap()
        cache = s.inp["a_cache"].t.ap()
        bm = s.sb([128, 32, 256], BF16, name="a_bm")
        bs = s.sb([64, 2, 4, 144], BF16, name="a_bs")
        sink = s.sb([128, 34], F32, name="a_sink")
        tmpb = s.nxt(s.tokp, "toki")
        s.dma("sp", sink.t[:, 0:32], s.inp["a_sink_p"].t.ap()[:, :], [], [sink.r])
        s.dma("sp", sink.t[0:64, 32:34], s.inp["a_sink_s"].t.ap()[:, :], [], [sink.r])
        mk = s.nxt(s.ev4, "ev4i")
        s.dma("sp", mk.t[:, 0:256], s.inp["a_mask_p"].t.ap()[:, :], [], [mk.r])
        for hq in range(4):
            tmpb = s.nxt(s.tokp, "toki")
            s.dma("sp", tmpb.t[:, :].rearrange("p (a b) -> p a b", a=8), s.inp["a_bias_p"].t.ap()[:, hq * 8:hq * 8 + 8, :],
                  [], [tmpb.r])
            s.tt("dve", bm.t[:, hq * 8:hq * 8 + 8, :], tmpb.t[:, :].rearrange("p (a b) -> p a b", a=8),
                 mk.t[:, 0:256].unsqueeze(1).to_broadcast([128, 8, 256]), ALU.add, [tmpb.r, mk.r], [bm.r])
        tmps = s.nxt(s.tokp, "toki")
        s.dma("sp", tmps.t[0:64, 0:1152], s.inp["a_bias_s"].t.ap()[:, :], [], [tmps.r])
        s.dma("sp", tmps.t[0:64, 1152:2304 - 256], s.inp["a_mask_s"].t.ap()[:, 0:896], [], [tmps.r])
        tmps2 = s.nxt(s.ev4, "ev4i")
        s.dma("sp", tmps2.t[0:64, 0:256], s.inp["a_mask_s"].t.ap()[:, 896:1152], [], [tmps2.r])
        bsf = bs.t[:, :, :, :].rearrange("p a b c -> p (a b c)")
        s.tt("dve", bsf[:, 0:896], tmps.t[0:64, 0:896], tmps.t[0:64, 1152:2048], ALU.add, [tmps.r], [bs.r])
        s.tt("dve", bsf[:, 896:1152], tmps.t[0:64, 896:1152], tmps2.t[0:64, 0:256], ALU.add, [tmps.r, tmps2.r], [bs.r])
        qb_p = s.pool([64, 32, 128], BF16, 2)
        zb_p = s.pool([128, 16, 128], BF16, 2)
        gs_p = s.pool([128, 16, 128], BF16, 2)
        Ep = s.pool([128, 2, 256], BF16, 2)
        Pp = s.pool([128, 2, 256], BF16, 2)
        PTp = s.pool([128, 4, 128], BF16, 2)
        smp = s.pool([128, 16], F32, 4)
        it = 0
        for n in range(16):
            t0 = n * 128
            qb = qb_p[n % 2]
            zb = zb_p[n % 2]
            gs = gs_p[n % 2]
            s.dma("sp", qb.t[:, :, :], QTap[:, :, t0:t0 + 128], [s.QT.r], [qb.r])
            s.dma("sp", zb.t[:, :, :], ZTap[:, 0:16, t0:t0 + 128], [s.ZT.r], [zb.r])
            nk = 128 if n == 0 else 256
            k0 = 0 if n == 0 else t0 - 128
            bo = 128 if n == 0 else 0
            nkt = nk // 128
            for pr in range(16):
                kvh = pr // 4
                E = Ep[it % 2]
                P = Pp[it % 2]
                PT = PTp[it % 2]
                sm = smp[it % 4]
                it += 1
                ps = s.ps()
                for j in range(2):
                    h = pr * 2 + j
                    s.mm(ps.t[:, j * 256:j * 256 + nk], s.idb.t[:, :], bm.t[:, h, bo:bo + nk], True, False,
                         [s.idb.r, bm.r], [ps.r], inc=False)
                    s.mm(ps.t[:, j * 256:j * 256 + nk], qb.t[:, h, :], kT.t[:, kvh, k0:k0 + nk], False, True,
                         [qb.r, kT.r], [ps.r], inc=(j == 1))
                pv = ps.t[:, :].rearrange("p (a b) -> p a b", a=2)[:, :, 0:nk]
                s.red("dve", sm.t[:, 0:2], pv, ALU.max, [ps.r], [sm.r])
                s.tt("dve", sm.t[:, 0:2], sm.t[:, 0:2], sink.t[:, pr * 2:pr * 2 + 2], ALU.max, [sm.r, sink.r], [sm.r])
                s.ts("dve", sm.t[:, 2:4], sm.t[:, 0:2], -1.0, None, ALU.mult, None, [sm.r], [sm.r])
                for j in range(2):
                    s.act(E.t[:, j, 0:nk], ps.t[:, j * 256:j * 256 + nk], AF.Exp, [ps.r, sm.r], [E.r, sm.r],
                          bias=sm.t[:, 2 + j:3 + j], scale=1.0, accum_out=sm.t[:, 4 + j:5 + j])
                s.tt("dve", sm.t[:, 6:8], sink.t[:, pr * 2:pr * 2 + 2], sm.t[:, 0:2], ALU.subtract, [sm.r, sink.r], [sm.r])
                s.act(sm.t[:, 8:10], sm.t[:, 6:8], AF.Exp, [sm.r], [sm.r])
                s.tt("dve", sm.t[:, 10:12], sm.t[:, 8:10], sm.t[:, 4:6], ALU.add, [sm.r], [sm.r])
                s.op("dve", lambda g: g.reciprocal(sm.t[:, 12:14], sm.t[:, 10:12]), [sm.r], [sm.r])
                s.tt("dve", P.t[:, :, 0:nk], E.t[:, :, 0:nk], sm.t[:, 12:14].unsqueeze(2).to_broadcast([128, 2, nk]),
                     ALU.mult, [E.r, sm.r], [P.r])
                pst = s.ps()
                pstb = pst.t[:, :].bitcast(BF16)
                for j in range(2):
                    for kt in range(nkt):
                        ix = j * nkt + kt
                        s.tp(pstb[:, ix * 128:(ix + 1) * 128], P.t[:, j, kt * 128:(kt + 1) * 128], s.idb.t[:, :],
                             [P.r, s.idb.r], [pst.r], inc=(ix == 2 * nkt - 1))
                s.cp(s.ev_eng(), PT.t[:, 0:2 * nkt, :], pstb[:, 0:2 * nkt * 128].rearrange("p (a b) -> p a b", a=2 * nkt),
                     [pst.r], [PT.r])
                po = s.ps()
                for j in range(2):
                    for kt in range(nkt):
                        ktile = n if n == 0 else n - 1 + kt
                        s.mm(po.t[j * 64:(j + 1) * 64, 0:128], vtok.t[:, ktile, kvh * 64:(kvh + 1) * 64],
                             PT.t[:, j * nkt + kt, :], kt == 0, kt == nkt - 1, [vtok.r, PT.r], [po.r],
                             inc=(j == 1 and kt == nkt - 1))
                s.tt("dve", gs.t[:, pr, :], po.t[:, 0:128], zb.t[:, pr, :], ALU.mult, [po.r, zb.r], [gs.r])
            s.dma("sp", Gap[:, 0:16, t0:t0 + 128], gs.t[:, :, :], [gs.r], [s.GTd.r])
        qs = s.sb([64, 32, NS], BF16, name="a_qs")
        zs = s.sb([128, 16, NS], BF16, name="a_zs")
        gss = s.sb([128, 16, NS], BF16, name="a_gss")
        s.dma("sp", qs.t[:, :, :], QTap[:, :, 2048:TT], [s.QT.r], [qs.r])
        qs2 = s.sb([64, 4, 128], BF16, name="a_qs2")
        for kvh in range(4):
            for par in range(2):
                o_ = qs2.t[:, :, kvh * 32 + par * 16:kvh * 32 + par * 16 + 16].rearrange("p b (g t) -> p g b t", g=4)
                i_ = qs.t[:, kvh * 8 + par:kvh * 8 + 8:2, :].rearrange("p g (b t) -> p g b t", b=4)
                s.cp("dve", o_, i_, [qs.r], [qs2.r])
        s.dma("sp", zs.t[:, :, :], ZTap[:, 0:16, 2048:TT], [s.ZT.r], [zs.r])
        cst = s.pool([128, 512], F32, 2)
        cbf = s.pool([128, 512], BF16, 2)
        kTs = s.pool([64, 4, 144], BF16, 2)
        Es = s.pool([64, 2, 144], BF16, 2)
        PTs = s.pool([128, 2, 128], BF16, 2)
        for bi in range(4):
            c32 = cst[bi % 2]
            cb = cbf[bi % 2]
            kts = kTs[bi % 2]
            E = Es[bi % 2]
            PT = PTs[bi % 2]
            sm = smp[bi % 4]
            s.dma("sp", c32.t[:, :], cache[bi, :, :], [], [c32.r])
            s.cp("dve", cb.t[:, :], c32.t[:, :], [c32.r], [cb.r])
            pst = s.ps()
            pstb = pst.t[:, :].bitcast(BF16)
            for kvh in range(4):
                s.tp(pstb[0:64, kvh * 128:(kvh + 1) * 128], cb.t[:, kvh * 64:(kvh + 1) * 64], s.idb.t[:, :],
                     [cb.r, s.idb.r], [pst.r], inc=(kvh == 3))
            s.cp("act", kts.t[:, :, 0:128], pstb[0:64, 0:512].rearrange("p (a b) -> p a b", a=4), [pst.r], [kts.r])
            s.cp("dve", kts.t[:, :, 128:144], kT.t[:, :, 2048:TT], [kT.r], [kts.r])
            pst2 = s.ps()
            pst2b = pst2.t[:, :].bitcast(BF16)
            for half in range(2):
                sm = smp[(bi * 2 + half) % 4]
                ps = s.ps()
                s.mm(ps.t[0:64, 0:144], s.idb.t[0:64, 0:64], bs.t[:, half, bi, :], True, False, [s.idb.r, bs.r], [ps.r],
                     inc=False)
                for k2 in range(2):
                    kvh = half * 2 + k2
                    s.mm(ps.t[k2 * 32:(k2 + 1) * 32, 0:144], qs2.t[:, bi, kvh * 32:(kvh + 1) * 32],
                         kts.t[:, kvh, :], False, k2 == 1, [qs2.r, kts.r], [ps.r], inc=(k2 == 1))
                sk = sink.t[0:64, 32 + half:33 + half]
                s.red("dve", sm.t[0:64, 0:1], ps.t[0:64, 0:144], ALU.max, [ps.r], [sm.r])
                s.tt("dve", sm.t[0:64, 0:1], sm.t[0:64, 0:1], sk, ALU.max, [sm.r, sink.r], [sm.r])
                s.ts("dve", sm.t[0:64, 2:3], sm.t[0:64, 0:1], -1.0, None, ALU.mult, None, [sm.r], [sm.r])
                s.act(E.t[0:64, half, :], ps.t[0:64, 0:144], AF.Exp, [ps.r, sm.r], [E.r, sm.r], bias=sm.t[0:64, 2:3],
                      scale=1.0, accum_out=sm.t[0:64, 4:5])
                s.tt("dve", sm.t[0:64, 6:7], sk, sm.t[0:64, 0:1], ALU.subtract, [sm.r, sink.r], [sm.r])
                s.act(sm.t[0:64, 8:9], sm.t[0:64, 6:7], AF.Exp, [sm.r], [sm.r])
                s.tt("dve", sm.t[0:64, 10:11], sm.t[0:64, 8:9], sm.t[0:64, 4:5], ALU.add, [sm.r], [sm.r])
                s.op("dve", lambda g: g.reciprocal(sm.t[0:64, 12:13], sm.t[0:64, 10:11]), [sm.r], [sm.r])
                s.ts("dve", E.t[0:64, half, :], E.t[0:64, half, :], sm.t[0:64, 12:13], None, ALU.mult, None,
                     [E.r, sm.r], [E.r])
                s.tp(pst2b[:, half * 64:(half + 1) * 64], E.t[0:64, half, 0:128], s.idb.t[0:64, 0:64],
                     [E.r, s.idb.r], [pst2.r], inc=False)
                s.tp(pst2b[0:16, 128 + half * 64:128 + (half + 1) * 64], E.t[0:64, half, 128:144], s.idb.t[0:64, 0:64],
                     [E.r, s.idb.r], [pst2.r], inc=True)
            s.cp("act", PT.t[:, 0, :], pst2b[:, 0:128], [pst2.r], [PT.r])
            s.cp("dve", PT.t[0:16, 1, :], pst2b[0:16, 128:256], [pst2.r], [PT.r, pst2.r])
            po = s.ps()
            for kvh in range(4):
                for par in range(2):
                    r0 = kvh * 32 + par * 16
                    rows = PT.t[:, 0, r0:r0 + 16]
                    rows2 = PT.t[0:16, 1, r0:r0 + 16]
                    o_ap = po.t[par * 64:(par + 1) * 64, kvh * 16:(kvh + 1) * 16]
                    s.mm(o_ap, cb.t[:, 256 + kvh * 64:256 + (kvh + 1) * 64], rows, True, False, [cb.r, PT.r], [po.r],
                         inc=False)
                    s.mm(o_ap, vtok.t[0:16, 16, kvh * 64:(kvh + 1) * 64], rows2, False, True, [vtok.r, PT.r], [po.r],
                         inc=(kvh == 3 and par == 1))
            s.tt("dve", gss.t[:, :, bi * 4:bi * 4 + 4],
                 po.t[:, 0:64].rearrange("p (a b) -> p a b", a=16), zs.t[:, :, bi * 4:bi * 4 + 4], ALU.mult,
                 [po.r, zs.r], [gss.r])
        s.dma("sp", Gap[:, 0:16, 2048:TT], gss.t[:, :, :], [gss.r], [s.GTd.r])

    def setup_s5(self):
        s = self
        s.din("s5_w_in", [D, 4096])
        s.din("s5_w_glu", [D, D])
        s.din("s5_w_out", [D, D])
        s.din("s5_a_re", [128, 64])
        s.din("s5_a_im", [128, 64])
        s.din("s5_log_dt", [128, 1])
        s.din("s5_b_re", [128, 64, 16])
        s.din("s5_b_im", [128, 64, 16])
        s.din("s5_c_re", [128, 16, 64])
        s.din("s5_c_im", [128, 16, 64])
        s.din("s5_d_fm", [128, 16])
        s.din("s5_state", [4, 128, 128])
        s.din("s5_const", [128, 256])
        s.dout("s5_p", [128, 128])
        s.dout("s5_s", [4, 128, 128])
        s.UT = s.dram("UT", [128, 16, TT], BF16)
        s.Y1T = s.dram("Y1T", [128, 16, TT], BF16)

    def s5_proj(self, xT):
        s = self
        W = s.inp["s5_w_in"].t.ap()
        ust = s.pool([128, 2, 512], BF16, 2)
        KC = list(range(16))
        UTap = s.UT.t.ap()
        ZTap = s.ZT.t.ap()
        qi = 0
        for blk in range(16):
            wt = s.nxt(s.wp, "wi")
            s.load_w(wt, W, D, blk * 256, 256)
            for (t0, tn) in TBS:
                st = ust[qi % 2]
                qi += 1
                for j in range(2):
                    ps = s.ps()
                    s.ws_mm(ps.t[:, 0:tn], ps.r, wt, (j * 128, j * 128 + 128), xT, KC, t0, tn, [xT.r])
                    if blk < 8:
                        s.cp(s.ev_eng(), st.t[:, j, 0:tn], ps.t[:, 0:tn], [ps.r], [st.r])
                    else:
                        s.act(st.t[:, j, 0:tn], ps.t[:, 0:tn], AF.Silu, [ps.r], [st.r])
                if blk < 8:
                    s.dma("sp", UTap[:, blk * 2:blk * 2 + 2, t0:t0 + tn], st.t[:, :, 0:tn], [st.r], [s.UT.r])
                else:
                    zb = blk - 8
                    s.dma("sp", ZTap[:, zb * 2:zb * 2 + 2, t0:t0 + tn], st.t[:, :, 0:tn], [st.r], [s.ZT.r])

    def cmul(self, e, o_re, o_im, a_re, a_im, b_re, b_im, tmp, rs, ws):
        s = self
        s.tt(e, o_re, a_re, b_re, ALU.mult, rs, ws)
        s.tt(e, tmp, a_im, b_im, ALU.mult, rs, ws)
        s.tt(e, o_re, o_re, tmp, ALU.subtract, rs + ws, ws)
        s.tt(e, o_im, a_re, b_im, ALU.mult, rs, ws)
        s.tt(e, tmp, a_im, b_re, ALU.mult, rs, ws)
        s.tt(e, o_im, o_im, tmp, ALU.add, rs + ws, ws)

    def s5_core(self):
        s = self
        PI = float(np.pi)
        cst = s.sb([128, 256], F32, name="s5c")
        s.dma("sp", cst.t[:, :], s.inp["s5_const"].t.ap()[:, :], [], [cst.r])
        selp = cst.t[:, 0:64]
        pm = cst.t[:, 64:66]
        lm = cst.t[:, 66:68]
        I2 = cst.t[:, 68:132]
        L = s.sb([128, 16, 64], F32, name="s5L")
        NST = 11
        apw = s.sb([128, NST, 3, 64], F32, name="s5apw")
        ah0 = s.sb([128, 4, 2, 64], F32, name="s5ah0")
        Bb = s.sb([128, 2, 64, 16], F32, name="s5Bb")
        with s.scope():
            pr_ = s.sb([128, 16, 64], F32, name="s5par")
            P = pr_.t
            R_ = [pr_.r]
            s.dma("sp", P[:, 0, :], s.inp["s5_a_re"].t.ap()[:, :], [], R_)
            s.dma("sp", P[:, 1, :], s.inp["s5_a_im"].t.ap()[:, :], [], R_)
            s.dma("sp", P[:, 2, 0:1], s.inp["s5_log_dt"].t.ap()[:, :], [], R_)
            s.act(P[:, 2, 1:2], P[:, 2, 0:1], AF.Exp, R_, R_)
            s.act(P[:, 3, :], P[:, 0, :], AF.Exp, R_, R_, scale=P[:, 2, 1:2])
            s.ts("dve", P[:, 4, :], P[:, 1, :], P[:, 2, 1:2], None, ALU.mult, None, R_, R_)

            def rangered(off):
                s.ts("dve", P[:, 5, :], P[:, 4, :], off, None, ALU.add, None, R_, R_)
                s.cp("dve", P[:, 15, :], P[:, 5, :], R_, R_)
                for j in range(1, 8):
                    s.ts("dve", P[:, 11, :], P[:, 15, :], 2 * PI * j - PI, -2 * PI, ALU.is_ge, ALU.mult, R_, R_)
                    s.tt("dve", P[:, 5, :], P[:, 5, :], P[:, 11, :], ALU.add, R_, R_)
            rangered(0.0)
            s.act(P[:, 6, :], P[:, 5, :], AF.Sin, R_, R_)
            rangered(0.5 * PI)
            s.act(P[:, 7, :], P[:, 5, :], AF.Sin, R_, R_)
            s.tt("dve", P[:, 8, :], P[:, 3, :], P[:, 7, :], ALU.mult, R_, R_)
            s.tt("dve", P[:, 9, :], P[:, 3, :], P[:, 6, :], ALU.mult, R_, R_)
            s.ts("dve", P[:, 10, :], P[:, 8, :], -1.0, None, ALU.add, None, R_, R_)
            s.tt("dve", P[:, 11, :], P[:, 0, :], P[:, 0, :], ALU.mult, R_, R_)
            s.tt("dve", P[:, 12, :], P[:, 1, :], P[:, 1, :], ALU.mult, R_, R_)
            s.tt("dve", P[:, 11, :], P[:, 11, :], P[:, 12, :], ALU.add, R_, R_)
            s.op("dve", lambda g: g.reciprocal(P[:, 12, :], P[:, 11, :]), R_, R_)
            s.tt("dve", P[:, 13, :], P[:, 10, :], P[:, 0, :], ALU.mult, R_, R_)
            s.tt("dve", P[:, 14, :], P[:, 9, :], P[:, 1, :], ALU.mult, R_, R_)
            s.tt("dve", P[:, 13, :], P[:, 13, :], P[:, 14, :], ALU.add, R_, R_)
            s.tt("dve", P[:, 13, :], P[:, 13, :], P[:, 12, :], ALU.mult, R_, R_)
            s.tt("dve", P[:, 14, :], P[:, 9, :], P[:, 0, :], ALU.mult, R_, R_)
            s.tt("dve", P[:, 15, :], P[:, 10, :], P[:, 1, :], ALU.mult, R_, R_)
            s.tt("dve", P[:, 14, :], P[:, 14, :], P[:, 15, :], ALU.subtract, R_, R_)
            s.tt("dve", P[:, 14, :], P[:, 14, :], P[:, 12, :], ALU.mult, R_, R_)
            st0 = s.sb([128, 4, 128], F32, name="s5st")
            s.dma("sp", st0.t[:, :, :], s.inp["s5_state"].t.ap().rearrange("b g x -> g b x"), [], [st0.r])
            dm = s.sb([128, 2, 64], F32, name="s5dm")

            def to_lanes(src_ap, src_r, dst_idx):
                s.tt("dve", dm.t[:, :, :], src_ap.unsqueeze(1).to_broadcast([128, 2, 64]),
                     pm.unsqueeze(2).to_broadcast([128, 2, 64]), ALU.mult, src_r + [cst.r], [dm.r])
                ps = s.ps()
                s.mm(ps.t[:, 0:64], dm.t[:, :, :].rearrange("p a b -> p (a b)"), selp, True, True, [dm.r, cst.r], [ps.r])
                s.cp("dve", L.t[:, dst_idx, :], ps.t[:, 0:64], [ps.r], [L.r])

            for i, k in enumerate((8, 9, 13, 14)):
                to_lanes(P[:, k, :], R_, i)
            for b in range(4):
                for c in range(2):
                    to_lanes(st0.t[:, b, :].rearrange("p (x c) -> p x c", c=2)[:, :, c], [st0.r], 4 + b * 2 + c)
            tmpl = s.sb([128, 64], F32, name="s5tmpl")
            s.cp("dve", apw.t[:, 0, 0, :], L.t[:, 0, :], [L.r], [apw.r])
            s.cp("dve", apw.t[:, 0, 1, :], L.t[:, 1, :], [L.r], [apw.r])
            for k in range(1, NST):
                s.cmul("dve", apw.t[:, k, 0, :], apw.t[:, k, 1, :], apw.t[:, k - 1, 0, :], apw.t[:, k - 1, 1, :],
                       apw.t[:, k - 1, 0, :], apw.t[:, k - 1, 1, :], tmpl.t[:, :], [apw.r], [apw.r, tmpl.r])
            for k in range(NST):
                s.ts("dve", apw.t[:, k, 2, :], apw.t[:, k, 1, :], -1.0, None, ALU.mult, None, [apw.r], [apw.r])
            for b in range(4):
                s.cmul("dve", ah0.t[:, b, 0, :], ah0.t[:, b, 1, :], L.t[:, 0, :], L.t[:, 1, :], L.t[:, 4 + b * 2, :],
                       L.t[:, 5 + b * 2, :], tmpl.t[:, :], [L.r], [ah0.r, tmpl.r])
            Bn = s.sb([128, 2, 64, 16], F32, name="s5Bn")
            tmpB = s.sb([128, 64, 16], F32, name="s5tmpB")
            for c, nm in enumerate(("s5_b_re", "s5_b_im")):
                bap = s.inp[nm].t.ap()
                for q8 in range(8):
                    srcap = bass.AP(bap.tensor, q8 * 8 * 2048, [[16, 128], [2048, 8], [1, 16]])
                    s.dma("sp", Bn.t[:, c, q8 * 8:(q8 + 1) * 8, :], srcap, [], [Bn.r])
            cre_b = L.t[:, 2, :].unsqueeze(2).to_broadcast([128, 64, 16])
            cim_b = L.t[:, 3, :].unsqueeze(2).to_broadcast([128, 64, 16])
            s.cmul("dve", Bb.t[:, 0, :, :], Bb.t[:, 1, :, :], cre_b, cim_b, Bn.t[:, 0, :, :], Bn.t[:, 1, :, :],
                   tmpB.t[:, :, :], [L.r, Bn.r], [Bb.r, tmpB.r])
        Ct = s.sb([128, 2, 16, 64], F32, name="s5Ct")
        for c, nm in enumerate(("s5_c_re", "s5_c_im")):
            cap = s.inp[nm].t.ap()
            for h2 in range(2):
                srcap = bass.AP(cap.tensor, h2 * 8 * 8192, [[64, 128], [8192, 8], [1, 64]])
                s.dma("sp", Ct.t[:, c, h2 * 8:(h2 + 1) * 8, :], srcap, [], [Ct.r])
        s.ts("dve", Ct.t[:, 1, :, :], Ct.t[:, 1, :, :], -1.0, None, ALU.mult, None, [Ct.r], [Ct.r])
        pm8 = s.sb([128, 2], F32, name="s5pm8")
        s.dma("sp", pm8.t[:, :], s.inp["s5_const"].t.ap()[:, 132:134], [], [pm8.r])
        dfm = s.sb([128, 16], F32, name="s5d")
        s.dma("sp", dfm.t[:, :], s.inp["s5_d_fm"].t.ap()[:, :], [], [dfm.r])
        CTp = s.sb([128, 4, 2, 128], F32, name="s5CTp")
        s.memset("pool", CTp.t[:, :, :, :], 0.0, [CTp.r])
        Ssrc = [s.sb([128, 128], F32, name="s5S") for _ in range(2)]
        for t_ in Ssrc:
            s.memset("pool", t_.t[:, :], 0.0, [t_.r])
        BTp = s.pool([128, 2, 128], BF16, 2)
        Cdm = s.sb([128, 2, 64], F32, name="s5Cdm")
        Hb = [s.sb([128, 2, TT], F32, name="s5H") for _ in range(5)]
        Hl = s.sb([128, 5, 2, 64], F32, name="s5Hl")
        ug_p = s.pool([128, TT], BF16, 2)
        yst = s.pool([128, TT], BF16, 2)
        gtmp = s.pool([128, 512], F32, 2)
        ptmp = s.sb([128, SEQ], F32, name="s5ptmp")
        UTap = s.UT.t.ap()
        Y1ap = s.Y1T.t.ap()
        Tm = Hb[4]
        LP = SEQ
        for gc in range(16):
            ug = ug_p[gc % 2]
            s.dma("sp", ug.t[:, :], UTap[:, gc, :], [s.UT.r], [ug.r])
            for pr in range(4):
                pair = gc * 4 + pr
                X = Hb[pr]
                bt = BTp[pair % 2]
                for c in range(2):
                    S_ = Ssrc[c]
                    if pr > 0:
                        pc = (2 * (pr - 1)) * 16
                        s.memset("pool", S_.t[:, pc:pc + 32], 0.0, [S_.r])
                    elif gc > 0:
                        s.memset("pool", S_.t[:, 96:128], 0.0, [S_.r])
                    c0 = 2 * pr * 16
                    s.cp("dve", S_.t[0:64, c0:c0 + 16], Bb.t[0:64, c, pair, :], [Bb.r], [S_.r])
                    s.cp("dve", S_.t[64:128, c0 + 16:c0 + 32], Bb.t[64:128, c, pair, :], [Bb.r], [S_.r])
                    ps = s.ps()
                    s.tp(ps.t[:, 0:128], S_.t[:, :], s.idf.t[:, :], [S_.r, s.idf.r], [ps.r])
                    s.cp("act", bt.t[:, c, :], ps.t[:, 0:128], [ps.r], [bt.r])
                for (t0, tn) in TBS:
                    for c in range(2):
                        ps = s.ps()
                        s.mm(ps.t[:, 0:tn], bt.t[:, c, :], ug.t[:, t0:t0 + tn], True, True, [bt.r, ug.r], [ps.r])
                        s.cp("act" if c else "dve", Tm.t[:, c, t0:t0 + tn], ps.t[:, 0:tn], [ps.r], [Tm.r])
                for c in range(2):
                    sv = Tm.t[:, c, LP:TT].rearrange("p (b t) -> p b t", b=4)[:, :, 0]
                    s.tt("dve", sv, sv, ah0.t[:, :, c, pair], ALU.add, [Tm.r, ah0.r], [Tm.r])
                src, dst = Tm, X
                for k in range(NST):
                    sh = 1 << k
                    ar = apw.t[:, k, 0, pair:pair + 1]
                    ai = apw.t[:, k, 1, pair:pair + 1]
                    nai = apw.t[:, k, 2, pair:pair + 1]
                    for c, e in ((0, "dve"), (1, "dve")):
                        oth = 1 - c
                        s.cp(e, dst.t[:, c, 0:sh], src.t[:, c, 0:sh], [src.r], [dst.r])
                        if e == "dve":
                            s.stt(e, dst.t[:, c, sh:LP], src.t[:, c, 0:LP - sh], ar, src.t[:, c, sh:LP], ALU.mult, ALU.add,
                                  [src.r, apw.r], [dst.r])
                            s.stt(e, dst.t[:, c, sh:LP], src.t[:, oth, 0:LP - sh], nai if c == 0 else ai,
                                  dst.t[:, c, sh:LP], ALU.mult, ALU.add, [src.r, apw.r, dst.r], [dst.r])
                        else:
                            s.ts(e, dst.t[:, c, sh:LP], src.t[:, c, 0:LP - sh], ar, None, ALU.mult, None,
                                 [src.r, apw.r], [dst.r])
                            s.tt(e, dst.t[:, c, sh:LP], dst.t[:, c, sh:LP], src.t[:, c, sh:LP], ALU.add,
                                 [src.r, dst.r], [dst.r])
                            s.ts(e, ptmp.t[:, 0:LP - sh], src.t[:, oth, 0:LP - sh], nai if c == 0 else ai, None, ALU.mult,
                                 None, [src.r, apw.r], [ptmp.r])
                            s.tt(e, dst.t[:, c, sh:LP], dst.t[:, c, sh:LP], ptmp.t[:, 0:LP - sh], ALU.add,
                                 [ptmp.r, dst.r], [dst.r])
                        sv = src.t[:, c, LP:TT].rearrange("p (b t) -> p b t", b=4)
                        so = src.t[:, oth, LP:TT].rearrange("p (b t) -> p b t", b=4)
                        dv = dst.t[:, c, LP:TT].rearrange("p (b t) -> p b t", b=4)
                        if sh < 4:
                            s.cp("dve", dv[:, :, 0:sh], sv[:, :, 0:sh], [src.r], [dst.r])
                            s.stt("dve", dv[:, :, sh:4], sv[:, :, 0:4 - sh], ar, sv[:, :, sh:4], ALU.mult, ALU.add,
                                  [src.r, apw.r], [dst.r])
                            s.stt("dve", dv[:, :, sh:4], so[:, :, 0:4 - sh], nai if c == 0 else ai, dv[:, :, sh:4],
                                  ALU.mult, ALU.add, [src.r, apw.r, dst.r], [dst.r])
                        else:
                            s.cp(e, dv, sv, [src.r], [dst.r])
                    src, dst = dst, src
                for c in range(2):
                    s.cp("dve", Hl.t[:, 0, c, pair:pair + 1], X.t[:, c, LP - 1:LP], [X.r], [Hl.r])
                    s.cp("dve", Hl.t[:, 1:5, c, pair], X.t[:, c, LP:TT].rearrange("p (b t) -> p b t", b=4)[:, :, 3],
                         [X.r], [Hl.r])
            for c in range(2):
                s.tt("dve", Cdm.t[:, :, :], Ct.t[:, c, gc, :].unsqueeze(1).to_broadcast([128, 2, 64]),
                     pm8.t[:, :].unsqueeze(2).to_broadcast([128, 2, 64]), ALU.mult, [Ct.r, pm8.r], [Cdm.r])
                ps = s.ps()
                s.tp(ps.t[:, 0:128], Cdm.t[:, :, :].rearrange("p a b -> p (a b)"), s.idf.t[:, :], [Cdm.r, s.idf.r], [ps.r])
                for pr in range(4):
                    s.cp("dve", CTp.t[:, pr, c, pr * 32:(pr + 1) * 32], ps.t[:, pr * 32:(pr + 1) * 32], [ps.r], [CTp.r])
            ys = yst[gc % 2]
            for (t0, tn) in TBS:
                ps = s.ps()
                n = 0
                for pr in range(4):
                    for c in range(2):
                        s.mm(ps.t[:, 0:tn], CTp.t[:, pr, c, :], Hb[pr].t[:, c, t0:t0 + tn], n == 0, n == 7,
                             [CTp.r, Hb[pr].r], [ps.r])
                        n += 1
                g1 = gtmp[0]
                g2 = gtmp[1]
                s.stt("dve", g1.t[:, 0:tn], ug.t[:, t0:t0 + tn], dfm.t[:, gc:gc + 1], ps.t[:, 0:tn], ALU.mult, ALU.add,
                      [ug.r, dfm.r, ps.r], [g1.r])
                s.act(g2.t[:, 0:tn], g1.t[:, 0:tn], AF.Square, [g1.r], [g2.r])
                s.ts("dve", g2.t[:, 0:tn], g2.t[:, 0:tn], 0.044715, 1.0, ALU.mult, ALU.add, [g2.r], [g2.r])
                s.tt("dve", g2.t[:, 0:tn], g2.t[:, 0:tn], g1.t[:, 0:tn], ALU.mult, [g1.r, g2.r], [g2.r])
                s.act(g2.t[:, 0:tn], g2.t[:, 0:tn], AF.Sigmoid, [g2.r], [g2.r], scale=1.5957691216057308)
                s.tt("dve", ys.t[:, t0:t0 + tn], g2.t[:, 0:tn], g1.t[:, 0:tn], ALU.mult, [g1.r, g2.r], [ys.r])
            s.dma("sp", Y1ap[:, gc, :], ys.t[:, :], [ys.r], [s.Y1T.r])
        lh = s.sb([128, 64, 2], F32, name="s5lh")
        og = s.sb([128, 5, 64, 2], F32, name="s5og")
        for w in range(5):
            for c in range(2):
                s.tt("dve", lh.t[:, :, :], Hl.t[:, w, c, :].unsqueeze(2).to_broadcast([128, 64, 2]),
                     lm.unsqueeze(1).to_broadcast([128, 64, 2]), ALU.mult, [Hl.r, cst.r], [lh.r])
                ps = s.ps()
                s.mm(ps.t[:, 0:64], lh.t[:, :, :].rearrange("p a b -> p (a b)"), I2, True, True, [lh.r, cst.r], [ps.r])
                s.cp("dve", og.t[:, w, :, c], ps.t[:, 0:64], [ps.r], [og.r])
        s.dma("sp", s.out["s5_p"].t.ap()[:, :], og.t[:, 0, :, :].rearrange("p a b -> p (a b)"), [og.r], [s.out["s5_p"].r])
        s.dma("sp", s.out["s5_s"].t.ap().rearrange("b g x -> g b x"),
              og.t[:, 1:5, :, :].rearrange("p w a b -> p w (a b)"), [og.r], [s.out["s5_s"].r])

    def s5_glu(self, y1T):
        s = self
        W = s.inp["s5_w_glu"].t.ap()
        s.dma("sp", y1T.t[:, :, :], s.Y1T.t.ap()[:, :, :], [s.Y1T.r], [y1T.r])
        zst = s.pool([128, 2, 512], BF16, 2)
        gst = s.pool([128, 2, 512], BF16, 2)
        sg = s.pool([128, 512], BF16, 2)
        KC = list(range(16))
        ZTap = s.ZT.t.ap()
        Gap = s.GTd.t.ap()
        qi = 0
        for blk in range(8):
            wt = s.nxt(s.wp, "wi")
            s.load_w(wt, W, D, blk * 256, 256)
            for (t0, tn) in TBS:
                zt = zst[qi % 2]
                gt = gst[qi % 2]
                qi += 1
                s.dma("sp", zt.t[:, :, 0:tn], ZTap[:, blk * 2:blk * 2 + 2, t0:t0 + tn], [s.ZT.r], [zt.r])
                for j in range(2):
                    ch = blk * 2 + j
                    ps = s.ps()
                    s.ws_mm(ps.t[:, 0:tn], ps.r, wt, (j * 128, j * 128 + 128), y1T, KC, t0, tn, [y1T.r])
                    sgt = sg[j]
                    s.act(sgt.t[:, 0:tn], ps.t[:, 0:tn], AF.Sigmoid, [ps.r], [sgt.r])
                    s.tt("dve", sgt.t[:, 0:tn], sgt.t[:, 0:tn], y1T.t[:, ch, t0:t0 + tn], ALU.mult, [sgt.r, y1T.r], [sgt.r])
                    s.tt("dve", gt.t[:, j, 0:tn], sgt.t[:, 0:tn], zt.t[:, j, 0:tn], ALU.mult, [sgt.r, zt.r], [gt.r])
                s.dma("sp", Gap[:, blk * 2:blk * 2 + 2, t0:t0 + tn], gt.t[:, :, 0:tn], [gt.r], [s.GTd.r])

    def setup_gdn(self):
        s = self
        s.din("gdn_w_in", [D, 12352])
        s.din("gdn_w_out", [4096, D])
        s.din("gdn_cw", [128, 64, 4])
        s.din("gdn_ab", [128, 64])
        s.din("gdn_nw", [128, 128])
        s.din("gdn_state", [4, 32, 128, 128])
        s.din("gdn_cbuf", [12, 8192])
        s.din("gdn_mp", [128, 770])
        s.din("gdn_ms", [16, 580])
        s.dout("gd_p", [32, 128, 128])
        s.dout("gd_s", [4, 32, 128, 128])
        s.dout("gc", [15, 8192])
        s.QN = s.dram("QN", [16, 128, TT], BF16)
        s.KN = s.dram("KN", [16, 128, TT], BF16)
        s.VT = s.dram("VT", [32, 128, TT], BF16)
        s.GBd = s.dram("GBd", [128, NTILE, 64], F32)

    def gdn_proj(self, xT):
        s = self
        W = s.inp["gdn_w_in"].t.ap()
        KC = list(range(16))
        cw = s.sb([128, 64, 4], F32, name="g_cw")
        s.dma("sp", cw.t[:, :, :], s.inp["gdn_cw"].t.ap()[:, :, :], [], [cw.r])
        ab = s.sb([128, 96], F32, name="g_ab")
        s.dma("sp", ab.t[:, 0:64], s.inp["gdn_ab"].t.ap()[:, :], [], [ab.r])
        s.act(ab.t[:, 64:96], ab.t[:, 0:32], AF.Exp, [ab.r], [ab.r])
        onesb = s.sb([128, 128], BF16, name="g_ones")
        s.memset("dve", onesb.t[:, :], 1.0, [onesb.r])
        Tl = s.sb([128, 64, 15], F32, name="g_Tl")
        pre_p = s.pool([128, 3 + SEQ], F32, 2)
        pres_p = s.pool([128, 4, 7], F32, 2)
        cv_p = s.pool([128, TT], F32, 2)
        sq_p = s.pool([128, TT], BF16, 2)
        ob_p = s.pool([128, TT], BF16, 2)
        cb_p = s.pool([12, 128], F32, 2)
        zst = s.pool([128, 2, 512], BF16, 2)
        GB = s.sb([128, NTILE, 64], F32, name="g_GB")
        cbap = s.inp["gdn_cbuf"].t.ap()
        ZTap = s.ZT.t.ap()
        qi = 0
        for blk in range(49):
            wt = s.nxt(s.wp, "wi")
            ncol = 256 if blk < 48 else 64
            s.load_w(wt, W, D, blk * 256, ncol)
            if blk < 32:
                for j in range(2):
                    ch = blk * 2 + j
                    pre = pre_p[ch % 2]
                    prs = pres_p[ch % 2]
                    cv = cv_p[ch % 2]
                    s.memset("pool", pre.t[:, 0:3], 0.0, [pre.r])
                    cb = cb_p[ch % 2]
                    s.dma("sp", cb.t[:, :], cbap[:, ch * 128:(ch + 1) * 128], [], [cb.r])
                    psb = s.ps()
                    s.tp(psb.t[:, 0:12], cb.t[:, :], s.idf.t[0:12, 0:12], [cb.r, s.idf.r], [psb.r])
                    s.cp("act", prs.t[:, :, 0:3], psb.t[:, 0:12].rearrange("p (b r) -> p b r", b=4), [psb.r], [prs.r])
                    for (t0, tn) in TBS:
                        ps = s.ps()
                        s.ws_mm(ps.t[:, 0:tn], ps.r, wt, (j * 128, j * 128 + 128), xT, KC, t0, tn, [xT.r])
                        if t0 < SEQ:
                            s.cp("act", pre.t[:, 3 + t0:3 + t0 + tn], ps.t[:, 0:tn], [ps.r], [pre.r])
                        else:
                            s.cp("act", prs.t[:, :, 3:7], ps.t[:, 0:tn].rearrange("p (b t) -> p b t", b=4), [ps.r], [prs.r])
                    s.ts("dve", cv.t[:, 0:SEQ], pre.t[:, 0:SEQ], cw.t[:, ch, 0:1], None, ALU.mult, None, [pre.r, cw.r], [cv.r])
                    for k in range(1, 4):
                        s.stt("dve", cv.t[:, 0:SEQ], pre.t[:, k:k + SEQ], cw.t[:, ch, k:k + 1], cv.t[:, 0:SEQ], ALU.mult,
                              ALU.add, [pre.r, cw.r, cv.r], [cv.r])
                    cvs = cv.t[:, SEQ:TT].rearrange("p (b t) -> p b t", b=4)
                    s.ts("dve", cvs, prs.t[:, :, 0:4], cw.t[:, ch, 0:1], None, ALU.mult, None, [prs.r, cw.r], [cv.r])
                    for k in range(1, 4):
                        s.stt("dve", cvs, prs.t[:, :, k:k + 4], cw.t[:, ch, k:k + 1], cvs, ALU.mult, ALU.add,
                              [prs.r, cw.r, cv.r], [cv.r])
                    s.cp("dve", Tl.t[:, ch, 0:3], pre.t[:, SEQ:SEQ + 3], [pre.r], [Tl.r])
                    s.cp("dve", Tl.t[:, ch, 3:15].rearrange("p (b r) -> p b r", b=4), prs.t[:, :, 4:7], [prs.r], [Tl.r])
                    s.act(cv.t[:, :], cv.t[:, :], AF.Silu, [cv.r], [cv.r])
                    ob = ob_p[ch % 2]
                    if ch < 32:
                        sq = sq_p[ch % 2]
                        s.act(sq.t[:, :], cv.t[:, :], AF.Square, [cv.r], [sq.r])
                        for (t0, tn) in TBS:
                            ps = s.ps()
                            s.mm(ps.t[:, 0:tn], onesb.t[:, :], sq.t[:, t0:t0 + tn], True, True, [onesb.r, sq.r], [ps.r])
                            g1 = s.nxt(s.ev4, "ev4i")
                            s.ts("dve", g1.t[:, 0:tn], ps.t[:, 0:tn], 1e-6, None, ALU.add, None, [ps.r], [g1.r])
                            s.act(g1.t[:, 0:tn], g1.t[:, 0:tn], AF.Sqrt, [g1.r], [g1.r])
                            s.op("dve", lambda g, g1=g1, tn=tn: g.reciprocal(g1.t[:, 0:tn], g1.t[:, 0:tn]), [g1.r], [g1.r])
                            sc = (128 ** -0.5) if ch < 16 else 1.0
                            s.stt("dve", ob.t[:, t0:t0 + tn], cv.t[:, t0:t0 + tn], sc, g1.t[:, 0:tn], ALU.mult, ALU.mult,
                                  [cv.r, g1.r], [ob.r])
                        dst = s.QN if ch < 16 else s.KN
                        s.dma("sp", dst.t.ap()[ch % 16, :, :], ob.t[:, :], [ob.r], [dst.r])
                    else:
                        s.cp("dve", ob.t[:, :], cv.t[:, :], [cv.r], [ob.r])
                        s.dma("sp", s.VT.t.ap()[ch - 32, :, :], ob.t[:, :], [ob.r], [s.VT.r])
            elif blk < 48:
                zb = blk - 32
                for (t0, tn) in TBS:
                    st = zst[qi % 2]
                    qi += 1
                    for j in range(2):
                        ps = s.ps()
                        s.ws_mm(ps.t[:, 0:tn], ps.r, wt, (j * 128, j * 128 + 128), xT, KC, t0, tn, [xT.r])
                        s.act(st.t[:, j, 0:tn], ps.t[:, 0:tn], AF.Silu, [ps.r], [st.r])
                    s.dma("sp", ZTap[:, zb * 2:zb * 2 + 2, t0:t0 + tn], st.t[:, :, 0:tn], [st.r], [s.ZT.r])
            else:
                for i, (t0, tn) in enumerate(TILES):
                    ps = s.ps()
                    s.as_mm(ps.t[0:tn, 0:64], ps.r, wt, (0, 64), xT, KC, t0, tn, [xT.r])
                    g1 = s.nxt(s.ev4, "ev4i")
                    s.tt("dve", g1.t[0:tn, 0:32], ps.t[0:tn, 0:32], ab.t[0:tn, 32:64], ALU.add, [ps.r, ab.r], [g1.r])
                    s.act(g1.t[0:tn, 0:32], g1.t[0:tn, 0:32], AF.Exp, [g1.r], [g1.r])
                    s.act(g1.t[0:tn, 0:32], g1.t[0:tn, 0:32], AF.Ln, [g1.r], [g1.r], bias=1.0, scale=1.0)
                    s.stt("dve", GB.t[0:tn, i, 0:32], g1.t[0:tn, 0:32], -1.0, ab.t[0:tn, 64:96], ALU.mult, ALU.mult,
                          [g1.r, ab.r], [GB.r])
                    s.act(GB.t[0:tn, i, 32:64], ps.t[0:tn, 32:64], AF.Sigmoid, [ps.r], [GB.r, ps.r])
        s.dma("sp", s.GBd.t.ap()[:, :, :], GB.t[:, :, :], [GB.r], [s.GBd.r])
        gcap = s.out["gc"].t.ap()
        rt_p = s.pool([15, 512], F32, 2)
        for g4 in range(16):
            ps = s.ps()
            for j in range(4):
                ch = g4 * 4 + j
                s.tp(ps.t[0:15, j * 128:(j + 1) * 128], Tl.t[:, ch, :], s.idf.t[:, :], [Tl.r, s.idf.r], [ps.r], inc=(j == 3))
            rt = rt_p[g4 % 2]
            s.cp("act", rt.t[:, :], ps.t[0:15, :], [ps.r], [rt.r])
            s.dma("sp", gcap[:, g4 * 512:(g4 + 1) * 512], rt.t[:, :], [rt.r], [s.out["gc"].r])

    def gdn_core(self):
        s = self
        mp = s.sb([128, 770], F32, name="g_mp")
        ms = s.sb([16, 580], F32, name="g_ms")
        s.dma("sp", mp.t[:, :], s.inp["gdn_mp"].t.ap()[:, :], [], [mp.r])
        s.dma("sp", ms.t[:, :], s.inp["gdn_ms"].t.ap()[:, :], [], [ms.r])
        nw = s.sb([128, 128], F32, name="g_nw")
        s.dma("sp", nw.t[:, :], s.inp["gdn_nw"].t.ap()[:, :], [], [nw.r])
        ones = s.sb([128, 128], F32, name="g_ones32")
        s.memset("dve", ones.t[:, :], 1.0, [ones.r])
        GB = s.sb([128, NTILE, 64], F32, name="g_GB2")
        s.dma("sp", GB.t[:, :, :], s.GBd.t.ap()[:, :, :], [s.GBd.r], [GB.r])
        GX = s.sb([128, NTILE, 3, 32], F32, name="g_GX")
        EC = s.sb([128, NTILE, 4, 32], F32, name="g_EC")

        def geom(i):
            if i < 16:
                M = mp.t
                return dict(np=128, ncs=2, triU=M[:, 0:128], bones=M[:, 128:256], Mb=M[:, 256:384], strict=M[:, 384:512],
                            sel=[M[:, 512:640], M[:, 640:768]], cm=M[:, 768:770], mr=mp.r, nlev=5)
            M = ms.t
            return dict(np=16, ncs=4, triU=M[:, 0:16], bones=M[:, 16:32], Mb=M[:, 32:48], strict=M[:, 48:64],
                        sel=[M[:, 64 + c * 128:64 + (c + 1) * 128] for c in range(4)], cm=M[:, 576:580], mr=ms.r, nlev=1)

        for i in range(NTILE):
            G = geom(i)
            n = G["np"]
            g_ap = GB.t[0:n, i, 0:32]
            ps = s.ps()
            s.mm(ps.t[0:n, 0:32], G["triU"], g_ap, True, True, [G["mr"], GB.r], [ps.r])
            s.mm(ps.t[0:n, 32:64], G["bones"], g_ap, True, True, [G["mr"], GB.r], [ps.r])
            s.cp("dve", GX.t[0:n, i, 0, :], ps.t[0:n, 0:32], [ps.r], [GX.r])
            s.act(GX.t[0:n, i, 1, :], ps.t[0:n, 0:32], AF.Exp, [ps.r], [GX.r, ps.r])
            s.tt("dve", GX.t[0:n, i, 2, :], ps.t[0:n, 32:64], GX.t[0:n, i, 0, :], ALU.subtract, [ps.r, GX.r], [GX.r, ps.r])
            s.act(GX.t[0:n, i, 2, :], GX.t[0:n, i, 2, :], AF.Exp, [GX.r], [GX.r])
            ps2 = s.ps()
            for c in range(G["ncs"]):
                s.mm(ps2.t[:, c * 32:(c + 1) * 32], G["sel"][c], g_ap, True, True, [G["mr"], GB.r], [ps2.r],
                     inc=(c == G["ncs"] - 1))
            s.act(EC.t[:, i, 0:G["ncs"], :], ps2.t[:, 0:32 * G["ncs"]].rearrange("p (c h) -> p c h", h=32), AF.Exp,
                  [ps2.r], [EC.r])
        kq_p = s.pool([128, 2, TT], BF16, 2)
        v_p = s.pool([128, TT], BF16, 2)
        z_p = s.pool([128, TT], BF16, 2)
        gs_p = s.pool([128, TT], BF16, 2)
        W = {}
        for nm in ("Gtri", "D", "Ds", "A", "At", "RT", "B0", "Bt0", "B1", "Bt1", "Kbg", "bV", "u", "wT", "P", "PT", "vn",
                   "oa", "qf", "o2", "kd0", "kd1", "kd2", "kd3", "on"):
            W[nm] = s.pool([128, 128], F32, 2)
        colp = s.pool([128, 16], F32, 2)
        Sp = s.sb([128, 128], F32, name="g_S")
        S0 = [s.sb([128, 128], F32, name="g_S0") for _ in range(4)]
        QNap, KNap, VTap, ZTap, Gap = s.QN.t.ap(), s.KN.t.ap(), s.VT.t.ap(), s.ZT.t.ap(), s.GTd.t.ap()
        stap = s.inp["gdn_state"].t.ap()
        it = 0
        GSTEP = int(os.environ.get("G_STEP", "99"))
        for hq in range(int(os.environ.get("G_NHQ", "16"))):
            kq = kq_p[hq % 2]
            s.dma("sp", kq.t[:, 0, :], KNap[hq, :, :], [s.KN.r], [kq.r])
            s.dma("sp", kq.t[:, 1, :], QNap[hq, :, :], [s.QN.r], [kq.r])
            for h in (2 * hq, 2 * hq + 1):
                vT = v_p[h % 2]
                zT = z_p[h % 2]
                gs = gs_p[h % 2]
                s.dma("sp", vT.t[:, :], VTap[h, :, :], [s.VT.r], [vT.r])
                s.dma("sp", zT.t[:, :], ZTap[:, h, :], [s.ZT.r], [zT.r])
                s.memset("pool", Sp.t[:, :], 0.0, [Sp.r])
                for b in range(4):
                    s.dma("sp", S0[b].t[:, :], stap[b, h, :, :], [], [S0[b].r])
                for i in range(NTILE):
                    if i >= int(os.environ.get("G_NT", "17")) and i < 16:
                        continue
                    if i == 16 and os.environ.get("G_NOS"):
                        continue
                    G = geom(i)
                    n = G["np"]
                    t0 = TILES[i][0]
                    w = {k: v[it % 2] for k, v in W.items()}
                    col = colp[it % 2]
                    it += 1
                    gcol = GB.t[0:n, i, h:h + 1]
                    bcol = GB.t[0:n, i, 32 + h:33 + h]
                    gam = GX.t[0:n, i, 0, h:h + 1]
                    eg = GX.t[0:n, i, 1, h:h + 1]
                    edk = GX.t[0:n, i, 2, h:h + 1]
                    mr = G["mr"]
                    kT = kq.t[:, 0, t0:t0 + n]
                    qT = kq.t[:, 1, t0:t0 + n]
                    s.ts("dve", w["Gtri"].t[0:n, 0:n], G["triU"], gcol, None, ALU.mult, None, [mr, GB.r], [w["Gtri"].r])
                    ps = s.ps()
                    s.mm(ps.t[0:n, 0:n], ones.t[0:n, 0:n], w["Gtri"].t[0:n, 0:n], True, False, [ones.r, w["Gtri"].r], [ps.r],
                         inc=False)
                    s.mm(ps.t[0:n, 0:n], s.idf.t[0:n, 0:n], G["Mb"], False, True, [s.idf.r, mr], [ps.r])
                    s.act(w["D"].t[0:n, 0:n], ps.t[0:n, 0:n], AF.Exp, [ps.r, GX.r], [w["D"].r], bias=gam, scale=-1.0)
                    if GSTEP <= 1:
                        continue
                    psk = s.ps()
                    s.mm(psk.t[0:n, 0:n], kT, kT, True, True, [kq.r], [psk.r])
                    s.tt("dve", w["Ds"].t[0:n, 0:n], w["D"].t[0:n, 0:n], G["strict"], ALU.mult, [w["D"].r, mr], [w["Ds"].r])
                    s.stt("dve", w["A"].t[0:n, 0:n], psk.t[0:n, 0:n], bcol, w["Ds"].t[0:n, 0:n], ALU.mult, ALU.mult,
                          [psk.r, GB.r, w["Ds"].r], [w["A"].r])
                    if GSTEP <= 2:
                        continue
                    pst = s.ps()
                    s.tp(pst.t[0:n, 0:n], w["A"].t[0:n, 0:n], s.idf.t[0:n, 0:n], [w["A"].r, s.idf.r], [pst.r])
                    s.cp("act", w["At"].t[0:n, 0:n], pst.t[0:n, 0:n], [pst.r], [w["At"].r])
                    s.tt("dve", w["RT"].t[0:n, 0:n], s.idf.t[0:n, 0:n], pst.t[0:n, 0:n], ALU.subtract, [pst.r, s.idf.r],
                         [w["RT"].r, pst.r])
                    if GSTEP <= 3:
                        continue
                    Bc, Btc = w["A"], w["At"]
                    for lev in range(1, G["nlev"] + 1):
                        Bn_ = w["B%d" % (lev % 2)]
                        Btn = w["Bt%d" % (lev % 2)]
                        p1 = s.ps()
                        s.mm(p1.t[0:n, 0:n], Btc.t[0:n, 0:n], Bc.t[0:n, 0:n], True, True, [Btc.r, Bc.r], [p1.r])
                        s.cp("act", Bn_.t[0:n, 0:n], p1.t[0:n, 0:n], [p1.r], [Bn_.r])
                        if lev < G["nlev"]:
                            p2 = s.ps()
                            s.mm(p2.t[0:n, 0:n], Bc.t[0:n, 0:n], Btc.t[0:n, 0:n], True, True, [Btc.r, Bc.r], [p2.r])
                            s.cp("pool" if False else "act", Btn.t[0:n, 0:n], p2.t[0:n, 0:n], [p2.r], [Btn.r])
                        p3 = s.ps()
                        s.mm(p3.t[0:n, 0:n], Bn_.t[0:n, 0:n], w["RT"].t[0:n, 0:n], True, True, [Bn_.r, w["RT"].r], [p3.r])
                        s.tt("dve", w["RT"].t[0:n, 0:n], w["RT"].t[0:n, 0:n], p3.t[0:n, 0:n], ALU.add, [p3.r, w["RT"].r],
                             [w["RT"].r])
                        Bc, Btc = Bn_, Btn
                    if GSTEP <= 4:
                        continue
                    s.tt("dve", col.t[0:n, 0:1], bcol, eg, ALU.mult, [GB.r, GX.r], [col.r])
                    for c in range(G["ncs"]):
                        s.tt("dve", col.t[0:n, 1 + c:2 + c], edk, G["cm"][:, c:c + 1], ALU.mult, [GX.r, mr], [col.r])
                        s.tt("dve", col.t[0:n, 5 + c:6 + c], eg, G["cm"][:, c:c + 1], ALU.mult, [GX.r, mr], [col.r])
                        s.ts("dve", col.t[0:n, 9 + c:10 + c], G["cm"][:, c:c + 1], -1.0, None, ALU.mult, None, [mr], [col.r])
                    pkt = s.ps()
                    pktb = pkt.t[:, :].bitcast(BF16)
                    s.tp(pktb[0:n, 0:128], kT, s.idb.t[:, :], [kq.r, s.idb.r], [pkt.r], inc=False)
                    s.tp(pktb[0:n, 128:256], vT.t[:, t0:t0 + n], s.idb.t[:, :], [vT.r, s.idb.r], [pkt.r])
                    s.ts("dve", w["Kbg"].t[0:n, :], pktb[0:n, 0:128], col.t[0:n, 0:1], None, ALU.mult, None, [pkt.r, col.r],
                         [w["Kbg"].r])
                    s.ts("dve", w["bV"].t[0:n, :], pktb[0:n, 128:256], bcol, None, ALU.mult, None, [pkt.r, GB.r],
                         [w["bV"].r, pkt.r])
                    for c in range(G["ncs"]):
                        s.ts("dve", w["kd%d" % c].t[0:n, :], pktb[0:n, 0:128], col.t[0:n, 1 + c:2 + c], None, ALU.mult, None,
                             [pkt.r, col.r], [w["kd%d" % c].r, pkt.r])
                    if GSTEP <= 5:
                        continue
                    pu = s.ps()
                    s.mm(pu.t[0:n, 0:128], w["RT"].t[0:n, 0:n], w["bV"].t[0:n, :], True, True, [w["RT"].r, w["bV"].r], [pu.r])
                    s.cp("act", w["vn"].t[0:n, :], pu.t[0:n, 0:128], [pu.r], [w["vn"].r])
                    pw = s.ps()
                    s.mm(pw.t[:, 0:n], w["Kbg"].t[0:n, :], w["RT"].t[0:n, 0:n], True, True, [w["RT"].r, w["Kbg"].r], [pw.r])
                    s.cp("act", w["wT"].t[:, 0:n], pw.t[:, 0:n], [pw.r], [w["wT"].r])
                    if GSTEP <= 6:
                        continue
                    pq = s.ps()
                    s.mm(pq.t[0:n, 0:n], qT, kT, True, True, [kq.r], [pq.r])
                    s.tt("dve", w["P"].t[0:n, 0:n], pq.t[0:n, 0:n], w["D"].t[0:n, 0:n], ALU.mult, [pq.r, w["D"].r], [w["P"].r])
                    ppt = s.ps()
                    s.tp(ppt.t[0:n, 0:n], w["P"].t[0:n, 0:n], s.idf.t[0:n, 0:n], [w["P"].r, s.idf.r], [ppt.r])
                    s.cp("act", w["PT"].t[0:n, 0:n], ppt.t[0:n, 0:n], [ppt.r], [w["PT"].r])
                    s.cp("act", w["qf"].t[:, 0:n], qT, [kq.r], [w["qf"].r])
                    if GSTEP <= 7:
                        continue
                    for c in range(G["ncs"]):
                        Sc = Sp if i < 16 else S0[c]
                        p1 = s.ps()
                        s.mm(p1.t[0:n, 0:128], w["wT"].t[:, 0:n], Sc.t[:, :], True, True, [w["wT"].r, Sc.r], [p1.r])
                        s.stt("dve", w["vn"].t[0:n, :], p1.t[0:n, 0:128], col.t[0:n, 9 + c:10 + c], w["vn"].t[0:n, :],
                              ALU.mult, ALU.add, [p1.r, col.r, w["vn"].r], [w["vn"].r])
                        p2 = s.ps()
                        s.mm(p2.t[0:n, 0:128], w["qf"].t[:, 0:n], Sc.t[:, :], True, True, [w["qf"].r, Sc.r], [p2.r])
                        if c == 0:
                            s.ts("dve", w["oa"].t[0:n, :], p2.t[0:n, 0:128], col.t[0:n, 5:6], None, ALU.mult, None,
                                 [p2.r, col.r], [w["oa"].r])
                        else:
                            s.stt("dve", w["oa"].t[0:n, :], p2.t[0:n, 0:128], col.t[0:n, 5 + c:6 + c], w["oa"].t[0:n, :],
                                  ALU.mult, ALU.add, [p2.r, col.r, w["oa"].r], [w["oa"].r])
                        p4 = s.ps()
                        s.mm(p4.t[:, 0:128], w["kd%d" % c].t[0:n, :], w["vn"].t[0:n, :], True, True,
                             [w["kd%d" % c].r, w["vn"].r], [p4.r])
                        s.stt("dve", Sc.t[:, :], Sc.t[:, :], EC.t[:, i, c, h:h + 1], p4.t[:, 0:128], ALU.mult, ALU.add,
                              [Sc.r, EC.r, p4.r], [Sc.r])
                    p5 = s.ps()
                    s.mm(p5.t[0:n, 0:128], w["PT"].t[0:n, 0:n], w["vn"].t[0:n, :], True, True, [w["PT"].r, w["vn"].r], [p5.r])
                    s.tt("dve", w["oa"].t[0:n, :], w["oa"].t[0:n, :], p5.t[0:n, 0:128], ALU.add, [p5.r, w["oa"].r], [w["oa"].r])
                    if GSTEP <= 8:
                        continue
                    s.act(w["o2"].t[0:n, :], w["oa"].t[0:n, :], AF.Square, [w["oa"].r], [w["o2"].r, col.r],
                          accum_out=col.t[0:n, 13:14])
                    s.ts("dve", col.t[0:n, 14:15], col.t[0:n, 13:14], 1.0 / 128, 1e-6, ALU.mult, ALU.add, [col.r], [col.r])
                    s.act(col.t[0:n, 14:15], col.t[0:n, 14:15], AF.Sqrt, [col.r], [col.r])
                    s.op("dve", lambda g, col=col, n=n: g.reciprocal(col.t[0:n, 15:16], col.t[0:n, 14:15]), [col.r], [col.r])
                    s.stt("dve", w["on"].t[0:n, :], w["oa"].t[0:n, :], col.t[0:n, 15:16], nw.t[0:n, :], ALU.mult, ALU.mult,
                          [w["oa"].r, col.r, nw.r], [w["on"].r])
                    pf = s.ps()
                    s.tp(pf.t[:, 0:n], w["on"].t[0:n, :], s.idf.t[0:n, 0:n], [w["on"].r, s.idf.r], [pf.r])
                    s.tt("dve", gs.t[:, t0:t0 + n], pf.t[:, 0:n], zT.t[:, t0:t0 + n], ALU.mult, [pf.r, zT.r], [gs.r])
                s.dma("sp", Gap[:, h, :], gs.t[:, :], [gs.r], [s.GTd.r])
                s.dma("sp", s.out["gd_p"].t.ap()[h, :, :], Sp.t[:, :], [Sp.r], [s.out["gd_p"].r])
                for b in range(4):
                    s.dma("sp", s.out["gd_s"].t.ap()[b, h, :, :], S0[b].t[:, :], [S0[b].r], [s.out["gd_s"].r])

    def setup_dsa(self):
        s = self
        s.din("dsa_w_in", [D, 7312])
        s.din("dsa_w_out", [D, D])
        s.din("d_bias_p", [32, 128, 2048])
        s.din("d_causal", [128, 128])
        s.dout("d_kv", [TT, 1024])
        s.dout("d_ki", [TT, 128])
        s.KT = s.dram("KT", [64, 8, TT], BF16)
        s.VK = s.dram("VK", [128, NTILE, 512], BF16)
        s.KIT = s.dram("KIT", [128, TT], BF16)
        s.WI = s.dram("WI", [128, NTILE, 16], F32)
        s.MK = s.dram("MK", [16, 128, 2048], BF16)

    def dsa_proj(self, xT):
        s = self
        W = s.inp["dsa_w_in"].t.ap()
        KC = list(range(16))
        qst = s.pool([64, 4, 512], BF16, 2)
        zst = s.pool([128, 2, 512], BF16, 2)
        kst = s.pool([64, 4, 512], BF16, 2)
        kvs = s.pool([128, NTILE, 256], F32, 2)
        vbs = s.pool([128, NTILE, 256], BF16, 2)
        QTap, ZTap, KTap, VKap, QIap = s.QT.t.ap(), s.ZT.t.ap(), s.KT.t.ap(), s.VK.t.ap(), s.QN.t.ap()
        okv = s.out["d_kv"].t.ap()
        oki = s.out["d_ki"].t.ap()
        qi = 0
        for blk in range(29):
            wt = s.nxt(s.wp, "wi")
            ncol = 256 if blk < 28 else 144
            s.load_w(wt, W, D, blk * 256, ncol)
            if blk < 8:
                for (t0, tn) in TBS:
                    st = qst[qi % 2]
                    qi += 1
                    for j in range(4):
                        ps = s.ps()
                        s.ws_mm(ps.t[0:64, 0:tn], ps.r, wt, (j * 64, j * 64 + 64), xT, KC, t0, tn, [xT.r])
                        if j % 2:
                            s.act(st.t[:, j, 0:tn], ps.t[0:64, 0:tn], AF.Copy, [ps.r], [st.r], scale=0.125)
                        else:
                            s.ts("dve", st.t[:, j, 0:tn], ps.t[0:64, 0:tn], 0.125, None, ALU.mult, None, [ps.r], [st.r])
                    s.dma("sp", QTap[:, blk * 4:blk * 4 + 4, t0:t0 + tn], st.t[:, :, 0:tn], [st.r], [s.QT.r])
            elif blk < 12:
                kb = blk - 8
                kv = kvs[blk % 2]
                for i, (t0, tn) in enumerate(TILES):
                    ps = s.ps()
                    s.as_mm(ps.t[0:tn, 0:256], ps.r, wt, (0, 256), xT, KC, t0, tn, [xT.r])
                    s.cp(s.ev_eng(), kv.t[0:tn, i, :], ps.t[0:tn, 0:256], [ps.r], [kv.r])
                s.dma("sp", okv[0:2048, kb * 256:(kb + 1) * 256].rearrange("(i p) n -> p i n", p=128), kv.t[:, 0:16, :],
                      [kv.r], [s.out["d_kv"].r])
                s.dma("sp", okv[2048:TT, kb * 256:(kb + 1) * 256], kv.t[0:NS, 16, :], [kv.r], [s.out["d_kv"].r])
                if blk < 10:
                    for (t0, tn) in TBS:
                        st = kst[qi % 2]
                        qi += 1
                        for j in range(4):
                            ps = s.ps()
                            s.ws_mm(ps.t[0:64, 0:tn], ps.r, wt, (j * 64, j * 64 + 64), xT, KC, t0, tn, [xT.r])
                            s.cp(s.ev_eng(), st.t[:, j, 0:tn], ps.t[0:64, 0:tn], [ps.r], [st.r])
                        s.dma("sp", KTap[:, kb * 4:kb * 4 + 4, t0:t0 + tn], st.t[:, :, 0:tn], [st.r], [s.KT.r])
                else:
                    vb = vbs[blk % 2]
                    s.cp("dve", vb.t[:, 0:16, :], kv.t[:, 0:16, :], [kv.r], [vb.r])
                    s.cp("dve", vb.t[0:NS, 16, :], kv.t[0:NS, 16, :], [kv.r], [vb.r])
                    vo = (blk - 10) * 256
                    s.dma("sp", VKap[:, 0:16, vo:vo + 256], vb.t[:, 0:16, :], [vb.r], [s.VK.r])
                    s.dma("sp", VKap[0:NS, 16, vo:vo + 256], vb.t[0:NS, 16, :], [vb.r], [s.VK.r])
            elif blk < 20:
                zb = blk - 12
                for (t0, tn) in TBS:
                    st = zst[qi % 2]
                    qi += 1
                    for j in range(2):
                        ps = s.ps()
                        s.ws_mm(ps.t[:, 0:tn], ps.r, wt, (j * 128, j * 128 + 128), xT, KC, t0, tn, [xT.r])
                        s.act(st.t[:, j, 0:tn], ps.t[:, 0:tn], AF.Silu, [ps.r], [st.r])
                    s.dma("sp", ZTap[:, zb * 2:zb * 2 + 2, t0:t0 + tn], st.t[:, :, 0:tn], [st.r], [s.ZT.r])
            elif blk < 28:
                ib = blk - 20
                for (t0, tn) in TBS:
                    st = zst[qi % 2]
                    qi += 1
                    for j in range(2):
                        ps = s.ps()
                        s.ws_mm(ps.t[:, 0:tn], ps.r, wt, (j * 128, j * 128 + 128), xT, KC, t0, tn, [xT.r])
                        s.cp(s.ev_eng(), st.t[:, j, 0:tn], ps.t[:, 0:tn], [ps.r], [st.r])
                    for j in range(2):
                        s.dma("sp", QIap[ib * 2 + j, :, t0:t0 + tn], st.t[:, j, 0:tn], [st.r], [s.QN.r])
            else:
                for (t0, tn) in TBS:
                    st = zst[qi % 2]
                    qi += 1
                    ps = s.ps()
                    s.ws_mm(ps.t[:, 0:tn], ps.r, wt, (0, 128), xT, KC, t0, tn, [xT.r])
                    s.cp("act", st.t[:, 0, 0:tn], ps.t[:, 0:tn], [ps.r], [st.r])
                    s.dma("sp", s.KIT.t.ap()[:, t0:t0 + tn], st.t[:, 0, 0:tn], [st.r], [s.KIT.r])
                kv = kvs[blk % 2]
                for i, (t0, tn) in enumerate(TILES):
                    ps = s.ps()
                    s.as_mm(ps.t[0:tn, 0:144], ps.r, wt, (0, 144), xT, KC, t0, tn, [xT.r])
                    s.cp("dve", kv.t[0:tn, i, 0:144], ps.t[0:tn, 0:144], [ps.r], [kv.r])
                s.dma("sp", oki[0:2048, :].rearrange("(i p) n -> p i n", p=128), kv.t[:, 0:16, 0:128], [kv.r],
                      [s.out["d_ki"].r])
                s.dma("sp", oki[2048:TT, :], kv.t[0:NS, 16, 0:128], [kv.r], [s.out["d_ki"].r])
                s.dma("sp", s.WI.t.ap()[:, 0:16, :], kv.t[:, 0:16, 128:144], [kv.r], [s.WI.r])
                s.dma("sp", s.WI.t.ap()[0:NS, 16, :], kv.t[0:NS, 16, 128:144], [kv.r], [s.WI.r])

    def dsa_prompt(self):
        s = self
        QIap, MKap, QTap, ZTap, Gap = s.QN.t.ap(), s.MK.t.ap(), s.QT.t.ap(), s.ZT.t.ap(), s.GTd.t.ap()
        kiT = s.sb([128, TT], BF16, name="d_kiT")
        s.dma("sp", kiT.t[:, :], s.KIT.t.ap()[:, :], [s.KIT.r], [kiT.r])
        wi = s.sb([128, NTILE, 16], F32, name="d_wi")
        s.dma("sp", wi.t[:, :, :], s.WI.t.ap()[:, :, :], [s.WI.r], [wi.r])
        s.ts("dve", wi.t[:, :, :], wi.t[:, :, :], (128 ** -0.5) * 0.25, None, ALU.mult, None, [wi.r], [wi.r])
        cneg = s.sb([128, 128], F32, name="d_cneg")
        s.dma("sp", cneg.t[:, :], s.inp["d_causal"].t.ap()[:, :], [], [cneg.r])
        with s.scope():
            qib_p = s.pool([128, 16, 128], BF16, 2)
            I_p = s.pool([128, 2048], F32, 2)
            R_p = s.pool([128, 512], F32, 3)
            jk = s.sb([128, 2048], BF16, name="d_jk")
            mk_p = s.pool([128, 2048], BF16, 2)
            cl = s.pool([128, 8], F32, 2)
            for n in range(16):
                t0 = n * 128
                nk = t0 + 128
                qib = qib_p[n % 2]
                s.dma("sp", qib.t[:, :, :], QIap[:, :, t0:t0 + 128].rearrange("h p t -> p h t"), [s.QN.r], [qib.r])
                I = I_p[n % 2]
                for h in range(16):
                    for c0 in range(0, nk, 512):
                        cn = min(512, nk - c0)
                        ps = s.ps()
                        s.mm(ps.t[:, 0:cn], qib.t[:, h, :], kiT.t[:, c0:c0 + cn], True, True, [qib.r, kiT.r], [ps.r])
                        R = s.nxt(R_p, "dri")
                        s.act(R.t[:, 0:cn], ps.t[:, 0:cn], AF.Relu, [ps.r], [R.r])
                        if h == 0:
                            s.ts("dve", I.t[:, c0:c0 + cn], R.t[:, 0:cn], wi.t[:, n, 0:1], None, ALU.mult, None,
                                 [R.r, wi.r], [I.r])
                        else:
                            s.stt("dve", I.t[:, c0:c0 + cn], R.t[:, 0:cn], wi.t[:, n, h:h + 1], I.t[:, c0:c0 + cn],
                                  ALU.mult, ALU.add, [R.r, wi.r, I.r], [I.r])
                s.tt("dve", I.t[:, t0:nk], I.t[:, t0:nk], cneg.t[:, :], ALU.add, [I.r, cneg.r], [I.r])
                c = cl[n % 2]
                s.memset("dve", c.t[:, 0:1], -256.0, [c.r])
                for itn in range(32):
                    wdt = 256.0 * (0.5 ** itn)
                    s.ts("dve", jk.t[:, 0:nk], I.t[:, 0:nk], c.t[:, 0:1], wdt, ALU.subtract, ALU.is_ge, [I.r, c.r], [jk.r])
                    s.red("dve", c.t[:, 1:2], jk.t[:, 0:nk], ALU.add, [jk.r], [c.r])
                    s.ts("dve", c.t[:, 2:3], c.t[:, 1:2], 256.0, wdt, ALU.is_ge, ALU.mult, [c.r], [c.r])
                    s.tt("dve", c.t[:, 0:1], c.t[:, 0:1], c.t[:, 2:3], ALU.add, [c.r], [c.r])
                mk = mk_p[n % 2]
                s.ts("dve", mk.t[:, 0:nk], I.t[:, 0:nk], c.t[:, 0:1], NEG, ALU.is_lt, ALU.mult, [I.r, c.r], [mk.r])
                s.dma("sp", MKap[n, :, 0:nk], mk.t[:, 0:nk], [mk.r], [s.MK.r])
        with s.scope():
            kT = s.sb([64, 8, SEQ], BF16, name="d_kT")
            s.dma("sp", kT.t[:, :, :], s.KT.t.ap()[:, :, 0:SEQ], [s.KT.r], [kT.r])
            vtok = s.sb([128, 16, 512], BF16, name="d_vtok")
            s.dma("sp", vtok.t[:, :, :], s.VK.t.ap()[:, 0:16, :], [s.VK.r], [vtok.r])
            mk_p = s.pool([128, 2048], BF16, 2)
            tb_p = s.pool([128, 2048], BF16, 3)
            qb_p = s.pool([64, 32, 128], BF16, 2)
            zb_p = s.pool([128, 16, 128], BF16, 2)
            gs_p = s.pool([128, 16, 128], BF16, 2)
            S_p = s.pool([128, 2048], F32, 2)
            E_p = s.pool([128, 2048], BF16, 2)
            P_p = s.pool([128, 2048], BF16, 2)
            PT_p = s.pool([128, 16, 128], BF16, 2)
            smp = s.pool([128, 8], F32, 4)
            Tbap = s.inp["d_bias_p"].t.ap()
            it = 0
            for n in range(16):
                t0 = n * 128
                nk = t0 + 128
                nkt = nk // 128
                mk = mk_p[n % 2]
                qb = qb_p[n % 2]
                zb = zb_p[n % 2]
                gs = gs_p[n % 2]
                s.dma("sp", mk.t[:, 0:nk], MKap[n, :, 0:nk], [s.MK.r], [mk.r])
                s.dma("sp", qb.t[:, :, :], QTap[:, :, t0:t0 + 128], [s.QT.r], [qb.r])
                s.dma("sp", zb.t[:, :, :], ZTap[:, 0:16, t0:t0 + 128], [s.ZT.r], [zb.r])
                po = None
                for h in range(32):
                    kvh = h // 4
                    tb = s.nxt(tb_p, "dtbi")
                    s.dma("pool", tb.t[:, 0:nk], Tbap[h, :, 1920 - t0:2048], [], [tb.r])
                    Sb = S_p[it % 2]
                    E = E_p[it % 2]
                    P = P_p[it % 2]
                    PT = PT_p[it % 2]
                    sm = smp[it % 4]
                    it += 1
                    for c0 in range(0, nk, 512):
                        cn = min(512, nk - c0)
                        ps = s.ps()
                        s.mm(ps.t[:, 0:cn], s.idb.t[:, :], tb.t[:, c0:c0 + cn], True, False, [s.idb.r, tb.r], [ps.r], inc=False)
                        s.mm(ps.t[:, 0:cn], s.idb.t[:, :], mk.t[:, c0:c0 + cn], False, False, [s.idb.r, mk.r], [ps.r],
                             inc=False)
                        s.mm(ps.t[:, 0:cn], qb.t[:, h, :], kT.t[:, kvh, c0:c0 + cn], False, True, [qb.r, kT.r], [ps.r])
                        s.cp("act", Sb.t[:, c0:c0 + cn], ps.t[:, 0:cn], [ps.r], [Sb.r])
                    s.red("dve", sm.t[:, 0:1], Sb.t[:, 0:nk], ALU.max, [Sb.r], [sm.r])
                    s.ts("dve", sm.t[:, 1:2], sm.t[:, 0:1], -1.0, None, ALU.mult, None, [sm.r], [sm.r])
                    s.act(E.t[:, 0:nk], Sb.t[:, 0:nk], AF.Exp, [Sb.r, sm.r], [E.r, sm.r], bias=sm.t[:, 1:2], scale=1.0,
                          accum_out=sm.t[:, 2:3])
                    s.op("dve", lambda g, sm=sm: g.reciprocal(sm.t[:, 3:4], sm.t[:, 2:3]), [sm.r], [sm.r])
                    s.ts("dve", P.t[:, 0:nk], E.t[:, 0:nk], sm.t[:, 3:4], None, ALU.mult, None, [E.r, sm.r], [P.r])
                    for g0 in range(0, nkt, 8):
                        gn = min(8, nkt - g0)
                        pst = s.ps()
                        pstb = pst.t[:, :].bitcast(BF16)
                        for kt in range(gn):
                            s.tp(pstb[:, kt * 128:(kt + 1) * 128], P.t[:, (g0 + kt) * 128:(g0 + kt + 1) * 128], s.idb.t[:, :],
                                 [P.r, s.idb.r], [pst.r], inc=(kt == gn - 1))
                        s.cp(s.ev_eng(), PT.t[:, g0:g0 + gn, :], pstb[:, 0:gn * 128].rearrange("p (a b) -> p a b", a=gn),
                             [pst.r], [PT.r])
                    j = h % 2
                    if j == 0:
                        po = s.ps()
                    for kt in range(nkt):
                        s.mm(po.t[j * 64:(j + 1) * 64, 0:128], vtok.t[:, kt, kvh * 64:(kvh + 1) * 64], PT.t[:, kt, :],
                             kt == 0, kt == nkt - 1, [vtok.r, PT.r], [po.r], inc=(j == 1 and kt == nkt - 1))
                    if j == 1:
                        pr = h // 2
                        s.tt("dve", gs.t[:, pr, :], po.t[:, 0:128], zb.t[:, pr, :], ALU.mult, [po.r, zb.r], [gs.r])
                s.dma("sp", Gap[:, 0:16, t0:t0 + 128], gs.t[:, :, :], [gs.r], [s.GTd.r])

    def setup_dsa_sample(self):
        s = self
        s.din("d_kidx_pool", [NPOOL * 128, 128])
        s.din("d_k_pool", [NPOOL * 128, 512])
        s.din("d_v_pool", [NPOOL * 128, 512])
        s.din("pt_loc", [4, 128], I32)
        s.din("d_iota", [128, 1])
        s.din("d_bias_l", [128, 16, 128])
        s.din("d_bias_31", [128, 128])
        s.din("d_bias_n", [4, 128])
        s.din("d_cneg4", [4, 4])
        s.din("d_selm", [128, 8])
        s.din("d_perm", [128, 128])
        s.OS = s.dram("OS", [NS, D], F32)

    def dsa_sample(self):
        s = self
        iot = s.sb([128, 1], F32, name="ds_iota")
        s.dma("sp", iot.t[:, :], s.inp["d_iota"].t.ap()[:, :], [], [iot.r])
        B31 = s.sb([128, 128], F32, name="ds_b31")
        s.dma("sp", B31.t[:, :], s.inp["d_bias_31"].t.ap()[:, :], [], [B31.r])
        Bl = s.sb([128, 16, 128], F32, name="ds_bl")
        s.dma("sp", Bl.t[:, :, :], s.inp["d_bias_l"].t.ap()[:, :, :], [], [Bl.r])
        cs = s.sb([128, 400], F32, name="ds_cs")
        s.dma("sp", cs.t[0:4, 0:128], s.inp["d_bias_n"].t.ap()[:, :], [], [cs.r])
        s.dma("sp", cs.t[0:4, 128:132], s.inp["d_cneg4"].t.ap()[:, :], [], [cs.r])
        s.dma("sp", cs.t[:, 136:144], s.inp["d_selm"].t.ap()[:, :], [], [cs.r])
        s.dma("sp", cs.t[:, 144:272], s.inp["d_perm"].t.ap()[:, :], [], [cs.r])
        ones = s.sb([128, 128], F32, name="ds_ones")
        s.memset("dve", ones.t[:, :], 1.0, [ones.r])
        ptb = s.sb([128, 128], I32, name="ds_ptb")
        ridx = s.sb([128, 3, 128], I32, name="ds_ridx")
        qiTb = s.sb([128, 16, 4], BF16, name="ds_qi")
        wrow = s.sb([1, 64], F32, name="ds_wrow")
        Wb = s.sb([128, 64], F32, name="ds_Wb")
        IT = s.sb([128, 129, 4], F32, name="ds_IT")
        MT = s.sb([128, 129, 4], F32, name="ds_MT")
        cmpt = s.sb([128, 129, 4], F32, name="ds_cmp")
        lo = s.sb([128, 16], F32, name="ds_lo")
        qT2 = s.sb([128, 32, 4], BF16, name="ds_qT2")
        kn2 = s.sb([128, 8, 4], BF16, name="ds_kn2")
        kin = s.sb([128, 4], BF16, name="ds_kin")
        vnew = s.sb([4, 512], BF16, name="ds_vnew")
        LT = s.sb([128, 129, 128], F32, name="ds_LT")
        ET = s.sb([128, 129, 128], BF16, name="ds_ET")
        kip_p = s.pool([128, 128], F32, 4)
        kvp_p = s.pool([128, 512], F32, 3)
        ktb_p = s.pool([128, 4, 128], BF16, 2)
        R_p = s.pool([128, 512], F32, 2)
        vb_p = s.pool([128, 512], BF16, 2)
        w128 = s.pool([128, 128], F32, 4)
        osl = s.sb([128, 64], F32, name="ds_osl")
        KIpool = s.inp["d_kidx_pool"].t.ap()
        Kpool = s.inp["d_k_pool"].t.ap()
        Vpool = s.inp["d_v_pool"].t.ap()
        QIap, QTap, KTap, VKap, WIap = s.QN.t.ap(), s.QT.t.ap(), s.KT.t.ap(), s.VK.t.ap(), s.WI.t.ap()
        OSap = s.OS.t.ap()
        WSC = (128 ** -0.5) * 0.25
        selm = cs.t[:, 136:144]
        perm = cs.t[:, 144:272]
        DSTEP = int(os.environ.get("DS_STEP", "99"))
        for bi in range(int(os.environ.get("DS_NB", "4"))):
            c0 = SEQ + bi * 4
            s.dma("sp", ptb.t[:, :], s.inp["pt_loc"].t.ap()[bi:bi + 1, :].to_broadcast([128, 128]), [], [ptb.r])
            s.ts("dve", ridx.t[:, 0, :], ptb.t[:, :], 128.0, iot.t[:, 0:1], ALU.mult, ALU.add, [ptb.r, iot.r], [ridx.r])
            s.ts("dve", ridx.t[:, 1, :], ridx.t[:, 0, :], 2.0, None, ALU.mult, None, [ridx.r], [ridx.r])
            s.ts("dve", ridx.t[:, 2, :], ridx.t[:, 0, :], 2.0, 1.0, ALU.mult, ALU.add, [ridx.r], [ridx.r])
            s.dma("sp", qiTb.t[:, :, :], QIap[:, :, c0:c0 + 4].rearrange("h p t -> p h t"), [s.QN.r], [qiTb.r])
            s.dma("sp", wrow.t[0:1, :].rearrange("p (t h) -> p t h", t=4),
                  WIap[bi * 4:bi * 4 + 4, 16, :].unsqueeze(0), [s.WI.r], [wrow.r])
            ps = s.ps()
            s.mm(ps.t[:, 0:64], ones.t[0:1, :], wrow.t[0:1, :], True, True, [ones.r, wrow.r], [ps.r])
            s.ts("dve", Wb.t[:, :].rearrange("p (h t) -> p h t", h=16), ps.t[:, 0:64].rearrange("p (t h) -> p h t", t=4),
                 WSC, None, ALU.mult, None, [ps.r], [Wb.r])
            s.memset("pool", qT2.t[:, :, :], 0.0, [qT2.r])
            for kvh in range(8):
                b0 = (kvh % 2) * 64
                s.dma("sp", qT2.t[b0:b0 + 64, kvh * 4:kvh * 4 + 4, :], QTap[:, kvh * 4:kvh * 4 + 4, c0:c0 + 4], [s.QT.r], [qT2.r])
            for par in range(2):
                s.dma("sp", kn2.t[par * 64:(par + 1) * 64, 0:4, :], KTap[:, par:8:2, c0:c0 + 4], [s.KT.r], [kn2.r])
            s.dma("sp", kin.t[:, :], s.KIT.t.ap()[:, c0:c0 + 4], [s.KIT.r], [kin.r])
            s.dma("sp", vnew.t[0:4, :], VKap[bi * 4:bi * 4 + 4, 16, :], [s.VK.r], [vnew.r])
            qif = qiTb.t[:, :, :].rearrange("p h t -> p (h t)")

            def score_tail(psS, nrow, ncols, dst):
                R = s.nxt(R_p, "dsr")
                npg = ncols // 64
                s.act(R.t[0:nrow, 0:ncols], psS.t[0:nrow, 0:ncols], AF.Relu, [psS.r], [R.r])
                Rv = R.t[0:nrow, 0:ncols].rearrange("p (g c) -> p g c", g=npg)
                s.tt("dve", Rv, Rv, Wb.t[0:nrow, :].unsqueeze(1).to_broadcast([nrow, npg, 64]), ALU.mult, [R.r, Wb.r], [R.r])
                s.red("dve", dst, R.t[0:nrow, 0:ncols].rearrange("p (g h t) -> p g t h", g=npg, h=16), ALU.add, [R.r], [IT.r])

            for j0 in range(0, 128, 8):
                psS = s.ps()
                for half in range(2):
                    pst = s.ps()
                    for jj in range(4):
                        j = j0 + half * 4 + jj
                        kip = s.nxt(kip_p, "dskip")
                        s.dma_gather(kip.t[:, :], KIpool, ridx.t[:, 0, j:j + 1], [ridx.r], [kip.r])
                        s.tp(pst.t[:, jj * 128:(jj + 1) * 128], kip.t[:, :], s.idf.t[:, :], [kip.r, s.idf.r], [pst.r],
                             inc=(jj == 3))
                    ktb = s.nxt(ktb_p, "dsktb")
                    s.cp("act", ktb.t[:, :, :], pst.t[:, :].rearrange("p (a b) -> p a b", a=4), [pst.r], [ktb.r])
                    for jj in range(4):
                        cc = (half * 4 + jj) * 64
                        s.mm(psS.t[:, cc:cc + 64], ktb.t[:, jj, :], qif, True, True, [ktb.r, qiTb.r], [psS.r],
                             inc=(half == 1 and jj == 3))
                score_tail(psS, 128, 512, IT.t[:, j0:j0 + 8, :])
            s.memset("pool", IT.t[:, 128, :], -1e30, [IT.r])
            psN = s.ps()
            s.mm(psN.t[0:4, 0:64], kin.t[:, :], qif, True, True, [kin.r, qiTb.r], [psN.r])
            score_tail(psN, 4, 64, IT.t[0:4, 128:129, :])
            s.tt("dve", IT.t[0:4, 128, :], IT.t[0:4, 128, :], cs.t[0:4, 128:132], ALU.add, [IT.r, cs.r], [IT.r])
            if DSTEP <= 1:
                continue
            s.memset("dve", lo.t[:, 0:4], -256.0, [lo.r])
            for itn in range(32):
                wdt = 256.0 * (0.5 ** itn)
                s.ts("dve", lo.t[:, 4:8], lo.t[:, 0:4], wdt, None, ALU.add, None, [lo.r], [lo.r])
                s.tt("dve", cmpt.t[:, :, :], IT.t[:, :, :], lo.t[:, 4:8].unsqueeze(1).to_broadcast([128, 129, 4]), ALU.is_ge,
                     [IT.r, lo.r], [cmpt.r])
                s.red("dve", lo.t[:, 8:12], cmpt.t[:, :, :].rearrange("p j t -> p t j"), ALU.add, [cmpt.r], [lo.r])
                pc = s.ps()
                s.mm(pc.t[:, 0:4], ones.t[:, :], lo.t[:, 8:12], True, True, [ones.r, lo.r], [pc.r])
                s.ts("dve", lo.t[:, 12:16], pc.t[:, 0:4], 256.0, wdt, ALU.is_ge, ALU.mult, [pc.r], [lo.r])
                s.tt("dve", lo.t[:, 0:4], lo.t[:, 0:4], lo.t[:, 12:16], ALU.add, [lo.r], [lo.r])
            s.tt("dve", MT.t[:, :, :], IT.t[:, :, :], lo.t[:, 0:4].unsqueeze(1).to_broadcast([128, 129, 4]), ALU.is_lt,
                 [IT.r, lo.r], [MT.r])
            s.ts("dve", MT.t[:, :, :], MT.t[:, :, :], NEG, None, ALU.mult, None, [MT.r], [MT.r])
            if DSTEP <= 2:
                continue
            for j0 in range(0, 128, 4):
                psL = s.ps()
                for jj in range(4):
                    j = j0 + jj
                    kvp = s.nxt(kvp_p, "dskvp")
                    s.dma_gather(kvp.t[:, :], Kpool, ridx.t[:, 0, j:j + 1], [ridx.r], [kvp.r])
                    psK = s.ps()
                    for q4 in range(4):
                        s.tp(psK.t[:, q4 * 128:(q4 + 1) * 128], kvp.t[:, q4 * 128:(q4 + 1) * 128], s.idf.t[:, :],
                             [kvp.r, s.idf.r], [psK.r], inc=(q4 == 3))
                    ktb = s.nxt(ktb_p, "dsktb")
                    s.cp("act", ktb.t[:, :, :], psK.t[:, :].rearrange("p (a b) -> p a b", a=4), [psK.r], [ktb.r])
                    for kvh in range(8):
                        s.mm(psL.t[:, jj * 128 + kvh * 16:jj * 128 + kvh * 16 + 16], ktb.t[:, kvh // 2, :],
                             qT2.t[:, kvh * 4:kvh * 4 + 4, :].rearrange("p g t -> p (g t)"), True, True,
                             [ktb.r, qT2.r], [psL.r], inc=(jj == 3 and kvh == 7))
                pv = psL.t[:, :].rearrange("p (a b) -> p a b", a=4)
                if j0 < 112:
                    bias_ap = B31.t[:, :].unsqueeze(1).to_broadcast([128, 4, 128])
                    br = B31.r
                else:
                    bias_ap = Bl.t[:, j0 - 112:j0 - 108, :]
                    br = Bl.r
                s.tt("dve", LT.t[:, j0:j0 + 4, :], pv, bias_ap, ALU.add, [psL.r, br], [LT.r])
                s.tt("dve", LT.t[:, j0:j0 + 4, :].rearrange("p a (h t) -> p a h t", t=4),
                     LT.t[:, j0:j0 + 4, :].rearrange("p a (h t) -> p a h t", t=4),
                     MT.t[:, j0:j0 + 4, :].unsqueeze(2).to_broadcast([128, 4, 32, 4]), ALU.add, [LT.r, MT.r], [LT.r])
            s.memset("pool", LT.t[:, 128, :], NEG, [LT.r])
            psLn = s.ps()
            for kvh in range(8):
                s.mm(psLn.t[0:4, kvh * 16:kvh * 16 + 16], kn2.t[:, kvh // 2, :],
                     qT2.t[:, kvh * 4:kvh * 4 + 4, :].rearrange("p g t -> p (g t)"), True, True, [kn2.r, qT2.r],
                     [psLn.r], inc=(kvh == 7))
            s.tt("dve", LT.t[0:4, 128, :], psLn.t[0:4, 0:128], cs.t[0:4, 0:128], ALU.add, [psLn.r, cs.r], [LT.r])
            s.tt("dve", LT.t[0:4, 128, :].rearrange("p (h t) -> p h t", t=4), LT.t[0:4, 128, :].rearrange("p (h t) -> p h t", t=4),
                 MT.t[0:4, 128, :].unsqueeze(1).to_broadcast([4, 32, 4]), ALU.add, [LT.r, MT.r], [LT.r])
            if DSTEP <= 3:
                continue
            pm = s.nxt(w128, "dsw")
            s.red("dve", pm.t[:, :], LT.t[:, :, :].rearrange("p j c -> p c j"), ALU.max, [LT.r], [pm.r])
            pt_ = s.ps()
            s.tp(pt_.t[:, 0:128], pm.t[:, :], s.idf.t[:, :], [pm.r, s.idf.r], [pt_.r])
            s.red("dve", lo.t[:, 4:5], pt_.t[:, 0:128], ALU.max, [pt_.r], [lo.r])
            dmx = s.nxt(w128, "dsw")
            s.ts("dve", dmx.t[:, :], s.idf.t[:, :], lo.t[:, 4:5], None, ALU.mult, None, [s.idf.r, lo.r], [dmx.r])
            pb = s.ps()
            s.mm(pb.t[:, 0:128], ones.t[:, :], dmx.t[:, :], True, True, [ones.r, dmx.r], [pb.r])
            mxb = s.nxt(w128, "dsw")
            s.cp("act", mxb.t[:, :], pb.t[:, 0:128], [pb.r], [mxb.r])
            s.tt("dve", LT.t[:, :, :], LT.t[:, :, :], mxb.t[:, :].unsqueeze(1).to_broadcast([128, 129, 128]), ALU.subtract,
                 [LT.r, mxb.r], [LT.r])
            s.act(ET.t[:, :, :], LT.t[:, :, :], AF.Exp, [LT.r], [ET.r])
            ets = s.nxt(w128, "dsw")
            s.red("dve", ets.t[:, :], ET.t[:, :, :].rearrange("p j c -> p c j"), ALU.add, [ET.r], [ets.r])
            pd = s.ps()
            s.mm(pd.t[:, 0:1], ets.t[:, :], ones.t[:, 0:1], True, True, [ets.r, ones.r], [pd.r])
            s.op("dve", lambda g, pd=pd: g.reciprocal(lo.t[:, 5:6], pd.t[:, 0:1]), [pd.r], [lo.r])
            if DSTEP <= 4:
                continue
            po = s.ps()
            for j in range(128):
                kvp = s.nxt(kvp_p, "dskvp")
                s.dma_gather(kvp.t[:, :], Vpool, ridx.t[:, 0, j:j + 1], [ridx.r], [kvp.r])
                vb = s.nxt(vb_p, "dsvb")
                s.cp("act" if j % 2 else "dve", vb.t[:, :], kvp.t[:, :], [kvp.r], [vb.r])
                s.mm(po.t[:, :], ET.t[:, j, :], vb.t[:, :], j == 0, False, [ET.r, vb.r], [po.r], inc=False)
            s.mm(po.t[:, :], ET.t[0:4, 128, :], vnew.t[0:4, :], False, True, [ET.r, vnew.r], [po.r])
            R = s.nxt(R_p, "dsr")
            s.tt("dve", R.t[:, :].rearrange("p (k d) -> p k d", k=8), po.t[:, :].rearrange("p (k d) -> p k d", k=8),
                 selm.unsqueeze(2).to_broadcast([128, 8, 64]), ALU.mult, [po.r, cs.r], [R.r])
            s.red("dve", osl.t[:, :], R.t[:, :].rearrange("p (k d) -> p d k", k=8), ALU.add, [R.r], [osl.r])
            s.ts("dve", osl.t[:, :], osl.t[:, :], lo.t[:, 5:6], None, ALU.mult, None, [osl.r, lo.r], [osl.r])
            pp = s.ps()
            s.mm(pp.t[:, 0:64], perm, osl.t[:, :], True, True, [cs.r, osl.r], [pp.r])
            op_ = s.nxt(w128, "dsw")
            s.cp("act", op_.t[:, 0:64], pp.t[:, 0:64], [pp.r], [op_.r])
            for t in range(4):
                s.dma("sp", OSap[bi * 4 + t:bi * 4 + t + 1, :].rearrange("o (h d) -> (o h) d", d=64),
                      op_.t[t * 32:(t + 1) * 32, 0:64], [op_.r], [s.OS.r])
        ot = s.nxt(s.tokp, "toki")
        s.dma("sp", ot.t[0:NS, :], OSap[:, :], [s.OS.r], [ot.r])
        zs = s.sb([128, 16, NS], BF16, name="ds_zs")
        gss = s.sb([128, 16, NS], BF16, name="ds_gss")
        s.dma("sp", zs.t[:, :, :], s.ZT.t.ap()[:, 0:16, SEQ:TT], [s.ZT.r], [zs.r])
        for g in range(0, 16, 4):
            ps = s.ps()
            for j in range(4):
                cc = (g + j) * 128
                s.tp(ps.t[:, j * 16:j * 16 + 16], ot.t[0:NS, cc:cc + 128], s.idf.t[0:NS, 0:NS], [ot.r, s.idf.r], [ps.r],
                     inc=(j == 3))
            s.tt("dve", gss.t[:, g:g + 4, :], ps.t[:, 0:64].rearrange("p (a b) -> p a b", a=4), zs.t[:, g:g + 4, :], ALU.mult,
                 [ps.r, zs.r], [gss.r])
        s.dma("sp", s.GTd.t.ap()[:, 0:16, SEQ:TT], gss.t[:, :, :], [gss.r], [s.GTd.r])

    def build(self):
        s = self
        s.setup()
        s.setup_swa()
        s.setup_s5()
        s.setup_gdn()
        s.setup_dsa()
        s.setup_dsa_sample()
        Xcur = s.inp["xin"]
        for li in range(s.n_layers):
            if os.environ.get("DS_DEBUG") and li < 3:
                continue
            if li == 0:
                with s.scope():
                    kT = s.sb([64, 4, TT], BF16, name="a_kT")
                    vtok = s.sb([128, NTILE, 256], BF16, name="a_vtok")
                    kv32 = s.sb([128, 2, 512], F32, name="a_kv32")
                    with s.scope():
                        xT = s.sb([128, 16, TT], BF16, name="BIG")
                        s.load_input(xT)
                        if s.stop == "load":
                            s.dma("sp", s.XTd.t.ap()[:, :, :], xT.t[:, :, :], [xT.r], [s.XTd.r])
                        else:
                            s.swa_proj(xT, kT, vtok, kv32)
                    if s.stop not in ("load", "proj"):
                        with s.scope():
                            s.swa_attn(kT, vtok)
                if s.stop in ("load", "proj", "attn"):
                    break
                nkc = 16
                Wo = s.inp["a_w_out"].t.ap()
            elif li == 1:
                with s.scope():
                    xT = s.sb([128, 16, TT], BF16, name="BIG")
                    s.dma("sp", xT.t[:, :, :], s.XTd.t.ap()[:, :, :], [s.XTd.r], [xT.r])
                    s.s5_proj(xT)
                with s.scope():
                    s.s5_core()
                with s.scope():
                    y1T = s.sb([128, 16, TT], BF16, name="BIG")
                    s.s5_glu(y1T)
                nkc = 16
                Wo = s.inp["s5_w_out"].t.ap()
            elif li == 2:
                with s.scope():
                    xT = s.sb([128, 16, TT], BF16, name="BIG")
                    s.dma("sp", xT.t[:, :, :], s.XTd.t.ap()[:, :, :], [s.XTd.r], [xT.r])
                    s.gdn_proj(xT)
                if s.stop == "gproj":
                    break
                with s.scope():
                    s.gdn_core()
                nkc = 32
                Wo = s.inp["gdn_w_out"].t.ap()
            elif li == 3:
                with s.scope():
                    xT = s.sb([128, 16, TT], BF16, name="BIG")
                    s.dma("sp", xT.t[:, :, :], s.XTd.t.ap()[:, :, :], [s.XTd.r], [xT.r])
                    s.dsa_proj(xT)
                if s.stop == "dproj":
                    break
                with s.scope():
                    s.dsa_prompt()
                if s.stop != "dprompt":
                    with s.scope():
                        s.dsa_sample()
                nkc = 16
                Wo = s.inp["dsa_w_out"].t.ap()
            last = (li == s.n_layers - 1)
            Xnext = s.out["y"] if last else s.X[li % 2]
            with s.scope():
                s.out_proj(li, nkc, Wo, Xcur)
            if s.stop == "outproj":
                break
            with s.scope():
                xnT = s.sb([128, 16, TT], BF16, name="BIG")
                s.ln_pass(li, xnT)
                if s.stop != "ln":
                    s.ple_stage(li, xnT, Xnext, last)
            Xcur = Xnext
        s.finish()
        return s.nc


def _rel_bucket_np(dist):
    n = np.maximum(dist, 0)
    exact = 16
    with np.errstate(divide="ignore"):
        logb = exact + (np.log(np.maximum(n, exact).astype(np.float32) / np.float32(exact))
                        / np.float32(np.log(2048 / exact)) * (32 - exact)).astype(np.int32)
    return np.where(n < exact, n, np.minimum(logb, 31)).astype(np.int64)


def _static_tables():
    q = np.arange(128)[:, None]
    k = np.arange(256)[None, :]
    dist = q + 128 - k
    bkt_p = _rel_bucket_np(dist)
    mask_p = np.where((dist >= 0) & (dist < 128), 0.0, NEG).astype(np.float32)
    tok = np.arange(4)
    bkt_s = np.zeros((4, 4, 144), np.int64)
    mask_s = np.full((4, 4, 144), NEG, np.float32)
    for bi in range(4):
        for i in range(4):
            for j in range(128):
                d = 128 + i - j
                bkt_s[i, bi, j] = _rel_bucket_np(np.array(d))
                if 0 <= d < 128:
                    mask_s[i, bi, j] = 0.0
            for jj in range(16):
                bj, i2 = jj // 4, jj % 4
                d = i - i2
                bkt_s[i, bi, 128 + jj] = _rel_bucket_np(np.array(d))
                if bj == bi and d >= 0:
                    mask_s[i, bi, 128 + jj] = 0.0
    return bkt_p, mask_p, bkt_s, mask_s


_PROG_CACHE = {}


def _get_prog(n_layers=4, debug=False):
    key = (n_layers, debug)
    if key not in _PROG_CACHE:
        p = Prog(n_layers, debug)
        p.build()
        _PROG_CACHE[key] = p
    return _PROG_CACHE[key]


def make_in_maps(inputs):
    f = lambda a: np.ascontiguousarray(np.asarray(a, dtype=np.float32))
    x_prompt = f(inputs["x_prompt"])
    x_sample = f(inputs["x_sample"])
    p_prompt = f(inputs["p_prompt"])
    p_sample = f(inputs["p_sample"])
    rel_bias = f(inputs["rel_bias"])
    bkt_p, mask_p, bkt_s, mask_s = _static_tables()
    ident = np.eye(128, dtype=np.float32)
    a_bias_p = np.ascontiguousarray(rel_bias[bkt_p].transpose(0, 2, 1))
    bs = rel_bias[bkt_s]
    hord = np.array([kvh * 8 + g2 * 2 + par for kvh in range(4) for par in range(2) for g2 in range(4)])
    bs = bs[..., hord]
    a_bias_s = bs.transpose(3, 0, 1, 2).reshape(2, 64, 4, 144)
    a_bias_s = np.ascontiguousarray(a_bias_s.transpose(1, 0, 2, 3).reshape(64, 1152))
    a_mask_s = np.broadcast_to(mask_s[None], (32, 4, 4, 144)).reshape(2, 64, 4, 144)
    a_mask_s = np.ascontiguousarray(a_mask_s.transpose(1, 0, 2, 3).reshape(64, 1152))
    sinks = f(inputs["a_sinks"])[0]
    a_sink_p = np.ascontiguousarray(np.broadcast_to(sinks[None, :], (128, 32)))
    a_sink_s = np.ascontiguousarray(np.repeat(sinks[hord], 4).reshape(2, 64).T)
    shared = dict(
        ident=ident, ln_g=f(inputs["ln_g"]), ln_b=f(inputs["ln_b"]), ple_gate_w=f(inputs["ple_gate_w"]),
        ple_w=f(inputs["ple_w"]), a_w_in=f(inputs["a_w_in"])[0], a_w_out=f(inputs["a_w_out"])[0],
        a_bias_p=a_bias_p, a_mask_p=mask_p, a_bias_s=a_bias_s, a_mask_s=a_mask_s, a_sink_p=a_sink_p,
        a_sink_s=a_sink_s,
    )
    if "s5_w_in" in inputs:
        g = np.arange(128)
        cst = np.zeros((128, 256), np.float32)
        cst[g, g // 2] = 1.0
        cst[g, 64 + g % 2] = 1.0
        cst[g, 66 + g // 64] = 1.0
        cst[g, 68 + g % 64] = 1.0
        cst[g, 132 + (g // 16) % 2] = 1.0
        shared.update(
            s5_w_in=f(inputs["s5_w_in"])[0], s5_w_glu=f(inputs["s5_w_glu"])[0], s5_w_out=f(inputs["s5_w_out"])[0],
            s5_a_re=f(inputs["s5_a_re"])[0], s5_a_im=f(inputs["s5_a_im"])[0],
            s5_log_dt=f(inputs["s5_log_dt"])[0].reshape(128, 1), s5_b_re=f(inputs["s5_b_re"])[0],
            s5_b_im=f(inputs["s5_b_im"])[0], s5_c_re=f(inputs["s5_c_re"])[0], s5_c_im=f(inputs["s5_c_im"])[0],
            s5_d_fm=np.ascontiguousarray(f(inputs["s5_d"])[0].reshape(16, 128).T), s5_const=cst)
        state_s5 = f(inputs["state_s5"])[0].reshape(32, 128, 128)
    if "gdn_w_in" in inputs:
        def masks(n, cs):
            i = np.arange(n)
            same = (i[:, None] // cs) == (i[None, :] // cs)
            triU = (same & (i[:, None] <= i[None, :])).astype(np.float32)
            bones = same.astype(np.float32)
            Mb = np.where(same & (i[None, :] <= i[:, None]), 0.0, 30000.0).astype(np.float32)
            strict = (same & (i[None, :] < i[:, None])).astype(np.float32)
            ncs = n // cs
            sel = [np.broadcast_to(((i // cs) == c)[:, None], (n, 128)).astype(np.float32) for c in range(ncs)]
            cm = np.stack([((i // cs) == c) for c in range(ncs)], 1).astype(np.float32)
            return np.ascontiguousarray(np.concatenate([triU, bones, Mb, strict] + sel + [cm], 1))
        cwv = f(inputs["gdn_conv_w"])[0]
        shared.update(
            gdn_w_in=f(inputs["gdn_w_in"])[0], gdn_w_out=f(inputs["gdn_w_out"])[0],
            gdn_cw=np.ascontiguousarray(cwv.reshape(4, 64, 128).transpose(2, 1, 0)),
            gdn_ab=np.ascontiguousarray(np.broadcast_to(
                np.concatenate([f(inputs["gdn_a_log"])[0], f(inputs["gdn_dt_bias"])[0]])[None], (128, 64))),
            gdn_nw=np.ascontiguousarray(np.broadcast_to(f(inputs["gdn_norm_w"])[0][None], (128, 128))),
            gdn_mp=masks(128, 64), gdn_ms=masks(16, 4))
        state_gdn = f(inputs["state_gdn"])[0]
        state_gc = f(inputs["state_gdn_conv"])[0]
    if "dsa_w_in" in inputs:
        qq = np.arange(128)[:, None]
        uu = np.arange(2048)[None, :]
        bk = _rel_bucket_np(qq + 1920 - uu)
        d_bias_p = np.ascontiguousarray(rel_bias[bk].transpose(2, 0, 1))
        d_causal = np.where(np.arange(128)[None, :] > np.arange(128)[:, None], -1e30, 0.0).astype(np.float32)
        shared.update(dsa_w_in=f(inputs["dsa_w_in"])[0], dsa_w_out=f(inputs["dsa_w_out"])[0], d_bias_p=d_bias_p,
                      d_causal=d_causal)
    if "cache_d_kv" in inputs:
        cc = np.arange(128)
        head_c = (cc // 16) * 4 + (cc // 4) % 4
        t_c = cc % 4
        pp_ = np.arange(128)[:, None, None]
        jj_ = np.arange(16)[None, :, None]
        dist_l = 16384 + t_c[None, None, :] - ((112 + jj_) * 128 + pp_)
        d_bias_l = np.ascontiguousarray(rel_bias[_rel_bucket_np(dist_l), head_c[None, None, :]])
        d_bias_31 = np.ascontiguousarray(np.broadcast_to(rel_bias[31, head_c][None, :], (128, 128)))
        dist_n = t_c[None, :] - np.arange(4)[:, None]
        d_bias_n = np.ascontiguousarray(rel_bias[_rel_bucket_np(dist_n), head_c[None, :]])
        d_cneg4 = np.where(np.arange(4)[:, None] > np.arange(4)[None, :], -1e30, 0.0).astype(np.float32)
        d_selm = (np.arange(8)[None, :] == (cc // 16)[:, None]).astype(np.float32)
        d_perm = np.zeros((128, 128), np.float32)
        d_perm[cc, t_c * 32 + head_c] = 1.0
        shared.update(
            d_kidx_pool=f(inputs["cache_d_kidx"])[0].reshape(5120 * 128, 128)[:NPOOL * 128],
            d_k_pool=np.ascontiguousarray(f(inputs["cache_d_kv"])[0].reshape(5120 * 128, 2, 512)[:NPOOL * 128, 0, :]),
            d_v_pool=np.ascontiguousarray(f(inputs["cache_d_kv"])[0].reshape(5120 * 128, 2, 512)[:NPOOL * 128, 1, :]),
            d_iota=np.arange(128, dtype=np.float32).reshape(128, 1), d_bias_l=d_bias_l, d_bias_31=d_bias_31,
            d_bias_n=d_bias_n, d_cneg4=d_cneg4, d_selm=d_selm, d_perm=d_perm)
        page_table = np.ascontiguousarray(np.asarray(inputs["page_table"], dtype=np.int32) % NPOOL)
    cache_a = f(inputs["cache_a_kv"])[0].reshape(32, 128, 512)
    maps = []
    for c in range(8):
        b = c % 4
        m = dict(shared)
        m["xin"] = np.ascontiguousarray(np.concatenate([x_prompt[b], x_sample[4 * c:4 * c + 4].reshape(NS, D)], 0))
        m["p_all"] = np.ascontiguousarray(
            np.concatenate([p_prompt[:, b], p_sample[:, 4 * c:4 * c + 4].reshape(4, NS, 256)], 1))
        m["a_cache"] = np.ascontiguousarray(cache_a[4 * c:4 * c + 4])
        if "s5_w_in" in inputs:
            m["s5_state"] = np.ascontiguousarray(state_s5[4 * c:4 * c + 4])
        if "cache_d_kv" in inputs:
            m["pt_loc"] = np.ascontiguousarray(page_table[4 * c:4 * c + 4])
        if "gdn_w_in" in inputs:
            m["gdn_state"] = np.ascontiguousarray(state_gdn[4 * c:4 * c + 4])
            m["gdn_cbuf"] = np.ascontiguousarray(state_gc[4 * c:4 * c + 4].reshape(12, 8192))
        maps.append(m)
    return maps


def kernel(**inputs):
    prog = _get_prog()
    maps = make_in_maps(inputs)
    maps = [{k: v for k, v in m.items() if k in prog.inp} for m in maps]
    res = run_bass_kernel_spmd(prog.nc, maps, core_ids=list(range(8)))
    R = res.results
    y_prompt = np.stack([R[b]["y"][:SEQ] for b in range(4)])
    y_sample = np.concatenate([R[c]["y"][SEQ:].reshape(4, 4, D) for c in range(8)], 0)
    a_kv_p = np.stack([R[b]["a_kv_p"].reshape(128, 2, 4, 64) for b in range(4)])[None]
    a_kv_s = np.concatenate([R[c]["a_kv_s"].reshape(4, 128, 2, 4, 64) for c in range(8)], 0)[None]
    z = lambda *sh: np.zeros(sh, np.float32)
    gd_p = np.stack([R[b]["gd_p"] for b in range(4)])[None]
    gd_s = np.concatenate([R[c]["gd_s"] for c in range(8)], 0)[None]
    gc_p = np.stack([R[b]["gc"][0:3] for b in range(4)])[None]
    gc_s = np.concatenate([R[c]["gc"][3:15].reshape(4, 3, 8192) for c in range(8)], 0)[None]
    dkv_p = np.stack([R[b]["d_kv"][:SEQ].reshape(SEQ, 2, 8, 64) for b in range(4)])[None]
    dkv_s = np.concatenate([R[c]["d_kv"][SEQ:].reshape(4, 4, 2, 8, 64) for c in range(8)], 0)[None]
    dki_p = np.stack([R[b]["d_ki"][:SEQ] for b in range(4)])[None]
    dki_s = np.concatenate([R[c]["d_ki"][SEQ:].reshape(4, 4, 128) for c in range(8)], 0)[None]
    s5_p = np.stack([R[b]["s5_p"].reshape(128, 64, 2) for b in range(4)])[None]
    s5_s = np.concatenate([R[c]["s5_s"].reshape(4, 128, 64, 2) for c in range(8)], 0)[None]
    return (y_prompt, y_sample, a_kv_p, a_kv_s,
            s5_p, s5_s, gd_p, gd_s, gc_p, gc_s, dkv_p, dkv_s, dki_p, dki_s)
```
